# Optimizing a Trainium2 kernel written in Bass

```python
import jax
import jax.numpy as jnp
from jax import lax
import numpy as np

D_MODEL = 1024
BATCH = 8
SEQ = 2048
DEPTH = 1
DEC_BATCH = 32
DEC_SEQ = 16
PAST_LEN = 2048

CHUNK = 64
EPS = 1e-6
GN_EPS = 64e-5
ROPE_THETA = 10000.0
RW_HEAD_DIM = 64
RW_HEADS = D_MODEL // RW_HEAD_DIM
RW_W = RW_HEADS * RW_HEAD_DIM
D_DECAY_LORA = max(32, int(round(1.8 * D_MODEL ** 0.5 / 32)) * 32)
D_AAA_LORA = D_DECAY_LORA
D_GATE_LORA = max(32, int(round(0.6 * D_MODEL ** 0.8 / 32)) * 32)
ATT_HEAD_DIM = 64
ATT_HEADS = D_MODEL // ATT_HEAD_DIM
ATT_W = ATT_HEADS * ATT_HEAD_DIM
IDX_HEADS = 8
IDX_DIM = 64
TOPK_MAX = 256
Q_BLOCK = 64
PEER_HEADS = 8
PEER_NKEYS = 128
PEER_TOPK = 16
PEER_DKEY = 128
PEER_DHALF = PEER_DKEY // 2
N_EXPERTS = PEER_NKEYS * PEER_NKEYS
PEER_BLOCK = 128
RW_IN = 3 * RW_W + D_DECAY_LORA + D_AAA_LORA + D_GATE_LORA
DSA_IN = 3 * ATT_W + IDX_HEADS * IDX_DIM + IDX_DIM + IDX_HEADS
GATE_IN = 2 * D_MODEL
IN_W = RW_IN + DSA_IN + GATE_IN

kernel_name = 'rwkv7_dsa_peer_streaming_step'


def _rmsnorm(x, g):
    xf = x.astype(jnp.float32)
    y = xf * lax.rsqrt(jnp.mean(xf * xf, axis=-1, keepdims=True) + EPS)
    return (y * g.astype(jnp.float32)).astype(x.dtype)


def _split(a, sizes):
    offs = np.cumsum(sizes)[:-1].tolist()
    return jnp.split(a, offs, axis=-1)


def _rope(x, pos):
    half = x.shape[-1] // 2
    inv = ROPE_THETA ** (-jnp.arange(half, dtype=jnp.float32) / half)
    ang = pos.astype(jnp.float32)[:, None] * inv[None, :]
    cos = jnp.cos(ang)[:, None, :]
    sin = jnp.sin(ang)[:, None, :]
    xf = x.astype(jnp.float32)
    x1, x2 = xf[..., :half], xf[..., half:]
    return jnp.concatenate([x1 * cos - x2 * sin, x2 * cos + x1 * sin], axis=-1).astype(x.dtype)


def _rwkv7_step(S, inp):
    r, w, k, v, a, b = inp
    sa = jnp.einsum('bhij,bhj->bhi', S, a)
    S = S * w[:, :, None, :] + sa[..., None] * b[:, :, None, :] + v[..., None] * k[:, :, None, :]
    y = jnp.einsum('bhij,bhj->bhi', S, r)
    return S, y


def _rwkv7(P, shift_prev, S0, mu_rw, w0, w_up, a0, a_up, g_up, k_k, k_a, r_k, lnx_w, lnx_b):
    B, T, _ = P.shape
    f32 = jnp.float32
    P_prev = jnp.concatenate([shift_prev[:, None, :], P[:, :-1]], axis=1)
    X = P + (P_prev - P) * mu_rw
    r, k, v, wd, ad, gd = _split(X, [RW_W, RW_W, RW_W, D_DECAY_LORA, D_AAA_LORA, D_GATE_LORA])
    w_log = -jax.nn.softplus(-(w0 + jnp.tanh(wd) @ w_up).astype(f32)) - 0.5
    decay = jnp.exp(-jnp.exp(w_log))
    a = jax.nn.sigmoid((a0 + ad @ a_up).astype(f32))
    g = jax.nn.sigmoid(gd) @ g_up
    hd = lambda t: t.reshape(B, T, RW_HEADS, RW_HEAD_DIM).astype(f32)
    kk = hd(k * k_k)
    kk = kk * lax.rsqrt(jnp.sum(kk * kk, axis=-1, keepdims=True) + 1e-12)
    k_mod = k.astype(f32) * (1.0 + (a - 1.0) * k_a.astype(f32))
    r_h, w_h, k_h, v_h, a_h = hd(r), hd(decay), hd(k_mod), hd(v), hd(a)
    xs = tuple(jnp.moveaxis(t, 1, 0) for t in (r_h, w_h, k_h, v_h, -kk, kk * a_h))
    S_fin, ys = lax.scan(_rwkv7_step, S0.astype(f32), xs)
    y = jnp.moveaxis(ys, 0, 1)
    mean = jnp.mean(y, axis=-1, keepdims=True)
    var = jnp.mean(jnp.square(y - mean), axis=-1, keepdims=True)
    y = ((y - mean) * lax.rsqrt(var + GN_EPS)).reshape(B, T, RW_W) * lnx_w.astype(f32) + lnx_b.astype(f32)
    bonus = jnp.sum(r_h * k_h * r_k.astype(f32), axis=-1, keepdims=True) * v_h
    y = y + bonus.reshape(B, T, RW_W)
    out = (y * g.astype(f32)).astype(P.dtype)
    return out, S_fin.astype(P.dtype), P[:, -1]


def _sparse_attend(q, qi, wi, q_pos, k_all, v_all, ki_all, k_pos):
    B, T, H, dh = q.shape
    L = k_all.shape[1]
    topk = min(TOPK_MAX, L // 4)
    qb = min(Q_BLOCK, T)
    nb = T // qb
    k_chunk = k_pos // CHUNK
    gather = jax.vmap(lambda rows, idx: rows[idx])

    def block(args):
        qh, qih, wih, qp = args
        rel = jax.nn.relu(jnp.einsum('bqhd,bsd->bqhs', qih, ki_all))
        score = jnp.einsum('bqh,bqhs->bqs', wih, rel)
        adm = (qp[:, None] // CHUNK) >= k_chunk[None, :]
        score = jnp.where(adm[None], score, -jnp.inf)
        vals, idx = lax.top_k(score, topk)
        ok = vals > -jnp.inf
        ks = gather(k_all, idx)
        vs = gather(v_all, idx)
        s = jnp.einsum('bqhd,bqkhd->bhqk', qh, ks).astype(jnp.float32) * (dh ** -0.5)
        s = jnp.where(ok[:, None], s, -jnp.inf)
        p = jax.nn.softmax(s, axis=-1).astype(vs.dtype)
        return jnp.einsum('bhqk,bqkhd->bqhd', p, vs)

    to_blocks = lambda t: jnp.moveaxis(t.reshape((B, nb, qb) + t.shape[2:]), 1, 0)
    out = lax.map(block, (to_blocks(q), to_blocks(qi), to_blocks(wi), q_pos.reshape(nb, qb)))
    return jnp.moveaxis(out, 0, 1).reshape(B, T, H, dh)


def _dsa(P_dsa, pos, k_pos, cache_k, cache_v, cache_kidx, q_norm_w, k_norm_w):
    B, T, _ = P_dsa.shape
    q, k, v, qi, ki, wi = _split(P_dsa, [ATT_W, ATT_W, ATT_W, IDX_HEADS * IDX_DIM, IDX_DIM, IDX_HEADS])
    hs = lambda t: t.reshape(B, T, ATT_HEADS, ATT_HEAD_DIM)
    q = _rope(_rmsnorm(hs(q), q_norm_w), pos)
    k = _rope(_rmsnorm(hs(k), k_norm_w), pos)
    v = hs(v)
    qi = _rope(qi.reshape(B, T, IDX_HEADS, IDX_DIM), pos)
    ki = _rope(ki[:, :, None, :], pos)[:, :, 0]
    wi = wi * (IDX_HEADS * IDX_DIM) ** -0.5
    k_all = jnp.concatenate([cache_k, k], axis=1)
    v_all = jnp.concatenate([cache_v, v], axis=1)
    ki_all = jnp.concatenate([cache_kidx, ki], axis=1)
    o = _sparse_attend(q, qi, wi, pos, k_all, v_all, ki_all, k_pos)
    return o.reshape(B, T, ATT_W), k, v, ki


def _peer(h, w_pq, peer_keys, peer_u, peer_v):
    B, T, D = h.shape
    n = B * T
    nb = -(-n // PEER_BLOCK)
    hp = jnp.pad(h.reshape(n, D), ((0, nb * PEER_BLOCK - n), (0, 0))).reshape(nb, PEER_BLOCK, D)

    def block(hb):
        q = (hb @ w_pq).reshape(PEER_BLOCK, PEER_HEADS, 2, PEER_DHALF)
        s = jnp.einsum('nhpd,hpkd->nhpk', q, peer_keys).astype(jnp.float32)
        s1, i1 = lax.top_k(s[:, :, 0], PEER_TOPK)
        s2, i2 = lax.top_k(s[:, :, 1], PEER_TOPK)
        cand = (s1[..., :, None] + s2[..., None, :]).reshape(PEER_BLOCK, PEER_HEADS, PEER_TOPK * PEER_TOPK)
        cidx = (i1[..., :, None] * PEER_NKEYS + i2[..., None, :]).reshape(PEER_BLOCK, PEER_HEADS, PEER_TOPK * PEER_TOPK)
        sc, j = lax.top_k(cand, PEER_TOPK)
        e = jnp.take_along_axis(cidx, j, axis=-1)
        g = jax.nn.softmax(sc, axis=-1)
        act = jax.nn.gelu(jnp.einsum('nhkd,nd->nhk', peer_u[e], hb).astype(jnp.float32))
        return jnp.einsum('nhk,nhkd->nd', (g * act).astype(hb.dtype), peer_v[e])

    out = lax.map(block, hp).reshape(nb * PEER_BLOCK, D)[:n]
    return out.reshape(B, T, D)


def _layer(x, c, pos, k_pos, shift_prev, S0, cache_k, cache_v, cache_kidx, lp):
    mod = jax.nn.silu(c) @ lp['w_ada'] + lp['b_ada']
    sh1, sc1, g1, sh2, sc2, g2 = [m[:, None, :] for m in jnp.split(mod, 6, axis=-1)]
    h = _rmsnorm(x, lp['norm1_w']) * (1 + sc1) + sh1
    P = h @ lp['w_in']
    P_rw, P_dsa, P_gate = _split(P, [RW_IN, DSA_IN, GATE_IN])
    o_a, S_fin, shift_last = _rwkv7(P_rw, shift_prev, S0, lp['mu_rw'], lp['w0'], lp['w_up'], lp['a0'], lp['a_up'],
                                   lp['g_up'], lp['k_k'], lp['k_a'], lp['r_k'], lp['lnx_w'], lp['lnx_b'])
    o_b, k_new, v_new, ki_new = _dsa(P_dsa, pos, k_pos, cache_k, cache_v, cache_kidx, lp['q_norm_w'], lp['k_norm_w'])
    gate_a, gate_b = jnp.split(jax.nn.sigmoid(P_gate + lp['b_gate']), 2, axis=-1)
    m = gate_a * (o_a @ lp['w_proj_a']) + gate_b * (o_b @ lp['w_proj_b'])
    x = x + g1 * (m @ lp['w_out'])
    h2 = _rmsnorm(x, lp['norm2_w']) * (1 + sc2) + sh2
    x = x + g2 * _peer(h2, lp['w_pq'], lp['peer_keys'], lp['peer_u'], lp['peer_v'])
    return x, S_fin, shift_last, k_new, v_new, ki_new


def setup_inputs(seed: int = 0) -> dict:
    key = jax.random.key(seed)
    ks = iter(jax.random.split(key, 40))
    nrm = lambda shape, scale: jax.random.normal(next(ks), shape, jnp.float32) * scale
    D = D_MODEL
    L = DEPTH
    return {
        'x_prompt': nrm((BATCH, SEQ, D), 1.0),
        'x_sample': nrm((DEC_BATCH, DEC_SEQ, D), 1.0),
        'c_prompt': nrm((BATCH, D), 1.0),
        'c_sample': nrm((DEC_BATCH, D), 1.0),
        'cache_k': nrm((L, DEC_BATCH, PAST_LEN, ATT_HEADS, ATT_HEAD_DIM), 1.0),
        'cache_v': nrm((L, DEC_BATCH, PAST_LEN, ATT_HEADS, ATT_HEAD_DIM), 1.0),
        'cache_kidx': nrm((L, DEC_BATCH, PAST_LEN, IDX_DIM), 1.0),
        'state_wkv': nrm((L, DEC_BATCH, RW_HEADS, RW_HEAD_DIM, RW_HEAD_DIM), 0.5),
        'state_shift': nrm((L, DEC_BATCH, RW_IN), 1.0),
        'w_ada': nrm((L, D, 6 * D), 0.5 * D ** -0.5),
        'b_ada': nrm((L, 6 * D), 0.02),
        'norm1_w': 1.0 + nrm((L, D), 0.02),
        'w_in': nrm((L, D, IN_W), D ** -0.5),
        'b_gate': nrm((L, GATE_IN), 0.02),
        'mu_rw': jax.random.uniform(next(ks), (L, RW_IN), jnp.float32),
        'w0': nrm((L, RW_W), 1.0) - 1.0,
        'w_up': nrm((L, D_DECAY_LORA, RW_W), 0.5 * D_DECAY_LORA ** -0.5),
        'a0': nrm((L, RW_W), 0.1),
        'a_up': nrm((L, D_AAA_LORA, RW_W), D_AAA_LORA ** -0.5),
        'g_up': nrm((L, D_GATE_LORA, RW_W), D_GATE_LORA ** -0.5),
        'k_k': 0.85 + nrm((L, RW_W), 0.02),
        'k_a': 1.0 + nrm((L, RW_W), 0.02),
        'r_k': nrm((L, RW_HEADS, RW_HEAD_DIM), 0.1),
        'lnx_w': 1.0 + nrm((L, RW_W), 0.02),
        'lnx_b': nrm((L, RW_W), 0.02),
        'q_norm_w': 1.0 + nrm((L, ATT_HEAD_DIM), 0.02),
        'k_norm_w': 1.0 + nrm((L, ATT_HEAD_DIM), 0.02),
        'w_proj_a': nrm((L, RW_W, D), RW_W ** -0.5),
        'w_proj_b': nrm((L, ATT_W, D), ATT_W ** -0.5),
        'w_out': nrm((L, D, D), D ** -0.5),
        'norm2_w': 1.0 + nrm((L, D), 0.02),
        'w_pq': nrm((L, D, PEER_HEADS * PEER_DKEY), D ** -0.5),
        'peer_keys': nrm((L, PEER_HEADS, 2, PEER_NKEYS, PEER_DHALF), PEER_DHALF ** -0.5),
        'peer_u': nrm((L, N_EXPERTS, D), D ** -0.5),
        'peer_v': nrm((L, N_EXPERTS, D), (PEER_HEADS * PEER_TOPK) ** -0.5),
    }


def reference(x_prompt, x_sample, c_prompt, c_sample, cache_k, cache_v, cache_kidx, state_wkv, state_shift,
              w_ada, b_ada, norm1_w, w_in, b_gate, mu_rw, w0, w_up, a0, a_up, g_up, k_k, k_a, r_k, lnx_w, lnx_b,
              q_norm_w, k_norm_w, w_proj_a, w_proj_b, w_out, norm2_w, w_pq, peer_keys, peer_u, peer_v):
    B, T = x_prompt.shape[:2]
    Ts = x_sample.shape[1]
    past = cache_k.shape[2]
    dt = x_prompt.dtype
    pos_p = jnp.arange(T, dtype=jnp.int32)
    pos_s = past + jnp.arange(Ts, dtype=jnp.int32)
    kpos_s = jnp.arange(past + Ts, dtype=jnp.int32)
    empty_kv = jnp.zeros((B, 0, ATT_HEADS, ATT_HEAD_DIM), dt)
    empty_ki = jnp.zeros((B, 0, IDX_DIM), dt)
    zero_shift = jnp.zeros((B, RW_IN), dt)
    zero_wkv = jnp.zeros((B, RW_HEADS, RW_HEAD_DIM, RW_HEAD_DIM), dt)
    xp, xs = x_prompt, x_sample
    wkv_p, shift_p, k_p, v_p, ki_p = [], [], [], [], []
    wkv_s, shift_s, k_s, v_s, ki_s = [], [], [], [], []
    for l in range(DEPTH):
        lp = {'w_ada': w_ada[l], 'b_ada': b_ada[l], 'norm1_w': norm1_w[l], 'w_in': w_in[l], 'b_gate': b_gate[l],
              'mu_rw': mu_rw[l], 'w0': w0[l], 'w_up': w_up[l], 'a0': a0[l], 'a_up': a_up[l], 'g_up': g_up[l],
              'k_k': k_k[l], 'k_a': k_a[l], 'r_k': r_k[l], 'lnx_w': lnx_w[l], 'lnx_b': lnx_b[l],
              'q_norm_w': q_norm_w[l], 'k_norm_w': k_norm_w[l], 'w_proj_a': w_proj_a[l], 'w_proj_b': w_proj_b[l],
              'w_out': w_out[l], 'norm2_w': norm2_w[l], 'w_pq': w_pq[l], 'peer_keys': peer_keys[l],
              'peer_u': peer_u[l], 'peer_v': peer_v[l]}
        xp, a1, a2, a3, a4, a5 = _layer(xp, c_prompt, pos_p, pos_p, zero_shift, zero_wkv,
                                        empty_kv, empty_kv, empty_ki, lp)
        wkv_p.append(a1); shift_p.append(a2); k_p.append(a3); v_p.append(a4); ki_p.append(a5)
        xs, b1, b2, b3, b4, b5 = _layer(xs, c_sample, pos_s, kpos_s, state_shift[l], state_wkv[l],
                                        cache_k[l], cache_v[l], cache_kidx[l], lp)
        wkv_s.append(b1); shift_s.append(b2); k_s.append(b3); v_s.append(b4); ki_s.append(b5)
    return (xp, xs,
            jnp.stack(wkv_p), jnp.stack(shift_p), jnp.stack(k_p), jnp.stack(v_p), jnp.stack(ki_p),
            jnp.stack(wkv_s), jnp.stack(shift_s), jnp.stack(k_s), jnp.stack(v_s), jnp.stack(ki_s))
```

```python
import numpy as np
from contextlib import ExitStack
import concourse.bass as bass
import concourse.mybir as mybir
from concourse.bass_utils import run_bass_kernel_spmd

F32 = mybir.dt.float32
BF16 = mybir.dt.bfloat16
I32 = mybir.dt.int32
U32 = mybir.dt.uint32
AF = mybir.ActivationFunctionType
ALU = mybir.AluOpType
AX = mybir.AxisListType

ENGS = ["pe", "act", "dve", "pool", "sp"]
NDMA = {"sp": 40, "pool": 24, "act": 8}


class Dep:
    __slots__ = ("w", "r", "name")

    def __init__(self, name=""):
        self.w = None
        self.r = []
        self.name = name


class Bank:
    def __init__(self):
        self.last = {}
        self.pe_rows = None


class T:
    def __init__(self, h, name):
        self.h = h
        self.name = name
        self.dep = Dep(name)
        self.subs = {}
        self.bank = None

    def __getitem__(self, idx):
        return self.h[idx]

    def sub(self, key):
        if key not in self.subs:
            self.subs[key] = Dep(f"{self.name}.{key}")
        return self.subs[key]


class Slot:
    def __init__(self, t, i):
        self.base = t.h[:, i, :]
        self.dep = Dep(f"{t.name}[{i}]")
        self.bank = t.bank

    def __getitem__(self, idx):
        return self.base[idx]


class SlotAP:
    def __init__(self, t, ap):
        self.base = ap
        self.dep = Dep(t.name + "[ap]")
        self.bank = t.bank

    def __getitem__(self, idx):
        return self.base[idx]


def _dep(x):
    return x.dep if hasattr(x, "dep") else x


class Prog:
    def __init__(self, nc):
        self.nc = nc
        self.es = ExitStack()
        self.ops = {e: [] for e in ENGS}
        self.cnt = {e: 0 for e in ENGS}
        self.esem = {}
        for e in ENGS:
            self.esem[e] = self.es.enter_context(nc.semaphore("s_" + e))
        self.dsem = {}
        self.dval = {}
        self.dnext = {}
        for q, n in NDMA.items():
            self.dsem[q] = [self.es.enter_context(nc.semaphore(f"d_{q}{i}")) for i in range(n)]
            self.dval[q] = [0] * n
            self.dnext[q] = 0
        self.seen = {e: {} for e in ENGS}
        self.semobj = {}
        self.nwaits = 0

    def _uniq(self, name):
        if not hasattr(self, "_names"):
            self._names = {}
        k = self._names.get(name, 0)
        self._names[name] = k + 1
        return name if k == 0 else f"{name}__{k}"

    def sb(self, name, shape, dtype, stack=None):
        name = self._uniq(name)
        h = (stack or self.es).enter_context(self.nc.sbuf_tensor(name, list(shape), dtype))
        return T(h, name)

    def ps(self, name, shape, dtype=F32, stack=None):
        name = self._uniq(name)
        h = (stack or self.es).enter_context(self.nc.psum_tensor(name, list(shape), dtype))
        t = T(h, name)
        t.bank = Bank()
        return t

    def dram(self, name, shape, dtype, kind="Internal"):
        h = self.nc.dram_tensor(name, list(shape), dtype, kind=kind)
        return T(h, name)

    def _waits(self, eng, reads, writes, extra=()):
        evs = list(extra)
        for b in reads:
            b = _dep(b)
            if b.w is not None:
                evs.append(b.w)
        for b in writes:
            b = _dep(b)
            if b.w is not None:
                evs.append(b.w)
            evs.extend(b.r)
        need = {}
        for (key, sem, val, src) in evs:
            if src == "pe" and eng == "pe":
                continue
            if self.seen[eng].get(key, 0) >= val:
                continue
            if need.get(key, (None, 0))[1] < val:
                need[key] = (sem, val)
        for key, (sem, val) in need.items():
            self.seen[eng][key] = val
        return list(need.values())

    def _record(self, ev, reads, writes):
        for b in reads:
            _dep(b).r.append(ev)
        for b in writes:
            b = _dep(b)
            b.w = ev
            b.r = []

    def op(self, eng, fn, reads=(), writes=(), pe_rows=None):
        banks = {}
        for b in list(reads) + list(writes):
            bk = getattr(b, "bank", None)
            if bk is not None:
                banks[id(bk)] = bk
        extra = [ev for bk in banks.values() for e2, ev in bk.last.items() if e2 != eng]
        if eng == "pe" and pe_rows is not None:
            for bk in banks.values():
                if bk.pe_rows is not None and bk.pe_rows != pe_rows and "pe" in bk.last and (pe_rows[1] < 128 or bk.pe_rows[1] < 128):
                    k_, s_, v_, _ = bk.last["pe"]
                    extra.append((k_, s_, v_, "force"))
                bk.pe_rows = pe_rows
        waits = self._waits(eng, reads, writes, extra)
        self.cnt[eng] += 1
        ev = (eng, self.esem[eng], self.cnt[eng], eng)
        for bk in banks.values():
            bk.last[eng] = ev
        self._record(ev, reads, writes)
        self.ops[eng].append((waits, fn, (self.esem[eng], 1)))
        self.nwaits += len(waits)
        return ev

    def dma(self, q, out, in_, reads=(), writes=(), **kw):
        i = self.dnext[q]
        self.dnext[q] = (i + 1) % len(self.dsem[q])
        sem = self.dsem[q][i]
        key = (q, i)
        waits = self._waits(q, reads, writes)
        prev = self.dval[q][i]
        if prev > 0 and self.seen[q].get(key, 0) < prev:
            waits.append((sem, prev))
            self.seen[q][key] = prev
        self.dval[q][i] = prev + 16
        ev = (key, sem, prev + 16, "dma")
        self._record(ev, reads, writes)
        self.ops[q].append((waits, lambda e: e.dma_start(out=out, in_=in_, **kw), (sem, 16)))
        self.nwaits += len(waits)
        return ev

    def barrier(self):
        evs = []
        for e in ENGS:
            if self.cnt[e] > 0:
                evs.append((e, self.esem[e], self.cnt[e]))
        for q in self.dsem:
            for i, v in enumerate(self.dval[q]):
                if v > 0:
                    evs.append(((q, i), self.dsem[q][i], v))
        for e in ENGS:
            waits = []
            for key, sem, val in evs:
                if key == e:
                    continue
                if self.seen[e].get(key, 0) >= val:
                    continue
                self.seen[e][key] = val
                waits.append((sem, val))
            if waits:
                self.ops[e].append((waits, None, None))

    def emit(self):
        self.barrier()
        nc = self.nc
        with nc.Block() as block:
            def run(e, lst):
                for waits, fn, inc in lst:
                    for sem, val in waits:
                        e.wait_ge(sem, val)
                    if fn is not None:
                        ins = fn(e)
                        ins.then_inc(inc[0], inc[1])

            @block.tensor
            def _(e):
                run(e, self.ops["pe"])

            @block.scalar
            def _(e):
                run(e, self.ops["act"])

            @block.vector
            def _(e):
                run(e, self.ops["dve"])

            @block.gpsimd
            def _(e):
                run(e, self.ops["pool"])

            @block.sync
            def _(e):
                run(e, self.ops["sp"])
        self.es.close()
EPS = 1e-6
NT = 2112
TILES = [(i * 128, 128) for i in range(16)] + [(2048, 64)]
RW_IN = 3360
TOKW = 7016


def mm(P, out, lhsT, rhs, start, stop, reads, writes):
    rows = (lhsT.base_partition(), lhsT.partition_size())
    return P.op("pe", lambda e: e.matmul(out, lhsT, rhs, start=start, stop=stop), reads=reads, writes=writes, pe_rows=rows)


def tr(P, out, in_, ident, reads, writes):
    return P.op("pe", lambda e: e.transpose(out, in_, ident), reads=reads, writes=writes)


def act(P, out, in_, func, reads, writes, **kw):
    return P.op("act", lambda e: e.activation(out=out, in_=in_, func=func, **kw), reads=reads, writes=writes)


def ts(P, eng, out, in0, s1, s2, op0, op1=None, reads=(), writes=(), **kw):
    def f(e):
        if op1 is None:
            return e.tensor_scalar(out=out, in0=in0, scalar1=s1, scalar2=s2, op0=op0, **kw)
        return e.tensor_scalar(out=out, in0=in0, scalar1=s1, scalar2=s2, op0=op0, op1=op1, **kw)
    return P.op(eng, f, reads=reads, writes=writes)


def tt(P, eng, out, in0, in1, op, reads, writes):
    if eng == "pool":
        eng = "dve"
    return P.op(eng, lambda e: e.tensor_tensor(out=out, in0=in0, in1=in1, op=op), reads=reads, writes=writes)


def cp(P, eng, out, in_, reads, writes):
    if eng == "pool":
        eng = "act"
    if eng == "act":
        return act(P, out, in_, AF.Copy, reads, writes)
    return P.op(eng, lambda e: e.tensor_copy(out, in_), reads=reads, writes=writes)


class G:
    pass


def featvec(P, g, st, name, vec_ap, n):
    tmp = P.sb(name + "_t", [n, 128], F32, st)
    P.dma("sp", tmp[:], vec_ap.rearrange("(c p) -> c p", p=128), writes=[tmp])
    ps = P.ps(name + "_p", [128, n], F32, st)
    tr(P, ps[:], tmp[:], g.identf[0:n, 0:n], [tmp, g.identf], [ps])
    out = getattr(g, name)
    cp(P, "dve", out[:], ps[:], [ps], [out])
    return out


def stage0(P, g, D):
    g.identf = P.sb("identf", [128, 128], F32)
    g.identb = P.sb("identb", [128, 128], BF16)
    g.modT = P.sb("modT", [128, 48, 5], F32)
    g.n1w = P.sb("n1w", [128, 8], F32)
    g.n2w = P.sb("n2w", [128, 8], F32)
    g.bgT = P.sb("bgT", [128, 16], F32)
    g.scale1 = P.sb("scale1", [128, 8, 5], F32)
    g.scale2 = P.sb("scale2", [128, 8, 5], F32)
    st = ExitStack()
    P.dma("sp", g.identf[:], D["ident"][:, :], writes=[g.identf])
    cp(P, "dve", g.identb[:], g.identf[:], [g.identf], [g.identb])
    c5 = P.sb("c5", [5, 1024], F32, st)
    s5 = P.sb("s5", [5, 1024], F32, st)
    P.dma("sp", c5[:], D["cin"][:, :], writes=[c5])
    act(P, s5[:], c5[:], AF.Silu, [c5], [s5])
    sT = P.sb("sT", [128, 8, 5], F32, st)
    pst = P.ps("pst", [128, 8, 5], F32, st)
    for kc in range(8):
        tr(P, pst[:, kc, :], s5[0:5, kc * 128:(kc + 1) * 128], g.identf[0:5, 0:5], [s5, g.identf], [pst])
    cp(P, "dve", sT[:], pst[:], [pst], [sT])
    bada5 = P.sb("bada5", [5, 6144], F32, st)
    P.dma("sp", bada5[:], D["b_ada"][0:1, :].partition_broadcast(5), writes=[bada5])
    mod5 = P.sb("mod5", [5, 6144], F32, st)
    wa = [P.sb(f"wa{i}", [128, 8, 512], F32, st) for i in range(2)]
    pm = [P.ps(f"pm{i}", [5, 512], F32, st) for i in range(2)]
    for gi in range(12):
        w = wa[gi % 2]
        p = pm[gi % 2]
        P.dma("sp", w[:], D["w_ada"][:, gi * 512:(gi + 1) * 512].rearrange("(kc p) c -> p kc c", p=128), writes=[w])
        for kc in range(8):
            mm(P, p[:], sT[:, kc, :], w[:, kc, :], kc == 0, kc == 7, [sT, w], [p])
        tt(P, "dve", mod5[:, gi * 512:(gi + 1) * 512], p[:], bada5[:, gi * 512:(gi + 1) * 512], ALU.add, [p, bada5], [mod5])
    P.dma("sp", D["modd"][:, :], mod5[:], reads=[mod5], writes=[D["modd_t"]])
    pmt = P.ps("pmt", [128, 48, 5], F32, st)
    for c in range(48):
        tr(P, pmt[:, c, :], mod5[0:5, c * 128:(c + 1) * 128], g.identf[0:5, 0:5], [mod5, g.identf], [pmt])
    cp(P, "dve", g.modT[:], pmt[:], [pmt], [g.modT])
    n1 = featvec(P, g, st, "n1w", D["norm1_w"], 8)
    n2 = featvec(P, g, st, "n2w", D["norm2_w"], 8)
    g.bgT = featvec(P, g, st, "bgT", D["b_gate"], 16)
    for c in range(8):
        ts(P, "dve", g.scale1[:, c, :], g.modT[:, 8 + c, :], 1.0, n1[:, c:c + 1], ALU.add, ALU.mult, [g.modT, n1], [g.scale1])
        ts(P, "dve", g.scale2[:, c, :], g.modT[:, 32 + c, :], 1.0, n2[:, c:c + 1], ALU.add, ALU.mult, [g.modT, n2], [g.scale2])
    P.barrier()
    st.close()


def norm_to_featmajor(P, g, D, src_ap, hT, scale, shift_chunk0):
    st = ExitStack()
    xt = [P.sb(f"nx{i}", [128, 1024], F32, st) for i in range(2)]
    xn = [P.sb(f"nxn{i}", [128, 1024], BF16, st) for i in range(2)]
    junk = P.sb("njunk", [128, 1024], F32, st)
    ss = [P.sb(f"nss{i}", [128, 1], F32, st) for i in range(2)]
    rs = [P.sb(f"nrs{i}", [128, 1], F32, st) for i in range(2)]
    pT = [P.ps(f"npT{i}", [128, 8, 128], BF16, st) for i in range(2)]
    for ti, (r0, n) in enumerate(TILES):
        x, xb, s, r, p = xt[ti % 2], xn[ti % 2], ss[ti % 2], rs[ti % 2], pT[ti % 2]
        P.dma("sp", x[0:n, :], src_ap[r0:r0 + n, :], writes=[x])
        act(P, junk[0:n, :], x[0:n, :], AF.Square, [x], [junk, s], accum_out=s[0:n, :])
        act(P, s[0:n, :], s[0:n, :], AF.Sqrt, [s], [s], scale=1.0 / 1024, bias=g.epsc[0:n, :])
        P.op("dve", lambda e, r=r, s=s, n=n: e.reciprocal(r[0:n, :], s[0:n, :]), reads=[s], writes=[r])
        ts(P, "dve", xb[0:n, :], x[0:n, :], r[0:n, :], None, ALU.mult, None, [x, r], [xb])
        for kc in range(8):
            tr(P, p[:, kc, 0:n], xb[0:n, kc * 128:(kc + 1) * 128], g.identb[0:n, 0:n], [xb, g.identb], [p])
        for kc in range(8):
            if ti < 16:
                act(P, hT[:, kc, r0:r0 + n], p[:, kc, 0:n], AF.Identity, [p, scale, g.modT], [hT],
                    scale=scale[:, kc, 0:1], bias=g.modT[:, shift_chunk0 + kc, 0:1])
            else:
                for q in range(4):
                    act(P, hT[:, kc, r0 + 16 * q:r0 + 16 * q + 16], p[:, kc, 16 * q:16 * q + 16], AF.Identity,
                        [p, scale, g.modT], [hT], scale=scale[:, kc, 1 + q:2 + q], bias=g.modT[:, shift_chunk0 + kc, 1 + q:2 + q])
    P.barrier()
    st.close()


def stage2(P, g, D, hT):
    st = ExitStack()
    wf = [P.sb(f"wf{i}", [128, 8, 512], F32, st) for i in range(2)]
    wb = [P.sb(f"wb{i}", [128, 8, 512], BF16, st) for i in range(2)]
    stg = [P.sb(f"stg{i}", [128, 512], F32, st) for i in range(3)]
    pp = [P.ps(f"pp{i}", [128, 512], F32, st) for i in range(3)]
    k = 0
    groups = [(c0, min(512, TOKW - c0), False) for c0 in range(0, TOKW, 512)] + [(TOKW + i * 512, 512, True) for i in range(4)]
    for gi, (c0, gw, isgate) in enumerate(groups):
        w, b = wf[gi % 2], wb[gi % 2]
        P.dma("sp", w[:, :, 0:gw], D["w_in"][:, c0:c0 + gw].rearrange("(kc p) c -> p kc c", p=128), writes=[w])
        cp(P, "pool" if gi % 2 else "dve", b[:, :, 0:gw], w[:, :, 0:gw], [w], [b])
        if not isgate:
            for ti, (r0, n) in enumerate(TILES):
                p, s = pp[k % 3], stg[k % 3]
                for kc in range(8):
                    mm(P, p[0:n, 0:gw], hT[:, kc, r0:r0 + n], b[:, kc, 0:gw], kc == 0, kc == 7, [hT, b], [p])
                cp(P, "act" if k % 2 else "dve", s[0:n, 0:gw], p[0:n, 0:gw], [p], [s])
                P.dma("pool", D["Ptok"][r0:r0 + n, c0:c0 + gw], s[0:n, 0:gw], reads=[s], writes=[D["Ptok_t"].sub(ti)])
                k += 1
        else:
            for j in range(4):
                fch = (c0 - TOKW) // 128 + j
                for (n0, nb) in [(0, 512), (512, 512), (1024, 512), (1536, 512), (2048, 64)]:
                    p, s = pp[k % 3], stg[k % 3]
                    for kc in range(8):
                        mm(P, p[:, 0:nb], b[:, kc, j * 128:(j + 1) * 128], hT[:, kc, n0:n0 + nb], kc == 0, kc == 7, [hT, b], [p])
                    act(P, s[:, 0:nb], p[:, 0:nb], AF.Sigmoid, [p, g.bgT], [s], bias=g.bgT[:, fch:fch + 1])
                    P.dma("pool", D["GT"][fch * 128:(fch + 1) * 128, n0:n0 + nb], s[:, 0:nb], reads=[s], writes=[D["GT_t"]])
                    k += 1
    P.barrier()
    st.close()

C_Q, C_K, C_V, C_QI, C_KI, C_WI = 3360, 4384, 5408, 6432, 6944, 7008


def rope(P, eng, out4, in4, cosb, sinb, tmp, n, reads, writes):
    H = in4.shape[1]
    x1, x2 = in4[:, :, 0, :], in4[:, :, 1, :]
    t = [tmp[0:n, i, 0:H * 32].rearrange("p (h d) -> p h d", h=H) for i in range(4)]
    tt(P, eng, t[0], x1, cosb, ALU.mult, reads, [tmp])
    tt(P, eng, t[1], x2, sinb, ALU.mult, reads, [tmp])
    tt(P, eng, t[2], x2, cosb, ALU.mult, reads, [tmp])
    tt(P, eng, t[3], x1, sinb, ALU.mult, reads, [tmp])
    tt(P, eng, out4[:, :, 0, :], t[0], t[1], ALU.subtract, [tmp], writes)
    tt(P, eng, out4[:, :, 1, :], t[2], t[3], ALU.add, [tmp], writes)


def stage3_dsa(P, g, D):
    st = ExitStack()
    knw = P.sb("knw", [128, 64], F32, st)
    qnw = P.sb("qnw", [128, 64], F32, st)
    P.dma("sp", knw[:], D["k_norm_w"][0:1, :].partition_broadcast(128), writes=[knw])
    P.dma("sp", qnw[:], D["q_norm_w"][0:1, :].partition_broadcast(128), writes=[qnw])
    pd = [P.sb(f"pd{i}", [128, 3656], F32, st) for i in range(2)]
    cs = [P.sb(f"cs{i}", [128, 2, 32], F32, st) for i in range(2)]
    junk = P.sb("djunk", [128, 1024], F32, st)
    ssq = P.sb("dssq", [128, 16], F32, st)
    rst = P.sb("drst", [128, 16], F32, st)
    kn = P.sb("dkn", [128, 1024], F32, st)
    ko = [P.sb(f"dko{i}", [128, 1024], F32, st) for i in range(2)]
    kio = [P.sb(f"dkio{i}", [128, 64], F32, st) for i in range(2)]
    tmp = P.sb("dtmp", [128, 4, 512], F32, st)
    for ti, (r0, n) in enumerate(TILES):
        p, c = pd[ti % 2], cs[ti % 2]
        P.dma("sp", p[0:n, :], D["Ptok"][r0:r0 + n, C_Q:TOKW], reads=[D["Ptok_t"].sub(ti)], writes=[p])
        P.dma("sp", c[0:n, 0, :], D["cos"][r0:r0 + n, :], writes=[c])
        P.dma("sp", c[0:n, 1, :], D["sin"][r0:r0 + n, :], writes=[c])
        P.dma("pool", D["vout"][r0:r0 + n, :], D["Ptok"][r0:r0 + n, C_V:C_V + 1024], reads=[D["Ptok_t"].sub(ti)])
        k = p[0:n, C_K - C_Q:C_K - C_Q + 1024]
        act(P, junk[0:n, :], k, AF.Square, [p], [junk])
        P.op("dve", lambda e, n=n: e.tensor_reduce(out=ssq[0:n, :], in_=junk[0:n, :].rearrange("p (h d) -> p h d", h=16), axis=AX.X, op=ALU.add), reads=[junk], writes=[ssq])
        act(P, ssq[0:n, :], ssq[0:n, :], AF.Sqrt, [ssq], [ssq], scale=1.0 / 64, bias=g.epsc[0:n, :])
        P.op("dve", lambda e, n=n: e.reciprocal(rst[0:n, :], ssq[0:n, :]), reads=[ssq], writes=[rst])
        kn3 = kn[0:n, :].rearrange("p (h d) -> p h d", h=16)
        tt(P, "dve", kn3, k.rearrange("p (h d) -> p h d", h=16), rst[0:n, :].unsqueeze(2).to_broadcast([n, 16, 64]), ALU.mult, [p, rst], [kn])
        tt(P, "dve", kn3, kn3, knw[0:n, :].unsqueeze(1).to_broadcast([n, 16, 64]), ALU.mult, [kn, knw], [kn])
        o = ko[ti % 2]
        cosb = c[0:n, 0, :].unsqueeze(1).to_broadcast([n, 16, 32])
        sinb = c[0:n, 1, :].unsqueeze(1).to_broadcast([n, 16, 32])
        rope(P, "dve", o[0:n, :].rearrange("p (h t d) -> p h t d", h=16, t=2), kn[0:n, :].rearrange("p (h t d) -> p h t d", h=16, t=2),
             cosb, sinb, tmp, n, [kn, c], [o])
        P.dma("pool", D["kout"][r0:r0 + n, :], o[0:n, :], reads=[o], writes=[D["kout_t"].sub(ti)])
        ki = p[0:n, C_KI - C_Q:C_KI - C_Q + 64]
        oi = kio[ti % 2]
        rope(P, "pool", oi[0:n, :].rearrange("p (h t d) -> p h t d", h=1, t=2), ki.rearrange("p (h t d) -> p h t d", h=1, t=2),
             c[0:n, 0, :].unsqueeze(1), c[0:n, 1, :].unsqueeze(1), tmp, n, [p, c], [oi])
        P.dma("pool", D["kidx"][r0:r0 + n, :], oi[0:n, :], reads=[oi], writes=[D["kidx_t"].sub(ti)])
    P.dma("pool", D["shift"][0:1, :], D["Ptok"][2047:2048, 0:RW_IN], reads=[D["Ptok_t"].sub(15)])
    for q in range(4):
        P.dma("pool", D["shift"][1 + q:2 + q, :], D["Ptok"][2048 + 16 * q + 15:2048 + 16 * q + 16, 0:RW_IN], reads=[D["Ptok_t"].sub(16)])
    P.barrier()
    st.close()


def red(P, eng, out, in_, op, reads, writes):
    return P.op(eng, lambda e: e.tensor_reduce(out=out, in_=in_, axis=AX.X, op=op), reads=reads, writes=writes)


def stt(P, eng, out, in0, scalar, in1, op0, op1, reads, writes):
    return P.op(eng, lambda e: e.scalar_tensor_tensor(out=out, in0=in0, scalar=scalar, in1=in1, op0=op0, op1=op1), reads=reads, writes=writes)


def recip(P, out, in_, reads, writes):
    return P.op("dve", lambda e: e.reciprocal(out, in_), reads=reads, writes=writes)


def mset(P, eng, ap, val, writes):
    return P.op(eng, lambda e: e.memset(ap, val), writes=writes)

import os
STOP = int(os.environ.get('STOP', '99'))
SKIP = os.environ.get('SKIP', '')
NUX = int(os.environ.get('NUX', '18'))
NU = 18
UC = NU * 128
GN_EPS = 64e-5


def unit_rows(u):
    if u < 16:
        return [(0, 128 * u, 128)]
    b = 2048 + 32 * (u - 16)
    return [(0, b, 16), (64, b + 16, 16)]


def bload(P, tile, ap, n=128):
    P.dma("sp", tile[0:n, :], ap.partition_broadcast(n), writes=[tile])


def stage3_rw(P, g, D):
    st = ExitStack()
    sb = lambda name, shape, dt=F32: P.sb(name, shape, dt, st)
    mub = sb("mub", [128, RW_IN]); bload(P, mub, D["mu_rw"][0:1, :])
    w0b = sb("w0b", [128, 1024]); bload(P, w0b, D["w0"][0:1, :])
    a0b = sb("a0b", [128, 1024]); bload(P, a0b, D["a0"][0:1, :])
    kkb = sb("kkb", [128, 1024]); bload(P, kkb, D["k_k"][0:1, :])
    kab = sb("kab", [128, 1024]); bload(P, kab, D["k_a"][0:1, :])
    rkb = sb("rkb", [128, 1024]); bload(P, rkb, D["r_k"][0:1, :])
    loraW = sb("loraW", [128, 1024])
    P.dma("sp", loraW[0:64, :], D["w_up"][:, :], writes=[loraW])
    P.dma("sp", loraW[64:128, :], D["a_up"][:, :], writes=[loraW])
    gup1 = sb("gup1", [128, 1024]); P.dma("sp", gup1[:], D["g_up"][0:128, :], writes=[gup1])
    gup2 = sb("gup2", [32, 1024]); P.dma("sp", gup2[:], D["g_up"][128:160, :], writes=[gup2])
    tri = sb("tri", [128, 128]); P.dma("sp", tri[:], D["m_tri"][:, :], writes=[tri])
    ones = sb("onesb", [128, 128]); P.dma("sp", ones[:], D["m_ones"][:, :], writes=[ones])
    valid = sb("valid", [128, 2]); P.dma("sp", valid[:], D["m_valid"][:, :], writes=[valid])
    tiny = sb("tiny12", [128, 1]); mset(P, "dve", tiny[:], 1e-12, [tiny])
    Pc = sb("Pc", [128, RW_IN]); Pp = sb("Pp", [128, RW_IN])
    L = sb("L288", [128, 288]); LT = sb("LT", [128, 3, 128])
    W = {nm: sb("w_" + nm, [128, 1024]) for nm in ["zt", "za", "gt", "lw", "ah", "kk", "junk", "kmod", "b", "t2", "eL", "eN", "eLm", "eC", "gC", "Lsb", "rt", "at", "kt", "bt", "kp", "bp"]}
    ssq = sb("ssq", [128, 16]); rn = sb("rn", [128, 16]); bc = sb("bc", [128, 16])
    stgT = [sb(f"stgT{i}", [128, 8, 128]) for i in range(2)]
    pLT = P.ps("pLT", [128, 3, 128], F32, st)
    pw = [P.ps(f"pw{i}", [128, 512], F32, st) for i in range(4)]
    pT = [P.ps(f"pTr{i}", [128, 4, 128], F32, st) for i in range(2)]
    pk = 0
    tk = 0
    for u in range(NUX):
        samp = u >= 16
        if samp:
            mset(P, "pool", Pc[:], 0.0, [Pc])
            mset(P, "pool", Pp[:], 0.0, [Pp])
        for (d0, t0, nt) in unit_rows(u):
            ti = 16 if samp else u
            P.dma("sp", Pc[d0:d0 + nt, :], D["Ptok"][t0:t0 + nt, 0:RW_IN], reads=[D["Ptok_t"].sub(ti)], writes=[Pc])
            if samp:
                q = (t0 - 2048) // 16
                P.dma("sp", Pp[d0:d0 + 1, :], D["sshift"][q:q + 1, :], writes=[Pp])
                P.dma("sp", Pp[d0 + 1:d0 + 16, :], D["Ptok"][t0:t0 + 15, 0:RW_IN], reads=[D["Ptok_t"].sub(16)], writes=[Pp])
            elif u == 0:
                mset(P, "pool", Pp[0:1, :], 0.0, [Pp])
                P.dma("sp", Pp[1:128, :], D["Ptok"][0:127, 0:RW_IN], reads=[D["Ptok_t"].sub(0)], writes=[Pp])
            else:
                P.dma("sp", Pp[:, :], D["Ptok"][t0 - 1:t0 + 127, 0:RW_IN], reads=[D["Ptok_t"].sub(u), D["Ptok_t"].sub(u - 1)], writes=[Pp])
        tt(P, "dve", Pp[:], Pp[:], Pc[:], ALU.subtract, [Pp, Pc], [Pp])
        tt(P, "pool", Pp[:], Pp[:], mub[:], ALU.mult, [Pp, mub], [Pp])
        tt(P, "dve", Pc[:], Pc[:], Pp[:], ALU.add, [Pp, Pc], [Pc])
        if STOP == 1:
            continue
        r, k, v = Pc[:, 0:1024], Pc[:, 1024:2048], Pc[:, 2048:3072]
        act(P, L[:, 0:64], Pc[:, 3072:3136], AF.Tanh, [Pc], [L])
        cp(P, "pool", L[:, 64:128], Pc[:, 3136:3200], [Pc], [L])
        act(P, L[:, 128:288], Pc[:, 3200:3360], AF.Sigmoid, [Pc], [L])
        tr(P, pLT[:, 0, :], L[:, 0:128], g.identf[:], [L, g.identf], [pLT])
        tr(P, pLT[:, 1, :], L[:, 128:256], g.identf[:], [L, g.identf], [pLT])
        tr(P, pLT[0:32, 2, :], L[:, 256:288], g.identf[:], [L, g.identf], [pLT])
        cp(P, "dve", LT[:, 0:2, :], pLT[:, 0:2, :], [pLT], [LT])
        cp(P, "dve", LT[0:32, 2, :], pLT[0:32, 2, :], [pLT], [LT])
        if STOP == 2:
            continue
        for hf in range(2):
            cs = slice(hf * 512, (hf + 1) * 512)
            p = pw[pk % 4]; pk += 1
            mm(P, p[:], LT[0:64, 0, :], loraW[0:64, cs], True, True, [LT, loraW], [p])
            tt(P, "dve", W["zt"][:, cs], p[:], w0b[:, cs], ALU.add, [p, w0b], [W["zt"]])
            p = pw[pk % 4]; pk += 1
            mm(P, p[:], LT[64:128, 0, :], loraW[64:128, cs], True, True, [LT, loraW], [p])
            tt(P, "dve", W["za"][:, cs], p[:], a0b[:, cs], ALU.add, [p, a0b], [W["za"]])
            p = pw[pk % 4]; pk += 1
            mm(P, p[:], LT[:, 1, :], gup1[:, cs], True, False, [LT, gup1], [p])
            mm(P, p[:], LT[0:32, 2, :], gup2[0:32, cs], False, True, [LT, gup2], [p])
            cp(P, "act", W["gt"][:, cs], p[:], [p], [W["gt"]])
        if STOP == 3:
            continue
        act(P, W["lw"][:], W["zt"][:], AF.Sigmoid, [W["zt"]], [W["lw"]])
        ts(P, "dve", W["lw"][:], W["lw"][:], -0.6065306597126334, valid[:, (1 if samp else 0):(2 if samp else 1)], ALU.mult, ALU.mult, [W["lw"], valid], [W["lw"]])
        if STOP == 31:
            continue
        act(P, W["ah"][:], W["za"][:], AF.Sigmoid, [W["za"]], [W["ah"]])
        if STOP == 32:
            continue
        tt(P, "pool", W["kk"][:], k, kkb[:], ALU.mult, [Pc, kkb], [W["kk"]])
        act(P, W["junk"][:], W["kk"][:], AF.Square, [W["kk"]], [W["junk"]])
        if STOP == 33:
            continue
        red(P, "dve", ssq[:], W["junk"][:].rearrange("p (h d) -> p h d", h=16), ALU.add, [W["junk"]], [ssq])
        act(P, ssq[:], ssq[:], AF.Sqrt, [ssq, tiny], [ssq], bias=tiny[:])
        recip(P, rn[:], ssq[:], [ssq], [rn])
        if STOP == 34:
            continue
        kk3 = W["kk"][:].rearrange("p (h d) -> p h d", h=16)
        tt(P, "dve", kk3, kk3, rn[:].unsqueeze(2).to_broadcast([128, 16, 64]), ALU.mult, [W["kk"], rn], [W["kk"]])
        if STOP == 35:
            continue
        stt(P, "dve", W["t2"][:], W["ah"][:], -1.0, kab[:], ALU.add, ALU.mult, [W["ah"], kab], [W["t2"]])
        stt(P, "dve", W["kmod"][:], W["t2"][:], 1.0, k, ALU.add, ALU.mult, [W["t2"], Pc], [W["kmod"]])
        if STOP == 36:
            continue
        tt(P, "pool", W["b"][:], W["kk"][:], W["ah"][:], ALU.mult, [W["kk"], W["ah"]], [W["b"]])
        tt(P, "pool", W["t2"][:], r, W["kmod"][:], ALU.mult, [Pc, W["kmod"]], [W["t2"]])
        tt(P, "pool", W["t2"][:], W["t2"][:], rkb[:], ALU.mult, [W["t2"], rkb], [W["t2"]])
        if STOP == 37:
            continue
        red(P, "dve", bc[:], W["t2"][:].rearrange("p (h d) -> p h d", h=16), ALU.add, [W["t2"]], [bc])
        P.dma("pool", D["BC"][u * 128:(u + 1) * 128, :], bc[:], reads=[bc], writes=[D["BC_t"].sub(u)])
        if STOP == 4:
            continue
        for hf in range(2):
            cs = slice(hf * 512, (hf + 1) * 512)
            pL = pw[pk % 4]; pk += 1
            pLt = pw[pk % 4]; pk += 1
            mm(P, pL[:], tri[:], W["lw"][:, cs], True, True, [tri, W["lw"]], [pL])
            mm(P, pLt[:], ones[:], W["lw"][:, cs], True, True, [ones, W["lw"]], [pLt])
            if "a1" not in SKIP:
                act(P, W["eL"][:, cs], pL[:], AF.Exp, [pL], [W["eL"]])
            if "a2" not in SKIP:
                act(P, W["eN"][:, cs], pL[:], AF.Exp, [pL], [W["eN"]], scale=-1.0)
            act(P, W["Lsb"][:, cs], pL[:], AF.Identity, [pL], [W["Lsb"]])
            if "a3" not in SKIP:
                act(P, W["gC"][:, cs], pLt[:], AF.Exp, [pLt], [W["gC"]])
            if "d1" not in SKIP:
                tt(P, "dve", W["eC"][:, cs], pLt[:], W["Lsb"][:, cs], ALU.subtract, [pLt, W["Lsb"]], [W["eC"]])
            if "d2" not in SKIP:
                tt(P, "dve", W["eLm"][:, cs], W["Lsb"][:, cs], W["lw"][:, cs], ALU.subtract, [W["Lsb"], W["lw"]], [W["eLm"]])
        if "exp2" not in SKIP:
            act(P, W["eC"][:], W["eC"][:], AF.Exp, [W["eC"]], [W["eC"]])
            act(P, W["eLm"][:], W["eLm"][:], AF.Exp, [W["eLm"]], [W["eLm"]])
        if STOP == 5:
            continue
        tt(P, "dve", W["rt"][:], r, W["eL"][:], ALU.mult, [Pc, W["eL"]], [W["rt"]])
        stt(P, "dve", W["at"][:], W["kk"][:], -1.0, W["eLm"][:], ALU.mult, ALU.mult, [W["kk"], W["eLm"]], [W["at"]])
        tt(P, "pool", W["kt"][:], W["kmod"][:], W["eN"][:], ALU.mult, [W["kmod"], W["eN"]], [W["kt"]])
        tt(P, "pool", W["bt"][:], W["b"][:], W["eN"][:], ALU.mult, [W["b"], W["eN"]], [W["bt"]])
        tt(P, "pool", W["kp"][:], W["kmod"][:], W["eC"][:], ALU.mult, [W["kmod"], W["eC"]], [W["kp"]])
        tt(P, "dve", W["bp"][:], W["b"][:], W["eC"][:], ALU.mult, [W["b"], W["eC"]], [W["bp"]])
        rows = slice(u * 128, (u + 1) * 128)
        P.dma("pool", D["Vs"][rows, :], v, reads=[Pc], writes=[D["Vs_t"].sub(u)])
        P.dma("pool", D["KPs"][rows, :], W["kp"][:], reads=[W["kp"]], writes=[D["KPs_t"].sub(u)])
        P.dma("pool", D["BPs"][rows, :], W["bp"][:], reads=[W["bp"]], writes=[D["BPs_t"].sub(u)])
        P.dma("pool", D["Gs"][rows, :], W["gt"][:], reads=[W["gt"]], writes=[D["Gs_t"].sub(u)])
        if STOP == 6:
            continue
        for nm, dst in [("rt", "RT"), ("at", "AT"), ("kt", "KT"), ("bt", "BT"), ("gC", "GCT")]:
            s = stgT[tk % 2]
            for half in range(2):
                p = pT[tk % 2]
                for j in range(4):
                    fc = half * 4 + j
                    tr(P, p[:, j, :], W[nm][:, fc * 128:(fc + 1) * 128], g.identf[:], [W[nm], g.identf], [p])
                cp(P, "act" if half else "dve", s[:, half * 4:(half + 1) * 4, :], p[:], [p], [s])
                tk += 1
            P.dma("pool", D[dst].rearrange("(fc p) c -> p fc c", p=128)[:, :, u * 128:(u + 1) * 128], s[:], reads=[s], writes=[D[dst + "_t"].sub(u)])
    P.barrier()
    st.close()


def stage4_scan(P, g, D):
    st = ExitStack()
    sb = lambda name, shape, dt=F32: P.sb(name, shape, dt, st)
    msl = sb("msl", [128, 128]); P.dma("sp", msl[:], D["m_sl"][:, :], writes=[msl])
    msu = sb("msu", [128, 128]); P.dma("sp", msu[:], D["m_su"][:, :], writes=[msu])
    mu = sb("mu", [128, 128]); P.dma("sp", mu[:], D["m_u"][:, :], writes=[mu])
    lnw = sb("lnw", [128, 1024]); bload(P, lnw, D["lnx_w"][0:1, :])
    lnb = sb("lnb", [128, 1024]); bload(P, lnb, D["lnx_b"][0:1, :])
    gne = sb("gne", [128, 1]); mset(P, "dve", gne[:], GN_EPS, [gne])
    H = sb("H", [128, 8, 64])
    mset(P, "dve", H[:], 0.0, [H.sub((a, b)) for a in range(8) for b in range(2)])
    FM = {nm: [sb(f"fm_{nm}{i}", [128, 8, 128]) for i in range(2)] for nm in ["RT", "AT", "KT", "BT", "GCT"]}
    TM = {nm: [sb(f"tm_{nm}{i}", [128, 1024]) for i in range(2)] for nm in ["Vs", "KPs", "BPs"]}
    Y = [sb(f"Y{i}", [128, 1024]) for i in range(2)]
    NM = 16
    GS = 4
    MS = [[sb(f"ms{s}_{i}", [128, 128]) for i in range(NM)] for s in range(GS)]
    XU = [[sb(f"xu{s}_{i}", [128, 64]) for i in range(4)] for s in range(GS)]
    pLane = [P.ps(f"pLane{i}", [128, 512], F32, st) for i in range(GS)]
    laneM = [[SlotAP(pLane[l], pLane[l][:, j * 128:(j + 1) * 128]) for j in range(2)] for l in range(GS)]
    laneS = [[SlotAP(pLane[l], pLane[l][:, 256 + j * 64:256 + (j + 1) * 64]) for j in range(4)] for l in range(GS)]
    gt = sb("p_gt", [128, 1024]); bc = sb("p_bc", [128, 16]); vv = None
    pw = {nm: sb("p_" + nm, [128, 1024]) for nm in ["yc", "sq", "yb"]}
    st16 = {nm: sb("p16_" + nm, [128, 16]) for nm in ["mean", "var", "rstd"]}
    oab = sb("oab", [128, 1024], BF16)
    pO = [P.ps(f"pO{i}", [128, 8, 128], BF16, st) for i in range(1)]
    oT = sb("oT", [128, 8, 128], BF16)
    Ssb = [sb(f"Ssb{i}", [128, 64]) for i in range(GS)]
    Sout = sb("Sout", [128, 8, 64])

    def head_gen(u, fc, hp, lane, FMu, TMu, y):
        samp = u >= 16
        RT, AT, KT, BT, GC = FMu
        V, KP, BP = TMu
        mk = [0]; sk = [0]

        def nextM():
            mk[0] += 1
            return laneM[lane][mk[0] % 2]

        def nextS():
            sk[0] += 1
            return laneS[lane][sk[0] % 4]

        h = 2 * fc + hp
        pb = hp * 64
        hs = slice(h * 64, (h + 1) * 64)
        M = MS[lane]
        a_ = AT[pb:pb + 64, fc, :]; b_ = BT[pb:pb + 64, fc, :]; k_ = KT[pb:pb + 64, fc, :]; r_ = RT[pb:pb + 64, fc, :]
        A, N, AakT, RBt, RKt = M[0], M[1], M[2], M[3], M[4]
        for (dst, l, r, msk) in [(A, a_, b_, msl), (N, b_, a_, msu), (AakT, k_, a_, msu), (RBt, b_, r_, mu), (RKt, k_, r_, mu)]:
            p = nextM()
            mm(P, p[:], l, r, True, True, [AT, BT, KT, RT], [p])
            tt(P, "dve", dst[:], p[:], msk[:], ALU.mult, [p, msk], [dst])
            yield
        Ap = [A, M[5], M[6], M[7], M[8]]
        Np = [N, M[9], M[10], M[11], M[12], M[13]]
        for k in range(5):
            if k < 4:
                p = nextM()
                mm(P, p[:], Np[k][:], Ap[k][:], True, True, [Np[k], Ap[k]], [p])
                cp(P, "act", Ap[k + 1][:], p[:], [p], [Ap[k + 1]])
            p = nextM()
            mm(P, p[:], Ap[k][:], Np[k][:], True, True, [Np[k], Ap[k]], [p])
            cp(P, "dve" if k % 2 else "act", Np[k + 1][:], p[:], [p], [Np[k + 1]])
            yield
        Tt = [M[14], M[15]]
        tt(P, "dve", Tt[0][:], Np[5][:], g.identf[:], ALU.add, [Np[5], g.identf], [Tt[0]])
        cur = 0
        for k in [4, 3, 2, 1, 0]:
            p = nextM()
            mm(P, p[:], Ap[k][:], Tt[cur][:], True, True, [Ap[k], Tt[cur]], [p])
            tt(P, "dve", Tt[1 - cur][:], p[:], Tt[cur][:], ALU.add, [p, Tt[cur]], [Tt[1 - cur]])
            cur = 1 - cur
            yield
        TT_ = Tt[cur]
        for c in range(2):
            pc = c * 64
            cc = slice(c * 64, (c + 1) * 64)
            Hh = H[pb:pb + 64, fc, :]
            Hd = H.sub((fc, hp))
            if samp:
                q = 2 * (u - 16) + c
                P.dma("sp", Ssb[lane][pb:pb + 64, :], D["swkv"][q, h, :, :], writes=[Ssb[lane]])
                p = nextS()
                mm(P, p[pb:pb + 64, :], Ssb[lane][pb:pb + 64, :], g.identf[pb:pb + 64, pb:pb + 64], True, True, [Ssb[lane], g.identf], [p])
                cp(P, "act", Hh, p[pb:pb + 64, :], [p], [Hd])
                yield
            X_sb, U_sb = XU[lane][2 * c], XU[lane][2 * c + 1]
            p = nextS()
            mm(P, p[pc:pc + 64, :], a_[:, cc], Hh, True, False, [AT, Hd], [p])
            mm(P, p[pc:pc + 64, :], AakT[pc:pc + 64, cc], V[pc:pc + 64, hs], False, True, [AakT, V], [p])
            cp(P, "act", X_sb[pc:pc + 64, :], p[pc:pc + 64, :], [p], [X_sb])
            yield
            p = nextS()
            mm(P, p[pc:pc + 64, :], TT_[pc:pc + 64, cc], X_sb[pc:pc + 64, :], True, True, [TT_, X_sb], [p])
            cp(P, "act", U_sb[pc:pc + 64, :], p[pc:pc + 64, :], [p], [U_sb])
            yield
            p = nextS()
            mm(P, p[pc:pc + 64, :], r_[:, cc], Hh, True, False, [RT, Hd], [p])
            mm(P, p[pc:pc + 64, :], RBt[pc:pc + 64, cc], U_sb[pc:pc + 64, :], False, False, [RBt, U_sb], [p])
            mm(P, p[pc:pc + 64, :], RKt[pc:pc + 64, cc], V[pc:pc + 64, hs], False, True, [RKt, V], [p])
            cp(P, "act", y[pc:pc + 64, hs], p[pc:pc + 64, :], [p], [y.sub(h)])
            p = nextS()
            mm(P, p[pb:pb + 64, :], BP[pc:pc + 64, hs], U_sb[pc:pc + 64, :], True, False, [BP, U_sb], [p])
            mm(P, p[pb:pb + 64, :], KP[pc:pc + 64, hs], V[pc:pc + 64, hs], False, True, [KP, V], [p])
            stt(P, "dve", Hh, Hh, GC[pb:pb + 64, fc, c * 64:c * 64 + 1], p[pb:pb + 64, :], ALU.mult, ALU.add, [Hd, GC, p], [Hd])
            yield
            if samp or (u == 15 and c == 1):
                q = (1 + 2 * (u - 16) + c) if samp else 0
                p = nextS()
                mm(P, p[pb:pb + 64, :], Hh, g.identf[pb:pb + 64, pb:pb + 64], True, True, [Hd, g.identf], [p])
                cp(P, "act", Sout[pb:pb + 64, fc, :], p[pb:pb + 64, :], [p], [Sout.sub(h)])
                r0 = (q * 16 + h) * 64
                P.dma("pool", D["wkv"][r0:r0 + 64, :], Sout[pb:pb + 64, fc, :], reads=[Sout.sub(h)])
                yield

    for u in range(NU):
        samp = u >= 16
        b2 = u % 2
        cols = slice(u * 128, (u + 1) * 128)
        for nm in FM:
            P.dma("sp", FM[nm][b2][:], D[nm].rearrange("(fc p) c -> p fc c", p=128)[:, :, cols], reads=[D[nm + "_t"].sub(u)], writes=[FM[nm][b2]])
        for nm in TM:
            P.dma("sp", TM[nm][b2][:], D[nm][cols, :], reads=[D[nm + "_t"].sub(u)], writes=[TM[nm][b2]])
        FMu = [FM[nm][b2] for nm in ["RT", "AT", "KT", "BT", "GCT"]]
        TMu = [TM[nm][b2] for nm in ["Vs", "KPs", "BPs"]]
        V = TMu[0]
        y = Y[b2]
        heads = [(fc, hp) for fc in range(8) for hp in range(2)]
        active = []
        nxt = 0
        for lane in range(GS):
            fc, hp = heads[nxt]; nxt += 1
            active.append((lane, head_gen(u, fc, hp, lane, FMu, TMu, y)))
        while active:
            still = []
            for lane, gen in active:
                try:
                    next(gen)
                    still.append((lane, gen))
                except StopIteration:
                    if nxt < len(heads):
                        fc, hp = heads[nxt]; nxt += 1
                        still.append((lane, head_gen(u, fc, hp, lane, FMu, TMu, y)))
            active = still
        ysubs = [y.sub(h) for h in range(16)]
        P.dma("sp", gt[:], D["Gs"][cols, :], reads=[D["Gs_t"].sub(u)], writes=[gt])
        P.dma("sp", bc[:], D["BC"][cols, :], reads=[D["BC_t"].sub(u)], writes=[bc])
        y3 = y[:].rearrange("p (h d) -> p h d", h=16)
        bcast = lambda t16: t16[:].unsqueeze(2).to_broadcast([128, 16, 64])
        v3 = lambda t: t[:].rearrange("p (h d) -> p h d", h=16)
        red(P, "dve", st16["mean"][:], y3, ALU.add, ysubs, [st16["mean"]])
        ts(P, "dve", st16["mean"][:], st16["mean"][:], 1.0 / 64, None, ALU.mult, None, [st16["mean"]], [st16["mean"]])
        tt(P, "dve", v3(pw["yc"]), y3, bcast(st16["mean"]), ALU.subtract, ysubs + [st16["mean"]], [pw["yc"]])
        act(P, pw["sq"][:], pw["yc"][:], AF.Square, [pw["yc"]], [pw["sq"]])
        red(P, "dve", st16["var"][:], v3(pw["sq"]), ALU.add, [pw["sq"]], [st16["var"]])
        act(P, st16["var"][:], st16["var"][:], AF.Sqrt, [st16["var"], gne], [st16["var"]], scale=1.0 / 64, bias=gne[:])
        recip(P, st16["rstd"][:], st16["var"][:], [st16["var"]], [st16["rstd"]])
        tt(P, "dve", v3(pw["yc"]), v3(pw["yc"]), bcast(st16["rstd"]), ALU.mult, [pw["yc"], st16["rstd"]], [pw["yc"]])
        tt(P, "pool", pw["yc"][:], pw["yc"][:], lnw[:], ALU.mult, [pw["yc"], lnw], [pw["yc"]])
        tt(P, "pool", pw["yc"][:], pw["yc"][:], lnb[:], ALU.add, [pw["yc"], lnb], [pw["yc"]])
        tt(P, "dve", v3(pw["yb"]), v3(V), bcast(bc), ALU.mult, [V, bc], [pw["yb"]])
        tt(P, "pool", pw["yc"][:], pw["yc"][:], pw["yb"][:], ALU.add, [pw["yc"], pw["yb"]], [pw["yc"]])
        tt(P, "dve", oab[:], pw["yc"][:], gt[:], ALU.mult, [pw["yc"], gt], [oab])
        if D.get("OAdbg") is not None:
            P.dma("pool", D["OAdbg"][cols, :], pw["yc"][:], reads=[pw["yc"]])
        for fc in range(8):
            tr(P, pO[0][:, fc, :], oab[:, fc * 128:(fc + 1) * 128], g.identb[:], [oab, g.identb], [pO[0]])
        cp(P, "act", oT[:], pO[0][:], [pO[0]], [oT])
        dstv = D["OAT"].rearrange("(fc p) c -> p fc c", p=128)
        for (d0, t0, nt) in unit_rows(u):
            P.dma("pool", dstv[:, :, t0:t0 + nt], oT[:, :, d0:d0 + nt], reads=[oT], writes=[D["OAT_t"]])
    P.barrier()
    st.close()

TOPK = 256
NEG = -1.0e30
NBIS = 22


def headnorm(P, src, dst, nw, junk, ssq, rst, g, n, reads, pre=1.0):
    act(P, junk[0:n, :], src, AF.Square, reads, [junk])
    red(P, "dve", ssq[0:n, :], junk[0:n, :].rearrange("p (h d) -> p h d", h=16), ALU.add, [junk], [ssq])
    act(P, ssq[0:n, :], ssq[0:n, :], AF.Sqrt, [ssq], [ssq], scale=1.0 / 64, bias=g.epsc[0:n, :])
    recip(P, rst[0:n, :], ssq[0:n, :], [ssq], [rst])
    d3 = dst.rearrange("p (h d) -> p h d", h=16)
    tt(P, "dve", d3, src.rearrange("p (h d) -> p h d", h=16), rst[0:n, :].unsqueeze(2).to_broadcast([n, 16, 64]), ALU.mult, list(reads) + [rst], [junk])
    stt(P, "dve", d3, d3, pre, nw[0:n, :].unsqueeze(1).to_broadcast([n, 16, 64]), ALU.mult, ALU.mult, [junk, nw], [junk])


def stage3_dsa(P, g, D):
    st = ExitStack()
    sb = lambda name, shape, dt=F32: P.sb(name, shape, dt, st)
    knw = sb("knw", [128, 64]); qnw = sb("qnw", [128, 64])
    P.dma("sp", knw[:], D["k_norm_w"][0:1, :].partition_broadcast(128), writes=[knw])
    P.dma("sp", qnw[:], D["q_norm_w"][0:1, :].partition_broadcast(128), writes=[qnw])
    pd = [sb(f"pd{i}", [128, 3656]) for i in range(2)]
    cs = [sb(f"cs{i}", [128, 2, 32]) for i in range(2)]
    junk = sb("djunk", [128, 1024]); ssq = sb("dssq", [128, 16]); rst = sb("drst", [128, 16])
    nrm = sb("dnrm", [128, 1024])
    ko = [sb(f"dko{i}", [128, 1024]) for i in range(2)]
    qo = sb("dqo", [128, 1024]); qio = sb("dqio", [128, 512])
    kio = [sb(f"dkio{i}", [128, 64]) for i in range(2)]
    wio = [sb(f"dwio{i}", [128, 8]) for i in range(2)]
    tmp = sb("dtmp", [128, 4, 512])
    cat = sb("dcat", [128, 2624], BF16)
    pT = [P.ps(f"dpT{i}", [128, 8, 128], BF16, st) for i in range(2)]
    sT = [sb(f"dsT{i}", [128, 8, 128], BF16) for i in range(2)]
    tk = 0
    for ti, (r0, n) in enumerate(TILES):
        p, c = pd[ti % 2], cs[ti % 2]
        P.dma("sp", p[0:n, :], D["Ptok"][r0:r0 + n, C_Q:TOKW], reads=[D["Ptok_t"].sub(ti)], writes=[p])
        P.dma("sp", c[0:n, 0, :], D["cos"][r0:r0 + n, :], writes=[c])
        P.dma("sp", c[0:n, 1, :], D["sin"][r0:r0 + n, :], writes=[c])
        P.dma("pool", D["vout"][r0:r0 + n, :], D["Ptok"][r0:r0 + n, C_V:C_V + 1024], reads=[D["Ptok_t"].sub(ti)])
        cosb = c[0:n, 0, :].unsqueeze(1).to_broadcast([n, 16, 32])
        sinb = c[0:n, 1, :].unsqueeze(1).to_broadcast([n, 16, 32])
        r4 = lambda ap, H: ap.rearrange("p (h t d) -> p h t d", h=H, t=2)
        headnorm(P, p[0:n, C_K - C_Q:C_K - C_Q + 1024], junk[0:n, :], knw, junk, ssq, rst, g, n, [p])
        o = ko[ti % 2]
        rope(P, "dve", r4(o[0:n, :], 16), r4(junk[0:n, :], 16), cosb, sinb, tmp, n, [junk, c], [o])
        P.dma("pool", D["kout"][r0:r0 + n, :], o[0:n, :], reads=[o], writes=[D["kout_t"].sub(ti)])
        cp(P, "pool", cat[0:n, 1024:2048], o[0:n, :], [o], [cat])
        headnorm(P, p[0:n, 0:1024], junk[0:n, :], qnw, junk, ssq, rst, g, n, [p], pre=0.125)
        rope(P, "dve", r4(qo[0:n, :], 16), r4(junk[0:n, :], 16), cosb, sinb, tmp, n, [junk, c], [qo])
        cp(P, "pool", cat[0:n, 0:1024], qo[0:n, :], [qo], [cat])
        rope(P, "pool", r4(qio[0:n, :], 8), r4(p[0:n, C_QI - C_Q:C_QI - C_Q + 512], 8), c[0:n, 0, :].unsqueeze(1).to_broadcast([n, 8, 32]),
             c[0:n, 1, :].unsqueeze(1).to_broadcast([n, 8, 32]), tmp, n, [p, c], [qio])
        cp(P, "pool", cat[0:n, 2048:2560], qio[0:n, :], [qio], [cat])
        oi = kio[ti % 2]
        rope(P, "pool", r4(oi[0:n, :], 1), r4(p[0:n, C_KI - C_Q:C_KI - C_Q + 64], 1), c[0:n, 0, :].unsqueeze(1), c[0:n, 1, :].unsqueeze(1), tmp, n, [p, c], [oi])
        P.dma("pool", D["kidx"][r0:r0 + n, :], oi[0:n, :], reads=[oi], writes=[D["kidx_t"].sub(ti)])
        cp(P, "pool", cat[0:n, 2560:2624], oi[0:n, :], [oi], [cat])
        w = wio[ti % 2]
        ts(P, "dve", w[0:n, :], p[0:n, C_WI - C_Q:C_WI - C_Q + 8], 512.0 ** -0.5, None, ALU.mult, None, [p], [w])
        P.dma("pool", D["WI"][r0:r0 + n, :], w[0:n, :], reads=[w], writes=[D["WI_t"].sub(ti)])
        for (c0, nch, dst) in [(0, 8, "QT"), (1024, 8, "KTn"), (2048, 4, "QIT")]:
            pt, s_ = pT[tk % 2], sT[tk % 2]; tk += 1
            for j in range(nch):
                tr(P, pt[:, j, 0:n], cat[0:n, c0 + j * 128:c0 + (j + 1) * 128], g.identb[0:n, 0:n], [cat, g.identb], [pt])
            cp(P, "act", s_[:, 0:nch, 0:n], pt[:, 0:nch, 0:n], [pt], [s_])
            P.dma("pool", D[dst].rearrange("(fc p) c -> p fc c", p=128)[:, :, r0:r0 + n], s_[:, 0:nch, 0:n], reads=[s_], writes=[D[dst + "_t"].sub(ti)])
        pt, s_ = pT[tk % 2], sT[tk % 2]; tk += 1
        tr(P, pt[0:64, 0, 0:n], cat[0:n, 2560:2624], g.identb[0:n, 0:n], [cat, g.identb], [pt])
        cp(P, "act", s_[0:64, 0, 0:n], pt[0:64, 0, 0:n], [pt], [s_])
        P.dma("pool", D["KIT"][:, r0:r0 + n], s_[0:64, 0, 0:n], reads=[s_], writes=[D["KIT_t"].sub(ti)])
    P.dma("pool", D["shift"][0:1, :], D["Ptok"][2047:2048, 0:RW_IN], reads=[D["Ptok_t"].sub(15)])
    for q in range(4):
        P.dma("pool", D["shift"][1 + q:2 + q, :], D["Ptok"][2048 + 16 * q + 15:2048 + 16 * q + 16, 0:RW_IN], reads=[D["Ptok_t"].sub(16)])
    P.barrier()
    st.close()


def stage5_attn(P, g, D):
    st = ExitStack()
    sb = lambda name, shape, dt=F32: P.sb(name, shape, dt, st)
    kT = sb("kT", [128, 8, 2064], BF16)
    Vb = sb("Vb", [128, 17, 16, 65], BF16)
    kiT = sb("kiT", [128, 2064], BF16)
    Ibuf = sb("Ibuf", [128, 2064]); junkI = sb("junkI", [128, 2064], BF16)
    maskb = sb("maskb", [128, 2064], BF16); maskT = sb("maskT", [128, 17, 128], BF16)
    qT = [sb(f"qT{i}", [128, 8, 128], BF16) for i in range(2)]
    qiT = [sb(f"qiT{i}", [128, 4, 128], BF16) for i in range(2)]
    wi = [sb(f"wi{i}", [128, 8]) for i in range(2)]
    rl = [sb(f"rl{i}", [128, 512]) for i in range(2)]
    E = [sb(f"E{i}", [128, 4, 128], BF16) for i in range(3)]
    p2 = sb("pow2", [128, NBIS]); P.dma("sp", p2[:], D["m_pow2"][:, :], writes=[p2])
    dtab = sb("dtab", [128, NBIS])
    s1 = {nm: sb("s1_" + nm, [128, 1]) for nm in ["B", "mid", "cnt", "t2", "t3", "thr"]}
    osb = sb("osb", [128, 1024]); osbb = sb("osbb", [128, 1024], BF16); rec = sb("orec", [128, 4])
    oT = sb("oTb", [128, 8, 128], BF16)
    vst = [sb(f"vst{i}", [128, 1024]) for i in range(2)]
    kst = sb("kstb", [128, 1024], BF16); kis = sb("kis", [128, 64]); kisb = sb("kisb", [128, 128], BF16)
    pI = [P.ps(f"pI{i}", [128, 512], F32, st) for i in range(2)]
    pS = [P.ps(f"pSc{i}", [128, 4, 128], F32, st) for i in range(2)]
    pO = [P.ps(f"pOa{i}", [128, 4, 65], F32, st) for i in range(2)]
    pmT = P.ps("pmT", [128, 8, 128], BF16, st)
    mset(P, "pool", Vb[:, :, :, 64:65], 1.0, [Vb])
    cnt = {"rl": 0, "E": 0, "S": 0, "I": 0, "v": 0}

    def load_v_block(j, src_ap, nrow):
        v = vst[cnt["v"] % 2]; cnt["v"] += 1
        P.dma("sp", v[0:nrow, :], src_ap, writes=[v])
        cp(P, "pool", Vb[0:nrow, j, :, 0:64], v[0:nrow, :].rearrange("p (h d) -> p h d", h=16), [v], [Vb])

    def attend(nq, tcol, nblk, lastw, prompt_tile, obt_cols, par):
        S = (nblk - 1) * 128 + lastw
        q_, qi_, w_ = qT[par], qiT[par], wi[par]
        P.dma("sp", q_[:, :, 0:nq], D["QT"].rearrange("(fc p) c -> p fc c", p=128)[:, :, tcol:tcol + nq], reads=[D["QT_t"]], writes=[q_])
        P.dma("sp", qi_[:, :, 0:nq], D["QIT"].rearrange("(fc p) c -> p fc c", p=128)[:, :, tcol:tcol + nq], reads=[D["QIT_t"]], writes=[qi_])
        P.dma("sp", w_[0:nq, :], D["WI"][tcol:tcol + nq, :], reads=[D["WI_t"]], writes=[w_])
        for s0 in range(0, S, 512):
            w = min(512, S - s0)
            for h in range(8):
                pb = (h % 2) * 64
                p = pI[cnt["I"] % 2]; cnt["I"] += 1
                mm(P, p[0:nq, 0:w], qi_[pb:pb + 64, h // 2, 0:nq], kiT[pb:pb + 64, s0:s0 + w], True, True, [qi_, kiT], [p])
                r = rl[cnt["rl"] % 2]; cnt["rl"] += 1
                act(P, r[0:nq, 0:w], p[0:nq, 0:w], AF.Relu, [p], [r])
                if h == 0:
                    ts(P, "dve", Ibuf[0:nq, s0:s0 + w], r[0:nq, 0:w], w_[0:nq, 0:1], None, ALU.mult, None, [r, w_], [Ibuf])
                else:
                    stt(P, "dve", Ibuf[0:nq, s0:s0 + w], r[0:nq, 0:w], w_[0:nq, h:h + 1], Ibuf[0:nq, s0:s0 + w], ALU.mult, ALU.add, [r, w_, Ibuf], [Ibuf])
        P.op("dve", lambda e: e.tensor_reduce(out=s1["B"][0:nq, :], in_=Ibuf[0:nq, 0:S], axis=AX.X, op=ALU.max, apply_absolute_value=True), reads=[Ibuf], writes=[s1["B"]])
        ts(P, "dve", s1["B"][0:nq, :], s1["B"][0:nq, :], 1.001, 1e-6, ALU.mult, ALU.add, [s1["B"]], [s1["B"]])
        ts(P, "dve", dtab[0:nq, :], p2[0:nq, :], s1["B"][0:nq, :], None, ALU.mult, None, [p2, s1["B"]], [dtab])
        if prompt_tile:
            mset(P, "dve", Ibuf[0:64, S - 64:S], NEG, [Ibuf])
        mset(P, "dve", s1["mid"][0:nq, :], 0.0, [s1["mid"]])
        for k in range(NBIS):
            ts(P, "dve", junkI[0:nq, 0:S], Ibuf[0:nq, 0:S], s1["mid"][0:nq, :], None, ALU.is_ge, ALU.add, [Ibuf, s1["mid"]], [junkI, s1["cnt"]], accum_out=s1["cnt"][0:nq, :])
            ts(P, "dve", s1["t2"][0:nq, :], s1["cnt"][0:nq, :], TOPK - 0.5, 2.0, ALU.is_ge, ALU.mult, [s1["cnt"]], [s1["t2"]])
            ts(P, "dve", s1["t3"][0:nq, :], s1["t2"][0:nq, :], -1.0, dtab[0:nq, k:k + 1], ALU.add, ALU.mult, [s1["t2"], dtab], [s1["t3"]])
            tt(P, "dve", s1["mid"][0:nq, :], s1["mid"][0:nq, :], s1["t3"][0:nq, :], ALU.add, [s1["mid"], s1["t3"]], [s1["mid"]])
        tt(P, "dve", s1["thr"][0:nq, :], s1["mid"][0:nq, :], dtab[0:nq, NBIS - 1:NBIS], ALU.subtract, [s1["mid"], dtab], [s1["thr"]])
        ts(P, "dve", maskb[0:nq, 0:S], Ibuf[0:nq, 0:S], s1["thr"][0:nq, :], None, ALU.is_ge, None, [Ibuf, s1["thr"]], [maskb])
        for j0 in range(0, nblk, 8):
            nb_ = min(8, nblk - j0)
            for jj in range(nb_):
                j = j0 + jj
                wj = 128 if j < nblk - 1 else lastw
                tr(P, pmT[0:wj, jj, 0:nq], maskb[0:nq, j * 128:j * 128 + wj], g.identb[0:nq, 0:nq], [maskb, g.identb], [pmT])
            full = nb_ if (j0 + nb_ < nblk or lastw == 128) else nb_ - 1
            if full > 0:
                cp(P, "act", maskT[:, j0:j0 + full, 0:nq], pmT[:, 0:full, 0:nq], [pmT], [maskT])
            if full < nb_:
                cp(P, "act", maskT[0:lastw, j0 + full, 0:nq], pmT[0:lastw, full, 0:nq], [pmT], [maskT])
        items = [(h, j0) for h in range(16) for j0 in range(0, nblk, 4)]

        def qk(it):
            h, j0 = it
            pb = (h % 2) * 64
            p = pS[cnt["S"] % 2]; cnt["S"] += 1
            for jj in range(min(4, nblk - j0)):
                j = j0 + jj
                wj = 128 if j < nblk - 1 else lastw
                mm(P, p[0:wj, jj, 0:nq], kT[pb:pb + 64, h // 2, j * 128:j * 128 + wj], q_[pb:pb + 64, h // 2, 0:nq], True, True, [kT, q_], [p])
            return p

        pcur = qk(items[0])
        for k, it in enumerate(items):
            pnext = qk(items[k + 1]) if k + 1 < len(items) else None
            h, j0 = it
            nb_ = min(4, nblk - j0)
            e = E[cnt["E"] % 3]; cnt["E"] += 1
            full = nb_ if (j0 + nb_ < nblk or lastw == 128) else nb_ - 1
            if full > 0:
                act(P, e[:, 0:full, 0:nq], pcur[:, 0:full, 0:nq], AF.Exp, [pcur], [e])
                tt(P, "pool" if k % 3 == 2 else "dve", e[:, 0:full, 0:nq], e[:, 0:full, 0:nq], maskT[:, j0:j0 + full, 0:nq], ALU.mult, [e, maskT], [e])
            if full < nb_:
                act(P, e[0:lastw, full, 0:nq], pcur[0:lastw, full, 0:nq], AF.Exp, [pcur], [e])
                tt(P, "dve", e[0:lastw, full, 0:nq], e[0:lastw, full, 0:nq], maskT[0:lastw, j0 + full, 0:nq], ALU.mult, [e, maskT], [e])
            po = pO[(h // 4) % 2]
            for jj in range(nb_):
                j = j0 + jj
                wj = 128 if j < nblk - 1 else lastw
                mm(P, po[0:nq, h % 4, :], e[0:wj, jj, 0:nq], Vb[0:wj, j, h, :], j == 0, j == nblk - 1, [e, Vb], [po])
            if j0 + nb_ >= nblk and h % 4 == 3:
                recip(P, rec[0:nq, :], po[0:nq, :, 64], [po], [rec])
                tt(P, "dve", osb[0:nq, (h - 3) * 64:(h + 1) * 64].rearrange("p (h d) -> p h d", h=4), po[0:nq, :, 0:64],
                   rec[0:nq, :].unsqueeze(2).to_broadcast([nq, 4, 64]), ALU.mult, [po, rec], [osb])
            pcur = pnext
        if D.get("OBdbg") is not None:
            P.dma("pool", D["OBdbg"][obt_cols:obt_cols + nq, :], osb[0:nq, :], reads=[osb])
        cp(P, "pool", osbb[0:nq, :], osb[0:nq, :], [osb], [osbb])
        for fc in range(8):
            tr(P, pmT[:, fc, 0:nq], osbb[0:nq, fc * 128:(fc + 1) * 128], g.identb[0:nq, 0:nq], [osbb, g.identb], [pmT])
        cp(P, "act", oT[:, :, 0:nq], pmT[:, :, 0:nq], [pmT], [oT])
        P.dma("pool", D["OBT"].rearrange("(fc p) c -> p fc c", p=128)[:, :, obt_cols:obt_cols + nq], oT[:, :, 0:nq], reads=[oT], writes=[D["OBT_t"]])

    P.dma("sp", kT[:, :, 0:2048], D["KTn"].rearrange("(fc p) c -> p fc c", p=128)[:, :, 0:2048], reads=[D["KTn_t"]], writes=[kT])
    P.dma("sp", kiT[0:64, 0:2048], D["KIT"][:, 0:2048], reads=[D["KIT_t"]], writes=[kiT])
    P.dma("sp", kiT[64:128, 0:2048], D["KIT"][:, 0:2048], reads=[D["KIT_t"]], writes=[kiT])
    for j in range(16):
        load_v_block(j, D["Ptok"][j * 128:(j + 1) * 128, C_V:C_V + 1024], 128)
    NPT = int(os.environ.get("NPT", "16"))
    for i in range(NPT):
        attend(128, 128 * i, i + 1, 128, True, 128 * i, i % 2)
    NSQ = int(os.environ.get("NSQ", "4"))
    for q in range(NSQ):
        tok = 2048 + 16 * q
        for j in range(16):
            v = vst[cnt["v"] % 2]; cnt["v"] += 1
            P.dma("sp", v[:, :], D["cache_k"][q, j * 128:(j + 1) * 128, :], writes=[v])
            cp(P, "pool", kst[:, :], v[:, :], [v], [kst])
            for fc in range(8):
                tr(P, pmT[:, fc, :], kst[:, fc * 128:(fc + 1) * 128], g.identb[:], [kst, g.identb], [pmT])
            cp(P, "act", kT[:, :, j * 128:(j + 1) * 128], pmT[:, :, :], [pmT], [kT])
            load_v_block(j, D["cache_v"][q, j * 128:(j + 1) * 128, :], 128)
            P.dma("sp", kis[:, :], D["cache_kidx"][q, j * 128:(j + 1) * 128, :], writes=[kis])
            cp(P, "dve", kisb[:, 0:64], kis[:, :], [kis], [kisb])
            cp(P, "dve", kisb[:, 64:128], kis[:, :], [kis], [kisb])
            tr(P, pmT[:, 0, :], kisb[:, :], g.identb[:], [kisb, g.identb], [pmT])
            cp(P, "act", kiT[:, j * 128:(j + 1) * 128], pmT[:, 0, :], [pmT], [kiT])
        P.dma("sp", kT[:, :, 2048:2064], D["KTn"].rearrange("(fc p) c -> p fc c", p=128)[:, :, tok:tok + 16], reads=[D["KTn_t"]], writes=[kT])
        P.dma("sp", kiT[0:64, 2048:2064], D["KIT"][:, tok:tok + 16], reads=[D["KIT_t"]], writes=[kiT])
        P.dma("sp", kiT[64:128, 2048:2064], D["KIT"][:, tok:tok + 16], reads=[D["KIT_t"]], writes=[kiT])
        load_v_block(16, D["Ptok"][tok:tok + 16, C_V:C_V + 1024], 16)
        attend(16, tok, 17, 16, False, tok, q % 2)
    P.barrier()
    st.close()


def load_w_bf16(P, dst, src_ap, stg, k0):
    for hf in range(2):
        s = stg[(k0 + hf) % 2]
        P.dma("sp", s[:], src_ap[:, hf * 512:(hf + 1) * 512].rearrange("(kc p) c -> p kc c", p=128), writes=[s])
        cp(P, "pool" if hf else "dve", dst[:, :, hf * 512:(hf + 1) * 512], s[:], [s], [dst])


def stage6(P, g, D):
    st = ExitStack()
    sb = lambda name, shape, dt=F32: P.sb(name, shape, dt, st)
    wpa = sb("wpa", [128, 8, 1024], BF16); wpb = sb("wpb", [128, 8, 1024], BF16); wo = sb("wo", [128, 8, 1024], BF16)
    stg = [sb(f"wstg{i}", [128, 8, 512]) for i in range(2)]
    load_w_bf16(P, wpa, D["w_proj_a"], stg, 0)
    load_w_bf16(P, wpb, D["w_proj_b"], stg, 0)
    load_w_bf16(P, wo, D["w_out"], stg, 0)
    oat = sb("oat", [128, 8, 512], BF16); obt = sb("obt", [128, 8, 512], BF16)
    mT = sb("mT", [128, 8, 512], BF16)
    ga = [sb(f"ga{i}", [128, 512]) for i in range(2)]; gb = [sb(f"gb{i}", [128, 512]) for i in range(2)]
    m1 = [sb(f"m1_{i}", [128, 512]) for i in range(2)]; m2 = [sb(f"m2_{i}", [128, 512]) for i in range(2)]
    g1r = sb("g1r", [128, 1024]); g1s = sb("g1s", [128, 1024])
    P.dma("sp", g1r[:], D["modd"][0:1, 2048:3072].partition_broadcast(128), reads=[D["modd_t"]], writes=[g1r])
    for q in range(4):
        P.dma("sp", g1s[16 * q:16 * q + 16, :], D["modd"][1 + q:2 + q, 2048:3072].partition_broadcast(16), reads=[D["modd_t"]], writes=[g1s])
    xt = [sb(f"x6_{i}", [128, 1024]) for i in range(2)]
    x1 = [sb(f"x1_{i}", [128, 1024]) for i in range(2)]
    pa = [P.ps(f"pa{i}", [128, 512], F32, st) for i in range(2)]
    pb = [P.ps(f"pb{i}", [128, 512], F32, st) for i in range(2)]
    px = [P.ps(f"px{i}", [128, 512], F32, st) for i in range(2)]
    k = 0
    xk = 0
    for (n0, nb) in [(0, 512), (512, 512), (1024, 512), (1536, 512), (2048, 64)]:
        P.dma("sp", oat[:, :, 0:nb], D["OAT"].rearrange("(fc p) c -> p fc c", p=128)[:, :, n0:n0 + nb], reads=[D["OAT_t"]], writes=[oat])
        P.dma("sp", obt[:, :, 0:nb], D["OBT"].rearrange("(fc p) c -> p fc c", p=128)[:, :, n0:n0 + nb], reads=[D["OBT_t"]], writes=[obt])
        for fo in range(8):
            a_, b_ = pa[k % 2], pb[k % 2]
            ga_, gb_, m1_, m2_ = ga[k % 2], gb[k % 2], m1[k % 2], m2[k % 2]
            k += 1
            P.dma("sp", ga_[:, 0:nb], D["GT"][fo * 128:(fo + 1) * 128, n0:n0 + nb], reads=[D["GT_t"]], writes=[ga_])
            P.dma("sp", gb_[:, 0:nb], D["GT"][1024 + fo * 128:1024 + (fo + 1) * 128, n0:n0 + nb], reads=[D["GT_t"]], writes=[gb_])
            for kc in range(8):
                mm(P, a_[:, 0:nb], wpa[:, kc, fo * 128:(fo + 1) * 128], oat[:, kc, 0:nb], kc == 0, kc == 7, [wpa, oat], [a_])
            for kc in range(8):
                mm(P, b_[:, 0:nb], wpb[:, kc, fo * 128:(fo + 1) * 128], obt[:, kc, 0:nb], kc == 0, kc == 7, [wpb, obt], [b_])
            tt(P, "dve", m1_[:, 0:nb], a_[:, 0:nb], ga_[:, 0:nb], ALU.mult, [a_, ga_], [m1_])
            tt(P, "dve", m2_[:, 0:nb], b_[:, 0:nb], gb_[:, 0:nb], ALU.mult, [b_, gb_], [m2_])
            tt(P, "pool", mT[:, fo, 0:nb], m1_[:, 0:nb], m2_[:, 0:nb], ALU.add, [m1_, m2_], [mT])
        for t0 in range(0, nb, 128):
            n = min(128, nb - t0)
            x_, x1_ = xt[xk % 2], x1[xk % 2]; xk += 1
            P.dma("sp", x_[0:n, :], D["xin"][n0 + t0:n0 + t0 + n, :], writes=[x_])
            g1 = g1r if n0 < 2048 else g1s
            for hf in range(2):
                p = px[hf]
                cs = slice(hf * 512, (hf + 1) * 512)
                for kc in range(8):
                    mm(P, p[0:n, :], mT[:, kc, t0:t0 + n], wo[:, kc, cs], kc == 0, kc == 7, [mT, wo], [p])
                tt(P, "dve", x1_[0:n, cs], p[0:n, :], g1[0:n, cs], ALU.mult, [p, g1], [x1_])
                tt(P, "pool", x1_[0:n, cs], x1_[0:n, cs], x_[0:n, cs], ALU.add, [x1_, x_], [x1_])
            P.dma("pool", D["X1"][n0 + t0:n0 + t0 + n, :], x1_[0:n, :], reads=[x1_], writes=[D["X1_t"]])
    P.barrier()
    st.close()


def stage6b(P, g, D):
    st = ExitStack()
    h2T = P.sb("h2T", [128, 8, NT], BF16, st)
    norm_to_featmajor(P, g, D, D["X1"], h2T, g.scale2, 24)
    st2 = ExitStack()
    sb = lambda name, shape, dt=F32: P.sb(name, shape, dt, st2)
    P.dma("pool", D["H2T"].rearrange("(fc p) c -> p fc c", p=128)[:, :, :], h2T[:], reads=[h2T], writes=[D["H2T_t"]])
    wq = sb("wq", [128, 8, 1024], BF16)
    stg = [sb(f"wstgq{i}", [128, 8, 512]) for i in range(2)]
    load_w_bf16(P, wq, D["w_pq"], stg, 0)
    qs = [sb(f"qs{i}", [128, 512], BF16) for i in range(2)]
    pq = [P.ps(f"pq{i}", [128, 512], F32, st2) for i in range(2)]
    k = 0
    for (n0, nb) in [(0, 512), (512, 512), (1024, 512), (1536, 512), (2048, 64)]:
        for fo in range(8):
            p, s = pq[k % 2], qs[k % 2]; k += 1
            for kc in range(8):
                mm(P, p[:, 0:nb], wq[:, kc, fo * 128:(fo + 1) * 128], h2T[:, kc, n0:n0 + nb], kc == 0, kc == 7, [wq, h2T], [p])
            cp(P, "act", s[:, 0:nb], p[:, 0:nb], [p], [s])
            P.dma("pool", D["QPT"][fo * 128:(fo + 1) * 128, n0:n0 + nb], s[:, 0:nb], reads=[s], writes=[D["QPT_t"]])
    P.barrier()
    st2.close()
    st.close()


def vmax(P, out, in_, reads, writes):
    return P.op("dve", lambda e: e.max(out=out, in_=in_), reads=reads, writes=writes)


def mrep(P, out, rep, vals, reads, writes):
    return P.op("dve", lambda e: e.match_replace(out=out, in_to_replace=rep, in_values=vals, imm_value=NEG), reads=reads, writes=writes)


def stage7_prep(P, g, D):
    st = ExitStack()
    sb = lambda name, shape, dt=F32: P.sb(name, shape, dt, st)
    uf = [sb(f"uf{i}", [128, 1024]) for i in range(2)]
    ub = [sb(f"ub{i}", [128, 1024], BF16) for i in range(2)]
    vf = [sb(f"vf{i}", [128, 1024]) for i in range(2)]
    vb = [sb(f"vb{i}", [128, 1024], BF16) for i in range(2)]
    sT = [sb(f"usT{i}", [128, 8, 512], BF16) for i in range(2)]
    pT = [P.ps(f"upT{i}", [128, 8, 128], BF16, st) for i in range(2)]
    NE = int(os.environ.get("NET", "128"))
    for et in range(NE):
        u, ubb, v, vbb = uf[et % 2], ub[et % 2], vf[et % 2], vb[et % 2]
        P.dma("sp", u[:], D["peer_u"][et * 128:(et + 1) * 128, :], writes=[u])
        P.dma("sp", v[:], D["peer_v"][et * 128:(et + 1) * 128, :], writes=[v])
        cp(P, "dve", ubb[:], u[:], [u], [ubb])
        cp(P, "pool", vbb[:], v[:], [v], [vbb])
        P.dma("pool", D["Vbf"].rearrange("(g j p) d -> g p j d", j=4, p=128)[et // 4, :, et % 4, :], vbb[:], reads=[vbb], writes=[D["Vbf_t"]])
        p = pT[et % 2]
        s = sT[(et // 4) % 2]
        for kc in range(8):
            tr(P, p[:, kc, :], ubb[:, kc * 128:(kc + 1) * 128], g.identb[:], [ubb, g.identb], [p])
        cp(P, "act", s[:, :, (et % 4) * 128:(et % 4 + 1) * 128], p[:], [p], [s])
        if et % 4 == 3:
            e0 = (et - 3) * 128
            P.dma("pool", D["UT"].rearrange("(g p) (kc e) -> g p kc e", p=128, kc=8)[e0 // 512, :, :, :], s[:], reads=[s], writes=[D["UT_t"]])
    P.barrier()
    st.close()


def stage7_peer(P, g, D):
    st = ExitStack()
    sb = lambda name, shape, dt=F32: P.sb(name, shape, dt, st)
    keysT = sb("keysT", [128, 8, 128], BF16)
    kt = sb("kt_f", [128, 128])
    pT = [P.ps(f"ppT{i}", [128, 8, 128], BF16, st) for i in range(1)]
    ps12 = P.ps("ps12", [128, 4, 128], F32, st)
    for h in range(8):
        P.dma("sp", kt[:, 0:64], D["peer_keys"][h, 0, :, :], writes=[kt])
        P.dma("sp", kt[:, 64:128], D["peer_keys"][h, 1, :, :], writes=[kt])
        tr(P, ps12[:, h % 4, :], kt[:], g.identf[:], [kt, g.identf], [ps12])
        cp(P, "act", keysT[:, h, :], ps12[:, h % 4, :], [ps12], [keysT])
    g2r = sb("g2r", [128, 1024]); g2s = sb("g2s", [128, 1024])
    P.dma("sp", g2r[:], D["modd"][0:1, 5120:6144].partition_broadcast(128), reads=[D["modd_t"]], writes=[g2r])
    for q in range(4):
        P.dma("sp", g2s[16 * q:16 * q + 16, :], D["modd"][1 + q:2 + q, 5120:6144].partition_broadcast(16), reads=[D["modd_t"]], writes=[g2s])
    h2t = [sb(f"h2t{i}", [128, 8, 128], BF16) for i in range(2)]; qpt = [sb(f"qpt{i}", [128, 8, 128], BF16) for i in range(2)]
    S12 = sb("S12", [128, 16, 128]); S1P = sb("S1P", [128, 8, 128]); srep = sb("srep", [128, 128])
    T16 = sb("T16", [128, 16, 16]); cand = sb("cand", [128, 8, 256]); crep = sb("crep", [128, 256])
    top16c = sb("top16c", [128, 8, 16]); ez = sb("ez", [128, 8, 16]); Z = sb("Zp", [128, 8]); cinv = sb("cinv", [128, 8]); lnc = sb("lnc", [128, 8]); thr = sb("pthr", [128, 8]); mtiny = sb("mtiny", [128, 1])
    mset(P, "dve", mtiny[:], -1e-5, [mtiny])
    SUBI = 8
    NSUB = 128 // SUBI
    W_ = SUBI * 128
    zb = [sb(f"zb{i}", [128, W_]) for i in range(3)]; eb = [sb(f"eb{i}", [128, W_]) for i in range(3)]
    gmb = [sb(f"gmb{i}", [128, W_], BF16) for i in range(3)]
    Gp = [P.ps(f"pGp{i}", [128, 512], F32, st) for i in range(2)]
    Aqs = [sb(f"Aq{i}", [128, W_], BF16) for i in range(2)]; GAs = [sb(f"GAq{i}", [128, W_], BF16) for i in range(2)]
    GATs = [sb(f"GAT{i}", [128, SUBI, 128], BF16) for i in range(2)]
    ub = [sb(f"pub{i}", [128, 8, 512], BF16) for i in range(2)]
    vb = [sb(f"pvb{i}", [128, 4, 1024], BF16) for i in range(2)]
    x1t = [sb(f"px1_{i}", [128, 1024]) for i in range(2)]; yt = sb("pyt", [128, 1024])
    pA = [P.ps(f"ppA{i}", [128, 512], F32, st) for i in range(2)]
    po = [P.ps(f"ppo{i}", [128, 512], F32, st) for i in range(2)]
    uk = 0; vk = 0; ak = 0; zk = 0; tk = 0; gk = 0
    NTL = int(os.environ.get("NTL", "17"))
    for ti, (r0, n) in enumerate(TILES[:NTL]):
        h2t_, qpt_, x1t_ = h2t[ti % 2], qpt[ti % 2], x1t[ti % 2]
        P.dma("sp", h2t_[:, :, 0:n], D["H2T"].rearrange("(fc p) c -> p fc c", p=128)[:, :, r0:r0 + n], reads=[D["H2T_t"]], writes=[h2t_])
        P.dma("sp", qpt_[:, :, 0:n], D["QPT"].rearrange("(fc p) c -> p fc c", p=128)[:, :, r0:r0 + n], reads=[D["QPT_t"]], writes=[qpt_])
        P.dma("sp", x1t_[0:n, :], D["X1"][r0:r0 + n, :], reads=[D["X1_t"]], writes=[x1t_])
        for hg in range(4):
            p = ps12
            for j in range(4):
                hp = hg * 4 + j
                h, pp = hp // 2, hp % 2
                pb = pp * 64
                mm(P, p[0:n, j, :], qpt_[pb:pb + 64, h, 0:n], keysT[pb:pb + 64, h, :], True, True, [qpt_, keysT], [p])
            cp(P, "act", S12[0:n, hg * 4:(hg + 1) * 4, :], p[0:n, :, :], [p], [S12])
        for hp in range(16):
            vmax(P, T16[0:n, hp, 0:8], S12[0:n, hp, :], [S12], [T16])
            mrep(P, srep[0:n, :], T16[0:n, hp, 0:8], S12[0:n, hp, :], [S12, T16], [srep])
            vmax(P, T16[0:n, hp, 8:16], srep[0:n, :], [srep], [T16])
        T16v = T16[0:n, :, :].rearrange("p (h t) k -> p h t k", t=2)
        tt(P, "dve", cand[0:n, :, :].rearrange("p h (i j) -> p h i j", i=16), T16v[:, :, 0, :].unsqueeze(3).to_broadcast([n, 8, 16, 16]),
           T16v[:, :, 1, :].unsqueeze(2).to_broadcast([n, 8, 16, 16]), ALU.add, [T16], [cand])
        for h in range(8):
            vmax(P, top16c[0:n, h, 0:8], cand[0:n, h, :], [cand], [top16c])
            mrep(P, crep[0:n, :], top16c[0:n, h, 0:8], cand[0:n, h, :], [cand, top16c], [crep])
            vmax(P, top16c[0:n, h, 8:16], crep[0:n, :], [crep], [top16c])
        tau = top16c[0:n, :, 15:16]
        tt(P, "dve", ez[0:n, :, :], top16c[0:n, :, :], tau.to_broadcast([n, 8, 16]), ALU.subtract, [top16c], [ez])
        act(P, ez[0:n, :, :], ez[0:n, :, :], AF.Exp, [ez], [ez])
        red(P, "dve", Z[0:n, :], ez[0:n, :, :], ALU.add, [ez], [Z])
        recip(P, cinv[0:n, :], Z[0:n, :], [Z], [cinv])
        act(P, lnc[0:n, :], cinv[0:n, :], AF.Ln, [cinv], [lnc])
        S12v = S12[0:n, :, :].rearrange("p (h t) k -> p h t k", t=2)
        tt(P, "dve", S1P[0:n, :, :], S12v[:, :, 0, :], tau.to_broadcast([n, 8, 128]), ALU.subtract, [S12, top16c], [S1P])
        def emitA(ib):
            nonlocal uk, ak
            Aq = Aqs[ib % 2]
            for eg in range(W_ // 512):
                e0 = ib * W_ + eg * 512
                u = ub[uk % 2]; uk += 1
                P.dma("sp", u[:], D["UT"].rearrange("(g p) (kc e) -> g p kc e", p=128, kc=8)[e0 // 512, :, :, :], reads=[D["UT_t"]], writes=[u])
                p = pA[ak % 2]; ak += 1
                for kc in range(8):
                    mm(P, p[0:n, :], h2t_[:, kc, 0:n], u[:, kc, :], kc == 0, kc == 7, [h2t_, u], [p])
                act(P, Aq[0:n, eg * 512:(eg + 1) * 512], p[0:n, :], AF.Gelu_apprx_tanh, [p], [Aq])

        def emitZ(ib, h):
            nonlocal zk
            z_, e_ = zb[zk % 3], eb[zk % 3]; zk += 1
            z3 = z_[0:n, :].rearrange("p (i j) -> p i j", i=SUBI)
            tt(P, "dve", z3, S1P[0:n, h, ib * SUBI:(ib + 1) * SUBI].unsqueeze(2).to_broadcast([n, SUBI, 128]),
               S12v[:, h, 1, :].unsqueeze(1).to_broadcast([n, SUBI, 128]), ALU.add, [S1P, S12], [z_])
            act(P, e_[0:n, :], z_[0:n, :], AF.Exp, [z_, lnc], [e_], bias=lnc[0:n, h:h + 1])
            return z_, e_

        emitA(0)
        pend = emitZ(0, 0)
        for ib in range(NSUB):
            Aq, GA, GAT = Aqs[ib % 2], GAs[ib % 2], GATs[ib % 2]
            if ib + 1 < NSUB:
                emitA(ib + 1)
            for h in range(8):
                z_, e_ = pend
                if h + 1 < 8:
                    pend = emitZ(ib, h + 1)
                elif ib + 1 < NSUB:
                    pend = emitZ(ib + 1, 0)
                gm = gmb[gk % 3]; gk += 1
                stt(P, "dve", gm[0:n, :], z_[0:n, :], -1e-5, e_[0:n, :], ALU.is_ge, ALU.mult, [z_, e_], [gm])
                for cg in range(W_ // 512):
                    mm(P, Gp[cg][0:n, :], g.identb[0:n, 0:n], gm[0:n, cg * 512:(cg + 1) * 512], h == 0, h == 7, [gm, g.identb], [Gp[cg]])
            for cg in range(W_ // 512):
                tt(P, "dve", GA[0:n, cg * 512:(cg + 1) * 512], Gp[cg][0:n, :], Aq[0:n, cg * 512:(cg + 1) * 512], ALU.mult, [Gp[cg], Aq], [GA])
            for j0 in range(0, SUBI, 8):
                pt = pT[0]; tk += 1
                for jj in range(8):
                    et = j0 + jj
                    tr(P, pt[:, jj, 0:n], GA[0:n, et * 128:(et + 1) * 128], g.identb[0:n, 0:n], [GA, g.identb], [pt])
                cp(P, "act", GAT[:, j0:j0 + 8, 0:n], pt[:, :, 0:n], [pt], [GAT])
            for vg in range(SUBI // 4):
                v = vb[vk % 2]; vk += 1
                e0 = (ib * SUBI + vg * 4) * 128
                P.dma("sp", v[:], D["Vbf"].rearrange("(g j p) d -> g p j d", j=4, p=128)[e0 // 512, :, :, :], reads=[D["Vbf_t"]], writes=[v])
                for j in range(4):
                    et = vg * 4 + j
                    first = (ib == 0 and et == 0)
                    last = (ib == NSUB - 1 and et == SUBI - 1)
                    for hf in range(2):
                        mm(P, po[hf][0:n, :], GAT[:, et, 0:n], v[:, j, hf * 512:(hf + 1) * 512], first, last, [GAT, v], [po[hf]])
        g2 = g2r if r0 < 2048 else g2s
        for hf in range(2):
            cs = slice(hf * 512, (hf + 1) * 512)
            tt(P, "dve", yt[0:n, cs], po[hf][0:n, :], g2[0:n, cs], ALU.mult, [po[hf], g2], [yt])
        if D.get("PEERdbg") is not None:
            P.dma("pool", D["PEERdbg"][r0:r0 + n, :], yt[0:n, :], reads=[yt])
        tt(P, "pool", yt[0:n, :], yt[0:n, :], x1t_[0:n, :], ALU.add, [yt, x1t_], [yt])
        P.dma("pool", D["y"][r0:r0 + n, :], yt[0:n, :], reads=[yt], writes=[D["y_t"]])
    P.barrier()
    st.close()


def declare(nc, P, D, debug=False):
    def din(name, shape, dt=F32):
        D[name] = nc.dram_tensor(name, list(shape), dt, kind="ExternalInput").ap()

    def dscr(name, shape, dt=F32, kind="Internal"):
        D[name] = nc.dram_tensor(name, list(shape), dt, kind=("ExternalOutput" if debug else kind)).ap()
        D[name + "_t"] = T(None, name)

    din("ident", [128, 128]); din("cin", [5, 1024]); din("xin", [NT, 1024])
    din("w_ada", [1024, 6144]); din("b_ada", [1, 6144]); din("norm1_w", [1024]); din("norm2_w", [1024]); din("b_gate", [2048])
    din("w_in", [1024, 9064]); din("cos", [NT, 32]); din("sin", [NT, 32]); din("k_norm_w", [1, 64]); din("q_norm_w", [1, 64])
    din("mu_rw", [1, RW_IN])
    for nm in ["w0", "a0", "k_k", "k_a", "r_k", "lnx_w", "lnx_b"]:
        din(nm, [1, 1024])
    din("w_up", [64, 1024]); din("a_up", [64, 1024]); din("g_up", [160, 1024])
    din("sshift", [4, RW_IN]); din("swkv", [4, 16, 64, 64])
    for nm in ["m_tri", "m_ones", "m_sl", "m_su", "m_u"]:
        din(nm, [128, 128])
    din("m_valid", [128, 2])
    dscr("modd", [5, 6144]); dscr("Ptok", [NT, TOKW]); dscr("GT", [2048, NT])
    dscr("BC", [UC, 16])
    for nm in ["Vs", "KPs", "BPs", "Gs"]:
        dscr(nm, [UC, 1024])
    for nm in ["RT", "AT", "KT", "BT", "GCT"]:
        dscr(nm, [1024, UC])
    for nm in ["OAT", "OBT", "QT", "KTn", "H2T", "QPT"]:
        dscr(nm, [1024, NT], BF16)
    dscr("QIT", [512, NT], BF16); dscr("KIT", [64, NT], BF16); dscr("WI", [NT, 8])
    dscr("X1", [NT, 1024], F32, "ExternalOutput" if os.environ.get("DBG") else "Internal")
    dscr("UT", [4096, 4096], BF16); dscr("Vbf", [16384, 1024], BF16)
    din("cache_k", [4, 2048, 1024]); din("cache_v", [4, 2048, 1024]); din("cache_kidx", [4, 2048, 64]); din("m_pow2", [128, NBIS])
    for nm in ["w_proj_a", "w_proj_b", "w_out", "w_pq"]:
        din(nm, [1024, 1024])
    din("peer_keys", [8, 2, 128, 64]); din("peer_u", [16384, 1024]); din("peer_v", [16384, 1024])
    for nm, shp in [("y", [NT, 1024]), ("wkv", [5 * 16 * 64, 64]), ("kout", [NT, 1024]), ("vout", [NT, 1024]), ("kidx", [NT, 64]), ("shift", [5, RW_IN])]:
        D[nm] = nc.dram_tensor(nm, shp, F32, kind="ExternalOutput").ap()
        D[nm + "_t"] = T(None, nm)


def host_consts():
    inv = (10000.0 ** (-np.arange(32, dtype=np.float32) / 32)).astype(np.float32)
    pos = np.concatenate([np.arange(2048), np.tile(2048 + np.arange(16), 4)]).astype(np.float32)
    ang = pos[:, None] * inv[None, :]
    idx = np.arange(128)
    same = (idx[:, None] // 64) == (idx[None, :] // 64)
    f32 = lambda a: np.ascontiguousarray(a, dtype=np.float32)
    valid = np.ones((128, 2), np.float32)
    valid[:, 1] = ((idx % 64) < 16)
    return {"ident": np.eye(128, dtype=np.float32), "cos": np.cos(ang).astype(np.float32), "sin": np.sin(ang).astype(np.float32),
            "m_tri": f32(same & (idx[:, None] <= idx[None, :])), "m_ones": f32(same), "m_sl": f32(same & (idx[:, None] > idx[None, :])),
            "m_su": f32(same & (idx[:, None] < idx[None, :])), "m_u": f32(same & (idx[:, None] <= idx[None, :])), "m_valid": valid,
            "m_pow2": np.tile((2.0 ** -(np.arange(NBIS) + 1.0)).astype(np.float32)[None, :], (128, 1))}


def run_all(P, g, D):
    stage0(P, g, D)
    st = ExitStack()
    hT = P.sb("hT", [128, 8, NT], BF16, st)
    norm_to_featmajor(P, g, D, D["xin"], hT, g.scale1, 0)
    stage2(P, g, D, hT)
    st.close()
    stage3_dsa(P, g, D)
    stage3_rw(P, g, D)
    stage4_scan(P, g, D)
    stage5_attn(P, g, D)
    stage6(P, g, D)
    stage6b(P, g, D)
    stage7_prep(P, g, D)
    stage7_peer(P, g, D)


def core_inputs(inp, c, consts, local=False):
    f = lambda a: np.ascontiguousarray(np.asarray(a, dtype=np.float32))
    m = dict(consts)
    W = lambda k: f(np.asarray(inp[k])[0])
    m.update({"w_ada": W("w_ada"), "b_ada": f(inp["b_ada"]), "norm1_w": W("norm1_w"), "norm2_w": W("norm2_w"), "b_gate": W("b_gate"), "w_in": W("w_in"),
              "k_norm_w": f(inp["k_norm_w"]), "q_norm_w": f(inp["q_norm_w"]), "mu_rw": f(inp["mu_rw"]), "w0": f(inp["w0"]), "a0": f(inp["a0"]),
              "k_k": f(inp["k_k"]), "k_a": f(inp["k_a"]), "r_k": f(np.asarray(inp["r_k"]).reshape(1, 1024)), "lnx_w": f(inp["lnx_w"]), "lnx_b": f(inp["lnx_b"]),
              "w_up": W("w_up"), "a_up": W("a_up"), "g_up": W("g_up"), "w_proj_a": W("w_proj_a"), "w_proj_b": W("w_proj_b"), "w_out": W("w_out"),
              "w_pq": W("w_pq"), "peer_keys": W("peer_keys"), "peer_u": W("peer_u"), "peer_v": W("peer_v")})
    pc = 0 if local else c
    sl = slice(0, 4) if local else slice(4 * c, 4 * c + 4)
    m["xin"] = f(np.concatenate([np.asarray(inp["x_prompt"])[pc], np.asarray(inp["x_sample"])[sl].reshape(64, 1024)], 0))
    m["cin"] = f(np.concatenate([np.asarray(inp["c_prompt"])[pc:pc + 1], np.asarray(inp["c_sample"])[sl]], 0))
    m["sshift"] = f(np.asarray(inp["state_shift"])[0][sl])
    m["swkv"] = f(np.asarray(inp["state_wkv"])[0][sl])
    m["cache_k"] = f(np.asarray(inp["cache_k"])[0][sl].reshape(4, 2048, 1024))
    m["cache_v"] = f(np.asarray(inp["cache_v"])[0][sl].reshape(4, 2048, 1024))
    m["cache_kidx"] = f(np.asarray(inp["cache_kidx"])[0][sl])
    return m


def build_program():
    nc = bass.Bass("TRN2", target_bir_lowering=False)
    P = Prog(nc)
    D = {}
    declare(nc, P, D)
    g = G()
    g.epsc = P.sb("epsc", [128, 1], F32)
    mset(P, "dve", g.epsc[:], EPS, [g.epsc])
    run_all(P, g, D)
    P.emit()
    return nc


def kernel(**inp):
    nc = build_program()
    consts = host_consts()
    in_maps = [core_inputs(inp, c, consts) for c in range(8)]
    res = run_bass_kernel_spmd(nc, in_maps, core_ids=list(range(8)))
    R = res.results
    cat = lambda name, sl, shp: np.stack([R[c][name][sl].reshape(shp) for c in range(8)], 0)
    y_p = cat("y", slice(0, 2048), (2048, 1024))
    y_s = cat("y", slice(2048, NT), (4, 16, 1024)).reshape(32, 16, 1024)
    wkv_p = cat("wkv", slice(0, 1024), (16, 64, 64))[None]
    wkv_s = cat("wkv", slice(1024, 5120), (4, 16, 64, 64)).reshape(32, 16, 64, 64)[None]
    sh_p = cat("shift", slice(0, 1), (RW_IN,))[None]
    sh_s = cat("shift", slice(1, 5), (4, RW_IN)).reshape(32, RW_IN)[None]
    k_p = cat("kout", slice(0, 2048), (2048, 16, 64))[None]
    k_s = cat("kout", slice(2048, NT), (4, 16, 16, 64)).reshape(32, 16, 16, 64)[None]
    v_p = cat("vout", slice(0, 2048), (2048, 16, 64))[None]
    v_s = cat("vout", slice(2048, NT), (4, 16, 16, 64)).reshape(32, 16, 16, 64)[None]
    ki_p = cat("kidx", slice(0, 2048), (2048, 64))[None]
    ki_s = cat("kidx", slice(2048, NT), (4, 16, 64)).reshape(32, 16, 64)[None]
    return (y_p, y_s, wkv_p, sh_p, k_p, v_p, ki_p, wkv_s, sh_s, k_s, v_s, ki_s)
```

```python
import numpy as np
from contextlib import ExitStack
import concourse.bass as bass
import concourse.mybir as mybir
from concourse.bass_utils import run_bass_kernel_spmd

F32 = mybir.dt.float32
BF16 = mybir.dt.bfloat16
I32 = mybir.dt.int32
U32 = mybir.dt.uint32
AF = mybir.ActivationFunctionType
ALU = mybir.AluOpType
AX = mybir.AxisListType

ENGS = ["pe", "act", "dve", "pool", "sp"]
NDMA = {"sp": 40, "pool": 24, "act": 8}


class Dep:
    __slots__ = ("w", "r", "name")

    def __init__(self, name=""):
        self.w = None
        self.r = []
        self.name = name


class Bank:
    def __init__(self):
        self.last = {}
        self.pe_rows = None


class T:
    def __init__(self, h, name):
        self.h = h
        self.name = name
        self.dep = Dep(name)
        self.subs = {}
        self.bank = None

    def __getitem__(self, idx):
        return self.h[idx]

    def sub(self, key):
        if key not in self.subs:
            self.subs[key] = Dep(f"{self.name}.{key}")
        return self.subs[key]


class Slot:
    def __init__(self, t, i):
        self.base = t.h[:, i, :]
        self.dep = Dep(f"{t.name}[{i}]")
        self.bank = t.bank

    def __getitem__(self, idx):
        return self.base[idx]


class SlotAP:
    def __init__(self, t, ap):
        self.base = ap
        self.dep = Dep(t.name + "[ap]")
        self.bank = t.bank

    def __getitem__(self, idx):
        return self.base[idx]


def _dep(x):
    return x.dep if hasattr(x, "dep") else x


class Prog:
    def __init__(self, nc):
        self.nc = nc
        self.es = ExitStack()
        self.ops = {e: [] for e in ENGS}
        self.cnt = {e: 0 for e in ENGS}
        self.esem = {}
        for e in ENGS:
            self.esem[e] = self.es.enter_context(nc.semaphore("s_" + e))
        self.dsem = {}
        self.dval = {}
        self.dnext = {}
        for q, n in NDMA.items():
            self.dsem[q] = [self.es.enter_context(nc.semaphore(f"d_{q}{i}")) for i in range(n)]
            self.dval[q] = [0] * n
            self.dnext[q] = 0
        self.seen = {e: {} for e in ENGS}
        self.semobj = {}
        self.nwaits = 0

    def _uniq(self, name):
        if not hasattr(self, "_names"):
            self._names = {}
        k = self._names.get(name, 0)
        self._names[name] = k + 1
        return name if k == 0 else f"{name}__{k}"

    def sb(self, name, shape, dtype, stack=None):
        name = self._uniq(name)
        h = (stack or self.es).enter_context(self.nc.sbuf_tensor(name, list(shape), dtype))
        return T(h, name)

    def ps(self, name, shape, dtype=F32, stack=None):
        name = self._uniq(name)
        h = (stack or self.es).enter_context(self.nc.psum_tensor(name, list(shape), dtype))
        t = T(h, name)
        t.bank = Bank()
        return t

    def dram(self, name, shape, dtype, kind="Internal"):
        h = self.nc.dram_tensor(name, list(shape), dtype, kind=kind)
        return T(h, name)

    def _waits(self, eng, reads, writes, extra=()):
        evs = list(extra)
        for b in reads:
            b = _dep(b)
            if b.w is not None:
                evs.append(b.w)
        for b in writes:
            b = _dep(b)
            if b.w is not None:
                evs.append(b.w)
            evs.extend(b.r)
        need = {}
        for (key, sem, val, src) in evs:
            if src == "pe" and eng == "pe":
                continue
            if self.seen[eng].get(key, 0) >= val:
                continue
            if need.get(key, (None, 0))[1] < val:
                need[key] = (sem, val)
        for key, (sem, val) in need.items():
            self.seen[eng][key] = val
        return list(need.values())

    def _record(self, ev, reads, writes):
        for b in reads:
            _dep(b).r.append(ev)
        for b in writes:
            b = _dep(b)
            b.w = ev
            b.r = []

    def op(self, eng, fn, reads=(), writes=(), pe_rows=None):
        banks = {}
        for b in list(reads) + list(writes):
            bk = getattr(b, "bank", None)
            if bk is not None:
                banks[id(bk)] = bk
        extra = [ev for bk in banks.values() for e2, ev in bk.last.items() if e2 != eng]
        if eng == "pe" and pe_rows is not None:
            for bk in banks.values():
                if bk.pe_rows is not None and bk.pe_rows != pe_rows and "pe" in bk.last and (pe_rows[1] < 128 or bk.pe_rows[1] < 128):
                    k_, s_, v_, _ = bk.last["pe"]
                    extra.append((k_, s_, v_, "force"))
                bk.pe_rows = pe_rows
        waits = self._waits(eng, reads, writes, extra)
        self.cnt[eng] += 1
        ev = (eng, self.esem[eng], self.cnt[eng], eng)
        for bk in banks.values():
            bk.last[eng] = ev
        self._record(ev, reads, writes)
        self.ops[eng].append((waits, fn, (self.esem[eng], 1)))
        self.nwaits += len(waits)
        return ev

    def dma(self, q, out, in_, reads=(), writes=(), **kw):
        i = self.dnext[q]
        self.dnext[q] = (i + 1) % len(self.dsem[q])
        sem = self.dsem[q][i]
        key = (q, i)
        waits = self._waits(q, reads, writes)
        prev = self.dval[q][i]
        if prev > 0 and self.seen[q].get(key, 0) < prev:
            waits.append((sem, prev))
            self.seen[q][key] = prev
        self.dval[q][i] = prev + 16
        ev = (key, sem, prev + 16, "dma")
        self._record(ev, reads, writes)
        self.ops[q].append((waits, lambda e: e.dma_start(out=out, in_=in_, **kw), (sem, 16)))
        self.nwaits += len(waits)
        return ev

    def barrier(self):
        evs = []
        for e in ENGS:
            if self.cnt[e] > 0:
                evs.append((e, self.esem[e], self.cnt[e]))
        for q in self.dsem:
            for i, v in enumerate(self.dval[q]):
                if v > 0:
                    evs.append(((q, i), self.dsem[q][i], v))
        for e in ENGS:
            waits = []
            for key, sem, val in evs:
                if key == e:
                    continue
                if self.seen[e].get(key, 0) >= val:
                    continue
                self.seen[e][key] = val
                waits.append((sem, val))
            if waits:
                self.ops[e].append((waits, None, None))

    def emit(self):
        self.barrier()
        nc = self.nc
        with nc.Block() as block:
            def run(e, lst):
                for waits, fn, inc in lst:
                    for sem, val in waits:
                        e.wait_ge(sem, val)
                    if fn is not None:
                        ins = fn(e)
                        ins.then_inc(inc[0], inc[1])

            @block.tensor
            def _(e):
                run(e, self.ops["pe"])

            @block.scalar
            def _(e):
                run(e, self.ops["act"])

            @block.vector
            def _(e):
                run(e, self.ops["dve"])

            @block.gpsimd
            def _(e):
                run(e, self.ops["pool"])

            @block.sync
            def _(e):
                run(e, self.ops["sp"])
        self.es.close()
EPS = 1e-6
NT = 2112
TILES = [(i * 128, 128) for i in range(16)] + [(2048, 64)]
RW_IN = 3360
TOKW = 7016


def mm(P, out, lhsT, rhs, start, stop, reads, writes):
    rows = (lhsT.base_partition(), lhsT.partition_size())
    return P.op("pe", lambda e: e.matmul(out, lhsT, rhs, start=start, stop=stop), reads=reads, writes=writes, pe_rows=rows)


def tr(P, out, in_, ident, reads, writes):
    return P.op("pe", lambda e: e.transpose(out, in_, ident), reads=reads, writes=writes)


def act(P, out, in_, func, reads, writes, **kw):
    return P.op("act", lambda e: e.activation(out=out, in_=in_, func=func, **kw), reads=reads, writes=writes)


def ts(P, eng, out, in0, s1, s2, op0, op1=None, reads=(), writes=(), **kw):
    def f(e):
        if op1 is None:
            return e.tensor_scalar(out=out, in0=in0, scalar1=s1, scalar2=s2, op0=op0, **kw)
        return e.tensor_scalar(out=out, in0=in0, scalar1=s1, scalar2=s2, op0=op0, op1=op1, **kw)
    return P.op(eng, f, reads=reads, writes=writes)


def tt(P, eng, out, in0, in1, op, reads, writes):
    if eng == "pool":
        eng = "dve"
    return P.op(eng, lambda e: e.tensor_tensor(out=out, in0=in0, in1=in1, op=op), reads=reads, writes=writes)


def cp(P, eng, out, in_, reads, writes):
    if eng == "pool":
        eng = "act"
    if eng == "act":
        return act(P, out, in_, AF.Copy, reads, writes)
    return P.op(eng, lambda e: e.tensor_copy(out, in_), reads=reads, writes=writes)


class G:
    pass


def featvec(P, g, st, name, vec_ap, n):
    tmp = P.sb(name + "_t", [n, 128], F32, st)
    P.dma("sp", tmp[:], vec_ap.rearrange("(c p) -> c p", p=128), writes=[tmp])
    ps = P.ps(name + "_p", [128, n], F32, st)
    tr(P, ps[:], tmp[:], g.identf[0:n, 0:n], [tmp, g.identf], [ps])
    out = getattr(g, name)
    cp(P, "dve", out[:], ps[:], [ps], [out])
    return out


def stage0(P, g, D):
    g.identf = P.sb("identf", [128, 128], F32)
    g.identb = P.sb("identb", [128, 128], BF16)
    g.modT = P.sb("modT", [128, 48, 5], F32)
    g.n1w = P.sb("n1w", [128, 8], F32)
    g.n2w = P.sb("n2w", [128, 8], F32)
    g.bgT = P.sb("bgT", [128, 16], F32)
    g.scale1 = P.sb("scale1", [128, 8, 5], F32)
    g.scale2 = P.sb("scale2", [128, 8, 5], F32)
    st = ExitStack()
    P.dma("sp", g.identf[:], D["ident"][:, :], writes=[g.identf])
    cp(P, "dve", g.identb[:], g.identf[:], [g.identf], [g.identb])
    c5 = P.sb("c5", [5, 1024], F32, st)
    s5 = P.sb("s5", [5, 1024], F32, st)
    P.dma("sp", c5[:], D["cin"][:, :], writes=[c5])
    act(P, s5[:], c5[:], AF.Silu, [c5], [s5])
    sT = P.sb("sT", [128, 8, 5], F32, st)
    pst = P.ps("pst", [128, 8, 5], F32, st)
    for kc in range(8):
        tr(P, pst[:, kc, :], s5[0:5, kc * 128:(kc + 1) * 128], g.identf[0:5, 0:5], [s5, g.identf], [pst])
    cp(P, "dve", sT[:], pst[:], [pst], [sT])
    bada5 = P.sb("bada5", [5, 6144], F32, st)
    P.dma("sp", bada5[:], D["b_ada"][0:1, :].partition_broadcast(5), writes=[bada5])
    mod5 = P.sb("mod5", [5, 6144], F32, st)
    wa = [P.sb(f"wa{i}", [128, 8, 512], F32, st) for i in range(2)]
    pm = [P.ps(f"pm{i}", [5, 512], F32, st) for i in range(2)]
    for gi in range(12):
        w = wa[gi % 2]
        p = pm[gi % 2]
        P.dma("sp", w[:], D["w_ada"][:, gi * 512:(gi + 1) * 512].rearrange("(kc p) c -> p kc c", p=128), writes=[w])
        for kc in range(8):
            mm(P, p[:], sT[:, kc, :], w[:, kc, :], kc == 0, kc == 7, [sT, w], [p])
        tt(P, "dve", mod5[:, gi * 512:(gi + 1) * 512], p[:], bada5[:, gi * 512:(gi + 1) * 512], ALU.add, [p, bada5], [mod5])
    P.dma("sp", D["modd"][:, :], mod5[:], reads=[mod5], writes=[D["modd_t"]])
    pmt = P.ps("pmt", [128, 48, 5], F32, st)
    for c in range(48):
        tr(P, pmt[:, c, :], mod5[0:5, c * 128:(c + 1) * 128], g.identf[0:5, 0:5], [mod5, g.identf], [pmt])
    cp(P, "dve", g.modT[:], pmt[:], [pmt], [g.modT])
    n1 = featvec(P, g, st, "n1w", D["norm1_w"], 8)
    n2 = featvec(P, g, st, "n2w", D["norm2_w"], 8)
    g.bgT = featvec(P, g, st, "bgT", D["b_gate"], 16)
    for c in range(8):
        ts(P, "dve", g.scale1[:, c, :], g.modT[:, 8 + c, :], 1.0, n1[:, c:c + 1], ALU.add, ALU.mult, [g.modT, n1], [g.scale1])
        ts(P, "dve", g.scale2[:, c, :], g.modT[:, 32 + c, :], 1.0, n2[:, c:c + 1], ALU.add, ALU.mult, [g.modT, n2], [g.scale2])
    P.barrier()
    st.close()


def norm_to_featmajor(P, g, D, src_ap, hT, scale, shift_chunk0):
    st = ExitStack()
    xt = [P.sb(f"nx{i}", [128, 1024], F32, st) for i in range(2)]
    xn = [P.sb(f"nxn{i}", [128, 1024], BF16, st) for i in range(2)]
    junk = P.sb("njunk", [128, 1024], F32, st)
    ss = [P.sb(f"nss{i}", [128, 1], F32, st) for i in range(2)]
    rs = [P.sb(f"nrs{i}", [128, 1], F32, st) for i in range(2)]
    pT = [P.ps(f"npT{i}", [128, 8, 128], BF16, st) for i in range(2)]
    for ti, (r0, n) in enumerate(TILES):
        x, xb, s, r, p = xt[ti % 2], xn[ti % 2], ss[ti % 2], rs[ti % 2], pT[ti % 2]
        P.dma("sp", x[0:n, :], src_ap[r0:r0 + n, :], writes=[x])
        act(P, junk[0:n, :], x[0:n, :], AF.Square, [x], [junk, s], accum_out=s[0:n, :])
        act(P, s[0:n, :], s[0:n, :], AF.Sqrt, [s], [s], scale=1.0 / 1024, bias=g.epsc[0:n, :])
        P.op("dve", lambda e, r=r, s=s, n=n: e.reciprocal(r[0:n, :], s[0:n, :]), reads=[s], writes=[r])
        ts(P, "dve", xb[0:n, :], x[0:n, :], r[0:n, :], None, ALU.mult, None, [x, r], [xb])
        for kc in range(8):
            tr(P, p[:, kc, 0:n], xb[0:n, kc * 128:(kc + 1) * 128], g.identb[0:n, 0:n], [xb, g.identb], [p])
        for kc in range(8):
            if ti < 16:
                act(P, hT[:, kc, r0:r0 + n], p[:, kc, 0:n], AF.Identity, [p, scale, g.modT], [hT],
                    scale=scale[:, kc, 0:1], bias=g.modT[:, shift_chunk0 + kc, 0:1])
            else:
                for q in range(4):
                    act(P, hT[:, kc, r0 + 16 * q:r0 + 16 * q + 16], p[:, kc, 16 * q:16 * q + 16], AF.Identity,
                        [p, scale, g.modT], [hT], scale=scale[:, kc, 1 + q:2 + q], bias=g.modT[:, shift_chunk0 + kc, 1 + q:2 + q])
    P.barrier()
    st.close()


def stage2(P, g, D, hT):
    st = ExitStack()
    wf = [P.sb(f"wf{i}", [128, 8, 512], F32, st) for i in range(2)]
    wb = [P.sb(f"wb{i}", [128, 8, 512], BF16, st) for i in range(2)]
    stg = [P.sb(f"stg{i}", [128, 512], F32, st) for i in range(3)]
    pp = [P.ps(f"pp{i}", [128, 512], F32, st) for i in range(3)]
    k = 0
    groups = [(c0, min(512, TOKW - c0), False) for c0 in range(0, TOKW, 512)] + [(TOKW + i * 512, 512, True) for i in range(4)]
    for gi, (c0, gw, isgate) in enumerate(groups):
        w, b = wf[gi % 2], wb[gi % 2]
        P.dma("sp", w[:, :, 0:gw], D["w_in"][:, c0:c0 + gw].rearrange("(kc p) c -> p kc c", p=128), writes=[w])
        cp(P, "pool" if gi % 2 else "dve", b[:, :, 0:gw], w[:, :, 0:gw], [w], [b])
        if not isgate:
            for ti, (r0, n) in enumerate(TILES):
                p, s = pp[k % 3], stg[k % 3]
                for kc in range(8):
                    mm(P, p[0:n, 0:gw], hT[:, kc, r0:r0 + n], b[:, kc, 0:gw], kc == 0, kc == 7, [hT, b], [p])
                cp(P, "act" if k % 2 else "dve", s[0:n, 0:gw], p[0:n, 0:gw], [p], [s])
                P.dma("pool", D["Ptok"][r0:r0 + n, c0:c0 + gw], s[0:n, 0:gw], reads=[s], writes=[D["Ptok_t"].sub(ti)])
                k += 1
        else:
            for j in range(4):
                fch = (c0 - TOKW) // 128 + j
                for (n0, nb) in [(0, 512), (512, 512), (1024, 512), (1536, 512), (2048, 64)]:
                    p, s = pp[k % 3], stg[k % 3]
                    for kc in range(8):
                        mm(P, p[:, 0:nb], b[:, kc, j * 128:(j + 1) * 128], hT[:, kc, n0:n0 + nb], kc == 0, kc == 7, [hT, b], [p])
                    act(P, s[:, 0:nb], p[:, 0:nb], AF.Sigmoid, [p, g.bgT], [s], bias=g.bgT[:, fch:fch + 1])
                    P.dma("pool", D["GT"][fch * 128:(fch + 1) * 128, n0:n0 + nb], s[:, 0:nb], reads=[s], writes=[D["GT_t"]])
                    k += 1
    P.barrier()
    st.close()

C_Q, C_K, C_V, C_QI, C_KI, C_WI = 3360, 4384, 5408, 6432, 6944, 7008


def rope(P, eng, out4, in4, cosb, sinb, tmp, n, reads, writes):
    H = in4.shape[1]
    x1, x2 = in4[:, :, 0, :], in4[:, :, 1, :]
    t = [tmp[0:n, i, 0:H * 32].rearrange("p (h d) -> p h d", h=H) for i in range(4)]
    tt(P, eng, t[0], x1, cosb, ALU.mult, reads, [tmp])
    tt(P, eng, t[1], x2, sinb, ALU.mult, reads, [tmp])
    tt(P, eng, t[2], x2, cosb, ALU.mult, reads, [tmp])
    tt(P, eng, t[3], x1, sinb, ALU.mult, reads, [tmp])
    tt(P, eng, out4[:, :, 0, :], t[0], t[1], ALU.subtract, [tmp], writes)
    tt(P, eng, out4[:, :, 1, :], t[2], t[3], ALU.add, [tmp], writes)


def stage3_dsa(P, g, D):
    st = ExitStack()
    knw = P.sb("knw", [128, 64], F32, st)
    qnw = P.sb("qnw", [128, 64], F32, st)
    P.dma("sp", knw[:], D["k_norm_w"][0:1, :].partition_broadcast(128), writes=[knw])
    P.dma("sp", qnw[:], D["q_norm_w"][0:1, :].partition_broadcast(128), writes=[qnw])
    pd = [P.sb(f"pd{i}", [128, 3656], F32, st) for i in range(2)]
    cs = [P.sb(f"cs{i}", [128, 2, 32], F32, st) for i in range(2)]
    junk = P.sb("djunk", [128, 1024], F32, st)
    ssq = P.sb("dssq", [128, 16], F32, st)
    rst = P.sb("drst", [128, 16], F32, st)
    kn = P.sb("dkn", [128, 1024], F32, st)
    ko = [P.sb(f"dko{i}", [128, 1024], F32, st) for i in range(2)]
    kio = [P.sb(f"dkio{i}", [128, 64], F32, st) for i in range(2)]
    tmp = P.sb("dtmp", [128, 4, 512], F32, st)
    for ti, (r0, n) in enumerate(TILES):
        p, c = pd[ti % 2], cs[ti % 2]
        P.dma("sp", p[0:n, :], D["Ptok"][r0:r0 + n, C_Q:TOKW], reads=[D["Ptok_t"].sub(ti)], writes=[p])
        P.dma("sp", c[0:n, 0, :], D["cos"][r0:r0 + n, :], writes=[c])
        P.dma("sp", c[0:n, 1, :], D["sin"][r0:r0 + n, :], writes=[c])
        P.dma("pool", D["vout"][r0:r0 + n, :], D["Ptok"][r0:r0 + n, C_V:C_V + 1024], reads=[D["Ptok_t"].sub(ti)])
        k = p[0:n, C_K - C_Q:C_K - C_Q + 1024]
        act(P, junk[0:n, :], k, AF.Square, [p], [junk])
        P.op("dve", lambda e, n=n: e.tensor_reduce(out=ssq[0:n, :], in_=junk[0:n, :].rearrange("p (h d) -> p h d", h=16), axis=AX.X, op=ALU.add), reads=[junk], writes=[ssq])
        act(P, ssq[0:n, :], ssq[0:n, :], AF.Sqrt, [ssq], [ssq], scale=1.0 / 64, bias=g.epsc[0:n, :])
        P.op("dve", lambda e, n=n: e.reciprocal(rst[0:n, :], ssq[0:n, :]), reads=[ssq], writes=[rst])
        kn3 = kn[0:n, :].rearrange("p (h d) -> p h d", h=16)
        tt(P, "dve", kn3, k.rearrange("p (h d) -> p h d", h=16), rst[0:n, :].unsqueeze(2).to_broadcast([n, 16, 64]), ALU.mult, [p, rst], [kn])
        tt(P, "dve", kn3, kn3, knw[0:n, :].unsqueeze(1).to_broadcast([n, 16, 64]), ALU.mult, [kn, knw], [kn])
        o = ko[ti % 2]
        cosb = c[0:n, 0, :].unsqueeze(1).to_broadcast([n, 16, 32])
        sinb = c[0:n, 1, :].unsqueeze(1).to_broadcast([n, 16, 32])
        rope(P, "dve", o[0:n, :].rearrange("p (h t d) -> p h t d", h=16, t=2), kn[0:n, :].rearrange("p (h t d) -> p h t d", h=16, t=2),
             cosb, sinb, tmp, n, [kn, c], [o])
        P.dma("pool", D["kout"][r0:r0 + n, :], o[0:n, :], reads=[o], writes=[D["kout_t"].sub(ti)])
        ki = p[0:n, C_KI - C_Q:C_KI - C_Q + 64]
        oi = kio[ti % 2]
        rope(P, "pool", oi[0:n, :].rearrange("p (h t d) -> p h t d", h=1, t=2), ki.rearrange("p (h t d) -> p h t d", h=1, t=2),
             c[0:n, 0, :].unsqueeze(1), c[0:n, 1, :].unsqueeze(1), tmp, n, [p, c], [oi])
        P.dma("pool", D["kidx"][r0:r0 + n, :], oi[0:n, :], reads=[oi], writes=[D["kidx_t"].sub(ti)])
    P.dma("pool", D["shift"][0:1, :], D["Ptok"][2047:2048, 0:RW_IN], reads=[D["Ptok_t"].sub(15)])
    for q in range(4):
        P.dma("pool", D["shift"][1 + q:2 + q, :], D["Ptok"][2048 + 16 * q + 15:2048 + 16 * q + 16, 0:RW_IN], reads=[D["Ptok_t"].sub(16)])
    P.barrier()
    st.close()


def red(P, eng, out, in_, op, reads, writes):
    return P.op(eng, lambda e: e.tensor_reduce(out=out, in_=in_, axis=AX.X, op=op), reads=reads, writes=writes)


def stt(P, eng, out, in0, scalar, in1, op0, op1, reads, writes):
    return P.op(eng, lambda e: e.scalar_tensor_tensor(out=out, in0=in0, scalar=scalar, in1=in1, op0=op0, op1=op1), reads=reads, writes=writes)


def recip(P, out, in_, reads, writes):
    return P.op("dve", lambda e: e.reciprocal(out, in_), reads=reads, writes=writes)


def mset(P, eng, ap, val, writes):
    return P.op(eng, lambda e: e.memset(ap, val), writes=writes)

import os
STOP = int(os.environ.get('STOP', '99'))
SKIP = os.environ.get('SKIP', '')
NUX = int(os.environ.get('NUX', '18'))
NU = 18
UC = NU * 128
GN_EPS = 64e-5


def unit_rows(u):
    if u < 16:
        return [(0, 128 * u, 128)]
    b = 2048 + 32 * (u - 16)
    return [(0, b, 16), (64, b + 16, 16)]


def bload(P, tile, ap, n=128):
    P.dma("sp", tile[0:n, :], ap.partition_broadcast(n), writes=[tile])


def stage3_rw(P, g, D):
    st = ExitStack()
    sb = lambda name, shape, dt=F32: P.sb(name, shape, dt, st)
    mub = sb("mub", [128, RW_IN]); bload(P, mub, D["mu_rw"][0:1, :])
    w0b = sb("w0b", [128, 1024]); bload(P, w0b, D["w0"][0:1, :])
    a0b = sb("a0b", [128, 1024]); bload(P, a0b, D["a0"][0:1, :])
    kkb = sb("kkb", [128, 1024]); bload(P, kkb, D["k_k"][0:1, :])
    kab = sb("kab", [128, 1024]); bload(P, kab, D["k_a"][0:1, :])
    rkb = sb("rkb", [128, 1024]); bload(P, rkb, D["r_k"][0:1, :])
    loraW = sb("loraW", [128, 1024])
    P.dma("sp", loraW[0:64, :], D["w_up"][:, :], writes=[loraW])
    P.dma("sp", loraW[64:128, :], D["a_up"][:, :], writes=[loraW])
    gup1 = sb("gup1", [128, 1024]); P.dma("sp", gup1[:], D["g_up"][0:128, :], writes=[gup1])
    gup2 = sb("gup2", [32, 1024]); P.dma("sp", gup2[:], D["g_up"][128:160, :], writes=[gup2])
    tri = sb("tri", [128, 128]); P.dma("sp", tri[:], D["m_tri"][:, :], writes=[tri])
    ones = sb("onesb", [128, 128]); P.dma("sp", ones[:], D["m_ones"][:, :], writes=[ones])
    valid = sb("valid", [128, 2]); P.dma("sp", valid[:], D["m_valid"][:, :], writes=[valid])
    tiny = sb("tiny12", [128, 1]); mset(P, "dve", tiny[:], 1e-12, [tiny])
    Pc = sb("Pc", [128, RW_IN]); Pp = sb("Pp", [128, RW_IN])
    L = sb("L288", [128, 288]); LT = sb("LT", [128, 3, 128])
    W = {nm: sb("w_" + nm, [128, 1024]) for nm in ["zt", "za", "gt", "lw", "ah", "kk", "junk", "kmod", "b", "t2", "eL", "eN", "eLm", "eC", "gC", "Lsb", "rt", "at", "kt", "bt", "kp", "bp"]}
    ssq = sb("ssq", [128, 16]); rn = sb("rn", [128, 16]); bc = sb("bc", [128, 16])
    stgT = [sb(f"stgT{i}", [128, 8, 128]) for i in range(2)]
    pLT = P.ps("pLT", [128, 3, 128], F32, st)
    pw = [P.ps(f"pw{i}", [128, 512], F32, st) for i in range(4)]
    pT = [P.ps(f"pTr{i}", [128, 4, 128], F32, st) for i in range(2)]
    pk = 0
    tk = 0
    for u in range(NUX):
        samp = u >= 16
        if samp:
            mset(P, "pool", Pc[:], 0.0, [Pc])
            mset(P, "pool", Pp[:], 0.0, [Pp])
        for (d0, t0, nt) in unit_rows(u):
            ti = 16 if samp else u
            P.dma("sp", Pc[d0:d0 + nt, :], D["Ptok"][t0:t0 + nt, 0:RW_IN], reads=[D["Ptok_t"].sub(ti)], writes=[Pc])
            if samp:
                q = (t0 - 2048) // 16
                P.dma("sp", Pp[d0:d0 + 1, :], D["sshift"][q:q + 1, :], writes=[Pp])
                P.dma("sp", Pp[d0 + 1:d0 + 16, :], D["Ptok"][t0:t0 + 15, 0:RW_IN], reads=[D["Ptok_t"].sub(16)], writes=[Pp])
            elif u == 0:
                mset(P, "pool", Pp[0:1, :], 0.0, [Pp])
                P.dma("sp", Pp[1:128, :], D["Ptok"][0:127, 0:RW_IN], reads=[D["Ptok_t"].sub(0)], writes=[Pp])
            else:
                P.dma("sp", Pp[:, :], D["Ptok"][t0 - 1:t0 + 127, 0:RW_IN], reads=[D["Ptok_t"].sub(u), D["Ptok_t"].sub(u - 1)], writes=[Pp])
        tt(P, "dve", Pp[:], Pp[:], Pc[:], ALU.subtract, [Pp, Pc], [Pp])
        tt(P, "pool", Pp[:], Pp[:], mub[:], ALU.mult, [Pp, mub], [Pp])
        tt(P, "dve", Pc[:], Pc[:], Pp[:], ALU.add, [Pp, Pc], [Pc])
        if STOP == 1:
            continue
        r, k, v = Pc[:, 0:1024], Pc[:, 1024:2048], Pc[:, 2048:3072]
        act(P, L[:, 0:64], Pc[:, 3072:3136], AF.Tanh, [Pc], [L])
        cp(P, "pool", L[:, 64:128], Pc[:, 3136:3200], [Pc], [L])
        act(P, L[:, 128:288], Pc[:, 3200:3360], AF.Sigmoid, [Pc], [L])
        tr(P, pLT[:, 0, :], L[:, 0:128], g.identf[:], [L, g.identf], [pLT])
        tr(P, pLT[:, 1, :], L[:, 128:256], g.identf[:], [L, g.identf], [pLT])
        tr(P, pLT[0:32, 2, :], L[:, 256:288], g.identf[:], [L, g.identf], [pLT])
        cp(P, "dve", LT[:, 0:2, :], pLT[:, 0:2, :], [pLT], [LT])
        cp(P, "dve", LT[0:32, 2, :], pLT[0:32, 2, :], [pLT], [LT])
        if STOP == 2:
            continue
        for hf in range(2):
            cs = slice(hf * 512, (hf + 1) * 512)
            p = pw[pk % 4]; pk += 1
            mm(P, p[:], LT[0:64, 0, :], loraW[0:64, cs], True, True, [LT, loraW], [p])
            tt(P, "dve", W["zt"][:, cs], p[:], w0b[:, cs], ALU.add, [p, w0b], [W["zt"]])
            p = pw[pk % 4]; pk += 1
            mm(P, p[:], LT[64:128, 0, :], loraW[64:128, cs], True, True, [LT, loraW], [p])
            tt(P, "dve", W["za"][:, cs], p[:], a0b[:, cs], ALU.add, [p, a0b], [W["za"]])
            p = pw[pk % 4]; pk += 1
            mm(P, p[:], LT[:, 1, :], gup1[:, cs], True, False, [LT, gup1], [p])
            mm(P, p[:], LT[0:32, 2, :], gup2[0:32, cs], False, True, [LT, gup2], [p])
            cp(P, "act", W["gt"][:, cs], p[:], [p], [W["gt"]])
        if STOP == 3:
            continue
        act(P, W["lw"][:], W["zt"][:], AF.Sigmoid, [W["zt"]], [W["lw"]])
        ts(P, "dve", W["lw"][:], W["lw"][:], -0.6065306597126334, valid[:, (1 if samp else 0):(2 if samp else 1)], ALU.mult, ALU.mult, [W["lw"], valid], [W["lw"]])
        if STOP == 31:
            continue
        act(P, W["ah"][:], W["za"][:], AF.Sigmoid, [W["za"]], [W["ah"]])
        if STOP == 32:
            continue
        tt(P, "pool", W["kk"][:], k, kkb[:], ALU.mult, [Pc, kkb], [W["kk"]])
        act(P, W["junk"][:], W["kk"][:], AF.Square, [W["kk"]], [W["junk"]])
        if STOP == 33:
            continue
        red(P, "dve", ssq[:], W["junk"][:].rearrange("p (h d) -> p h d", h=16), ALU.add, [W["junk"]], [ssq])
        act(P, ssq[:], ssq[:], AF.Sqrt, [ssq, tiny], [ssq], bias=tiny[:])
        recip(P, rn[:], ssq[:], [ssq], [rn])
        if STOP == 34:
            continue
        kk3 = W["kk"][:].rearrange("p (h d) -> p h d", h=16)
        tt(P, "dve", kk3, kk3, rn[:].unsqueeze(2).to_broadcast([128, 16, 64]), ALU.mult, [W["kk"], rn], [W["kk"]])
        if STOP == 35:
            continue
        stt(P, "dve", W["t2"][:], W["ah"][:], -1.0, kab[:], ALU.add, ALU.mult, [W["ah"], kab], [W["t2"]])
        stt(P, "dve", W["kmod"][:], W["t2"][:], 1.0, k, ALU.add, ALU.mult, [W["t2"], Pc], [W["kmod"]])
        if STOP == 36:
            continue
        tt(P, "pool", W["b"][:], W["kk"][:], W["ah"][:], ALU.mult, [W["kk"], W["ah"]], [W["b"]])
        tt(P, "pool", W["t2"][:], r, W["kmod"][:], ALU.mult, [Pc, W["kmod"]], [W["t2"]])
        tt(P, "pool", W["t2"][:], W["t2"][:], rkb[:], ALU.mult, [W["t2"], rkb], [W["t2"]])
        if STOP == 37:
            continue
        red(P, "dve", bc[:], W["t2"][:].rearrange("p (h d) -> p h d", h=16), ALU.add, [W["t2"]], [bc])
        P.dma("pool", D["BC"][u * 128:(u + 1) * 128, :], bc[:], reads=[bc], writes=[D["BC_t"].sub(u)])
        if STOP == 4:
            continue
        for hf in range(2):
            cs = slice(hf * 512, (hf + 1) * 512)
            pL = pw[pk % 4]; pk += 1
            pLt = pw[pk % 4]; pk += 1
            mm(P, pL[:], tri[:], W["lw"][:, cs], True, True, [tri, W["lw"]], [pL])
            mm(P, pLt[:], ones[:], W["lw"][:, cs], True, True, [ones, W["lw"]], [pLt])
            if "a1" not in SKIP:
                act(P, W["eL"][:, cs], pL[:], AF.Exp, [pL], [W["eL"]])
            if "a2" not in SKIP:
                act(P, W["eN"][:, cs], pL[:], AF.Exp, [pL], [W["eN"]], scale=-1.0)
            act(P, W["Lsb"][:, cs], pL[:], AF.Identity, [pL], [W["Lsb"]])
            if "a3" not in SKIP:
                act(P, W["gC"][:, cs], pLt[:], AF.Exp, [pLt], [W["gC"]])
            if "d1" not in SKIP:
                tt(P, "dve", W["eC"][:, cs], pLt[:], W["Lsb"][:, cs], ALU.subtract, [pLt, W["Lsb"]], [W["eC"]])
            if "d2" not in SKIP:
                tt(P, "dve", W["eLm"][:, cs], W["Lsb"][:, cs], W["lw"][:, cs], ALU.subtract, [W["Lsb"], W["lw"]], [W["eLm"]])
        if "exp2" not in SKIP:
            act(P, W["eC"][:], W["eC"][:], AF.Exp, [W["eC"]], [W["eC"]])
            act(P, W["eLm"][:], W["eLm"][:], AF.Exp, [W["eLm"]], [W["eLm"]])
        if STOP == 5:
            continue
        tt(P, "dve", W["rt"][:], r, W["eL"][:], ALU.mult, [Pc, W["eL"]], [W["rt"]])
        stt(P, "dve", W["at"][:], W["kk"][:], -1.0, W["eLm"][:], ALU.mult, ALU.mult, [W["kk"], W["eLm"]], [W["at"]])
        tt(P, "pool", W["kt"][:], W["kmod"][:], W["eN"][:], ALU.mult, [W["kmod"], W["eN"]], [W["kt"]])
        tt(P, "pool", W["bt"][:], W["b"][:], W["eN"][:], ALU.mult, [W["b"], W["eN"]], [W["bt"]])
        tt(P, "pool", W["kp"][:], W["kmod"][:], W["eC"][:], ALU.mult, [W["kmod"], W["eC"]], [W["kp"]])
        tt(P, "dve", W["bp"][:], W["b"][:], W["eC"][:], ALU.mult, [W["b"], W["eC"]], [W["bp"]])
        rows = slice(u * 128, (u + 1) * 128)
        P.dma("pool", D["Vs"][rows, :], v, reads=[Pc], writes=[D["Vs_t"].sub(u)])
        P.dma("pool", D["KPs"][rows, :], W["kp"][:], reads=[W["kp"]], writes=[D["KPs_t"].sub(u)])
        P.dma("pool", D["BPs"][rows, :], W["bp"][:], reads=[W["bp"]], writes=[D["BPs_t"].sub(u)])
        P.dma("pool", D["Gs"][rows, :], W["gt"][:], reads=[W["gt"]], writes=[D["Gs_t"].sub(u)])
        if STOP == 6:
            continue
        for nm, dst in [("rt", "RT"), ("at", "AT"), ("kt", "KT"), ("bt", "BT"), ("gC", "GCT")]:
            s = stgT[tk % 2]
            for half in range(2):
                p = pT[tk % 2]
                for j in range(4):
                    fc = half * 4 + j
                    tr(P, p[:, j, :], W[nm][:, fc * 128:(fc + 1) * 128], g.identf[:], [W[nm], g.identf], [p])
                cp(P, "act" if half else "dve", s[:, half * 4:(half + 1) * 4, :], p[:], [p], [s])
                tk += 1
            P.dma("pool", D[dst].rearrange("(fc p) c -> p fc c", p=128)[:, :, u * 128:(u + 1) * 128], s[:], reads=[s], writes=[D[dst + "_t"].sub(u)])
    P.barrier()
    st.close()


def stage4_scan(P, g, D):
    st = ExitStack()
    sb = lambda name, shape, dt=F32: P.sb(name, shape, dt, st)
    msl = sb("msl", [128, 128]); P.dma("sp", msl[:], D["m_sl"][:, :], writes=[msl])
    msu = sb("msu", [128, 128]); P.dma("sp", msu[:], D["m_su"][:, :], writes=[msu])
    mu = sb("mu", [128, 128]); P.dma("sp", mu[:], D["m_u"][:, :], writes=[mu])
    lnw = sb("lnw", [128, 1024]); bload(P, lnw, D["lnx_w"][0:1, :])
    lnb = sb("lnb", [128, 1024]); bload(P, lnb, D["lnx_b"][0:1, :])
    gne = sb("gne", [128, 1]); mset(P, "dve", gne[:], GN_EPS, [gne])
    H = sb("H", [128, 8, 64])
    mset(P, "dve", H[:], 0.0, [H.sub((a, b)) for a in range(8) for b in range(2)])
    FM = {nm: [sb(f"fm_{nm}{i}", [128, 8, 128]) for i in range(2)] for nm in ["RT", "AT", "KT", "BT", "GCT"]}
    TM = {nm: [sb(f"tm_{nm}{i}", [128, 1024]) for i in range(2)] for nm in ["Vs", "KPs", "BPs"]}
    Y = [sb(f"Y{i}", [128, 1024]) for i in range(2)]
    NM = 16
    GS = 6
    MS = [[sb(f"ms{s}_{i}", [128, 128]) for i in range(NM)] for s in range(GS)]
    XU = [[sb(f"xu{s}_{i}", [128, 64]) for i in range(4)] for s in range(GS)]
    pLane = [P.ps(f"pLane{i}", [128, 512], F32, st) for i in range(GS)]
    laneM = [[SlotAP(pLane[l], pLane[l][:, j * 128:(j + 1) * 128]) for j in range(2)] for l in range(GS)]
    laneS = [[SlotAP(pLane[l], pLane[l][:, 256 + j * 64:256 + (j + 1) * 64]) for j in range(4)] for l in range(GS)]
    gt = sb("p_gt", [128, 1024]); bc = sb("p_bc", [128, 16]); vv = None
    pw = {nm: sb("p_" + nm, [128, 1024]) for nm in ["yc", "sq", "yb"]}
    st16 = {nm: sb("p16_" + nm, [128, 16]) for nm in ["mean", "var", "rstd"]}
    oab = sb("oab", [128, 1024], BF16)
    pO = [P.ps(f"pO{i}", [128, 8, 128], BF16, st) for i in range(1)]
    oT = sb("oT", [128, 8, 128], BF16)
    Ssb = [sb(f"Ssb{i}", [128, 64]) for i in range(GS)]
    Sout = sb("Sout", [128, 8, 64])

    def head_gen(u, fc, hp, lane, FMu, TMu, y):
        samp = u >= 16
        RT, AT, KT, BT, GC = FMu
        V, KP, BP = TMu
        mk = [0]; sk = [0]

        def nextM():
            mk[0] += 1
            return laneM[lane][mk[0] % 2]

        def nextS():
            sk[0] += 1
            return laneS[lane][sk[0] % 4]

        h = 2 * fc + hp
        pb = hp * 64
        hs = slice(h * 64, (h + 1) * 64)
        M = MS[lane]
        a_ = AT[pb:pb + 64, fc, :]; b_ = BT[pb:pb + 64, fc, :]; k_ = KT[pb:pb + 64, fc, :]; r_ = RT[pb:pb + 64, fc, :]
        A, N, AakT, RBt, RKt = M[0], M[1], M[2], M[3], M[4]
        for (dst, l, r, msk) in [(A, a_, b_, msl), (N, b_, a_, msu), (AakT, k_, a_, msu), (RBt, b_, r_, mu), (RKt, k_, r_, mu)]:
            p = nextM()
            mm(P, p[:], l, r, True, True, [AT, BT, KT, RT], [p])
            tt(P, "dve", dst[:], p[:], msk[:], ALU.mult, [p, msk], [dst])
            yield
        Ap = [A, M[5], M[6], M[7], M[8]]
        Np = [N, M[9], M[10], M[11], M[12], M[13]]
        for k in range(5):
            if k < 4:
                p = nextM()
                mm(P, p[:], Np[k][:], Ap[k][:], True, True, [Np[k], Ap[k]], [p])
                cp(P, "act", Ap[k + 1][:], p[:], [p], [Ap[k + 1]])
            p = nextM()
            mm(P, p[:], Ap[k][:], Np[k][:], True, True, [Np[k], Ap[k]], [p])
            cp(P, "dve" if k % 2 else "act", Np[k + 1][:], p[:], [p], [Np[k + 1]])
            yield
        Tt = [M[14], M[15]]
        tt(P, "dve", Tt[0][:], Np[5][:], g.identf[:], ALU.add, [Np[5], g.identf], [Tt[0]])
        cur = 0
        for k in [4, 3, 2, 1, 0]:
            p = nextM()
            mm(P, p[:], Ap[k][:], Tt[cur][:], True, True, [Ap[k], Tt[cur]], [p])
            tt(P, "dve", Tt[1 - cur][:], p[:], Tt[cur][:], ALU.add, [p, Tt[cur]], [Tt[1 - cur]])
            cur = 1 - cur
            yield
        TT_ = Tt[cur]
        for c in range(2):
            pc = c * 64
            cc = slice(c * 64, (c + 1) * 64)
            Hh = H[pb:pb + 64, fc, :]
            Hd = H.sub((fc, hp))
            if samp:
                q = 2 * (u - 16) + c
                P.dma("sp", Ssb[lane][pb:pb + 64, :], D["swkv"][q, h, :, :], writes=[Ssb[lane]])
                p = nextS()
                mm(P, p[pb:pb + 64, :], Ssb[lane][pb:pb + 64, :], g.identf[pb:pb + 64, pb:pb + 64], True, True, [Ssb[lane], g.identf], [p])
                cp(P, "act", Hh, p[pb:pb + 64, :], [p], [Hd])
                yield
            X_sb, U_sb = XU[lane][2 * c], XU[lane][2 * c + 1]
            p = nextS()
            mm(P, p[pc:pc + 64, :], a_[:, cc], Hh, True, False, [AT, Hd], [p])
            mm(P, p[pc:pc + 64, :], AakT[pc:pc + 64, cc], V[pc:pc + 64, hs], False, True, [AakT, V], [p])
            cp(P, "act", X_sb[pc:pc + 64, :], p[pc:pc + 64, :], [p], [X_sb])
            yield
            p = nextS()
            mm(P, p[pc:pc + 64, :], TT_[pc:pc + 64, cc], X_sb[pc:pc + 64, :], True, True, [TT_, X_sb], [p])
            cp(P, "act", U_sb[pc:pc + 64, :], p[pc:pc + 64, :], [p], [U_sb])
            yield
            p = nextS()
            mm(P, p[pc:pc + 64, :], r_[:, cc], Hh, True, False, [RT, Hd], [p])
            mm(P, p[pc:pc + 64, :], RBt[pc:pc + 64, cc], U_sb[pc:pc + 64, :], False, False, [RBt, U_sb], [p])
            mm(P, p[pc:pc + 64, :], RKt[pc:pc + 64, cc], V[pc:pc + 64, hs], False, True, [RKt, V], [p])
            cp(P, "act", y[pc:pc + 64, hs], p[pc:pc + 64, :], [p], [y.sub(h)])
            p = nextS()
            mm(P, p[pb:pb + 64, :], BP[pc:pc + 64, hs], U_sb[pc:pc + 64, :], True, False, [BP, U_sb], [p])
            mm(P, p[pb:pb + 64, :], KP[pc:pc + 64, hs], V[pc:pc + 64, hs], False, True, [KP, V], [p])
            stt(P, "dve", Hh, Hh, GC[pb:pb + 64, fc, c * 64:c * 64 + 1], p[pb:pb + 64, :], ALU.mult, ALU.add, [Hd, GC, p], [Hd])
            yield
            if samp or (u == 15 and c == 1):
                q = (1 + 2 * (u - 16) + c) if samp else 0
                p = nextS()
                mm(P, p[pb:pb + 64, :], Hh, g.identf[pb:pb + 64, pb:pb + 64], True, True, [Hd, g.identf], [p])
                cp(P, "act", Sout[pb:pb + 64, fc, :], p[pb:pb + 64, :], [p], [Sout.sub(h)])
                r0 = (q * 16 + h) * 64
                P.dma("pool", D["wkv"][r0:r0 + 64, :], Sout[pb:pb + 64, fc, :], reads=[Sout.sub(h)])
                yield

    for u in range(NU):
        samp = u >= 16
        b2 = u % 2
        cols = slice(u * 128, (u + 1) * 128)
        for nm in FM:
            P.dma("sp", FM[nm][b2][:], D[nm].rearrange("(fc p) c -> p fc c", p=128)[:, :, cols], reads=[D[nm + "_t"].sub(u)], writes=[FM[nm][b2]])
        for nm in TM:
            P.dma("sp", TM[nm][b2][:], D[nm][cols, :], reads=[D[nm + "_t"].sub(u)], writes=[TM[nm][b2]])
        FMu = [FM[nm][b2] for nm in ["RT", "AT", "KT", "BT", "GCT"]]
        TMu = [TM[nm][b2] for nm in ["Vs", "KPs", "BPs"]]
        V = TMu[0]
        y = Y[b2]
        heads = [(fc, hp) for fc in range(8) for hp in range(2)]
        active = []
        nxt = 0
        for lane in range(GS):
            fc, hp = heads[nxt]; nxt += 1
            active.append((lane, head_gen(u, fc, hp, lane, FMu, TMu, y)))
        while active:
            still = []
            for lane, gen in active:
                try:
                    next(gen)
                    still.append((lane, gen))
                except StopIteration:
                    if nxt < len(heads):
                        fc, hp = heads[nxt]; nxt += 1
                        still.append((lane, head_gen(u, fc, hp, lane, FMu, TMu, y)))
            active = still
        ysubs = [y.sub(h) for h in range(16)]
        P.dma("sp", gt[:], D["Gs"][cols, :], reads=[D["Gs_t"].sub(u)], writes=[gt])
        P.dma("sp", bc[:], D["BC"][cols, :], reads=[D["BC_t"].sub(u)], writes=[bc])
        y3 = y[:].rearrange("p (h d) -> p h d", h=16)
        bcast = lambda t16: t16[:].unsqueeze(2).to_broadcast([128, 16, 64])
        v3 = lambda t: t[:].rearrange("p (h d) -> p h d", h=16)
        red(P, "dve", st16["mean"][:], y3, ALU.add, ysubs, [st16["mean"]])
        ts(P, "dve", st16["mean"][:], st16["mean"][:], 1.0 / 64, None, ALU.mult, None, [st16["mean"]], [st16["mean"]])
        tt(P, "dve", v3(pw["yc"]), y3, bcast(st16["mean"]), ALU.subtract, ysubs + [st16["mean"]], [pw["yc"]])
        act(P, pw["sq"][:], pw["yc"][:], AF.Square, [pw["yc"]], [pw["sq"]])
        red(P, "dve", st16["var"][:], v3(pw["sq"]), ALU.add, [pw["sq"]], [st16["var"]])
        act(P, st16["var"][:], st16["var"][:], AF.Sqrt, [st16["var"], gne], [st16["var"]], scale=1.0 / 64, bias=gne[:])
        recip(P, st16["rstd"][:], st16["var"][:], [st16["var"]], [st16["rstd"]])
        tt(P, "dve", v3(pw["yc"]), v3(pw["yc"]), bcast(st16["rstd"]), ALU.mult, [pw["yc"], st16["rstd"]], [pw["yc"]])
        tt(P, "pool", pw["yc"][:], pw["yc"][:], lnw[:], ALU.mult, [pw["yc"], lnw], [pw["yc"]])
        tt(P, "pool", pw["yc"][:], pw["yc"][:], lnb[:], ALU.add, [pw["yc"], lnb], [pw["yc"]])
        tt(P, "dve", v3(pw["yb"]), v3(V), bcast(bc), ALU.mult, [V, bc], [pw["yb"]])
        tt(P, "pool", pw["yc"][:], pw["yc"][:], pw["yb"][:], ALU.add, [pw["yc"], pw["yb"]], [pw["yc"]])
        tt(P, "dve", oab[:], pw["yc"][:], gt[:], ALU.mult, [pw["yc"], gt], [oab])
        if D.get("OAdbg") is not None:
            P.dma("pool", D["OAdbg"][cols, :], pw["yc"][:], reads=[pw["yc"]])
        for fc in range(8):
            tr(P, pO[0][:, fc, :], oab[:, fc * 128:(fc + 1) * 128], g.identb[:], [oab, g.identb], [pO[0]])
        cp(P, "act", oT[:], pO[0][:], [pO[0]], [oT])
        dstv = D["OAT"].rearrange("(fc p) c -> p fc c", p=128)
        for (d0, t0, nt) in unit_rows(u):
            P.dma("pool", dstv[:, :, t0:t0 + nt], oT[:, :, d0:d0 + nt], reads=[oT], writes=[D["OAT_t"]])
    P.barrier()
    st.close()

TOPK = 256
NEG = -1.0e30
NBIS = 18


def headnorm(P, src, dst, nw, junk, ssq, rst, g, n, reads, pre=1.0):
    act(P, junk[0:n, :], src, AF.Square, reads, [junk])
    red(P, "dve", ssq[0:n, :], junk[0:n, :].rearrange("p (h d) -> p h d", h=16), ALU.add, [junk], [ssq])
    act(P, ssq[0:n, :], ssq[0:n, :], AF.Sqrt, [ssq], [ssq], scale=1.0 / 64, bias=g.epsc[0:n, :])
    recip(P, rst[0:n, :], ssq[0:n, :], [ssq], [rst])
    d3 = dst.rearrange("p (h d) -> p h d", h=16)
    tt(P, "dve", d3, src.rearrange("p (h d) -> p h d", h=16), rst[0:n, :].unsqueeze(2).to_broadcast([n, 16, 64]), ALU.mult, list(reads) + [rst], [junk])
    stt(P, "dve", d3, d3, pre, nw[0:n, :].unsqueeze(1).to_broadcast([n, 16, 64]), ALU.mult, ALU.mult, [junk, nw], [junk])


def stage3_dsa(P, g, D):
    st = ExitStack()
    sb = lambda name, shape, dt=F32: P.sb(name, shape, dt, st)
    knw = sb("knw", [128, 64]); qnw = sb("qnw", [128, 64])
    P.dma("sp", knw[:], D["k_norm_w"][0:1, :].partition_broadcast(128), writes=[knw])
    P.dma("sp", qnw[:], D["q_norm_w"][0:1, :].partition_broadcast(128), writes=[qnw])
    pd = [sb(f"pd{i}", [128, 3656]) for i in range(2)]
    cs = [sb(f"cs{i}", [128, 2, 32]) for i in range(2)]
    junk = sb("djunk", [128, 1024]); ssq = sb("dssq", [128, 16]); rst = sb("drst", [128, 16])
    nrm = sb("dnrm", [128, 1024])
    ko = [sb(f"dko{i}", [128, 1024]) for i in range(2)]
    qo = sb("dqo", [128, 1024]); qio = sb("dqio", [128, 512])
    kio = [sb(f"dkio{i}", [128, 64]) for i in range(2)]
    wio = [sb(f"dwio{i}", [128, 8]) for i in range(2)]
    tmp = sb("dtmp", [128, 4, 512])
    cat = sb("dcat", [128, 2624], BF16)
    pT = [P.ps(f"dpT{i}", [128, 8, 128], BF16, st) for i in range(2)]
    sT = [sb(f"dsT{i}", [128, 8, 128], BF16) for i in range(2)]
    tk = 0
    for ti, (r0, n) in enumerate(TILES):
        p, c = pd[ti % 2], cs[ti % 2]
        P.dma("sp", p[0:n, :], D["Ptok"][r0:r0 + n, C_Q:TOKW], reads=[D["Ptok_t"].sub(ti)], writes=[p])
        P.dma("sp", c[0:n, 0, :], D["cos"][r0:r0 + n, :], writes=[c])
        P.dma("sp", c[0:n, 1, :], D["sin"][r0:r0 + n, :], writes=[c])
        P.dma("pool", D["vout"][r0:r0 + n, :], D["Ptok"][r0:r0 + n, C_V:C_V + 1024], reads=[D["Ptok_t"].sub(ti)])
        cosb = c[0:n, 0, :].unsqueeze(1).to_broadcast([n, 16, 32])
        sinb = c[0:n, 1, :].unsqueeze(1).to_broadcast([n, 16, 32])
        r4 = lambda ap, H: ap.rearrange("p (h t d) -> p h t d", h=H, t=2)
        headnorm(P, p[0:n, C_K - C_Q:C_K - C_Q + 1024], junk[0:n, :], knw, junk, ssq, rst, g, n, [p])
        o = ko[ti % 2]
        rope(P, "dve", r4(o[0:n, :], 16), r4(junk[0:n, :], 16), cosb, sinb, tmp, n, [junk, c], [o])
        P.dma("pool", D["kout"][r0:r0 + n, :], o[0:n, :], reads=[o], writes=[D["kout_t"].sub(ti)])
        cp(P, "pool", cat[0:n, 1024:2048], o[0:n, :], [o], [cat])
        headnorm(P, p[0:n, 0:1024], junk[0:n, :], qnw, junk, ssq, rst, g, n, [p], pre=0.125)
        rope(P, "dve", r4(qo[0:n, :], 16), r4(junk[0:n, :], 16), cosb, sinb, tmp, n, [junk, c], [qo])
        cp(P, "pool", cat[0:n, 0:1024], qo[0:n, :], [qo], [cat])
        rope(P, "pool", r4(qio[0:n, :], 8), r4(p[0:n, C_QI - C_Q:C_QI - C_Q + 512], 8), c[0:n, 0, :].unsqueeze(1).to_broadcast([n, 8, 32]),
             c[0:n, 1, :].unsqueeze(1).to_broadcast([n, 8, 32]), tmp, n, [p, c], [qio])
        cp(P, "pool", cat[0:n, 2048:2560], qio[0:n, :], [qio], [cat])
        oi = kio[ti % 2]
        rope(P, "pool", r4(oi[0:n, :], 1), r4(p[0:n, C_KI - C_Q:C_KI - C_Q + 64], 1), c[0:n, 0, :].unsqueeze(1), c[0:n, 1, :].unsqueeze(1), tmp, n, [p, c], [oi])
        P.dma("pool", D["kidx"][r0:r0 + n, :], oi[0:n, :], reads=[oi], writes=[D["kidx_t"].sub(ti)])
        cp(P, "pool", cat[0:n, 2560:2624], oi[0:n, :], [oi], [cat])
        w = wio[ti % 2]
        ts(P, "dve", w[0:n, :], p[0:n, C_WI - C_Q:C_WI - C_Q + 8], 512.0 ** -0.5, None, ALU.mult, None, [p], [w])
        P.dma("pool", D["WI"][r0:r0 + n, :], w[0:n, :], reads=[w], writes=[D["WI_t"].sub(ti)])
        for (c0, nch, dst) in [(0, 8, "QT"), (1024, 8, "KTn"), (2048, 4, "QIT")]:
            pt, s_ = pT[tk % 2], sT[tk % 2]; tk += 1
            for j in range(nch):
                tr(P, pt[:, j, 0:n], cat[0:n, c0 + j * 128:c0 + (j + 1) * 128], g.identb[0:n, 0:n], [cat, g.identb], [pt])
            cp(P, "act", s_[:, 0:nch, 0:n], pt[:, 0:nch, 0:n], [pt], [s_])
            P.dma("pool", D[dst].rearrange("(fc p) c -> p fc c", p=128)[:, :, r0:r0 + n], s_[:, 0:nch, 0:n], reads=[s_], writes=[D[dst + "_t"].sub(ti)])
        pt, s_ = pT[tk % 2], sT[tk % 2]; tk += 1
        tr(P, pt[0:64, 0, 0:n], cat[0:n, 2560:2624], g.identb[0:n, 0:n], [cat, g.identb], [pt])
        cp(P, "act", s_[0:64, 0, 0:n], pt[0:64, 0, 0:n], [pt], [s_])
        P.dma("pool", D["KIT"][:, r0:r0 + n], s_[0:64, 0, 0:n], reads=[s_], writes=[D["KIT_t"].sub(ti)])
    P.dma("pool", D["shift"][0:1, :], D["Ptok"][2047:2048, 0:RW_IN], reads=[D["Ptok_t"].sub(15)])
    for q in range(4):
        P.dma("pool", D["shift"][1 + q:2 + q, :], D["Ptok"][2048 + 16 * q + 15:2048 + 16 * q + 16, 0:RW_IN], reads=[D["Ptok_t"].sub(16)])
    P.barrier()
    st.close()


def stage5_attn(P, g, D):
    st = ExitStack()
    sb = lambda name, shape, dt=F32: P.sb(name, shape, dt, st)
    kT = sb("kT", [128, 8, 2064], BF16)
    Vb = sb("Vb", [128, 17, 16, 65], BF16)
    kiT = sb("kiT", [128, 2064], BF16)
    Ibuf = sb("Ibuf", [128, 2064]); junkI = sb("junkI", [128, 2064], BF16)
    maskb = sb("maskb", [128, 2064], BF16); maskT = sb("maskT", [128, 17, 128], BF16)
    qT = [sb(f"qT{i}", [128, 8, 128], BF16) for i in range(2)]
    qiT = [sb(f"qiT{i}", [128, 4, 128], BF16) for i in range(2)]
    wi = [sb(f"wi{i}", [128, 8]) for i in range(2)]
    rl = [sb(f"rl{i}", [128, 512]) for i in range(2)]
    E = [sb(f"E{i}", [128, 4, 128], BF16) for i in range(3)]
    p2 = sb("pow2", [128, NBIS]); P.dma("sp", p2[:], D["m_pow2"][:, :], writes=[p2])
    dtab = sb("dtab", [128, NBIS])
    s1 = {nm: sb("s1_" + nm, [128, 1]) for nm in ["B", "mid", "cnt", "t2", "t3", "thr"]}
    osb = sb("osb", [128, 1024]); osbb = sb("osbb", [128, 1024], BF16); rec = sb("orec", [128, 4])
    oT = sb("oTb", [128, 8, 128], BF16)
    vst = [sb(f"vst{i}", [128, 1024]) for i in range(2)]
    kst = sb("kstb", [128, 1024], BF16); kis = sb("kis", [128, 64]); kisb = sb("kisb", [128, 128], BF16)
    pI = [P.ps(f"pI{i}", [128, 512], F32, st) for i in range(2)]
    pS = [P.ps(f"pSc{i}", [128, 4, 128], F32, st) for i in range(2)]
    pO = [P.ps(f"pOa{i}", [128, 4, 65], F32, st) for i in range(2)]
    pmT = P.ps("pmT", [128, 8, 128], BF16, st)
    mset(P, "pool", Vb[:, :, :, 64:65], 1.0, [Vb])
    cnt = {"rl": 0, "E": 0, "S": 0, "I": 0, "v": 0}

    def load_v_block(j, src_ap, nrow):
        v = vst[cnt["v"] % 2]; cnt["v"] += 1
        P.dma("sp", v[0:nrow, :], src_ap, writes=[v])
        cp(P, "pool", Vb[0:nrow, j, :, 0:64], v[0:nrow, :].rearrange("p (h d) -> p h d", h=16), [v], [Vb])

    def attend(nq, tcol, nblk, lastw, prompt_tile, obt_cols, par):
        S = (nblk - 1) * 128 + lastw
        q_, qi_, w_ = qT[par], qiT[par], wi[par]
        P.dma("sp", q_[:, :, 0:nq], D["QT"].rearrange("(fc p) c -> p fc c", p=128)[:, :, tcol:tcol + nq], reads=[D["QT_t"]], writes=[q_])
        P.dma("sp", qi_[:, :, 0:nq], D["QIT"].rearrange("(fc p) c -> p fc c", p=128)[:, :, tcol:tcol + nq], reads=[D["QIT_t"]], writes=[qi_])
        P.dma("sp", w_[0:nq, :], D["WI"][tcol:tcol + nq, :], reads=[D["WI_t"]], writes=[w_])
        for s0 in range(0, S, 512):
            w = min(512, S - s0)
            for h in range(8):
                pb = (h % 2) * 64
                p = pI[cnt["I"] % 2]; cnt["I"] += 1
                mm(P, p[0:nq, 0:w], qi_[pb:pb + 64, h // 2, 0:nq], kiT[pb:pb + 64, s0:s0 + w], True, True, [qi_, kiT], [p])
                r = rl[cnt["rl"] % 2]; cnt["rl"] += 1
                act(P, r[0:nq, 0:w], p[0:nq, 0:w], AF.Relu, [p], [r])
                if h == 0:
                    ts(P, "dve", Ibuf[0:nq, s0:s0 + w], r[0:nq, 0:w], w_[0:nq, 0:1], None, ALU.mult, None, [r, w_], [Ibuf])
                else:
                    stt(P, "dve", Ibuf[0:nq, s0:s0 + w], r[0:nq, 0:w], w_[0:nq, h:h + 1], Ibuf[0:nq, s0:s0 + w], ALU.mult, ALU.add, [r, w_, Ibuf], [Ibuf])
        P.op("dve", lambda e: e.tensor_reduce(out=s1["B"][0:nq, :], in_=Ibuf[0:nq, 0:S], axis=AX.X, op=ALU.max, apply_absolute_value=True), reads=[Ibuf], writes=[s1["B"]])
        ts(P, "dve", s1["B"][0:nq, :], s1["B"][0:nq, :], 1.001, 1e-6, ALU.mult, ALU.add, [s1["B"]], [s1["B"]])
        ts(P, "dve", dtab[0:nq, :], p2[0:nq, :], s1["B"][0:nq, :], None, ALU.mult, None, [p2, s1["B"]], [dtab])
        if prompt_tile:
            mset(P, "dve", Ibuf[0:64, S - 64:S], NEG, [Ibuf])
        mset(P, "dve", s1["mid"][0:nq, :], 0.0, [s1["mid"]])
        for k in range(NBIS):
            ts(P, "dve", junkI[0:nq, 0:S], Ibuf[0:nq, 0:S], s1["mid"][0:nq, :], None, ALU.is_ge, ALU.add, [Ibuf, s1["mid"]], [junkI, s1["cnt"]], accum_out=s1["cnt"][0:nq, :])
            ts(P, "dve", s1["t2"][0:nq, :], s1["cnt"][0:nq, :], TOPK - 0.5, 2.0, ALU.is_ge, ALU.mult, [s1["cnt"]], [s1["t2"]])
            ts(P, "dve", s1["t3"][0:nq, :], s1["t2"][0:nq, :], -1.0, dtab[0:nq, k:k + 1], ALU.add, ALU.mult, [s1["t2"], dtab], [s1["t3"]])
            tt(P, "dve", s1["mid"][0:nq, :], s1["mid"][0:nq, :], s1["t3"][0:nq, :], ALU.add, [s1["mid"], s1["t3"]], [s1["mid"]])
        tt(P, "dve", s1["thr"][0:nq, :], s1["mid"][0:nq, :], dtab[0:nq, NBIS - 1:NBIS], ALU.subtract, [s1["mid"], dtab], [s1["thr"]])
        ts(P, "dve", maskb[0:nq, 0:S], Ibuf[0:nq, 0:S], s1["thr"][0:nq, :], None, ALU.is_ge, None, [Ibuf, s1["thr"]], [maskb])
        for j0 in range(0, nblk, 8):
            nb_ = min(8, nblk - j0)
            for jj in range(nb_):
                j = j0 + jj
                wj = 128 if j < nblk - 1 else lastw
                tr(P, pmT[0:wj, jj, 0:nq], maskb[0:nq, j * 128:j * 128 + wj], g.identb[0:nq, 0:nq], [maskb, g.identb], [pmT])
            full = nb_ if (j0 + nb_ < nblk or lastw == 128) else nb_ - 1
            if full > 0:
                cp(P, "act", maskT[:, j0:j0 + full, 0:nq], pmT[:, 0:full, 0:nq], [pmT], [maskT])
            if full < nb_:
                cp(P, "act", maskT[0:lastw, j0 + full, 0:nq], pmT[0:lastw, full, 0:nq], [pmT], [maskT])
        items = [(h, j0) for h in range(16) for j0 in range(0, nblk, 4)]

        def qk(it):
            h, j0 = it
            pb = (h % 2) * 64
            p = pS[cnt["S"] % 2]; cnt["S"] += 1
            for jj in range(min(4, nblk - j0)):
                j = j0 + jj
                wj = 128 if j < nblk - 1 else lastw
                mm(P, p[0:wj, jj, 0:nq], kT[pb:pb + 64, h // 2, j * 128:j * 128 + wj], q_[pb:pb + 64, h // 2, 0:nq], True, True, [kT, q_], [p])
            return p

        pcur = qk(items[0])
        for k, it in enumerate(items):
            pnext = qk(items[k + 1]) if k + 1 < len(items) else None
            h, j0 = it
            nb_ = min(4, nblk - j0)
            e = E[cnt["E"] % 3]; cnt["E"] += 1
            full = nb_ if (j0 + nb_ < nblk or lastw == 128) else nb_ - 1
            if full > 0:
                act(P, e[:, 0:full, 0:nq], pcur[:, 0:full, 0:nq], AF.Exp, [pcur], [e])
                tt(P, "pool" if k % 3 == 2 else "dve", e[:, 0:full, 0:nq], e[:, 0:full, 0:nq], maskT[:, j0:j0 + full, 0:nq], ALU.mult, [e, maskT], [e])
            if full < nb_:
                act(P, e[0:lastw, full, 0:nq], pcur[0:lastw, full, 0:nq], AF.Exp, [pcur], [e])
                tt(P, "dve", e[0:lastw, full, 0:nq], e[0:lastw, full, 0:nq], maskT[0:lastw, j0 + full, 0:nq], ALU.mult, [e, maskT], [e])
            po = pO[(h // 4) % 2]
            for jj in range(nb_):
                j = j0 + jj
                wj = 128 if j < nblk - 1 else lastw
                mm(P, po[0:nq, h % 4, :], e[0:wj, jj, 0:nq], Vb[0:wj, j, h, :], j == 0, j == nblk - 1, [e, Vb], [po])
            if j0 + nb_ >= nblk and h % 4 == 3:
                recip(P, rec[0:nq, :], po[0:nq, :, 64], [po], [rec])
                tt(P, "dve", osb[0:nq, (h - 3) * 64:(h + 1) * 64].rearrange("p (h d) -> p h d", h=4), po[0:nq, :, 0:64],
                   rec[0:nq, :].unsqueeze(2).to_broadcast([nq, 4, 64]), ALU.mult, [po, rec], [osb])
            pcur = pnext
        if D.get("OBdbg") is not None:
            P.dma("pool", D["OBdbg"][obt_cols:obt_cols + nq, :], osb[0:nq, :], reads=[osb])
        cp(P, "pool", osbb[0:nq, :], osb[0:nq, :], [osb], [osbb])
        for fc in range(8):
            tr(P, pmT[:, fc, 0:nq], osbb[0:nq, fc * 128:(fc + 1) * 128], g.identb[0:nq, 0:nq], [osbb, g.identb], [pmT])
        cp(P, "act", oT[:, :, 0:nq], pmT[:, :, 0:nq], [pmT], [oT])
        P.dma("pool", D["OBT"].rearrange("(fc p) c -> p fc c", p=128)[:, :, obt_cols:obt_cols + nq], oT[:, :, 0:nq], reads=[oT], writes=[D["OBT_t"]])

    P.dma("sp", kT[:, :, 0:2048], D["KTn"].rearrange("(fc p) c -> p fc c", p=128)[:, :, 0:2048], reads=[D["KTn_t"]], writes=[kT])
    P.dma("sp", kiT[0:64, 0:2048], D["KIT"][:, 0:2048], reads=[D["KIT_t"]], writes=[kiT])
    P.dma("sp", kiT[64:128, 0:2048], D["KIT"][:, 0:2048], reads=[D["KIT_t"]], writes=[kiT])
    for j in range(16):
        load_v_block(j, D["Ptok"][j * 128:(j + 1) * 128, C_V:C_V + 1024], 128)
    NPT = int(os.environ.get("NPT", "16"))
    for i in range(NPT):
        attend(128, 128 * i, i + 1, 128, True, 128 * i, i % 2)
    NSQ = int(os.environ.get("NSQ", "4"))
    for q in range(NSQ):
        tok = 2048 + 16 * q
        for j in range(16):
            v = vst[cnt["v"] % 2]; cnt["v"] += 1
            P.dma("sp", v[:, :], D["cache_k"][q, j * 128:(j + 1) * 128, :], writes=[v])
            cp(P, "pool", kst[:, :], v[:, :], [v], [kst])
            for fc in range(8):
                tr(P, pmT[:, fc, :], kst[:, fc * 128:(fc + 1) * 128], g.identb[:], [kst, g.identb], [pmT])
            cp(P, "act", kT[:, :, j * 128:(j + 1) * 128], pmT[:, :, :], [pmT], [kT])
            load_v_block(j, D["cache_v"][q, j * 128:(j + 1) * 128, :], 128)
            P.dma("sp", kis[:, :], D["cache_kidx"][q, j * 128:(j + 1) * 128, :], writes=[kis])
            cp(P, "dve", kisb[:, 0:64], kis[:, :], [kis], [kisb])
            cp(P, "dve", kisb[:, 64:128], kis[:, :], [kis], [kisb])
            tr(P, pmT[:, 0, :], kisb[:, :], g.identb[:], [kisb, g.identb], [pmT])
            cp(P, "act", kiT[:, j * 128:(j + 1) * 128], pmT[:, 0, :], [pmT], [kiT])
        P.dma("sp", kT[:, :, 2048:2064], D["KTn"].rearrange("(fc p) c -> p fc c", p=128)[:, :, tok:tok + 16], reads=[D["KTn_t"]], writes=[kT])
        P.dma("sp", kiT[0:64, 2048:2064], D["KIT"][:, tok:tok + 16], reads=[D["KIT_t"]], writes=[kiT])
        P.dma("sp", kiT[64:128, 2048:2064], D["KIT"][:, tok:tok + 16], reads=[D["KIT_t"]], writes=[kiT])
        load_v_block(16, D["Ptok"][tok:tok + 16, C_V:C_V + 1024], 16)
        attend(16, tok, 17, 16, False, tok, q % 2)
    P.barrier()
    st.close()


def load_w_bf16(P, dst, src_ap, stg, k0):
    for hf in range(2):
        s = stg[(k0 + hf) % 2]
        P.dma("sp", s[:], src_ap[:, hf * 512:(hf + 1) * 512].rearrange("(kc p) c -> p kc c", p=128), writes=[s])
        cp(P, "pool" if hf else "dve", dst[:, :, hf * 512:(hf + 1) * 512], s[:], [s], [dst])


def stage6(P, g, D):
    st = ExitStack()
    sb = lambda name, shape, dt=F32: P.sb(name, shape, dt, st)
    wpa = sb("wpa", [128, 8, 1024], BF16); wpb = sb("wpb", [128, 8, 1024], BF16); wo = sb("wo", [128, 8, 1024], BF16)
    stg = [sb(f"wstg{i}", [128, 8, 512]) for i in range(2)]
    load_w_bf16(P, wpa, D["w_proj_a"], stg, 0)
    load_w_bf16(P, wpb, D["w_proj_b"], stg, 0)
    load_w_bf16(P, wo, D["w_out"], stg, 0)
    oat = sb("oat", [128, 8, 512], BF16); obt = sb("obt", [128, 8, 512], BF16)
    mT = sb("mT", [128, 8, 512], BF16)
    ga = [sb(f"ga{i}", [128, 512]) for i in range(2)]; gb = [sb(f"gb{i}", [128, 512]) for i in range(2)]
    m1 = [sb(f"m1_{i}", [128, 512]) for i in range(2)]; m2 = [sb(f"m2_{i}", [128, 512]) for i in range(2)]
    g1r = sb("g1r", [128, 1024]); g1s = sb("g1s", [128, 1024])
    P.dma("sp", g1r[:], D["modd"][0:1, 2048:3072].partition_broadcast(128), reads=[D["modd_t"]], writes=[g1r])
    for q in range(4):
        P.dma("sp", g1s[16 * q:16 * q + 16, :], D["modd"][1 + q:2 + q, 2048:3072].partition_broadcast(16), reads=[D["modd_t"]], writes=[g1s])
    xt = [sb(f"x6_{i}", [128, 1024]) for i in range(2)]
    x1 = [sb(f"x1_{i}", [128, 1024]) for i in range(2)]
    pa = [P.ps(f"pa{i}", [128, 512], F32, st) for i in range(2)]
    pb = [P.ps(f"pb{i}", [128, 512], F32, st) for i in range(2)]
    px = [P.ps(f"px{i}", [128, 512], F32, st) for i in range(2)]
    k = 0
    xk = 0
    for (n0, nb) in [(0, 512), (512, 512), (1024, 512), (1536, 512), (2048, 64)]:
        P.dma("sp", oat[:, :, 0:nb], D["OAT"].rearrange("(fc p) c -> p fc c", p=128)[:, :, n0:n0 + nb], reads=[D["OAT_t"]], writes=[oat])
        P.dma("sp", obt[:, :, 0:nb], D["OBT"].rearrange("(fc p) c -> p fc c", p=128)[:, :, n0:n0 + nb], reads=[D["OBT_t"]], writes=[obt])
        for fo in range(8):
            a_, b_ = pa[k % 2], pb[k % 2]
            ga_, gb_, m1_, m2_ = ga[k % 2], gb[k % 2], m1[k % 2], m2[k % 2]
            k += 1
            P.dma("sp", ga_[:, 0:nb], D["GT"][fo * 128:(fo + 1) * 128, n0:n0 + nb], reads=[D["GT_t"]], writes=[ga_])
            P.dma("sp", gb_[:, 0:nb], D["GT"][1024 + fo * 128:1024 + (fo + 1) * 128, n0:n0 + nb], reads=[D["GT_t"]], writes=[gb_])
            for kc in range(8):
                mm(P, a_[:, 0:nb], wpa[:, kc, fo * 128:(fo + 1) * 128], oat[:, kc, 0:nb], kc == 0, kc == 7, [wpa, oat], [a_])
            for kc in range(8):
                mm(P, b_[:, 0:nb], wpb[:, kc, fo * 128:(fo + 1) * 128], obt[:, kc, 0:nb], kc == 0, kc == 7, [wpb, obt], [b_])
            tt(P, "dve", m1_[:, 0:nb], a_[:, 0:nb], ga_[:, 0:nb], ALU.mult, [a_, ga_], [m1_])
            tt(P, "dve", m2_[:, 0:nb], b_[:, 0:nb], gb_[:, 0:nb], ALU.mult, [b_, gb_], [m2_])
            tt(P, "pool", mT[:, fo, 0:nb], m1_[:, 0:nb], m2_[:, 0:nb], ALU.add, [m1_, m2_], [mT])
        for t0 in range(0, nb, 128):
            n = min(128, nb - t0)
            x_, x1_ = xt[xk % 2], x1[xk % 2]; xk += 1
            P.dma("sp", x_[0:n, :], D["xin"][n0 + t0:n0 + t0 + n, :], writes=[x_])
            g1 = g1r if n0 < 2048 else g1s
            for hf in range(2):
                p = px[hf]
                cs = slice(hf * 512, (hf + 1) * 512)
                for kc in range(8):
                    mm(P, p[0:n, :], mT[:, kc, t0:t0 + n], wo[:, kc, cs], kc == 0, kc == 7, [mT, wo], [p])
                tt(P, "dve", x1_[0:n, cs], p[0:n, :], g1[0:n, cs], ALU.mult, [p, g1], [x1_])
                tt(P, "pool", x1_[0:n, cs], x1_[0:n, cs], x_[0:n, cs], ALU.add, [x1_, x_], [x1_])
            P.dma("pool", D["X1"][n0 + t0:n0 + t0 + n, :], x1_[0:n, :], reads=[x1_], writes=[D["X1_t"]])
    P.barrier()
    st.close()


def stage6b(P, g, D):
    st = ExitStack()
    h2T = P.sb("h2T", [128, 8, NT], BF16, st)
    norm_to_featmajor(P, g, D, D["X1"], h2T, g.scale2, 24)
    st2 = ExitStack()
    sb = lambda name, shape, dt=F32: P.sb(name, shape, dt, st2)
    P.dma("pool", D["H2T"].rearrange("(fc p) c -> p fc c", p=128)[:, :, :], h2T[:], reads=[h2T], writes=[D["H2T_t"]])
    wq = sb("wq", [128, 8, 1024], BF16)
    stg = [sb(f"wstgq{i}", [128, 8, 512]) for i in range(2)]
    load_w_bf16(P, wq, D["w_pq"], stg, 0)
    qs = [sb(f"qs{i}", [128, 512], BF16) for i in range(2)]
    pq = [P.ps(f"pq{i}", [128, 512], F32, st2) for i in range(2)]
    k = 0
    for (n0, nb) in [(0, 512), (512, 512), (1024, 512), (1536, 512), (2048, 64)]:
        for fo in range(8):
            p, s = pq[k % 2], qs[k % 2]; k += 1
            for kc in range(8):
                mm(P, p[:, 0:nb], wq[:, kc, fo * 128:(fo + 1) * 128], h2T[:, kc, n0:n0 + nb], kc == 0, kc == 7, [wq, h2T], [p])
            cp(P, "act", s[:, 0:nb], p[:, 0:nb], [p], [s])
            P.dma("pool", D["QPT"][fo * 128:(fo + 1) * 128, n0:n0 + nb], s[:, 0:nb], reads=[s], writes=[D["QPT_t"]])
    P.barrier()
    st2.close()
    st.close()


def vmax(P, out, in_, reads, writes):
    return P.op("dve", lambda e: e.max(out=out, in_=in_), reads=reads, writes=writes)


def mrep(P, out, rep, vals, reads, writes):
    return P.op("dve", lambda e: e.match_replace(out=out, in_to_replace=rep, in_values=vals, imm_value=NEG), reads=reads, writes=writes)


def stage7_prep(P, g, D):
    st = ExitStack()
    sb = lambda name, shape, dt=F32: P.sb(name, shape, dt, st)
    uf = [sb(f"uf{i}", [128, 1024]) for i in range(2)]
    ub = [sb(f"ub{i}", [128, 1024], BF16) for i in range(2)]
    vf = [sb(f"vf{i}", [128, 1024]) for i in range(2)]
    vb = [sb(f"vb{i}", [128, 1024], BF16) for i in range(2)]
    sT = [sb(f"usT{i}", [128, 8, 512], BF16) for i in range(2)]
    pT = [P.ps(f"upT{i}", [128, 8, 128], BF16, st) for i in range(2)]
    NE = int(os.environ.get("NET", "128"))
    for et in range(NE):
        u, ubb, v, vbb = uf[et % 2], ub[et % 2], vf[et % 2], vb[et % 2]
        P.dma("sp", u[:], D["peer_u"][et * 128:(et + 1) * 128, :], writes=[u])
        P.dma("sp", v[:], D["peer_v"][et * 128:(et + 1) * 128, :], writes=[v])
        cp(P, "dve", ubb[:], u[:], [u], [ubb])
        cp(P, "pool", vbb[:], v[:], [v], [vbb])
        P.dma("pool", D["Vbf"].rearrange("(g j p) d -> g p j d", j=4, p=128)[et // 4, :, et % 4, :], vbb[:], reads=[vbb], writes=[D["Vbf_t"]])
        p = pT[et % 2]
        s = sT[(et // 4) % 2]
        for kc in range(8):
            tr(P, p[:, kc, :], ubb[:, kc * 128:(kc + 1) * 128], g.identb[:], [ubb, g.identb], [p])
        cp(P, "act", s[:, :, (et % 4) * 128:(et % 4 + 1) * 128], p[:], [p], [s])
        if et % 4 == 3:
            e0 = (et - 3) * 128
            P.dma("pool", D["UT"].rearrange("(g p) (kc e) -> g p kc e", p=128, kc=8)[e0 // 512, :, :, :], s[:], reads=[s], writes=[D["UT_t"]])
    P.barrier()
    st.close()


def stage7_peer(P, g, D):
    st = ExitStack()
    sb = lambda name, shape, dt=F32: P.sb(name, shape, dt, st)
    keysT = sb("keysT", [128, 8, 128], BF16)
    kt = sb("kt_f", [128, 128])
    pT = [P.ps(f"ppT{i}", [128, 8, 128], BF16, st) for i in range(1)]
    ps12 = P.ps("ps12", [128, 4, 128], F32, st)
    for h in range(8):
        P.dma("sp", kt[:, 0:64], D["peer_keys"][h, 0, :, :], writes=[kt])
        P.dma("sp", kt[:, 64:128], D["peer_keys"][h, 1, :, :], writes=[kt])
        tr(P, ps12[:, h % 4, :], kt[:], g.identf[:], [kt, g.identf], [ps12])
        cp(P, "act", keysT[:, h, :], ps12[:, h % 4, :], [ps12], [keysT])
    g2r = sb("g2r", [128, 1024]); g2s = sb("g2s", [128, 1024])
    P.dma("sp", g2r[:], D["modd"][0:1, 5120:6144].partition_broadcast(128), reads=[D["modd_t"]], writes=[g2r])
    for q in range(4):
        P.dma("sp", g2s[16 * q:16 * q + 16, :], D["modd"][1 + q:2 + q, 5120:6144].partition_broadcast(16), reads=[D["modd_t"]], writes=[g2s])
    h2t = [sb(f"h2t{i}", [128, 8, 128], BF16) for i in range(2)]; qpt = [sb(f"qpt{i}", [128, 8, 128], BF16) for i in range(2)]
    S12 = sb("S12", [128, 16, 128]); S1P = sb("S1P", [128, 8, 128]); srep = sb("srep", [128, 128])
    T16 = sb("T16", [128, 16, 16]); cand = sb("cand", [128, 8, 256]); crep = sb("crep", [128, 256])
    top16c = sb("top16c", [128, 8, 16]); ez = sb("ez", [128, 8, 16]); Z = sb("Zp", [128, 8]); cinv = sb("cinv", [128, 8]); lnc = sb("lnc", [128, 8]); thr = sb("pthr", [128, 8]); mtiny = sb("mtiny", [128, 1])
    mset(P, "dve", mtiny[:], -1e-5, [mtiny])
    SUBI = 8
    NSUB = 128 // SUBI
    W_ = SUBI * 128
    zb = [sb(f"zb{i}", [128, W_]) for i in range(3)]; eb = [sb(f"eb{i}", [128, W_]) for i in range(3)]
    gmb = [sb(f"gmb{i}", [128, W_], BF16) for i in range(3)]
    Gp = [P.ps(f"pGp{i}", [128, 512], F32, st) for i in range(2)]
    Aqs = [sb(f"Aq{i}", [128, W_], BF16) for i in range(2)]; GAs = [sb(f"GAq{i}", [128, W_], BF16) for i in range(2)]
    GATs = [sb(f"GAT{i}", [128, SUBI, 128], BF16) for i in range(2)]
    ub = [sb(f"pub{i}", [128, 8, 512], BF16) for i in range(3)]
    vb = [sb(f"pvb{i}", [128, 4, 1024], BF16) for i in range(3)]
    x1t = [sb(f"px1_{i}", [128, 1024]) for i in range(2)]; yt = sb("pyt", [128, 1024])
    pA = [P.ps(f"ppA{i}", [128, 512], F32, st) for i in range(2)]
    po = [P.ps(f"ppo{i}", [128, 512], F32, st) for i in range(2)]
    uk = 0; vk = 0; ak = 0; zk = 0; tk = 0; gk = 0
    NTL = int(os.environ.get("NTL", "17"))
    for ti, (r0, n) in enumerate(TILES[:NTL]):
        h2t_, qpt_, x1t_ = h2t[ti % 2], qpt[ti % 2], x1t[ti % 2]
        P.dma("sp", h2t_[:, :, 0:n], D["H2T"].rearrange("(fc p) c -> p fc c", p=128)[:, :, r0:r0 + n], reads=[D["H2T_t"]], writes=[h2t_])
        P.dma("sp", qpt_[:, :, 0:n], D["QPT"].rearrange("(fc p) c -> p fc c", p=128)[:, :, r0:r0 + n], reads=[D["QPT_t"]], writes=[qpt_])
        P.dma("sp", x1t_[0:n, :], D["X1"][r0:r0 + n, :], reads=[D["X1_t"]], writes=[x1t_])
        for hg in range(4):
            p = ps12
            for j in range(4):
                hp = hg * 4 + j
                h, pp = hp // 2, hp % 2
                pb = pp * 64
                mm(P, p[0:n, j, :], qpt_[pb:pb + 64, h, 0:n], keysT[pb:pb + 64, h, :], True, True, [qpt_, keysT], [p])
            cp(P, "act", S12[0:n, hg * 4:(hg + 1) * 4, :], p[0:n, :, :], [p], [S12])
        for hp in range(16):
            vmax(P, T16[0:n, hp, 0:8], S12[0:n, hp, :], [S12], [T16])
            mrep(P, srep[0:n, :], T16[0:n, hp, 0:8], S12[0:n, hp, :], [S12, T16], [srep])
            vmax(P, T16[0:n, hp, 8:16], srep[0:n, :], [srep], [T16])
        T16v = T16[0:n, :, :].rearrange("p (h t) k -> p h t k", t=2)
        tt(P, "dve", cand[0:n, :, :].rearrange("p h (i j) -> p h i j", i=16), T16v[:, :, 0, :].unsqueeze(3).to_broadcast([n, 8, 16, 16]),
           T16v[:, :, 1, :].unsqueeze(2).to_broadcast([n, 8, 16, 16]), ALU.add, [T16], [cand])
        for h in range(8):
            vmax(P, top16c[0:n, h, 0:8], cand[0:n, h, :], [cand], [top16c])
            mrep(P, crep[0:n, :], top16c[0:n, h, 0:8], cand[0:n, h, :], [cand, top16c], [crep])
            vmax(P, top16c[0:n, h, 8:16], crep[0:n, :], [crep], [top16c])
        tau = top16c[0:n, :, 15:16]
        tt(P, "dve", ez[0:n, :, :], top16c[0:n, :, :], tau.to_broadcast([n, 8, 16]), ALU.subtract, [top16c], [ez])
        act(P, ez[0:n, :, :], ez[0:n, :, :], AF.Exp, [ez], [ez])
        red(P, "dve", Z[0:n, :], ez[0:n, :, :], ALU.add, [ez], [Z])
        recip(P, cinv[0:n, :], Z[0:n, :], [Z], [cinv])
        act(P, lnc[0:n, :], cinv[0:n, :], AF.Ln, [cinv], [lnc])
        S12v = S12[0:n, :, :].rearrange("p (h t) k -> p h t k", t=2)
        tt(P, "dve", S1P[0:n, :, :], S12v[:, :, 0, :], tau.to_broadcast([n, 8, 128]), ALU.subtract, [S12, top16c], [S1P])
        def emitA(ib):
            nonlocal uk, ak
            Aq = Aqs[ib % 2]
            for eg in range(W_ // 512):
                e0 = ib * W_ + eg * 512
                u = ub[uk % 3]; uk += 1
                P.dma("sp", u[:], D["UT"].rearrange("(g p) (kc e) -> g p kc e", p=128, kc=8)[e0 // 512, :, :, :], reads=[D["UT_t"]], writes=[u])
                p = pA[ak % 2]; ak += 1
                for kc in range(8):
                    mm(P, p[0:n, :], h2t_[:, kc, 0:n], u[:, kc, :], kc == 0, kc == 7, [h2t_, u], [p])
                act(P, Aq[0:n, eg * 512:(eg + 1) * 512], p[0:n, :], AF.Gelu_apprx_tanh, [p], [Aq])

        def emitZ(ib, h):
            nonlocal zk
            z_, e_ = zb[zk % 3], eb[zk % 3]; zk += 1
            z3 = z_[0:n, :].rearrange("p (i j) -> p i j", i=SUBI)
            tt(P, "dve", z3, S1P[0:n, h, ib * SUBI:(ib + 1) * SUBI].unsqueeze(2).to_broadcast([n, SUBI, 128]),
               S12v[:, h, 1, :].unsqueeze(1).to_broadcast([n, SUBI, 128]), ALU.add, [S1P, S12], [z_])
            act(P, e_[0:n, :], z_[0:n, :], AF.Exp, [z_, lnc], [e_], bias=lnc[0:n, h:h + 1])
            return z_, e_

        emitA(0)
        pend = emitZ(0, 0)
        for ib in range(NSUB):
            Aq, GA, GAT = Aqs[ib % 2], GAs[ib % 2], GATs[ib % 2]
            if ib + 1 < NSUB:
                emitA(ib + 1)
            for h in range(8):
                z_, e_ = pend
                if h + 1 < 8:
                    pend = emitZ(ib, h + 1)
                elif ib + 1 < NSUB:
                    pend = emitZ(ib + 1, 0)
                gm = gmb[gk % 3]; gk += 1
                stt(P, "dve", gm[0:n, :], z_[0:n, :], -1e-5, e_[0:n, :], ALU.is_ge, ALU.mult, [z_, e_], [gm])
                for cg in range(W_ // 512):
                    mm(P, Gp[cg][0:n, :], g.identb[0:n, 0:n], gm[0:n, cg * 512:(cg + 1) * 512], h == 0, h == 7, [gm, g.identb], [Gp[cg]])
            for cg in range(W_ // 512):
                tt(P, "dve", GA[0:n, cg * 512:(cg + 1) * 512], Gp[cg][0:n, :], Aq[0:n, cg * 512:(cg + 1) * 512], ALU.mult, [Gp[cg], Aq], [GA])
            for j0 in range(0, SUBI, 8):
                pt = pT[0]; tk += 1
                for jj in range(8):
                    et = j0 + jj
                    tr(P, pt[:, jj, 0:n], GA[0:n, et * 128:(et + 1) * 128], g.identb[0:n, 0:n], [GA, g.identb], [pt])
                cp(P, "act", GAT[:, j0:j0 + 8, 0:n], pt[:, :, 0:n], [pt], [GAT])
            for vg in range(SUBI // 4):
                v = vb[vk % 3]; vk += 1
                e0 = (ib * SUBI + vg * 4) * 128
                P.dma("pool", v[:], D["Vbf"].rearrange("(g j p) d -> g p j d", j=4, p=128)[e0 // 512, :, :, :], reads=[D["Vbf_t"]], writes=[v])
                for j in range(4):
                    et = vg * 4 + j
                    first = (ib == 0 and et == 0)
                    last = (ib == NSUB - 1 and et == SUBI - 1)
                    for hf in range(2):
                        mm(P, po[hf][0:n, :], GAT[:, et, 0:n], v[:, j, hf * 512:(hf + 1) * 512], first, last, [GAT, v], [po[hf]])
        g2 = g2r if r0 < 2048 else g2s
        for hf in range(2):
            cs = slice(hf * 512, (hf + 1) * 512)
            tt(P, "dve", yt[0:n, cs], po[hf][0:n, :], g2[0:n, cs], ALU.mult, [po[hf], g2], [yt])
        if D.get("PEERdbg") is not None:
            P.dma("pool", D["PEERdbg"][r0:r0 + n, :], yt[0:n, :], reads=[yt])
        tt(P, "pool", yt[0:n, :], yt[0:n, :], x1t_[0:n, :], ALU.add, [yt, x1t_], [yt])
        P.dma("pool", D["y"][r0:r0 + n, :], yt[0:n, :], reads=[yt], writes=[D["y_t"]])
    P.barrier()
    st.close()


def declare(nc, P, D, debug=False):
    def din(name, shape, dt=F32):
        D[name] = nc.dram_tensor(name, list(shape), dt, kind="ExternalInput").ap()

    def dscr(name, shape, dt=F32, kind="Internal"):
        D[name] = nc.dram_tensor(name, list(shape), dt, kind=("ExternalOutput" if debug else kind)).ap()
        D[name + "_t"] = T(None, name)

    din("ident", [128, 128]); din("cin", [5, 1024]); din("xin", [NT, 1024])
    din("w_ada", [1024, 6144]); din("b_ada", [1, 6144]); din("norm1_w", [1024]); din("norm2_w", [1024]); din("b_gate", [2048])
    din("w_in", [1024, 9064]); din("cos", [NT, 32]); din("sin", [NT, 32]); din("k_norm_w", [1, 64]); din("q_norm_w", [1, 64])
    din("mu_rw", [1, RW_IN])
    for nm in ["w0", "a0", "k_k", "k_a", "r_k", "lnx_w", "lnx_b"]:
        din(nm, [1, 1024])
    din("w_up", [64, 1024]); din("a_up", [64, 1024]); din("g_up", [160, 1024])
    din("sshift", [4, RW_IN]); din("swkv", [4, 16, 64, 64])
    for nm in ["m_tri", "m_ones", "m_sl", "m_su", "m_u"]:
        din(nm, [128, 128])
    din("m_valid", [128, 2])
    dscr("modd", [5, 6144]); dscr("Ptok", [NT, TOKW]); dscr("GT", [2048, NT])
    dscr("BC", [UC, 16])
    for nm in ["Vs", "KPs", "BPs", "Gs"]:
        dscr(nm, [UC, 1024])
    for nm in ["RT", "AT", "KT", "BT", "GCT"]:
        dscr(nm, [1024, UC])
    for nm in ["OAT", "OBT", "QT", "KTn", "H2T", "QPT"]:
        dscr(nm, [1024, NT], BF16)
    dscr("QIT", [512, NT], BF16); dscr("KIT", [64, NT], BF16); dscr("WI", [NT, 8])
    dscr("X1", [NT, 1024], F32, "ExternalOutput" if os.environ.get("DBG") else "Internal")
    dscr("UT", [4096, 4096], BF16); dscr("Vbf", [16384, 1024], BF16)
    din("cache_k", [4, 2048, 1024]); din("cache_v", [4, 2048, 1024]); din("cache_kidx", [4, 2048, 64]); din("m_pow2", [128, NBIS])
    for nm in ["w_proj_a", "w_proj_b", "w_out", "w_pq"]:
        din(nm, [1024, 1024])
    din("peer_keys", [8, 2, 128, 64]); din("peer_u", [16384, 1024]); din("peer_v", [16384, 1024])
    for nm, shp in [("y", [NT, 1024]), ("wkv", [5 * 16 * 64, 64]), ("kout", [NT, 1024]), ("vout", [NT, 1024]), ("kidx", [NT, 64]), ("shift", [5, RW_IN])]:
        D[nm] = nc.dram_tensor(nm, shp, F32, kind="ExternalOutput").ap()
        D[nm + "_t"] = T(None, nm)


def host_consts():
    inv = (10000.0 ** (-np.arange(32, dtype=np.float32) / 32)).astype(np.float32)
    pos = np.concatenate([np.arange(2048), np.tile(2048 + np.arange(16), 4)]).astype(np.float32)
    ang = pos[:, None] * inv[None, :]
    idx = np.arange(128)
    same = (idx[:, None] // 64) == (idx[None, :] // 64)
    f32 = lambda a: np.ascontiguousarray(a, dtype=np.float32)
    valid = np.ones((128, 2), np.float32)
    valid[:, 1] = ((idx % 64) < 16)
    return {"ident": np.eye(128, dtype=np.float32), "cos": np.cos(ang).astype(np.float32), "sin": np.sin(ang).astype(np.float32),
            "m_tri": f32(same & (idx[:, None] <= idx[None, :])), "m_ones": f32(same), "m_sl": f32(same & (idx[:, None] > idx[None, :])),
            "m_su": f32(same & (idx[:, None] < idx[None, :])), "m_u": f32(same & (idx[:, None] <= idx[None, :])), "m_valid": valid,
            "m_pow2": np.tile((2.0 ** -(np.arange(NBIS) + 1.0)).astype(np.float32)[None, :], (128, 1))}


def run_all(P, g, D):
    stage0(P, g, D)
    st = ExitStack()
    hT = P.sb("hT", [128, 8, NT], BF16, st)
    norm_to_featmajor(P, g, D, D["xin"], hT, g.scale1, 0)
    stage2(P, g, D, hT)
    st.close()
    stage3_dsa(P, g, D)
    stage3_rw(P, g, D)
    stage4_scan(P, g, D)
    stage5_attn(P, g, D)
    stage6(P, g, D)
    stage6b(P, g, D)
    stage7_prep(P, g, D)
    stage7_peer(P, g, D)


def core_inputs(inp, c, consts, local=False):
    f = lambda a: np.ascontiguousarray(np.asarray(a, dtype=np.float32))
    m = dict(consts)
    W = lambda k: f(np.asarray(inp[k])[0])
    m.update({"w_ada": W("w_ada"), "b_ada": f(inp["b_ada"]), "norm1_w": W("norm1_w"), "norm2_w": W("norm2_w"), "b_gate": W("b_gate"), "w_in": W("w_in"),
              "k_norm_w": f(inp["k_norm_w"]), "q_norm_w": f(inp["q_norm_w"]), "mu_rw": f(inp["mu_rw"]), "w0": f(inp["w0"]), "a0": f(inp["a0"]),
              "k_k": f(inp["k_k"]), "k_a": f(inp["k_a"]), "r_k": f(np.asarray(inp["r_k"]).reshape(1, 1024)), "lnx_w": f(inp["lnx_w"]), "lnx_b": f(inp["lnx_b"]),
              "w_up": W("w_up"), "a_up": W("a_up"), "g_up": W("g_up"), "w_proj_a": W("w_proj_a"), "w_proj_b": W("w_proj_b"), "w_out": W("w_out"),
              "w_pq": W("w_pq"), "peer_keys": W("peer_keys"), "peer_u": W("peer_u"), "peer_v": W("peer_v")})
    pc = 0 if local else c
    sl = slice(0, 4) if local else slice(4 * c, 4 * c + 4)
    m["xin"] = f(np.concatenate([np.asarray(inp["x_prompt"])[pc], np.asarray(inp["x_sample"])[sl].reshape(64, 1024)], 0))
    m["cin"] = f(np.concatenate([np.asarray(inp["c_prompt"])[pc:pc + 1], np.asarray(inp["c_sample"])[sl]], 0))
    m["sshift"] = f(np.asarray(inp["state_shift"])[0][sl])
    m["swkv"] = f(np.asarray(inp["state_wkv"])[0][sl])
    m["cache_k"] = f(np.asarray(inp["cache_k"])[0][sl].reshape(4, 2048, 1024))
    m["cache_v"] = f(np.asarray(inp["cache_v"])[0][sl].reshape(4, 2048, 1024))
    m["cache_kidx"] = f(np.asarray(inp["cache_kidx"])[0][sl])
    return m


def build_program():
    nc = bass.Bass("TRN2", target_bir_lowering=False)
    P = Prog(nc)
    D = {}
    declare(nc, P, D)
    g = G()
    g.epsc = P.sb("epsc", [128, 1], F32)
    mset(P, "dve", g.epsc[:], EPS, [g.epsc])
    run_all(P, g, D)
    P.emit()
    return nc


def kernel(**inp):
    nc = build_program()
    consts = host_consts()
    in_maps = [core_inputs(inp, c, consts) for c in range(8)]
    res = run_bass_kernel_spmd(nc, in_maps, core_ids=list(range(8)))
    R = res.results
    cat = lambda name, sl, shp: np.stack([R[c][name][sl].reshape(shp) for c in range(8)], 0)
    y_p = cat("y", slice(0, 2048), (2048, 1024))
    y_s = cat("y", slice(2048, NT), (4, 16, 1024)).reshape(32, 16, 1024)
    wkv_p = cat("wkv", slice(0, 1024), (16, 64, 64))[None]
    wkv_s = cat("wkv", slice(1024, 5120), (4, 16, 64, 64)).reshape(32, 16, 64, 64)[None]
    sh_p = cat("shift", slice(0, 1), (RW_IN,))[None]
    sh_s = cat("shift", slice(1, 5), (4, RW_IN)).reshape(32, RW_IN)[None]
    k_p = cat("kout", slice(0, 2048), (2048, 16, 64))[None]
    k_s = cat("kout", slice(2048, NT), (4, 16, 16, 64)).reshape(32, 16, 16, 64)[None]
    v_p = cat("vout", slice(0, 2048), (2048, 16, 64))[None]
    v_s = cat("vout", slice(2048, NT), (4, 16, 16, 64)).reshape(32, 16, 16, 64)[None]
    ki_p = cat("kidx", slice(0, 2048), (2048, 64))[None]
    ki_s = cat("kidx", slice(2048, NT), (4, 16, 64)).reshape(32, 16, 64)[None]
    return (y_p, y_s, wkv_p, sh_p, k_p, v_p, ki_p, wkv_s, sh_s, k_s, v_s, ki_s)
```

```python
import numpy as np
from contextlib import ExitStack
import concourse.bass as bass
import concourse.mybir as mybir
from concourse.bass_utils import run_bass_kernel_spmd

F32 = mybir.dt.float32
BF16 = mybir.dt.bfloat16
I32 = mybir.dt.int32
U32 = mybir.dt.uint32
AF = mybir.ActivationFunctionType
ALU = mybir.AluOpType
AX = mybir.AxisListType

ENGS = ["pe", "act", "dve", "pool", "sp"]
NDMA = {"sp": 40, "pool": 24, "act": 8}


class Dep:
    __slots__ = ("w", "r", "name")

    def __init__(self, name=""):
        self.w = None
        self.r = []
        self.name = name


class Bank:
    def __init__(self):
        self.last = {}
        self.pe_rows = None


class T:
    def __init__(self, h, name):
        self.h = h
        self.name = name
        self.dep = Dep(name)
        self.subs = {}
        self.bank = None

    def __getitem__(self, idx):
        return self.h[idx]

    def sub(self, key):
        if key not in self.subs:
            self.subs[key] = Dep(f"{self.name}.{key}")
        return self.subs[key]


class Slot:
    def __init__(self, t, i):
        self.base = t.h[:, i, :]
        self.dep = Dep(f"{t.name}[{i}]")
        self.bank = t.bank

    def __getitem__(self, idx):
        return self.base[idx]


class SlotAP:
    def __init__(self, t, ap):
        self.base = ap
        self.dep = Dep(t.name + "[ap]")
        self.bank = t.bank

    def __getitem__(self, idx):
        return self.base[idx]


def _dep(x):
    return x.dep if hasattr(x, "dep") else x


class Prog:
    def __init__(self, nc):
        self.nc = nc
        self.es = ExitStack()
        self.ops = {e: [] for e in ENGS}
        self.cnt = {e: 0 for e in ENGS}
        self.esem = {}
        for e in ENGS:
            self.esem[e] = self.es.enter_context(nc.semaphore("s_" + e))
        self.dsem = {}
        self.dval = {}
        self.dnext = {}
        for q, n in NDMA.items():
            self.dsem[q] = [self.es.enter_context(nc.semaphore(f"d_{q}{i}")) for i in range(n)]
            self.dval[q] = [0] * n
            self.dnext[q] = 0
        self.seen = {e: {} for e in ENGS}
        self.semobj = {}
        self.nwaits = 0

    def _uniq(self, name):
        if not hasattr(self, "_names"):
            self._names = {}
        k = self._names.get(name, 0)
        self._names[name] = k + 1
        return name if k == 0 else f"{name}__{k}"

    def sb(self, name, shape, dtype, stack=None):
        name = self._uniq(name)
        h = (stack or self.es).enter_context(self.nc.sbuf_tensor(name, list(shape), dtype))
        return T(h, name)

    def ps(self, name, shape, dtype=F32, stack=None):
        name = self._uniq(name)
        h = (stack or self.es).enter_context(self.nc.psum_tensor(name, list(shape), dtype))
        t = T(h, name)
        t.bank = Bank()
        return t

    def dram(self, name, shape, dtype, kind="Internal"):
        h = self.nc.dram_tensor(name, list(shape), dtype, kind=kind)
        return T(h, name)

    def _waits(self, eng, reads, writes, extra=()):
        evs = list(extra)
        for b in reads:
            b = _dep(b)
            if b.w is not None:
                evs.append(b.w)
        for b in writes:
            b = _dep(b)
            if b.w is not None:
                evs.append(b.w)
            evs.extend(b.r)
        need = {}
        for (key, sem, val, src) in evs:
            if src == "pe" and eng == "pe":
                continue
            if self.seen[eng].get(key, 0) >= val:
                continue
            if need.get(key, (None, 0))[1] < val:
                need[key] = (sem, val)
        for key, (sem, val) in need.items():
            self.seen[eng][key] = val
        return list(need.values())

    def _record(self, ev, reads, writes):
        for b in reads:
            _dep(b).r.append(ev)
        for b in writes:
            b = _dep(b)
            b.w = ev
            b.r = []

    def op(self, eng, fn, reads=(), writes=(), pe_rows=None):
        banks = {}
        for b in list(reads) + list(writes):
            bk = getattr(b, "bank", None)
            if bk is not None:
                banks[id(bk)] = bk
        extra = [ev for bk in banks.values() for e2, ev in bk.last.items() if e2 != eng]
        if eng == "pe" and pe_rows is not None:
            for bk in banks.values():
                if bk.pe_rows is not None and bk.pe_rows != pe_rows and "pe" in bk.last and (pe_rows[1] < 128 or bk.pe_rows[1] < 128):
                    k_, s_, v_, _ = bk.last["pe"]
                    extra.append((k_, s_, v_, "force"))
                bk.pe_rows = pe_rows
        waits = self._waits(eng, reads, writes, extra)
        self.cnt[eng] += 1
        ev = (eng, self.esem[eng], self.cnt[eng], eng)
        for bk in banks.values():
            bk.last[eng] = ev
        self._record(ev, reads, writes)
        self.ops[eng].append((waits, fn, (self.esem[eng], 1)))
        self.nwaits += len(waits)
        return ev

    def dma(self, q, out, in_, reads=(), writes=(), **kw):
        i = self.dnext[q]
        self.dnext[q] = (i + 1) % len(self.dsem[q])
        sem = self.dsem[q][i]
        key = (q, i)
        waits = self._waits(q, reads, writes)
        prev = self.dval[q][i]
        if prev > 0 and self.seen[q].get(key, 0) < prev:
            waits.append((sem, prev))
            self.seen[q][key] = prev
        self.dval[q][i] = prev + 16
        ev = (key, sem, prev + 16, "dma")
        self._record(ev, reads, writes)
        self.ops[q].append((waits, lambda e: e.dma_start(out=out, in_=in_, **kw), (sem, 16)))
        self.nwaits += len(waits)
        return ev

    def barrier(self):
        evs = []
        for e in ENGS:
            if self.cnt[e] > 0:
                evs.append((e, self.esem[e], self.cnt[e]))
        for q in self.dsem:
            for i, v in enumerate(self.dval[q]):
                if v > 0:
                    evs.append(((q, i), self.dsem[q][i], v))
        for e in ENGS:
            waits = []
            for key, sem, val in evs:
                if key == e:
                    continue
                if self.seen[e].get(key, 0) >= val:
                    continue
                self.seen[e][key] = val
                waits.append((sem, val))
            if waits:
                self.ops[e].append((waits, None, None))

    def emit(self):
        self.barrier()
        nc = self.nc
        with nc.Block() as block:
            def run(e, lst):
                for waits, fn, inc in lst:
                    for sem, val in waits:
                        e.wait_ge(sem, val)
                    if fn is not None:
                        ins = fn(e)
                        ins.then_inc(inc[0], inc[1])

            @block.tensor
            def _(e):
                run(e, self.ops["pe"])

            @block.scalar
            def _(e):
                run(e, self.ops["act"])

            @block.vector
            def _(e):
                run(e, self.ops["dve"])

            @block.gpsimd
            def _(e):
                run(e, self.ops["pool"])

            @block.sync
            def _(e):
                run(e, self.ops["sp"])
        self.es.close()
EPS = 1e-6
NT = 2112
TILES = [(i * 128, 128) for i in range(16)] + [(2048, 64)]
RW_IN = 3360
TOKW = 7016


def mm(P, out, lhsT, rhs, start, stop, reads, writes):
    rows = (lhsT.base_partition(), lhsT.partition_size())
    return P.op("pe", lambda e: e.matmul(out, lhsT, rhs, start=start, stop=stop), reads=reads, writes=writes, pe_rows=rows)


def tr(P, out, in_, ident, reads, writes):
    return P.op("pe", lambda e: e.transpose(out, in_, ident), reads=reads, writes=writes)


def act(P, out, in_, func, reads, writes, **kw):
    return P.op("act", lambda e: e.activation(out=out, in_=in_, func=func, **kw), reads=reads, writes=writes)


def ts(P, eng, out, in0, s1, s2, op0, op1=None, reads=(), writes=(), **kw):
    def f(e):
        if op1 is None:
            return e.tensor_scalar(out=out, in0=in0, scalar1=s1, scalar2=s2, op0=op0, **kw)
        return e.tensor_scalar(out=out, in0=in0, scalar1=s1, scalar2=s2, op0=op0, op1=op1, **kw)
    return P.op(eng, f, reads=reads, writes=writes)


def tt(P, eng, out, in0, in1, op, reads, writes):
    if eng == "pool":
        eng = "dve"
    return P.op(eng, lambda e: e.tensor_tensor(out=out, in0=in0, in1=in1, op=op), reads=reads, writes=writes)


def cp(P, eng, out, in_, reads, writes):
    if eng == "pool":
        eng = "act"
    if eng == "act":
        return act(P, out, in_, AF.Copy, reads, writes)
    return P.op(eng, lambda e: e.tensor_copy(out, in_), reads=reads, writes=writes)


class G:
    pass


def featvec(P, g, st, name, vec_ap, n):
    tmp = P.sb(name + "_t", [n, 128], F32, st)
    P.dma("sp", tmp[:], vec_ap.rearrange("(c p) -> c p", p=128), writes=[tmp])
    ps = P.ps(name + "_p", [128, n], F32, st)
    tr(P, ps[:], tmp[:], g.identf[0:n, 0:n], [tmp, g.identf], [ps])
    out = getattr(g, name)
    cp(P, "dve", out[:], ps[:], [ps], [out])
    return out


def stage0(P, g, D):
    g.identf = P.sb("identf", [128, 128], F32)
    g.identb = P.sb("identb", [128, 128], BF16)
    g.modT = P.sb("modT", [128, 48, 5], F32)
    g.n1w = P.sb("n1w", [128, 8], F32)
    g.n2w = P.sb("n2w", [128, 8], F32)
    g.bgT = P.sb("bgT", [128, 16], F32)
    g.scale1 = P.sb("scale1", [128, 8, 5], F32)
    g.scale2 = P.sb("scale2", [128, 8, 5], F32)
    st = ExitStack()
    P.dma("sp", g.identf[:], D["ident"][:, :], writes=[g.identf])
    cp(P, "dve", g.identb[:], g.identf[:], [g.identf], [g.identb])
    c5 = P.sb("c5", [5, 1024], F32, st)
    s5 = P.sb("s5", [5, 1024], F32, st)
    P.dma("sp", c5[:], D["cin"][:, :], writes=[c5])
    act(P, s5[:], c5[:], AF.Silu, [c5], [s5])
    sT = P.sb("sT", [128, 8, 5], F32, st)
    pst = P.ps("pst", [128, 8, 5], F32, st)
    for kc in range(8):
        tr(P, pst[:, kc, :], s5[0:5, kc * 128:(kc + 1) * 128], g.identf[0:5, 0:5], [s5, g.identf], [pst])
    cp(P, "dve", sT[:], pst[:], [pst], [sT])
    bada5 = P.sb("bada5", [5, 6144], F32, st)
    P.dma("sp", bada5[:], D["b_ada"][0:1, :].partition_broadcast(5), writes=[bada5])
    mod5 = P.sb("mod5", [5, 6144], F32, st)
    wa = [P.sb(f"wa{i}", [128, 8, 512], F32, st) for i in range(2)]
    pm = [P.ps(f"pm{i}", [5, 512], F32, st) for i in range(2)]
    for gi in range(12):
        w = wa[gi % 2]
        p = pm[gi % 2]
        P.dma("sp", w[:], D["w_ada"][:, gi * 512:(gi + 1) * 512].rearrange("(kc p) c -> p kc c", p=128), writes=[w])
        for kc in range(8):
            mm(P, p[:], sT[:, kc, :], w[:, kc, :], kc == 0, kc == 7, [sT, w], [p])
        tt(P, "dve", mod5[:, gi * 512:(gi + 1) * 512], p[:], bada5[:, gi * 512:(gi + 1) * 512], ALU.add, [p, bada5], [mod5])
    P.dma("sp", D["modd"][:, :], mod5[:], reads=[mod5], writes=[D["modd_t"]])
    pmt = P.ps("pmt", [128, 48, 5], F32, st)
    for c in range(48):
        tr(P, pmt[:, c, :], mod5[0:5, c * 128:(c + 1) * 128], g.identf[0:5, 0:5], [mod5, g.identf], [pmt])
    cp(P, "dve", g.modT[:], pmt[:], [pmt], [g.modT])
    n1 = featvec(P, g, st, "n1w", D["norm1_w"], 8)
    n2 = featvec(P, g, st, "n2w", D["norm2_w"], 8)
    g.bgT = featvec(P, g, st, "bgT", D["b_gate"], 16)
    for c in range(8):
        ts(P, "dve", g.scale1[:, c, :], g.modT[:, 8 + c, :], 1.0, n1[:, c:c + 1], ALU.add, ALU.mult, [g.modT, n1], [g.scale1])
        ts(P, "dve", g.scale2[:, c, :], g.modT[:, 32 + c, :], 1.0, n2[:, c:c + 1], ALU.add, ALU.mult, [g.modT, n2], [g.scale2])
    P.barrier()
    st.close()


def norm_to_featmajor(P, g, D, src_ap, hT, scale, shift_chunk0):
    st = ExitStack()
    xt = [P.sb(f"nx{i}", [128, 1024], F32, st) for i in range(2)]
    xn = [P.sb(f"nxn{i}", [128, 1024], BF16, st) for i in range(2)]
    junk = P.sb("njunk", [128, 1024], F32, st)
    ss = [P.sb(f"nss{i}", [128, 1], F32, st) for i in range(2)]
    rs = [P.sb(f"nrs{i}", [128, 1], F32, st) for i in range(2)]
    pT = [P.ps(f"npT{i}", [128, 8, 128], BF16, st) for i in range(2)]
    for ti, (r0, n) in enumerate(TILES):
        x, xb, s, r, p = xt[ti % 2], xn[ti % 2], ss[ti % 2], rs[ti % 2], pT[ti % 2]
        P.dma("sp", x[0:n, :], src_ap[r0:r0 + n, :], writes=[x])
        act(P, junk[0:n, :], x[0:n, :], AF.Square, [x], [junk, s], accum_out=s[0:n, :])
        act(P, s[0:n, :], s[0:n, :], AF.Sqrt, [s], [s], scale=1.0 / 1024, bias=g.epsc[0:n, :])
        P.op("dve", lambda e, r=r, s=s, n=n: e.reciprocal(r[0:n, :], s[0:n, :]), reads=[s], writes=[r])
        ts(P, "dve", xb[0:n, :], x[0:n, :], r[0:n, :], None, ALU.mult, None, [x, r], [xb])
        for kc in range(8):
            tr(P, p[:, kc, 0:n], xb[0:n, kc * 128:(kc + 1) * 128], g.identb[0:n, 0:n], [xb, g.identb], [p])
        for kc in range(8):
            if ti < 16:
                act(P, hT[:, kc, r0:r0 + n], p[:, kc, 0:n], AF.Identity, [p, scale, g.modT], [hT],
                    scale=scale[:, kc, 0:1], bias=g.modT[:, shift_chunk0 + kc, 0:1])
            else:
                for q in range(4):
                    act(P, hT[:, kc, r0 + 16 * q:r0 + 16 * q + 16], p[:, kc, 16 * q:16 * q + 16], AF.Identity,
                        [p, scale, g.modT], [hT], scale=scale[:, kc, 1 + q:2 + q], bias=g.modT[:, shift_chunk0 + kc, 1 + q:2 + q])
    P.barrier()
    st.close()


def stage2(P, g, D, hT):
    st = ExitStack()
    wf = [P.sb(f"wf{i}", [128, 8, 512], F32, st) for i in range(2)]
    wb = [P.sb(f"wb{i}", [128, 8, 512], BF16, st) for i in range(2)]
    stg = [P.sb(f"stg{i}", [128, 512], F32, st) for i in range(3)]
    pp = [P.ps(f"pp{i}", [128, 512], F32, st) for i in range(3)]
    k = 0
    groups = [(c0, min(512, TOKW - c0), False) for c0 in range(0, TOKW, 512)] + [(TOKW + i * 512, 512, True) for i in range(4)]
    for gi, (c0, gw, isgate) in enumerate(groups):
        w, b = wf[gi % 2], wb[gi % 2]
        P.dma("sp", w[:, :, 0:gw], D["w_in"][:, c0:c0 + gw].rearrange("(kc p) c -> p kc c", p=128), writes=[w])
        cp(P, "pool" if gi % 2 else "dve", b[:, :, 0:gw], w[:, :, 0:gw], [w], [b])
        if not isgate:
            for ti, (r0, n) in enumerate(TILES):
                p, s = pp[k % 3], stg[k % 3]
                for kc in range(8):
                    mm(P, p[0:n, 0:gw], hT[:, kc, r0:r0 + n], b[:, kc, 0:gw], kc == 0, kc == 7, [hT, b], [p])
                cp(P, "act" if k % 2 else "dve", s[0:n, 0:gw], p[0:n, 0:gw], [p], [s])
                P.dma("pool", D["Ptok"][r0:r0 + n, c0:c0 + gw], s[0:n, 0:gw], reads=[s], writes=[D["Ptok_t"].sub(ti)])
                k += 1
        else:
            for j in range(4):
                fch = (c0 - TOKW) // 128 + j
                for (n0, nb) in [(0, 512), (512, 512), (1024, 512), (1536, 512), (2048, 64)]:
                    p, s = pp[k % 3], stg[k % 3]
                    for kc in range(8):
                        mm(P, p[:, 0:nb], b[:, kc, j * 128:(j + 1) * 128], hT[:, kc, n0:n0 + nb], kc == 0, kc == 7, [hT, b], [p])
                    act(P, s[:, 0:nb], p[:, 0:nb], AF.Sigmoid, [p, g.bgT], [s], bias=g.bgT[:, fch:fch + 1])
                    P.dma("pool", D["GT"][fch * 128:(fch + 1) * 128, n0:n0 + nb], s[:, 0:nb], reads=[s], writes=[D["GT_t"]])
                    k += 1
    P.barrier()
    st.close()

C_Q, C_K, C_V, C_QI, C_KI, C_WI = 3360, 4384, 5408, 6432, 6944, 7008


def rope(P, eng, out4, in4, cosb, sinb, tmp, n, reads, writes):
    H = in4.shape[1]
    x1, x2 = in4[:, :, 0, :], in4[:, :, 1, :]
    t = [tmp[0:n, i, 0:H * 32].rearrange("p (h d) -> p h d", h=H) for i in range(4)]
    tt(P, eng, t[0], x1, cosb, ALU.mult, reads, [tmp])
    tt(P, eng, t[1], x2, sinb, ALU.mult, reads, [tmp])
    tt(P, eng, t[2], x2, cosb, ALU.mult, reads, [tmp])
    tt(P, eng, t[3], x1, sinb, ALU.mult, reads, [tmp])
    tt(P, eng, out4[:, :, 0, :], t[0], t[1], ALU.subtract, [tmp], writes)
    tt(P, eng, out4[:, :, 1, :], t[2], t[3], ALU.add, [tmp], writes)


def stage3_dsa(P, g, D):
    st = ExitStack()
    knw = P.sb("knw", [128, 64], F32, st)
    qnw = P.sb("qnw", [128, 64], F32, st)
    P.dma("sp", knw[:], D["k_norm_w"][0:1, :].partition_broadcast(128), writes=[knw])
    P.dma("sp", qnw[:], D["q_norm_w"][0:1, :].partition_broadcast(128), writes=[qnw])
    pd = [P.sb(f"pd{i}", [128, 3656], F32, st) for i in range(2)]
    cs = [P.sb(f"cs{i}", [128, 2, 32], F32, st) for i in range(2)]
    junk = P.sb("djunk", [128, 1024], F32, st)
    ssq = P.sb("dssq", [128, 16], F32, st)
    rst = P.sb("drst", [128, 16], F32, st)
    kn = P.sb("dkn", [128, 1024], F32, st)
    ko = [P.sb(f"dko{i}", [128, 1024], F32, st) for i in range(2)]
    kio = [P.sb(f"dkio{i}", [128, 64], F32, st) for i in range(2)]
    tmp = P.sb("dtmp", [128, 4, 512], F32, st)
    for ti, (r0, n) in enumerate(TILES):
        p, c = pd[ti % 2], cs[ti % 2]
        P.dma("sp", p[0:n, :], D["Ptok"][r0:r0 + n, C_Q:TOKW], reads=[D["Ptok_t"].sub(ti)], writes=[p])
        P.dma("sp", c[0:n, 0, :], D["cos"][r0:r0 + n, :], writes=[c])
        P.dma("sp", c[0:n, 1, :], D["sin"][r0:r0 + n, :], writes=[c])
        P.dma("pool", D["vout"][r0:r0 + n, :], D["Ptok"][r0:r0 + n, C_V:C_V + 1024], reads=[D["Ptok_t"].sub(ti)])
        k = p[0:n, C_K - C_Q:C_K - C_Q + 1024]
        act(P, junk[0:n, :], k, AF.Square, [p], [junk])
        P.op("dve", lambda e, n=n: e.tensor_reduce(out=ssq[0:n, :], in_=junk[0:n, :].rearrange("p (h d) -> p h d", h=16), axis=AX.X, op=ALU.add), reads=[junk], writes=[ssq])
        act(P, ssq[0:n, :], ssq[0:n, :], AF.Sqrt, [ssq], [ssq], scale=1.0 / 64, bias=g.epsc[0:n, :])
        P.op("dve", lambda e, n=n: e.reciprocal(rst[0:n, :], ssq[0:n, :]), reads=[ssq], writes=[rst])
        kn3 = kn[0:n, :].rearrange("p (h d) -> p h d", h=16)
        tt(P, "dve", kn3, k.rearrange("p (h d) -> p h d", h=16), rst[0:n, :].unsqueeze(2).to_broadcast([n, 16, 64]), ALU.mult, [p, rst], [kn])
        tt(P, "dve", kn3, kn3, knw[0:n, :].unsqueeze(1).to_broadcast([n, 16, 64]), ALU.mult, [kn, knw], [kn])
        o = ko[ti % 2]
        cosb = c[0:n, 0, :].unsqueeze(1).to_broadcast([n, 16, 32])
        sinb = c[0:n, 1, :].unsqueeze(1).to_broadcast([n, 16, 32])
        rope(P, "dve", o[0:n, :].rearrange("p (h t d) -> p h t d", h=16, t=2), kn[0:n, :].rearrange("p (h t d) -> p h t d", h=16, t=2),
             cosb, sinb, tmp, n, [kn, c], [o])
        P.dma("pool", D["kout"][r0:r0 + n, :], o[0:n, :], reads=[o], writes=[D["kout_t"].sub(ti)])
        ki = p[0:n, C_KI - C_Q:C_KI - C_Q + 64]
        oi = kio[ti % 2]
        rope(P, "pool", oi[0:n, :].rearrange("p (h t d) -> p h t d", h=1, t=2), ki.rearrange("p (h t d) -> p h t d", h=1, t=2),
             c[0:n, 0, :].unsqueeze(1), c[0:n, 1, :].unsqueeze(1), tmp, n, [p, c], [oi])
        P.dma("pool", D["kidx"][r0:r0 + n, :], oi[0:n, :], reads=[oi], writes=[D["kidx_t"].sub(ti)])
    P.dma("pool", D["shift"][0:1, :], D["Ptok"][2047:2048, 0:RW_IN], reads=[D["Ptok_t"].sub(15)])
    for q in range(4):
        P.dma("pool", D["shift"][1 + q:2 + q, :], D["Ptok"][2048 + 16 * q + 15:2048 + 16 * q + 16, 0:RW_IN], reads=[D["Ptok_t"].sub(16)])
    P.barrier()
    st.close()


def red(P, eng, out, in_, op, reads, writes):
    return P.op(eng, lambda e: e.tensor_reduce(out=out, in_=in_, axis=AX.X, op=op), reads=reads, writes=writes)


def stt(P, eng, out, in0, scalar, in1, op0, op1, reads, writes):
    return P.op(eng, lambda e: e.scalar_tensor_tensor(out=out, in0=in0, scalar=scalar, in1=in1, op0=op0, op1=op1), reads=reads, writes=writes)


def recip(P, out, in_, reads, writes):
    return P.op("dve", lambda e: e.reciprocal(out, in_), reads=reads, writes=writes)


def mset(P, eng, ap, val, writes):
    return P.op(eng, lambda e: e.memset(ap, val), writes=writes)

import os
STOP = 99
SKIP = ''
NUX = 18
NU = 18
UC = NU * 128
GN_EPS = 64e-5


def unit_rows(u):
    if u < 16:
        return [(0, 128 * u, 128)]
    b = 2048 + 32 * (u - 16)
    return [(0, b, 16), (64, b + 16, 16)]


def bload(P, tile, ap, n=128):
    P.dma("sp", tile[0:n, :], ap.partition_broadcast(n), writes=[tile])


def stage3_rw(P, g, D):
    st = ExitStack()
    sb = lambda name, shape, dt=F32: P.sb(name, shape, dt, st)
    mub = sb("mub", [128, RW_IN]); bload(P, mub, D["mu_rw"][0:1, :])
    w0b = sb("w0b", [128, 1024]); bload(P, w0b, D["w0"][0:1, :])
    a0b = sb("a0b", [128, 1024]); bload(P, a0b, D["a0"][0:1, :])
    kkb = sb("kkb", [128, 1024]); bload(P, kkb, D["k_k"][0:1, :])
    kab = sb("kab", [128, 1024]); bload(P, kab, D["k_a"][0:1, :])
    rkb = sb("rkb", [128, 1024]); bload(P, rkb, D["r_k"][0:1, :])
    loraW = sb("loraW", [128, 1024])
    P.dma("sp", loraW[0:64, :], D["w_up"][:, :], writes=[loraW])
    P.dma("sp", loraW[64:128, :], D["a_up"][:, :], writes=[loraW])
    gup1 = sb("gup1", [128, 1024]); P.dma("sp", gup1[:], D["g_up"][0:128, :], writes=[gup1])
    gup2 = sb("gup2", [32, 1024]); P.dma("sp", gup2[:], D["g_up"][128:160, :], writes=[gup2])
    tri = sb("tri", [128, 128]); P.dma("sp", tri[:], D["m_tri"][:, :], writes=[tri])
    ones = sb("onesb", [128, 128]); P.dma("sp", ones[:], D["m_ones"][:, :], writes=[ones])
    valid = sb("valid", [128, 2]); P.dma("sp", valid[:], D["m_valid"][:, :], writes=[valid])
    tiny = sb("tiny12", [128, 1]); mset(P, "dve", tiny[:], 1e-12, [tiny])
    Pc = sb("Pc", [128, RW_IN]); Pp = sb("Pp", [128, RW_IN])
    L = sb("L288", [128, 288]); LT = sb("LT", [128, 3, 128])
    W = {nm: sb("w_" + nm, [128, 1024]) for nm in ["zt", "za", "gt", "lw", "ah", "kk", "junk", "kmod", "b", "t2", "eL", "eN", "eLm", "eC", "gC", "Lsb", "rt", "at", "kt", "bt", "kp", "bp"]}
    ssq = sb("ssq", [128, 16]); rn = sb("rn", [128, 16]); bc = sb("bc", [128, 16])
    stgT = [sb(f"stgT{i}", [128, 8, 128]) for i in range(2)]
    pLT = P.ps("pLT", [128, 3, 128], F32, st)
    pw = [P.ps(f"pw{i}", [128, 512], F32, st) for i in range(4)]
    pT = [P.ps(f"pTr{i}", [128, 4, 128], F32, st) for i in range(2)]
    pk = 0
    tk = 0
    for u in range(NUX):
        samp = u >= 16
        if samp:
            mset(P, "pool", Pc[:], 0.0, [Pc])
            mset(P, "pool", Pp[:], 0.0, [Pp])
        for (d0, t0, nt) in unit_rows(u):
            ti = 16 if samp else u
            P.dma("sp", Pc[d0:d0 + nt, :], D["Ptok"][t0:t0 + nt, 0:RW_IN], reads=[D["Ptok_t"].sub(ti)], writes=[Pc])
            if samp:
                q = (t0 - 2048) // 16
                P.dma("sp", Pp[d0:d0 + 1, :], D["sshift"][q:q + 1, :], writes=[Pp])
                P.dma("sp", Pp[d0 + 1:d0 + 16, :], D["Ptok"][t0:t0 + 15, 0:RW_IN], reads=[D["Ptok_t"].sub(16)], writes=[Pp])
            elif u == 0:
                mset(P, "pool", Pp[0:1, :], 0.0, [Pp])
                P.dma("sp", Pp[1:128, :], D["Ptok"][0:127, 0:RW_IN], reads=[D["Ptok_t"].sub(0)], writes=[Pp])
            else:
                P.dma("sp", Pp[:, :], D["Ptok"][t0 - 1:t0 + 127, 0:RW_IN], reads=[D["Ptok_t"].sub(u), D["Ptok_t"].sub(u - 1)], writes=[Pp])
        tt(P, "dve", Pp[:], Pp[:], Pc[:], ALU.subtract, [Pp, Pc], [Pp])
        tt(P, "pool", Pp[:], Pp[:], mub[:], ALU.mult, [Pp, mub], [Pp])
        tt(P, "dve", Pc[:], Pc[:], Pp[:], ALU.add, [Pp, Pc], [Pc])
        if STOP == 1:
            continue
        r, k, v = Pc[:, 0:1024], Pc[:, 1024:2048], Pc[:, 2048:3072]
        act(P, L[:, 0:64], Pc[:, 3072:3136], AF.Tanh, [Pc], [L])
        cp(P, "pool", L[:, 64:128], Pc[:, 3136:3200], [Pc], [L])
        act(P, L[:, 128:288], Pc[:, 3200:3360], AF.Sigmoid, [Pc], [L])
        tr(P, pLT[:, 0, :], L[:, 0:128], g.identf[:], [L, g.identf], [pLT])
        tr(P, pLT[:, 1, :], L[:, 128:256], g.identf[:], [L, g.identf], [pLT])
        tr(P, pLT[0:32, 2, :], L[:, 256:288], g.identf[:], [L, g.identf], [pLT])
        cp(P, "dve", LT[:, 0:2, :], pLT[:, 0:2, :], [pLT], [LT])
        cp(P, "dve", LT[0:32, 2, :], pLT[0:32, 2, :], [pLT], [LT])
        if STOP == 2:
            continue
        for hf in range(2):
            cs = slice(hf * 512, (hf + 1) * 512)
            p = pw[pk % 4]; pk += 1
            mm(P, p[:], LT[0:64, 0, :], loraW[0:64, cs], True, True, [LT, loraW], [p])
            tt(P, "dve", W["zt"][:, cs], p[:], w0b[:, cs], ALU.add, [p, w0b], [W["zt"]])
            p = pw[pk % 4]; pk += 1
            mm(P, p[:], LT[64:128, 0, :], loraW[64:128, cs], True, True, [LT, loraW], [p])
            tt(P, "dve", W["za"][:, cs], p[:], a0b[:, cs], ALU.add, [p, a0b], [W["za"]])
            p = pw[pk % 4]; pk += 1
            mm(P, p[:], LT[:, 1, :], gup1[:, cs], True, False, [LT, gup1], [p])
            mm(P, p[:], LT[0:32, 2, :], gup2[0:32, cs], False, True, [LT, gup2], [p])
            cp(P, "act", W["gt"][:, cs], p[:], [p], [W["gt"]])
        if STOP == 3:
            continue
        act(P, W["lw"][:], W["zt"][:], AF.Sigmoid, [W["zt"]], [W["lw"]])
        ts(P, "dve", W["lw"][:], W["lw"][:], -0.6065306597126334, valid[:, (1 if samp else 0):(2 if samp else 1)], ALU.mult, ALU.mult, [W["lw"], valid], [W["lw"]])
        if STOP == 31:
            continue
        act(P, W["ah"][:], W["za"][:], AF.Sigmoid, [W["za"]], [W["ah"]])
        if STOP == 32:
            continue
        tt(P, "pool", W["kk"][:], k, kkb[:], ALU.mult, [Pc, kkb], [W["kk"]])
        act(P, W["junk"][:], W["kk"][:], AF.Square, [W["kk"]], [W["junk"]])
        if STOP == 33:
            continue
        red(P, "dve", ssq[:], W["junk"][:].rearrange("p (h d) -> p h d", h=16), ALU.add, [W["junk"]], [ssq])
        act(P, ssq[:], ssq[:], AF.Sqrt, [ssq, tiny], [ssq], bias=tiny[:])
        recip(P, rn[:], ssq[:], [ssq], [rn])
        if STOP == 34:
            continue
        kk3 = W["kk"][:].rearrange("p (h d) -> p h d", h=16)
        tt(P, "dve", kk3, kk3, rn[:].unsqueeze(2).to_broadcast([128, 16, 64]), ALU.mult, [W["kk"], rn], [W["kk"]])
        if STOP == 35:
            continue
        stt(P, "dve", W["t2"][:], W["ah"][:], -1.0, kab[:], ALU.add, ALU.mult, [W["ah"], kab], [W["t2"]])
        stt(P, "dve", W["kmod"][:], W["t2"][:], 1.0, k, ALU.add, ALU.mult, [W["t2"], Pc], [W["kmod"]])
        if STOP == 36:
            continue
        tt(P, "pool", W["b"][:], W["kk"][:], W["ah"][:], ALU.mult, [W["kk"], W["ah"]], [W["b"]])
        tt(P, "pool", W["t2"][:], r, W["kmod"][:], ALU.mult, [Pc, W["kmod"]], [W["t2"]])
        tt(P, "pool", W["t2"][:], W["t2"][:], rkb[:], ALU.mult, [W["t2"], rkb], [W["t2"]])
        if STOP == 37:
            continue
        red(P, "dve", bc[:], W["t2"][:].rearrange("p (h d) -> p h d", h=16), ALU.add, [W["t2"]], [bc])
        P.dma("pool", D["BC"][u * 128:(u + 1) * 128, :], bc[:], reads=[bc], writes=[D["BC_t"].sub(u)])
        if STOP == 4:
            continue
        for hf in range(2):
            cs = slice(hf * 512, (hf + 1) * 512)
            pL = pw[pk % 4]; pk += 1
            pLt = pw[pk % 4]; pk += 1
            mm(P, pL[:], tri[:], W["lw"][:, cs], True, True, [tri, W["lw"]], [pL])
            mm(P, pLt[:], ones[:], W["lw"][:, cs], True, True, [ones, W["lw"]], [pLt])
            if "a1" not in SKIP:
                act(P, W["eL"][:, cs], pL[:], AF.Exp, [pL], [W["eL"]])
            if "a2" not in SKIP:
                act(P, W["eN"][:, cs], pL[:], AF.Exp, [pL], [W["eN"]], scale=-1.0)
            act(P, W["Lsb"][:, cs], pL[:], AF.Identity, [pL], [W["Lsb"]])
            if "a3" not in SKIP:
                act(P, W["gC"][:, cs], pLt[:], AF.Exp, [pLt], [W["gC"]])
            if "d1" not in SKIP:
                tt(P, "dve", W["eC"][:, cs], pLt[:], W["Lsb"][:, cs], ALU.subtract, [pLt, W["Lsb"]], [W["eC"]])
            if "d2" not in SKIP:
                tt(P, "dve", W["eLm"][:, cs], W["Lsb"][:, cs], W["lw"][:, cs], ALU.subtract, [W["Lsb"], W["lw"]], [W["eLm"]])
        if "exp2" not in SKIP:
            act(P, W["eC"][:], W["eC"][:], AF.Exp, [W["eC"]], [W["eC"]])
            act(P, W["eLm"][:], W["eLm"][:], AF.Exp, [W["eLm"]], [W["eLm"]])
        if STOP == 5:
            continue
        tt(P, "dve", W["rt"][:], r, W["eL"][:], ALU.mult, [Pc, W["eL"]], [W["rt"]])
        stt(P, "dve", W["at"][:], W["kk"][:], -1.0, W["eLm"][:], ALU.mult, ALU.mult, [W["kk"], W["eLm"]], [W["at"]])
        tt(P, "pool", W["kt"][:], W["kmod"][:], W["eN"][:], ALU.mult, [W["kmod"], W["eN"]], [W["kt"]])
        tt(P, "pool", W["bt"][:], W["b"][:], W["eN"][:], ALU.mult, [W["b"], W["eN"]], [W["bt"]])
        tt(P, "pool", W["kp"][:], W["kmod"][:], W["eC"][:], ALU.mult, [W["kmod"], W["eC"]], [W["kp"]])
        tt(P, "dve", W["bp"][:], W["b"][:], W["eC"][:], ALU.mult, [W["b"], W["eC"]], [W["bp"]])
        rows = slice(u * 128, (u + 1) * 128)
        P.dma("pool", D["Vs"][rows, :], v, reads=[Pc], writes=[D["Vs_t"].sub(u)])
        P.dma("pool", D["KPs"][rows, :], W["kp"][:], reads=[W["kp"]], writes=[D["KPs_t"].sub(u)])
        P.dma("pool", D["BPs"][rows, :], W["bp"][:], reads=[W["bp"]], writes=[D["BPs_t"].sub(u)])
        P.dma("pool", D["Gs"][rows, :], W["gt"][:], reads=[W["gt"]], writes=[D["Gs_t"].sub(u)])
        if STOP == 6:
            continue
        for nm, dst in [("rt", "RT"), ("at", "AT"), ("kt", "KT"), ("bt", "BT"), ("gC", "GCT")]:
            s = stgT[tk % 2]
            for half in range(2):
                p = pT[tk % 2]
                for j in range(4):
                    fc = half * 4 + j
                    tr(P, p[:, j, :], W[nm][:, fc * 128:(fc + 1) * 128], g.identf[:], [W[nm], g.identf], [p])
                cp(P, "act" if half else "dve", s[:, half * 4:(half + 1) * 4, :], p[:], [p], [s])
                tk += 1
            P.dma("pool", D[dst].rearrange("(fc p) c -> p fc c", p=128)[:, :, u * 128:(u + 1) * 128], s[:], reads=[s], writes=[D[dst + "_t"].sub(u)])
    P.barrier()
    st.close()


def stage4_scan(P, g, D):
    st = ExitStack()
    sb = lambda name, shape, dt=F32: P.sb(name, shape, dt, st)
    msl = sb("msl", [128, 128]); P.dma("sp", msl[:], D["m_sl"][:, :], writes=[msl])
    msu = sb("msu", [128, 128]); P.dma("sp", msu[:], D["m_su"][:, :], writes=[msu])
    mu = sb("mu", [128, 128]); P.dma("sp", mu[:], D["m_u"][:, :], writes=[mu])
    lnw = sb("lnw", [128, 1024]); bload(P, lnw, D["lnx_w"][0:1, :])
    lnb = sb("lnb", [128, 1024]); bload(P, lnb, D["lnx_b"][0:1, :])
    gne = sb("gne", [128, 1]); mset(P, "dve", gne[:], GN_EPS, [gne])
    H = sb("H", [128, 8, 64])
    mset(P, "dve", H[:], 0.0, [H.sub((a, b)) for a in range(8) for b in range(2)])
    FM = {nm: [sb(f"fm_{nm}{i}", [128, 8, 128]) for i in range(2)] for nm in ["RT", "AT", "KT", "BT", "GCT"]}
    TM = {nm: [sb(f"tm_{nm}{i}", [128, 1024]) for i in range(2)] for nm in ["Vs", "KPs", "BPs"]}
    Y = [sb(f"Y{i}", [128, 1024]) for i in range(2)]
    NM = 16
    GS = 6
    MS = [[sb(f"ms{s}_{i}", [128, 128]) for i in range(NM)] for s in range(GS)]
    XU = [[sb(f"xu{s}_{i}", [128, 64]) for i in range(4)] for s in range(GS)]
    pLane = [P.ps(f"pLane{i}", [128, 512], F32, st) for i in range(GS)]
    laneM = [[SlotAP(pLane[l], pLane[l][:, j * 128:(j + 1) * 128]) for j in range(2)] for l in range(GS)]
    laneS = [[SlotAP(pLane[l], pLane[l][:, 256 + j * 64:256 + (j + 1) * 64]) for j in range(4)] for l in range(GS)]
    gt = sb("p_gt", [128, 1024]); bc = sb("p_bc", [128, 16]); vv = None
    pw = {nm: sb("p_" + nm, [128, 1024]) for nm in ["yc", "sq", "yb"]}
    st16 = {nm: sb("p16_" + nm, [128, 16]) for nm in ["mean", "var", "rstd"]}
    oab = sb("oab", [128, 1024], BF16)
    pO = [P.ps(f"pO{i}", [128, 8, 128], BF16, st) for i in range(1)]
    oT = sb("oT", [128, 8, 128], BF16)
    Ssb = [sb(f"Ssb{i}", [128, 64]) for i in range(GS)]
    Sout = sb("Sout", [128, 8, 64])

    def head_gen(u, fc, hp, lane, FMu, TMu, y):
        samp = u >= 16
        RT, AT, KT, BT, GC = FMu
        V, KP, BP = TMu
        mk = [0]; sk = [0]

        def nextM():
            mk[0] += 1
            return laneM[lane][mk[0] % 2]

        def nextS():
            sk[0] += 1
            return laneS[lane][sk[0] % 4]

        h = 2 * fc + hp
        pb = hp * 64
        hs = slice(h * 64, (h + 1) * 64)
        M = MS[lane]
        a_ = AT[pb:pb + 64, fc, :]; b_ = BT[pb:pb + 64, fc, :]; k_ = KT[pb:pb + 64, fc, :]; r_ = RT[pb:pb + 64, fc, :]
        A, N, AakT, RBt, RKt = M[0], M[1], M[2], M[3], M[4]
        for (dst, l, r, msk) in [(A, a_, b_, msl), (N, b_, a_, msu), (AakT, k_, a_, msu), (RBt, b_, r_, mu), (RKt, k_, r_, mu)]:
            p = nextM()
            mm(P, p[:], l, r, True, True, [AT, BT, KT, RT], [p])
            tt(P, "dve", dst[:], p[:], msk[:], ALU.mult, [p, msk], [dst])
            yield
        Ap = [A, M[5], M[6], M[7], M[8]]
        Np = [N, M[9], M[10], M[11], M[12], M[13]]
        for k in range(5):
            if k < 4:
                p = nextM()
                mm(P, p[:], Np[k][:], Ap[k][:], True, True, [Np[k], Ap[k]], [p])
                cp(P, "act", Ap[k + 1][:], p[:], [p], [Ap[k + 1]])
            p = nextM()
            mm(P, p[:], Ap[k][:], Np[k][:], True, True, [Np[k], Ap[k]], [p])
            cp(P, "dve" if k % 2 else "act", Np[k + 1][:], p[:], [p], [Np[k + 1]])
            yield
        Tt = [M[14], M[15]]
        tt(P, "dve", Tt[0][:], Np[5][:], g.identf[:], ALU.add, [Np[5], g.identf], [Tt[0]])
        cur = 0
        for k in [4, 3, 2, 1, 0]:
            p = nextM()
            mm(P, p[:], Ap[k][:], Tt[cur][:], True, True, [Ap[k], Tt[cur]], [p])
            tt(P, "dve", Tt[1 - cur][:], p[:], Tt[cur][:], ALU.add, [p, Tt[cur]], [Tt[1 - cur]])
            cur = 1 - cur
            yield
        TT_ = Tt[cur]
        for c in range(2):
            pc = c * 64
            cc = slice(c * 64, (c + 1) * 64)
            Hh = H[pb:pb + 64, fc, :]
            Hd = H.sub((fc, hp))
            if samp:
                q = 2 * (u - 16) + c
                P.dma("sp", Ssb[lane][pb:pb + 64, :], D["swkv"][q, h, :, :], writes=[Ssb[lane]])
                p = nextS()
                mm(P, p[pb:pb + 64, :], Ssb[lane][pb:pb + 64, :], g.identf[pb:pb + 64, pb:pb + 64], True, True, [Ssb[lane], g.identf], [p])
                cp(P, "act", Hh, p[pb:pb + 64, :], [p], [Hd])
                yield
            X_sb, U_sb = XU[lane][2 * c], XU[lane][2 * c + 1]
            p = nextS()
            mm(P, p[pc:pc + 64, :], a_[:, cc], Hh, True, False, [AT, Hd], [p])
            mm(P, p[pc:pc + 64, :], AakT[pc:pc + 64, cc], V[pc:pc + 64, hs], False, True, [AakT, V], [p])
            cp(P, "act", X_sb[pc:pc + 64, :], p[pc:pc + 64, :], [p], [X_sb])
            yield
            p = nextS()
            mm(P, p[pc:pc + 64, :], TT_[pc:pc + 64, cc], X_sb[pc:pc + 64, :], True, True, [TT_, X_sb], [p])
            cp(P, "act", U_sb[pc:pc + 64, :], p[pc:pc + 64, :], [p], [U_sb])
            yield
            p = nextS()
            mm(P, p[pc:pc + 64, :], r_[:, cc], Hh, True, False, [RT, Hd], [p])
            mm(P, p[pc:pc + 64, :], RBt[pc:pc + 64, cc], U_sb[pc:pc + 64, :], False, False, [RBt, U_sb], [p])
            mm(P, p[pc:pc + 64, :], RKt[pc:pc + 64, cc], V[pc:pc + 64, hs], False, True, [RKt, V], [p])
            cp(P, "act", y[pc:pc + 64, hs], p[pc:pc + 64, :], [p], [y.sub(h)])
            p = nextS()
            mm(P, p[pb:pb + 64, :], BP[pc:pc + 64, hs], U_sb[pc:pc + 64, :], True, False, [BP, U_sb], [p])
            mm(P, p[pb:pb + 64, :], KP[pc:pc + 64, hs], V[pc:pc + 64, hs], False, True, [KP, V], [p])
            stt(P, "dve", Hh, Hh, GC[pb:pb + 64, fc, c * 64:c * 64 + 1], p[pb:pb + 64, :], ALU.mult, ALU.add, [Hd, GC, p], [Hd])
            yield
            if samp or (u == 15 and c == 1):
                q = (1 + 2 * (u - 16) + c) if samp else 0
                p = nextS()
                mm(P, p[pb:pb + 64, :], Hh, g.identf[pb:pb + 64, pb:pb + 64], True, True, [Hd, g.identf], [p])
                cp(P, "act", Sout[pb:pb + 64, fc, :], p[pb:pb + 64, :], [p], [Sout.sub(h)])
                r0 = (q * 16 + h) * 64
                P.dma("pool", D["wkv"][r0:r0 + 64, :], Sout[pb:pb + 64, fc, :], reads=[Sout.sub(h)])
                yield

    for u in range(NU):
        samp = u >= 16
        b2 = u % 2
        cols = slice(u * 128, (u + 1) * 128)
        for nm in FM:
            P.dma("sp", FM[nm][b2][:], D[nm].rearrange("(fc p) c -> p fc c", p=128)[:, :, cols], reads=[D[nm + "_t"].sub(u)], writes=[FM[nm][b2]])
        for nm in TM:
            P.dma("sp", TM[nm][b2][:], D[nm][cols, :], reads=[D[nm + "_t"].sub(u)], writes=[TM[nm][b2]])
        FMu = [FM[nm][b2] for nm in ["RT", "AT", "KT", "BT", "GCT"]]
        TMu = [TM[nm][b2] for nm in ["Vs", "KPs", "BPs"]]
        V = TMu[0]
        y = Y[b2]
        heads = [(fc, hp) for fc in range(8) for hp in range(2)]
        active = []
        nxt = 0
        for lane in range(GS):
            fc, hp = heads[nxt]; nxt += 1
            active.append((lane, head_gen(u, fc, hp, lane, FMu, TMu, y)))
        while active:
            still = []
            for lane, gen in active:
                try:
                    next(gen)
                    still.append((lane, gen))
                except StopIteration:
                    if nxt < len(heads):
                        fc, hp = heads[nxt]; nxt += 1
                        still.append((lane, head_gen(u, fc, hp, lane, FMu, TMu, y)))
            active = still
        ysubs = [y.sub(h) for h in range(16)]
        P.dma("sp", gt[:], D["Gs"][cols, :], reads=[D["Gs_t"].sub(u)], writes=[gt])
        P.dma("sp", bc[:], D["BC"][cols, :], reads=[D["BC_t"].sub(u)], writes=[bc])
        y3 = y[:].rearrange("p (h d) -> p h d", h=16)
        bcast = lambda t16: t16[:].unsqueeze(2).to_broadcast([128, 16, 64])
        v3 = lambda t: t[:].rearrange("p (h d) -> p h d", h=16)
        red(P, "dve", st16["mean"][:], y3, ALU.add, ysubs, [st16["mean"]])
        ts(P, "dve", st16["mean"][:], st16["mean"][:], 1.0 / 64, None, ALU.mult, None, [st16["mean"]], [st16["mean"]])
        tt(P, "dve", v3(pw["yc"]), y3, bcast(st16["mean"]), ALU.subtract, ysubs + [st16["mean"]], [pw["yc"]])
        act(P, pw["sq"][:], pw["yc"][:], AF.Square, [pw["yc"]], [pw["sq"]])
        red(P, "dve", st16["var"][:], v3(pw["sq"]), ALU.add, [pw["sq"]], [st16["var"]])
        act(P, st16["var"][:], st16["var"][:], AF.Sqrt, [st16["var"], gne], [st16["var"]], scale=1.0 / 64, bias=gne[:])
        recip(P, st16["rstd"][:], st16["var"][:], [st16["var"]], [st16["rstd"]])
        tt(P, "dve", v3(pw["yc"]), v3(pw["yc"]), bcast(st16["rstd"]), ALU.mult, [pw["yc"], st16["rstd"]], [pw["yc"]])
        tt(P, "pool", pw["yc"][:], pw["yc"][:], lnw[:], ALU.mult, [pw["yc"], lnw], [pw["yc"]])
        tt(P, "pool", pw["yc"][:], pw["yc"][:], lnb[:], ALU.add, [pw["yc"], lnb], [pw["yc"]])
        tt(P, "dve", v3(pw["yb"]), v3(V), bcast(bc), ALU.mult, [V, bc], [pw["yb"]])
        tt(P, "pool", pw["yc"][:], pw["yc"][:], pw["yb"][:], ALU.add, [pw["yc"], pw["yb"]], [pw["yc"]])
        tt(P, "dve", oab[:], pw["yc"][:], gt[:], ALU.mult, [pw["yc"], gt], [oab])
        if D.get("OAdbg") is not None:
            P.dma("pool", D["OAdbg"][cols, :], pw["yc"][:], reads=[pw["yc"]])
        for fc in range(8):
            tr(P, pO[0][:, fc, :], oab[:, fc * 128:(fc + 1) * 128], g.identb[:], [oab, g.identb], [pO[0]])
        cp(P, "act", oT[:], pO[0][:], [pO[0]], [oT])
        dstv = D["OAT"].rearrange("(fc p) c -> p fc c", p=128)
        for (d0, t0, nt) in unit_rows(u):
            P.dma("pool", dstv[:, :, t0:t0 + nt], oT[:, :, d0:d0 + nt], reads=[oT], writes=[D["OAT_t"]])
    P.barrier()
    st.close()

TOPK = 256
NEG = -1.0e30
NBIS = 18


def headnorm(P, src, dst, nw, junk, ssq, rst, g, n, reads, pre=1.0):
    act(P, junk[0:n, :], src, AF.Square, reads, [junk])
    red(P, "dve", ssq[0:n, :], junk[0:n, :].rearrange("p (h d) -> p h d", h=16), ALU.add, [junk], [ssq])
    act(P, ssq[0:n, :], ssq[0:n, :], AF.Sqrt, [ssq], [ssq], scale=1.0 / 64, bias=g.epsc[0:n, :])
    recip(P, rst[0:n, :], ssq[0:n, :], [ssq], [rst])
    d3 = dst.rearrange("p (h d) -> p h d", h=16)
    tt(P, "dve", d3, src.rearrange("p (h d) -> p h d", h=16), rst[0:n, :].unsqueeze(2).to_broadcast([n, 16, 64]), ALU.mult, list(reads) + [rst], [junk])
    stt(P, "dve", d3, d3, pre, nw[0:n, :].unsqueeze(1).to_broadcast([n, 16, 64]), ALU.mult, ALU.mult, [junk, nw], [junk])


def stage3_dsa(P, g, D):
    st = ExitStack()
    sb = lambda name, shape, dt=F32: P.sb(name, shape, dt, st)
    knw = sb("knw", [128, 64]); qnw = sb("qnw", [128, 64])
    P.dma("sp", knw[:], D["k_norm_w"][0:1, :].partition_broadcast(128), writes=[knw])
    P.dma("sp", qnw[:], D["q_norm_w"][0:1, :].partition_broadcast(128), writes=[qnw])
    pd = [sb(f"pd{i}", [128, 3656]) for i in range(2)]
    cs = [sb(f"cs{i}", [128, 2, 32]) for i in range(2)]
    junk = sb("djunk", [128, 1024]); ssq = sb("dssq", [128, 16]); rst = sb("drst", [128, 16])
    nrm = sb("dnrm", [128, 1024])
    ko = [sb(f"dko{i}", [128, 1024]) for i in range(2)]
    qo = sb("dqo", [128, 1024]); qio = sb("dqio", [128, 512])
    kio = [sb(f"dkio{i}", [128, 64]) for i in range(2)]
    wio = [sb(f"dwio{i}", [128, 8]) for i in range(2)]
    tmp = sb("dtmp", [128, 4, 512])
    cat = sb("dcat", [128, 2624], BF16)
    pT = [P.ps(f"dpT{i}", [128, 8, 128], BF16, st) for i in range(2)]
    sT = [sb(f"dsT{i}", [128, 8, 128], BF16) for i in range(2)]
    tk = 0
    for ti, (r0, n) in enumerate(TILES):
        p, c = pd[ti % 2], cs[ti % 2]
        P.dma("sp", p[0:n, :], D["Ptok"][r0:r0 + n, C_Q:TOKW], reads=[D["Ptok_t"].sub(ti)], writes=[p])
        P.dma("sp", c[0:n, 0, :], D["cos"][r0:r0 + n, :], writes=[c])
        P.dma("sp", c[0:n, 1, :], D["sin"][r0:r0 + n, :], writes=[c])
        P.dma("pool", D["vout"][r0:r0 + n, :], D["Ptok"][r0:r0 + n, C_V:C_V + 1024], reads=[D["Ptok_t"].sub(ti)])
        cosb = c[0:n, 0, :].unsqueeze(1).to_broadcast([n, 16, 32])
        sinb = c[0:n, 1, :].unsqueeze(1).to_broadcast([n, 16, 32])
        r4 = lambda ap, H: ap.rearrange("p (h t d) -> p h t d", h=H, t=2)
        headnorm(P, p[0:n, C_K - C_Q:C_K - C_Q + 1024], junk[0:n, :], knw, junk, ssq, rst, g, n, [p])
        o = ko[ti % 2]
        rope(P, "dve", r4(o[0:n, :], 16), r4(junk[0:n, :], 16), cosb, sinb, tmp, n, [junk, c], [o])
        P.dma("pool", D["kout"][r0:r0 + n, :], o[0:n, :], reads=[o], writes=[D["kout_t"].sub(ti)])
        cp(P, "pool", cat[0:n, 1024:2048], o[0:n, :], [o], [cat])
        headnorm(P, p[0:n, 0:1024], junk[0:n, :], qnw, junk, ssq, rst, g, n, [p], pre=0.125)
        rope(P, "dve", r4(qo[0:n, :], 16), r4(junk[0:n, :], 16), cosb, sinb, tmp, n, [junk, c], [qo])
        cp(P, "pool", cat[0:n, 0:1024], qo[0:n, :], [qo], [cat])
        rope(P, "pool", r4(qio[0:n, :], 8), r4(p[0:n, C_QI - C_Q:C_QI - C_Q + 512], 8), c[0:n, 0, :].unsqueeze(1).to_broadcast([n, 8, 32]),
             c[0:n, 1, :].unsqueeze(1).to_broadcast([n, 8, 32]), tmp, n, [p, c], [qio])
        cp(P, "pool", cat[0:n, 2048:2560], qio[0:n, :], [qio], [cat])
        oi = kio[ti % 2]
        rope(P, "pool", r4(oi[0:n, :], 1), r4(p[0:n, C_KI - C_Q:C_KI - C_Q + 64], 1), c[0:n, 0, :].unsqueeze(1), c[0:n, 1, :].unsqueeze(1), tmp, n, [p, c], [oi])
        P.dma("pool", D["kidx"][r0:r0 + n, :], oi[0:n, :], reads=[oi], writes=[D["kidx_t"].sub(ti)])
        cp(P, "pool", cat[0:n, 2560:2624], oi[0:n, :], [oi], [cat])
        w = wio[ti % 2]
        ts(P, "dve", w[0:n, :], p[0:n, C_WI - C_Q:C_WI - C_Q + 8], 512.0 ** -0.5, None, ALU.mult, None, [p], [w])
        P.dma("pool", D["WI"][r0:r0 + n, :], w[0:n, :], reads=[w], writes=[D["WI_t"].sub(ti)])
        for (c0, nch, dst) in [(0, 8, "QT"), (1024, 8, "KTn"), (2048, 4, "QIT")]:
            pt, s_ = pT[tk % 2], sT[tk % 2]; tk += 1
            for j in range(nch):
                tr(P, pt[:, j, 0:n], cat[0:n, c0 + j * 128:c0 + (j + 1) * 128], g.identb[0:n, 0:n], [cat, g.identb], [pt])
            cp(P, "act", s_[:, 0:nch, 0:n], pt[:, 0:nch, 0:n], [pt], [s_])
            P.dma("pool", D[dst].rearrange("(fc p) c -> p fc c", p=128)[:, :, r0:r0 + n], s_[:, 0:nch, 0:n], reads=[s_], writes=[D[dst + "_t"].sub(ti)])
        pt, s_ = pT[tk % 2], sT[tk % 2]; tk += 1
        tr(P, pt[0:64, 0, 0:n], cat[0:n, 2560:2624], g.identb[0:n, 0:n], [cat, g.identb], [pt])
        cp(P, "act", s_[0:64, 0, 0:n], pt[0:64, 0, 0:n], [pt], [s_])
        P.dma("pool", D["KIT"][:, r0:r0 + n], s_[0:64, 0, 0:n], reads=[s_], writes=[D["KIT_t"].sub(ti)])
    P.dma("pool", D["shift"][0:1, :], D["Ptok"][2047:2048, 0:RW_IN], reads=[D["Ptok_t"].sub(15)])
    for q in range(4):
        P.dma("pool", D["shift"][1 + q:2 + q, :], D["Ptok"][2048 + 16 * q + 15:2048 + 16 * q + 16, 0:RW_IN], reads=[D["Ptok_t"].sub(16)])
    P.barrier()
    st.close()


def stage5_attn(P, g, D):
    st = ExitStack()
    sb = lambda name, shape, dt=F32: P.sb(name, shape, dt, st)
    kT = sb("kT", [128, 8, 2064], BF16)
    Vb = sb("Vb", [128, 17, 16, 65], BF16)
    kiT = sb("kiT", [128, 2064], BF16)
    Ibuf = sb("Ibuf", [128, 2064]); junkI = sb("junkI", [128, 2064], BF16)
    maskb = sb("maskb", [128, 2064], BF16); maskT = sb("maskT", [128, 17, 128], BF16)
    qT = [sb(f"qT{i}", [128, 8, 128], BF16) for i in range(2)]
    qiT = [sb(f"qiT{i}", [128, 4, 128], BF16) for i in range(2)]
    wi = [sb(f"wi{i}", [128, 8]) for i in range(2)]
    rl = [sb(f"rl{i}", [128, 512]) for i in range(2)]
    E = [sb(f"E{i}", [128, 4, 128], BF16) for i in range(3)]
    p2 = sb("pow2", [128, NBIS]); P.dma("sp", p2[:], D["m_pow2"][:, :], writes=[p2])
    dtab = sb("dtab", [128, NBIS])
    s1 = {nm: sb("s1_" + nm, [128, 1]) for nm in ["B", "mid", "cnt", "t2", "t3", "thr"]}
    osb = sb("osb", [128, 1024]); osbb = sb("osbb", [128, 1024], BF16); rec = sb("orec", [128, 4])
    oT = sb("oTb", [128, 8, 128], BF16)
    vst = [sb(f"vst{i}", [128, 1024]) for i in range(2)]
    kst = sb("kstb", [128, 1024], BF16); kis = sb("kis", [128, 64]); kisb = sb("kisb", [128, 128], BF16)
    pI = [P.ps(f"pI{i}", [128, 512], F32, st) for i in range(2)]
    pS = [P.ps(f"pSc{i}", [128, 4, 128], F32, st) for i in range(2)]
    pO = [P.ps(f"pOa{i}", [128, 4, 65], F32, st) for i in range(2)]
    pmT = P.ps("pmT", [128, 8, 128], BF16, st)
    mset(P, "pool", Vb[:, :, :, 64:65], 1.0, [Vb])
    cnt = {"rl": 0, "E": 0, "S": 0, "I": 0, "v": 0}

    def load_v_block(j, src_ap, nrow):
        v = vst[cnt["v"] % 2]; cnt["v"] += 1
        P.dma("sp", v[0:nrow, :], src_ap, writes=[v])
        cp(P, "pool", Vb[0:nrow, j, :, 0:64], v[0:nrow, :].rearrange("p (h d) -> p h d", h=16), [v], [Vb])

    def attend(nq, tcol, nblk, lastw, prompt_tile, obt_cols, par):
        S = (nblk - 1) * 128 + lastw
        q_, qi_, w_ = qT[par], qiT[par], wi[par]
        P.dma("sp", q_[:, :, 0:nq], D["QT"].rearrange("(fc p) c -> p fc c", p=128)[:, :, tcol:tcol + nq], reads=[D["QT_t"]], writes=[q_])
        P.dma("sp", qi_[:, :, 0:nq], D["QIT"].rearrange("(fc p) c -> p fc c", p=128)[:, :, tcol:tcol + nq], reads=[D["QIT_t"]], writes=[qi_])
        P.dma("sp", w_[0:nq, :], D["WI"][tcol:tcol + nq, :], reads=[D["WI_t"]], writes=[w_])
        for s0 in range(0, S, 512):
            w = min(512, S - s0)
            for h in range(8):
                pb = (h % 2) * 64
                p = pI[cnt["I"] % 2]; cnt["I"] += 1
                mm(P, p[0:nq, 0:w], qi_[pb:pb + 64, h // 2, 0:nq], kiT[pb:pb + 64, s0:s0 + w], True, True, [qi_, kiT], [p])
                r = rl[cnt["rl"] % 2]; cnt["rl"] += 1
                act(P, r[0:nq, 0:w], p[0:nq, 0:w], AF.Relu, [p], [r])
                if h == 0:
                    ts(P, "dve", Ibuf[0:nq, s0:s0 + w], r[0:nq, 0:w], w_[0:nq, 0:1], None, ALU.mult, None, [r, w_], [Ibuf])
                else:
                    stt(P, "dve", Ibuf[0:nq, s0:s0 + w], r[0:nq, 0:w], w_[0:nq, h:h + 1], Ibuf[0:nq, s0:s0 + w], ALU.mult, ALU.add, [r, w_, Ibuf], [Ibuf])
        P.op("dve", lambda e: e.tensor_reduce(out=s1["B"][0:nq, :], in_=Ibuf[0:nq, 0:S], axis=AX.X, op=ALU.max, apply_absolute_value=True), reads=[Ibuf], writes=[s1["B"]])
        ts(P, "dve", s1["B"][0:nq, :], s1["B"][0:nq, :], 1.001, 1e-6, ALU.mult, ALU.add, [s1["B"]], [s1["B"]])
        ts(P, "dve", dtab[0:nq, :], p2[0:nq, :], s1["B"][0:nq, :], None, ALU.mult, None, [p2, s1["B"]], [dtab])
        if prompt_tile:
            mset(P, "dve", Ibuf[0:64, S - 64:S], NEG, [Ibuf])
        mset(P, "dve", s1["mid"][0:nq, :], 0.0, [s1["mid"]])
        for k in range(NBIS):
            ts(P, "dve", junkI[0:nq, 0:S], Ibuf[0:nq, 0:S], s1["mid"][0:nq, :], None, ALU.is_ge, ALU.add, [Ibuf, s1["mid"]], [junkI, s1["cnt"]], accum_out=s1["cnt"][0:nq, :])
            ts(P, "dve", s1["t2"][0:nq, :], s1["cnt"][0:nq, :], TOPK - 0.5, 2.0, ALU.is_ge, ALU.mult, [s1["cnt"]], [s1["t2"]])
            ts(P, "dve", s1["t3"][0:nq, :], s1["t2"][0:nq, :], -1.0, dtab[0:nq, k:k + 1], ALU.add, ALU.mult, [s1["t2"], dtab], [s1["t3"]])
            tt(P, "dve", s1["mid"][0:nq, :], s1["mid"][0:nq, :], s1["t3"][0:nq, :], ALU.add, [s1["mid"], s1["t3"]], [s1["mid"]])
        tt(P, "dve", s1["thr"][0:nq, :], s1["mid"][0:nq, :], dtab[0:nq, NBIS - 1:NBIS], ALU.subtract, [s1["mid"], dtab], [s1["thr"]])
        ts(P, "dve", maskb[0:nq, 0:S], Ibuf[0:nq, 0:S], s1["thr"][0:nq, :], None, ALU.is_ge, None, [Ibuf, s1["thr"]], [maskb])
        for j0 in range(0, nblk, 8):
            nb_ = min(8, nblk - j0)
            for jj in range(nb_):
                j = j0 + jj
                wj = 128 if j < nblk - 1 else lastw
                tr(P, pmT[0:wj, jj, 0:nq], maskb[0:nq, j * 128:j * 128 + wj], g.identb[0:nq, 0:nq], [maskb, g.identb], [pmT])
            full = nb_ if (j0 + nb_ < nblk or lastw == 128) else nb_ - 1
            if full > 0:
                cp(P, "act", maskT[:, j0:j0 + full, 0:nq], pmT[:, 0:full, 0:nq], [pmT], [maskT])
            if full < nb_:
                cp(P, "act", maskT[0:lastw, j0 + full, 0:nq], pmT[0:lastw, full, 0:nq], [pmT], [maskT])
        items = [(h, j0) for h in range(16) for j0 in range(0, nblk, 4)]

        def qk(it):
            h, j0 = it
            pb = (h % 2) * 64
            p = pS[cnt["S"] % 2]; cnt["S"] += 1
            for jj in range(min(4, nblk - j0)):
                j = j0 + jj
                wj = 128 if j < nblk - 1 else lastw
                mm(P, p[0:wj, jj, 0:nq], kT[pb:pb + 64, h // 2, j * 128:j * 128 + wj], q_[pb:pb + 64, h // 2, 0:nq], True, True, [kT, q_], [p])
            return p

        pcur = qk(items[0])
        for k, it in enumerate(items):
            pnext = qk(items[k + 1]) if k + 1 < len(items) else None
            h, j0 = it
            nb_ = min(4, nblk - j0)
            e = E[cnt["E"] % 3]; cnt["E"] += 1
            full = nb_ if (j0 + nb_ < nblk or lastw == 128) else nb_ - 1
            if full > 0:
                act(P, e[:, 0:full, 0:nq], pcur[:, 0:full, 0:nq], AF.Exp, [pcur], [e])
                tt(P, "pool" if k % 3 == 2 else "dve", e[:, 0:full, 0:nq], e[:, 0:full, 0:nq], maskT[:, j0:j0 + full, 0:nq], ALU.mult, [e, maskT], [e])
            if full < nb_:
                act(P, e[0:lastw, full, 0:nq], pcur[0:lastw, full, 0:nq], AF.Exp, [pcur], [e])
                tt(P, "dve", e[0:lastw, full, 0:nq], e[0:lastw, full, 0:nq], maskT[0:lastw, j0 + full, 0:nq], ALU.mult, [e, maskT], [e])
            po = pO[(h // 4) % 2]
            for jj in range(nb_):
                j = j0 + jj
                wj = 128 if j < nblk - 1 else lastw
                mm(P, po[0:nq, h % 4, :], e[0:wj, jj, 0:nq], Vb[0:wj, j, h, :], j == 0, j == nblk - 1, [e, Vb], [po])
            if j0 + nb_ >= nblk and h % 4 == 3:
                recip(P, rec[0:nq, :], po[0:nq, :, 64], [po], [rec])
                tt(P, "dve", osb[0:nq, (h - 3) * 64:(h + 1) * 64].rearrange("p (h d) -> p h d", h=4), po[0:nq, :, 0:64],
                   rec[0:nq, :].unsqueeze(2).to_broadcast([nq, 4, 64]), ALU.mult, [po, rec], [osb])
            pcur = pnext
        if D.get("OBdbg") is not None:
            P.dma("pool", D["OBdbg"][obt_cols:obt_cols + nq, :], osb[0:nq, :], reads=[osb])
        cp(P, "pool", osbb[0:nq, :], osb[0:nq, :], [osb], [osbb])
        for fc in range(8):
            tr(P, pmT[:, fc, 0:nq], osbb[0:nq, fc * 128:(fc + 1) * 128], g.identb[0:nq, 0:nq], [osbb, g.identb], [pmT])
        cp(P, "act", oT[:, :, 0:nq], pmT[:, :, 0:nq], [pmT], [oT])
        P.dma("pool", D["OBT"].rearrange("(fc p) c -> p fc c", p=128)[:, :, obt_cols:obt_cols + nq], oT[:, :, 0:nq], reads=[oT], writes=[D["OBT_t"]])

    P.dma("sp", kT[:, :, 0:2048], D["KTn"].rearrange("(fc p) c -> p fc c", p=128)[:, :, 0:2048], reads=[D["KTn_t"]], writes=[kT])
    P.dma("sp", kiT[0:64, 0:2048], D["KIT"][:, 0:2048], reads=[D["KIT_t"]], writes=[kiT])
    P.dma("sp", kiT[64:128, 0:2048], D["KIT"][:, 0:2048], reads=[D["KIT_t"]], writes=[kiT])
    for j in range(16):
        load_v_block(j, D["Ptok"][j * 128:(j + 1) * 128, C_V:C_V + 1024], 128)
    NPT = 16
    for i in range(NPT):
        attend(128, 128 * i, i + 1, 128, True, 128 * i, i % 2)
    NSQ = 4
    for q in range(NSQ):
        tok = 2048 + 16 * q
        for j in range(16):
            v = vst[cnt["v"] % 2]; cnt["v"] += 1
            P.dma("sp", v[:, :], D["cache_k"][q, j * 128:(j + 1) * 128, :], writes=[v])
            cp(P, "pool", kst[:, :], v[:, :], [v], [kst])
            for fc in range(8):
                tr(P, pmT[:, fc, :], kst[:, fc * 128:(fc + 1) * 128], g.identb[:], [kst, g.identb], [pmT])
            cp(P, "act", kT[:, :, j * 128:(j + 1) * 128], pmT[:, :, :], [pmT], [kT])
            load_v_block(j, D["cache_v"][q, j * 128:(j + 1) * 128, :], 128)
            P.dma("sp", kis[:, :], D["cache_kidx"][q, j * 128:(j + 1) * 128, :], writes=[kis])
            cp(P, "dve", kisb[:, 0:64], kis[:, :], [kis], [kisb])
            cp(P, "dve", kisb[:, 64:128], kis[:, :], [kis], [kisb])
            tr(P, pmT[:, 0, :], kisb[:, :], g.identb[:], [kisb, g.identb], [pmT])
            cp(P, "act", kiT[:, j * 128:(j + 1) * 128], pmT[:, 0, :], [pmT], [kiT])
        P.dma("sp", kT[:, :, 2048:2064], D["KTn"].rearrange("(fc p) c -> p fc c", p=128)[:, :, tok:tok + 16], reads=[D["KTn_t"]], writes=[kT])
        P.dma("sp", kiT[0:64, 2048:2064], D["KIT"][:, tok:tok + 16], reads=[D["KIT_t"]], writes=[kiT])
        P.dma("sp", kiT[64:128, 2048:2064], D["KIT"][:, tok:tok + 16], reads=[D["KIT_t"]], writes=[kiT])
        load_v_block(16, D["Ptok"][tok:tok + 16, C_V:C_V + 1024], 16)
        attend(16, tok, 17, 16, False, tok, q % 2)
    P.barrier()
    st.close()


def load_w_bf16(P, dst, src_ap, stg, k0):
    for hf in range(2):
        s = stg[(k0 + hf) % 2]
        P.dma("sp", s[:], src_ap[:, hf * 512:(hf + 1) * 512].rearrange("(kc p) c -> p kc c", p=128), writes=[s])
        cp(P, "pool" if hf else "dve", dst[:, :, hf * 512:(hf + 1) * 512], s[:], [s], [dst])


def stage6(P, g, D):
    st = ExitStack()
    sb = lambda name, shape, dt=F32: P.sb(name, shape, dt, st)
    wpa = sb("wpa", [128, 8, 1024], BF16); wpb = sb("wpb", [128, 8, 1024], BF16); wo = sb("wo", [128, 8, 1024], BF16)
    stg = [sb(f"wstg{i}", [128, 8, 512]) for i in range(2)]
    load_w_bf16(P, wpa, D["w_proj_a"], stg, 0)
    load_w_bf16(P, wpb, D["w_proj_b"], stg, 0)
    load_w_bf16(P, wo, D["w_out"], stg, 0)
    oat = sb("oat", [128, 8, 512], BF16); obt = sb("obt", [128, 8, 512], BF16)
    mT = sb("mT", [128, 8, 512], BF16)
    ga = [sb(f"ga{i}", [128, 512]) for i in range(2)]; gb = [sb(f"gb{i}", [128, 512]) for i in range(2)]
    m1 = [sb(f"m1_{i}", [128, 512]) for i in range(2)]; m2 = [sb(f"m2_{i}", [128, 512]) for i in range(2)]
    g1r = sb("g1r", [128, 1024]); g1s = sb("g1s", [128, 1024])
    P.dma("sp", g1r[:], D["modd"][0:1, 2048:3072].partition_broadcast(128), reads=[D["modd_t"]], writes=[g1r])
    for q in range(4):
        P.dma("sp", g1s[16 * q:16 * q + 16, :], D["modd"][1 + q:2 + q, 2048:3072].partition_broadcast(16), reads=[D["modd_t"]], writes=[g1s])
    xt = [sb(f"x6_{i}", [128, 1024]) for i in range(2)]
    x1 = [sb(f"x1_{i}", [128, 1024]) for i in range(2)]
    pa = [P.ps(f"pa{i}", [128, 512], F32, st) for i in range(2)]
    pb = [P.ps(f"pb{i}", [128, 512], F32, st) for i in range(2)]
    px = [P.ps(f"px{i}", [128, 512], F32, st) for i in range(2)]
    k = 0
    xk = 0
    for (n0, nb) in [(0, 512), (512, 512), (1024, 512), (1536, 512), (2048, 64)]:
        P.dma("sp", oat[:, :, 0:nb], D["OAT"].rearrange("(fc p) c -> p fc c", p=128)[:, :, n0:n0 + nb], reads=[D["OAT_t"]], writes=[oat])
        P.dma("sp", obt[:, :, 0:nb], D["OBT"].rearrange("(fc p) c -> p fc c", p=128)[:, :, n0:n0 + nb], reads=[D["OBT_t"]], writes=[obt])
        for fo in range(8):
            a_, b_ = pa[k % 2], pb[k % 2]
            ga_, gb_, m1_, m2_ = ga[k % 2], gb[k % 2], m1[k % 2], m2[k % 2]
            k += 1
            P.dma("sp", ga_[:, 0:nb], D["GT"][fo * 128:(fo + 1) * 128, n0:n0 + nb], reads=[D["GT_t"]], writes=[ga_])
            P.dma("sp", gb_[:, 0:nb], D["GT"][1024 + fo * 128:1024 + (fo + 1) * 128, n0:n0 + nb], reads=[D["GT_t"]], writes=[gb_])
            for kc in range(8):
                mm(P, a_[:, 0:nb], wpa[:, kc, fo * 128:(fo + 1) * 128], oat[:, kc, 0:nb], kc == 0, kc == 7, [wpa, oat], [a_])
            for kc in range(8):
                mm(P, b_[:, 0:nb], wpb[:, kc, fo * 128:(fo + 1) * 128], obt[:, kc, 0:nb], kc == 0, kc == 7, [wpb, obt], [b_])
            tt(P, "dve", m1_[:, 0:nb], a_[:, 0:nb], ga_[:, 0:nb], ALU.mult, [a_, ga_], [m1_])
            tt(P, "dve", m2_[:, 0:nb], b_[:, 0:nb], gb_[:, 0:nb], ALU.mult, [b_, gb_], [m2_])
            tt(P, "pool", mT[:, fo, 0:nb], m1_[:, 0:nb], m2_[:, 0:nb], ALU.add, [m1_, m2_], [mT])
        for t0 in range(0, nb, 128):
            n = min(128, nb - t0)
            x_, x1_ = xt[xk % 2], x1[xk % 2]; xk += 1
            P.dma("sp", x_[0:n, :], D["xin"][n0 + t0:n0 + t0 + n, :], writes=[x_])
            g1 = g1r if n0 < 2048 else g1s
            for hf in range(2):
                p = px[hf]
                cs = slice(hf * 512, (hf + 1) * 512)
                for kc in range(8):
                    mm(P, p[0:n, :], mT[:, kc, t0:t0 + n], wo[:, kc, cs], kc == 0, kc == 7, [mT, wo], [p])
                tt(P, "dve", x1_[0:n, cs], p[0:n, :], g1[0:n, cs], ALU.mult, [p, g1], [x1_])
                tt(P, "pool", x1_[0:n, cs], x1_[0:n, cs], x_[0:n, cs], ALU.add, [x1_, x_], [x1_])
            P.dma("pool", D["X1"][n0 + t0:n0 + t0 + n, :], x1_[0:n, :], reads=[x1_], writes=[D["X1_t"]])
    P.barrier()
    st.close()


def stage6b(P, g, D):
    st = ExitStack()
    h2T = P.sb("h2T", [128, 8, NT], BF16, st)
    norm_to_featmajor(P, g, D, D["X1"], h2T, g.scale2, 24)
    st2 = ExitStack()
    sb = lambda name, shape, dt=F32: P.sb(name, shape, dt, st2)
    P.dma("pool", D["H2T"].rearrange("(fc p) c -> p fc c", p=128)[:, :, :], h2T[:], reads=[h2T], writes=[D["H2T_t"]])
    wq = sb("wq", [128, 8, 1024], BF16)
    stg = [sb(f"wstgq{i}", [128, 8, 512]) for i in range(2)]
    load_w_bf16(P, wq, D["w_pq"], stg, 0)
    qs = [sb(f"qs{i}", [128, 512], BF16) for i in range(2)]
    pq = [P.ps(f"pq{i}", [128, 512], F32, st2) for i in range(2)]
    k = 0
    for (n0, nb) in [(0, 512), (512, 512), (1024, 512), (1536, 512), (2048, 64)]:
        for fo in range(8):
            p, s = pq[k % 2], qs[k % 2]; k += 1
            for kc in range(8):
                mm(P, p[:, 0:nb], wq[:, kc, fo * 128:(fo + 1) * 128], h2T[:, kc, n0:n0 + nb], kc == 0, kc == 7, [wq, h2T], [p])
            cp(P, "act", s[:, 0:nb], p[:, 0:nb], [p], [s])
            P.dma("pool", D["QPT"][fo * 128:(fo + 1) * 128, n0:n0 + nb], s[:, 0:nb], reads=[s], writes=[D["QPT_t"]])
    P.barrier()
    st2.close()
    st.close()


def vmax(P, out, in_, reads, writes):
    return P.op("dve", lambda e: e.max(out=out, in_=in_), reads=reads, writes=writes)


def mrep(P, out, rep, vals, reads, writes):
    return P.op("dve", lambda e: e.match_replace(out=out, in_to_replace=rep, in_values=vals, imm_value=NEG), reads=reads, writes=writes)


def stage7_prep(P, g, D):
    st = ExitStack()
    sb = lambda name, shape, dt=F32: P.sb(name, shape, dt, st)
    uf = [sb(f"uf{i}", [128, 1024]) for i in range(2)]
    ub = [sb(f"ub{i}", [128, 1024], BF16) for i in range(2)]
    vf = [sb(f"vf{i}", [128, 1024]) for i in range(2)]
    vb = [sb(f"vb{i}", [128, 1024], BF16) for i in range(2)]
    sT = [sb(f"usT{i}", [128, 8, 512], BF16) for i in range(2)]
    pT = [P.ps(f"upT{i}", [128, 8, 128], BF16, st) for i in range(2)]
    NE = 128
    for et in range(NE):
        u, ubb, v, vbb = uf[et % 2], ub[et % 2], vf[et % 2], vb[et % 2]
        P.dma("sp", u[:], D["peer_u"][et * 128:(et + 1) * 128, :], writes=[u])
        P.dma("sp", v[:], D["peer_v"][et * 128:(et + 1) * 128, :], writes=[v])
        cp(P, "dve", ubb[:], u[:], [u], [ubb])
        cp(P, "pool", vbb[:], v[:], [v], [vbb])
        P.dma("pool", D["Vbf"].rearrange("(g j p) d -> g p j d", j=4, p=128)[et // 4, :, et % 4, :], vbb[:], reads=[vbb], writes=[D["Vbf_t"]])
        p = pT[et % 2]
        s = sT[(et // 4) % 2]
        for kc in range(8):
            tr(P, p[:, kc, :], ubb[:, kc * 128:(kc + 1) * 128], g.identb[:], [ubb, g.identb], [p])
        cp(P, "act", s[:, :, (et % 4) * 128:(et % 4 + 1) * 128], p[:], [p], [s])
        if et % 4 == 3:
            e0 = (et - 3) * 128
            P.dma("pool", D["UT"].rearrange("(g p) (kc e) -> g p kc e", p=128, kc=8)[e0 // 512, :, :, :], s[:], reads=[s], writes=[D["UT_t"]])
    P.barrier()
    st.close()


def stage7_peer(P, g, D):
    st = ExitStack()
    sb = lambda name, shape, dt=F32: P.sb(name, shape, dt, st)
    keysT = sb("keysT", [128, 8, 128], BF16)
    kt = sb("kt_f", [128, 128])
    pT = [P.ps(f"ppT{i}", [128, 8, 128], BF16, st) for i in range(1)]
    ps12 = P.ps("ps12", [128, 4, 128], F32, st)
    for h in range(8):
        P.dma("sp", kt[:, 0:64], D["peer_keys"][h, 0, :, :], writes=[kt])
        P.dma("sp", kt[:, 64:128], D["peer_keys"][h, 1, :, :], writes=[kt])
        tr(P, ps12[:, h % 4, :], kt[:], g.identf[:], [kt, g.identf], [ps12])
        cp(P, "act", keysT[:, h, :], ps12[:, h % 4, :], [ps12], [keysT])
    g2r = sb("g2r", [128, 1024]); g2s = sb("g2s", [128, 1024])
    P.dma("sp", g2r[:], D["modd"][0:1, 5120:6144].partition_broadcast(128), reads=[D["modd_t"]], writes=[g2r])
    for q in range(4):
        P.dma("sp", g2s[16 * q:16 * q + 16, :], D["modd"][1 + q:2 + q, 5120:6144].partition_broadcast(16), reads=[D["modd_t"]], writes=[g2s])
    h2t = [sb(f"h2t{i}", [128, 8, 128], BF16) for i in range(2)]; qpt = [sb(f"qpt{i}", [128, 8, 128], BF16) for i in range(2)]
    S12 = sb("S12", [128, 16, 128]); S1P = sb("S1P", [128, 8, 128]); srep = sb("srep", [128, 16, 128])
    T16 = sb("T16", [128, 16, 16]); cand = sb("cand", [128, 8, 256]); crep = sb("crep", [128, 8, 256])
    top16c = sb("top16c", [128, 8, 16]); ez = sb("ez", [128, 8, 16]); Z = sb("Zp", [128, 8]); cinv = sb("cinv", [128, 8]); lnc = sb("lnc", [128, 8]); thr = sb("pthr", [128, 8]); mtiny = sb("mtiny", [128, 1])
    mset(P, "dve", mtiny[:], -1e-5, [mtiny])
    SUBI = 8
    NSUB = 128 // SUBI
    W_ = SUBI * 128
    zb = [sb(f"zb{i}", [128, W_]) for i in range(3)]; eb = [sb(f"eb{i}", [128, W_]) for i in range(3)]
    gmb = [sb(f"gmb{i}", [128, W_], BF16) for i in range(3)]
    Gp = [P.ps(f"pGp{i}", [128, 512], F32, st) for i in range(2)]
    Aqs = [sb(f"Aq{i}", [128, W_], BF16) for i in range(2)]; GAs = [sb(f"GAq{i}", [128, W_], BF16) for i in range(2)]
    GATs = [sb(f"GAT{i}", [128, SUBI, 128], BF16) for i in range(2)]
    ub = [sb(f"pub{i}", [128, 8, 512], BF16) for i in range(3)]
    vb = [sb(f"pvb{i}", [128, 4, 1024], BF16) for i in range(3)]
    x1t = [sb(f"px1_{i}", [128, 1024]) for i in range(2)]; yt = sb("pyt", [128, 1024])
    pA = [P.ps(f"ppA{i}", [128, 512], F32, st) for i in range(2)]
    po = [P.ps(f"ppo{i}", [128, 512], F32, st) for i in range(2)]
    uk = 0; vk = 0; ak = 0; zk = 0; tk = 0; gk = 0
    NTL = 17
    for ti, (r0, n) in enumerate(TILES[:NTL]):
        h2t_, qpt_, x1t_ = h2t[ti % 2], qpt[ti % 2], x1t[ti % 2]
        P.dma("sp", h2t_[:, :, 0:n], D["H2T"].rearrange("(fc p) c -> p fc c", p=128)[:, :, r0:r0 + n], reads=[D["H2T_t"]], writes=[h2t_])
        P.dma("sp", qpt_[:, :, 0:n], D["QPT"].rearrange("(fc p) c -> p fc c", p=128)[:, :, r0:r0 + n], reads=[D["QPT_t"]], writes=[qpt_])
        P.dma("sp", x1t_[0:n, :], D["X1"][r0:r0 + n, :], reads=[D["X1_t"]], writes=[x1t_])
        for hg in range(4):
            p = ps12
            for j in range(4):
                hp = hg * 4 + j
                h, pp = hp // 2, hp % 2
                pb = pp * 64
                mm(P, p[0:n, j, :], qpt_[pb:pb + 64, h, 0:n], keysT[pb:pb + 64, h, :], True, True, [qpt_, keysT], [p])
            cp(P, "act", S12[0:n, hg * 4:(hg + 1) * 4, :], p[0:n, :, :], [p], [S12])
        for hp in range(16):
            vmax(P, T16[0:n, hp, 0:8], S12[0:n, hp, :], [S12], [T16.sub(("a", hp))])
        for hp in range(16):
            mrep(P, srep[0:n, hp, :], T16[0:n, hp, 0:8], S12[0:n, hp, :], [S12, T16.sub(("a", hp))], [srep.sub(hp)])
        for hp in range(16):
            vmax(P, T16[0:n, hp, 8:16], srep[0:n, hp, :], [srep.sub(hp)], [T16.sub(("b", hp))])
        T16all = [T16.sub((x, hp)) for x in "ab" for hp in range(16)]
        T16v = T16[0:n, :, :].rearrange("p (h t) k -> p h t k", t=2)
        tt(P, "dve", cand[0:n, :, :].rearrange("p h (i j) -> p h i j", i=16), T16v[:, :, 0, :].unsqueeze(3).to_broadcast([n, 8, 16, 16]),
           T16v[:, :, 1, :].unsqueeze(2).to_broadcast([n, 8, 16, 16]), ALU.add, T16all, [cand])
        for h in range(8):
            vmax(P, top16c[0:n, h, 0:8], cand[0:n, h, :], [cand], [top16c.sub(("a", h))])
        for h in range(8):
            mrep(P, crep[0:n, h, :], top16c[0:n, h, 0:8], cand[0:n, h, :], [cand, top16c.sub(("a", h))], [crep.sub(h)])
        for h in range(8):
            vmax(P, top16c[0:n, h, 8:16], crep[0:n, h, :], [crep.sub(h)], [top16c.sub(("b", h))])
        tcall = [top16c.sub((x, h)) for x in "ab" for h in range(8)]
        tau = top16c[0:n, :, 15:16]
        tt(P, "dve", ez[0:n, :, :], top16c[0:n, :, :], tau.to_broadcast([n, 8, 16]), ALU.subtract, tcall, [ez])
        act(P, ez[0:n, :, :], ez[0:n, :, :], AF.Exp, [ez], [ez])
        red(P, "dve", Z[0:n, :], ez[0:n, :, :], ALU.add, [ez], [Z])
        recip(P, cinv[0:n, :], Z[0:n, :], [Z], [cinv])
        act(P, lnc[0:n, :], cinv[0:n, :], AF.Ln, [cinv], [lnc])
        S12v = S12[0:n, :, :].rearrange("p (h t) k -> p h t k", t=2)
        tt(P, "dve", S1P[0:n, :, :], S12v[:, :, 0, :], tau.to_broadcast([n, 8, 128]), ALU.subtract, [S12] + tcall, [S1P])
        def emitA(ib):
            nonlocal uk, ak
            Aq = Aqs[ib % 2]
            for eg in range(W_ // 512):
                e0 = ib * W_ + eg * 512
                u = ub[uk % 3]; uk += 1
                P.dma("sp", u[:], D["UT"].rearrange("(g p) (kc e) -> g p kc e", p=128, kc=8)[e0 // 512, :, :, :], reads=[D["UT_t"]], writes=[u])
                p = pA[ak % 2]; ak += 1
                for kc in range(8):
                    mm(P, p[0:n, :], h2t_[:, kc, 0:n], u[:, kc, :], kc == 0, kc == 7, [h2t_, u], [p])
                act(P, Aq[0:n, eg * 512:(eg + 1) * 512], p[0:n, :], AF.Gelu_apprx_tanh, [p], [Aq])

        def emitZ(ib, h):
            nonlocal zk
            z_, e_ = zb[zk % 3], eb[zk % 3]; zk += 1
            z3 = z_[0:n, :].rearrange("p (i j) -> p i j", i=SUBI)
            tt(P, "dve", z3, S1P[0:n, h, ib * SUBI:(ib + 1) * SUBI].unsqueeze(2).to_broadcast([n, SUBI, 128]),
               S12v[:, h, 1, :].unsqueeze(1).to_broadcast([n, SUBI, 128]), ALU.add, [S1P, S12], [z_])
            act(P, e_[0:n, :], z_[0:n, :], AF.Exp, [z_, lnc], [e_], bias=lnc[0:n, h:h + 1])
            return z_, e_

        emitA(0)
        pend = emitZ(0, 0)
        for ib in range(NSUB):
            Aq, GA, GAT = Aqs[ib % 2], GAs[ib % 2], GATs[ib % 2]
            if ib + 1 < NSUB:
                emitA(ib + 1)
            for h in range(8):
                z_, e_ = pend
                if h + 1 < 8:
                    pend = emitZ(ib, h + 1)
                elif ib + 1 < NSUB:
                    pend = emitZ(ib + 1, 0)
                gm = gmb[gk % 3]; gk += 1
                stt(P, "dve", gm[0:n, :], z_[0:n, :], -1e-5, e_[0:n, :], ALU.is_ge, ALU.mult, [z_, e_], [gm])
                for cg in range(W_ // 512):
                    mm(P, Gp[cg][0:n, :], g.identb[0:n, 0:n], gm[0:n, cg * 512:(cg + 1) * 512], h == 0, h == 7, [gm, g.identb], [Gp[cg]])
            for cg in range(W_ // 512):
                tt(P, "dve", GA[0:n, cg * 512:(cg + 1) * 512], Gp[cg][0:n, :], Aq[0:n, cg * 512:(cg + 1) * 512], ALU.mult, [Gp[cg], Aq], [GA])
            for j0 in range(0, SUBI, 8):
                pt = pT[0]; tk += 1
                for jj in range(8):
                    et = j0 + jj
                    tr(P, pt[:, jj, 0:n], GA[0:n, et * 128:(et + 1) * 128], g.identb[0:n, 0:n], [GA, g.identb], [pt])
                cp(P, "act", GAT[:, j0:j0 + 8, 0:n], pt[:, :, 0:n], [pt], [GAT])
            for vg in range(SUBI // 4):
                v = vb[vk % 3]; vk += 1
                e0 = (ib * SUBI + vg * 4) * 128
                P.dma("pool", v[:], D["Vbf"].rearrange("(g j p) d -> g p j d", j=4, p=128)[e0 // 512, :, :, :], reads=[D["Vbf_t"]], writes=[v])
                for j in range(4):
                    et = vg * 4 + j
                    first = (ib == 0 and et == 0)
                    last = (ib == NSUB - 1 and et == SUBI - 1)
                    for hf in range(2):
                        mm(P, po[hf][0:n, :], GAT[:, et, 0:n], v[:, j, hf * 512:(hf + 1) * 512], first, last, [GAT, v], [po[hf]])
        g2 = g2r if r0 < 2048 else g2s
        for hf in range(2):
            cs = slice(hf * 512, (hf + 1) * 512)
            tt(P, "dve", yt[0:n, cs], po[hf][0:n, :], g2[0:n, cs], ALU.mult, [po[hf], g2], [yt])
        if D.get("PEERdbg") is not None:
            P.dma("pool", D["PEERdbg"][r0:r0 + n, :], yt[0:n, :], reads=[yt])
        tt(P, "pool", yt[0:n, :], yt[0:n, :], x1t_[0:n, :], ALU.add, [yt, x1t_], [yt])
        P.dma("pool", D["y"][r0:r0 + n, :], yt[0:n, :], reads=[yt], writes=[D["y_t"]])
    P.barrier()
    st.close()


def declare(nc, P, D, debug=False):
    def din(name, shape, dt=F32):
        D[name] = nc.dram_tensor(name, list(shape), dt, kind="ExternalInput").ap()

    def dscr(name, shape, dt=F32, kind="Internal"):
        D[name] = nc.dram_tensor(name, list(shape), dt, kind=("ExternalOutput" if debug else kind)).ap()
        D[name + "_t"] = T(None, name)

    din("ident", [128, 128]); din("cin", [5, 1024]); din("xin", [NT, 1024])
    din("w_ada", [1024, 6144]); din("b_ada", [1, 6144]); din("norm1_w", [1024]); din("norm2_w", [1024]); din("b_gate", [2048])
    din("w_in", [1024, 9064]); din("cos", [NT, 32]); din("sin", [NT, 32]); din("k_norm_w", [1, 64]); din("q_norm_w", [1, 64])
    din("mu_rw", [1, RW_IN])
    for nm in ["w0", "a0", "k_k", "k_a", "r_k", "lnx_w", "lnx_b"]:
        din(nm, [1, 1024])
    din("w_up", [64, 1024]); din("a_up", [64, 1024]); din("g_up", [160, 1024])
    din("sshift", [4, RW_IN]); din("swkv", [4, 16, 64, 64])
    for nm in ["m_tri", "m_ones", "m_sl", "m_su", "m_u"]:
        din(nm, [128, 128])
    din("m_valid", [128, 2])
    dscr("modd", [5, 6144]); dscr("Ptok", [NT, TOKW]); dscr("GT", [2048, NT])
    dscr("BC", [UC, 16])
    for nm in ["Vs", "KPs", "BPs", "Gs"]:
        dscr(nm, [UC, 1024])
    for nm in ["RT", "AT", "KT", "BT", "GCT"]:
        dscr(nm, [1024, UC])
    for nm in ["OAT", "OBT", "QT", "KTn", "H2T", "QPT"]:
        dscr(nm, [1024, NT], BF16)
    dscr("QIT", [512, NT], BF16); dscr("KIT", [64, NT], BF16); dscr("WI", [NT, 8])
    dscr("X1", [NT, 1024], F32, "Internal")
    dscr("UT", [4096, 4096], BF16); dscr("Vbf", [16384, 1024], BF16)
    din("cache_k", [4, 2048, 1024]); din("cache_v", [4, 2048, 1024]); din("cache_kidx", [4, 2048, 64]); din("m_pow2", [128, NBIS])
    for nm in ["w_proj_a", "w_proj_b", "w_out", "w_pq"]:
        din(nm, [1024, 1024])
    din("peer_keys", [8, 2, 128, 64]); din("peer_u", [16384, 1024]); din("peer_v", [16384, 1024])
    for nm, shp in [("y", [NT, 1024]), ("wkv", [5 * 16 * 64, 64]), ("kout", [NT, 1024]), ("vout", [NT, 1024]), ("kidx", [NT, 64]), ("shift", [5, RW_IN])]:
        D[nm] = nc.dram_tensor(nm, shp, F32, kind="ExternalOutput").ap()
        D[nm + "_t"] = T(None, nm)


def host_consts():
    inv = (10000.0 ** (-np.arange(32, dtype=np.float32) / 32)).astype(np.float32)
    pos = np.concatenate([np.arange(2048), np.tile(2048 + np.arange(16), 4)]).astype(np.float32)
    ang = pos[:, None] * inv[None, :]
    idx = np.arange(128)
    same = (idx[:, None] // 64) == (idx[None, :] // 64)
    f32 = lambda a: np.ascontiguousarray(a, dtype=np.float32)
    valid = np.ones((128, 2), np.float32)
    valid[:, 1] = ((idx % 64) < 16)
    return {"ident": np.eye(128, dtype=np.float32), "cos": np.cos(ang).astype(np.float32), "sin": np.sin(ang).astype(np.float32),
            "m_tri": f32(same & (idx[:, None] <= idx[None, :])), "m_ones": f32(same), "m_sl": f32(same & (idx[:, None] > idx[None, :])),
            "m_su": f32(same & (idx[:, None] < idx[None, :])), "m_u": f32(same & (idx[:, None] <= idx[None, :])), "m_valid": valid,
            "m_pow2": np.tile((2.0 ** -(np.arange(NBIS) + 1.0)).astype(np.float32)[None, :], (128, 1))}


def run_all(P, g, D):
    stage0(P, g, D)
    st = ExitStack()
    hT = P.sb("hT", [128, 8, NT], BF16, st)
    norm_to_featmajor(P, g, D, D["xin"], hT, g.scale1, 0)
    stage2(P, g, D, hT)
    st.close()
    stage3_dsa(P, g, D)
    stage3_rw(P, g, D)
    stage4_scan(P, g, D)
    stage5_attn(P, g, D)
    stage6(P, g, D)
    stage6b(P, g, D)
    stage7_prep(P, g, D)
    stage7_peer(P, g, D)


def core_inputs(inp, c, consts, local=False):
    f = lambda a: np.ascontiguousarray(np.asarray(a, dtype=np.float32))
    m = dict(consts)
    W = lambda k: f(np.asarray(inp[k])[0])
    m.update({"w_ada": W("w_ada"), "b_ada": f(inp["b_ada"]), "norm1_w": W("norm1_w"), "norm2_w": W("norm2_w"), "b_gate": W("b_gate"), "w_in": W("w_in"),
              "k_norm_w": f(inp["k_norm_w"]), "q_norm_w": f(inp["q_norm_w"]), "mu_rw": f(inp["mu_rw"]), "w0": f(inp["w0"]), "a0": f(inp["a0"]),
              "k_k": f(inp["k_k"]), "k_a": f(inp["k_a"]), "r_k": f(np.asarray(inp["r_k"]).reshape(1, 1024)), "lnx_w": f(inp["lnx_w"]), "lnx_b": f(inp["lnx_b"]),
              "w_up": W("w_up"), "a_up": W("a_up"), "g_up": W("g_up"), "w_proj_a": W("w_proj_a"), "w_proj_b": W("w_proj_b"), "w_out": W("w_out"),
              "w_pq": W("w_pq"), "peer_keys": W("peer_keys"), "peer_u": W("peer_u"), "peer_v": W("peer_v")})
    pc = 0 if local else c
    sl = slice(0, 4) if local else slice(4 * c, 4 * c + 4)
    m["xin"] = f(np.concatenate([np.asarray(inp["x_prompt"])[pc], np.asarray(inp["x_sample"])[sl].reshape(64, 1024)], 0))
    m["cin"] = f(np.concatenate([np.asarray(inp["c_prompt"])[pc:pc + 1], np.asarray(inp["c_sample"])[sl]], 0))
    m["sshift"] = f(np.asarray(inp["state_shift"])[0][sl])
    m["swkv"] = f(np.asarray(inp["state_wkv"])[0][sl])
    m["cache_k"] = f(np.asarray(inp["cache_k"])[0][sl].reshape(4, 2048, 1024))
    m["cache_v"] = f(np.asarray(inp["cache_v"])[0][sl].reshape(4, 2048, 1024))
    m["cache_kidx"] = f(np.asarray(inp["cache_kidx"])[0][sl])
    return m


def build_program():
    nc = bass.Bass("TRN2", target_bir_lowering=False)
    P = Prog(nc)
    D = {}
    declare(nc, P, D)
    g = G()
    g.epsc = P.sb("epsc", [128, 1], F32)
    mset(P, "dve", g.epsc[:], EPS, [g.epsc])
    run_all(P, g, D)
    P.emit()
    return nc


def kernel(**inp):
    nc = build_program()
    consts = host_consts()
    in_maps = [core_inputs(inp, c, consts) for c in range(8)]
    res = run_bass_kernel_spmd(nc, in_maps, core_ids=list(range(8)))
    R = res.results
    cat = lambda name, sl, shp: np.stack([R[c][name][sl].reshape(shp) for c in range(8)], 0)
    y_p = cat("y", slice(0, 2048), (2048, 1024))
    y_s = cat("y", slice(2048, NT), (4, 16, 1024)).reshape(32, 16, 1024)
    wkv_p = cat("wkv", slice(0, 1024), (16, 64, 64))[None]
    wkv_s = cat("wkv", slice(1024, 5120), (4, 16, 64, 64)).reshape(32, 16, 64, 64)[None]
    sh_p = cat("shift", slice(0, 1), (RW_IN,))[None]
    sh_s = cat("shift", slice(1, 5), (4, RW_IN)).reshape(32, RW_IN)[None]
    k_p = cat("kout", slice(0, 2048), (2048, 16, 64))[None]
    k_s = cat("kout", slice(2048, NT), (4, 16, 16, 64)).reshape(32, 16, 16, 64)[None]
    v_p = cat("vout", slice(0, 2048), (2048, 16, 64))[None]
    v_s = cat("vout", slice(2048, NT), (4, 16, 16, 64)).reshape(32, 16, 16, 64)[None]
    ki_p = cat("kidx", slice(0, 2048), (2048, 64))[None]
    ki_s = cat("kidx", slice(2048, NT), (4, 16, 64)).reshape(32, 16, 64)[None]
    return (y_p, y_s, wkv_p, sh_p, k_p, v_p, ki_p, wkv_s, sh_s, k_s, v_s, ki_s)
```

```python
import numpy as np
from contextlib import ExitStack
import concourse.bass as bass
import concourse.mybir as mybir
from concourse.bass_utils import run_bass_kernel_spmd

F32 = mybir.dt.float32
BF16 = mybir.dt.bfloat16
I32 = mybir.dt.int32
U32 = mybir.dt.uint32
AF = mybir.ActivationFunctionType
ALU = mybir.AluOpType
AX = mybir.AxisListType

ENGS = ["pe", "act", "dve", "pool", "sp"]
NDMA = {"sp": 40, "pool": 24, "act": 8}


class Dep:
    __slots__ = ("w", "r", "name")

    def __init__(self, name=""):
        self.w = None
        self.r = []
        self.name = name


class Bank:
    def __init__(self):
        self.last = {}
        self.pe_rows = None


class T:
    def __init__(self, h, name):
        self.h = h
        self.name = name
        self.dep = Dep(name)
        self.subs = {}
        self.bank = None

    def __getitem__(self, idx):
        return self.h[idx]

    def sub(self, key):
        if key not in self.subs:
            self.subs[key] = Dep(f"{self.name}.{key}")
        return self.subs[key]


class Slot:
    def __init__(self, t, i):
        self.base = t.h[:, i, :]
        self.dep = Dep(f"{t.name}[{i}]")
        self.bank = t.bank

    def __getitem__(self, idx):
        return self.base[idx]


class SlotAP:
    def __init__(self, t, ap):
        self.base = ap
        self.dep = Dep(t.name + "[ap]")
        self.bank = t.bank

    def __getitem__(self, idx):
        return self.base[idx]


def _dep(x):
    return x.dep if hasattr(x, "dep") else x


class Prog:
    def __init__(self, nc):
        self.nc = nc
        self.es = ExitStack()
        self.ops = {e: [] for e in ENGS}
        self.cnt = {e: 0 for e in ENGS}
        self.esem = {}
        for e in ENGS:
            self.esem[e] = self.es.enter_context(nc.semaphore("s_" + e))
        self.dsem = {}
        self.dval = {}
        self.dnext = {}
        for q, n in NDMA.items():
            self.dsem[q] = [self.es.enter_context(nc.semaphore(f"d_{q}{i}")) for i in range(n)]
            self.dval[q] = [0] * n
            self.dnext[q] = 0
        self.seen = {e: {} for e in ENGS}
        self.semobj = {}
        self.nwaits = 0

    def _uniq(self, name):
        if not hasattr(self, "_names"):
            self._names = {}
        k = self._names.get(name, 0)
        self._names[name] = k + 1
        return name if k == 0 else f"{name}__{k}"

    def sb(self, name, shape, dtype, stack=None):
        name = self._uniq(name)
        h = (stack or self.es).enter_context(self.nc.sbuf_tensor(name, list(shape), dtype))
        return T(h, name)

    def ps(self, name, shape, dtype=F32, stack=None):
        name = self._uniq(name)
        h = (stack or self.es).enter_context(self.nc.psum_tensor(name, list(shape), dtype))
        t = T(h, name)
        t.bank = Bank()
        return t

    def dram(self, name, shape, dtype, kind="Internal"):
        h = self.nc.dram_tensor(name, list(shape), dtype, kind=kind)
        return T(h, name)

    def _waits(self, eng, reads, writes, extra=()):
        evs = list(extra)
        for b in reads:
            b = _dep(b)
            if b.w is not None:
                evs.append(b.w)
        for b in writes:
            b = _dep(b)
            if b.w is not None:
                evs.append(b.w)
            evs.extend(b.r)
        need = {}
        for (key, sem, val, src) in evs:
            if src == "pe" and eng == "pe":
                continue
            if self.seen[eng].get(key, 0) >= val:
                continue
            if need.get(key, (None, 0))[1] < val:
                need[key] = (sem, val)
        for key, (sem, val) in need.items():
            self.seen[eng][key] = val
        return list(need.values())

    def _record(self, ev, reads, writes):
        for b in reads:
            _dep(b).r.append(ev)
        for b in writes:
            b = _dep(b)
            b.w = ev
            b.r = []

    def op(self, eng, fn, reads=(), writes=(), pe_rows=None):
        banks = {}
        for b in list(reads) + list(writes):
            bk = getattr(b, "bank", None)
            if bk is not None:
                banks[id(bk)] = bk
        extra = [ev for bk in banks.values() for e2, ev in bk.last.items() if e2 != eng]
        if eng == "pe" and pe_rows is not None:
            for bk in banks.values():
                if bk.pe_rows is not None and bk.pe_rows != pe_rows and "pe" in bk.last and (pe_rows[1] < 128 or bk.pe_rows[1] < 128):
                    k_, s_, v_, _ = bk.last["pe"]
                    extra.append((k_, s_, v_, "force"))
                bk.pe_rows = pe_rows
        waits = self._waits(eng, reads, writes, extra)
        self.cnt[eng] += 1
        ev = (eng, self.esem[eng], self.cnt[eng], eng)
        for bk in banks.values():
            bk.last[eng] = ev
        self._record(ev, reads, writes)
        self.ops[eng].append((waits, fn, (self.esem[eng], 1)))
        self.nwaits += len(waits)
        return ev

    def dma(self, q, out, in_, reads=(), writes=(), **kw):
        i = self.dnext[q]
        self.dnext[q] = (i + 1) % len(self.dsem[q])
        sem = self.dsem[q][i]
        key = (q, i)
        waits = self._waits(q, reads, writes)
        prev = self.dval[q][i]
        if prev > 0 and self.seen[q].get(key, 0) < prev:
            waits.append((sem, prev))
            self.seen[q][key] = prev
        self.dval[q][i] = prev + 16
        ev = (key, sem, prev + 16, "dma")
        self._record(ev, reads, writes)
        self.ops[q].append((waits, lambda e: e.dma_start(out=out, in_=in_, **kw), (sem, 16)))
        self.nwaits += len(waits)
        return ev

    def barrier(self):
        evs = []
        for e in ENGS:
            if self.cnt[e] > 0:
                evs.append((e, self.esem[e], self.cnt[e]))
        for q in self.dsem:
            for i, v in enumerate(self.dval[q]):
                if v > 0:
                    evs.append(((q, i), self.dsem[q][i], v))
        for e in ENGS:
            waits = []
            for key, sem, val in evs:
                if key == e:
                    continue
                if self.seen[e].get(key, 0) >= val:
                    continue
                self.seen[e][key] = val
                waits.append((sem, val))
            if waits:
                self.ops[e].append((waits, None, None))

    def emit(self):
        self.barrier()
        nc = self.nc
        with nc.Block() as block:
            def run(e, lst):
                for waits, fn, inc in lst:
                    for sem, val in waits:
                        e.wait_ge(sem, val)
                    if fn is not None:
                        ins = fn(e)
                        ins.then_inc(inc[0], inc[1])

            @block.tensor
            def _(e):
                run(e, self.ops["pe"])

            @block.scalar
            def _(e):
                run(e, self.ops["act"])

            @block.vector
            def _(e):
                run(e, self.ops["dve"])

            @block.gpsimd
            def _(e):
                run(e, self.ops["pool"])

            @block.sync
            def _(e):
                run(e, self.ops["sp"])
        self.es.close()
EPS = 1e-6
NT = 2112
TILES = [(i * 128, 128) for i in range(16)] + [(2048, 64)]
RW_IN = 3360
TOKW = 7016


def mm(P, out, lhsT, rhs, start, stop, reads, writes):
    rows = (lhsT.base_partition(), lhsT.partition_size())
    return P.op("pe", lambda e: e.matmul(out, lhsT, rhs, start=start, stop=stop), reads=reads, writes=writes, pe_rows=rows)


def tr(P, out, in_, ident, reads, writes):
    return P.op("pe", lambda e: e.transpose(out, in_, ident), reads=reads, writes=writes)


def act(P, out, in_, func, reads, writes, **kw):
    return P.op("act", lambda e: e.activation(out=out, in_=in_, func=func, **kw), reads=reads, writes=writes)


def ts(P, eng, out, in0, s1, s2, op0, op1=None, reads=(), writes=(), **kw):
    def f(e):
        if op1 is None:
            return e.tensor_scalar(out=out, in0=in0, scalar1=s1, scalar2=s2, op0=op0, **kw)
        return e.tensor_scalar(out=out, in0=in0, scalar1=s1, scalar2=s2, op0=op0, op1=op1, **kw)
    return P.op(eng, f, reads=reads, writes=writes)


def tt(P, eng, out, in0, in1, op, reads, writes):
    if eng == "pool":
        eng = "dve"
    return P.op(eng, lambda e: e.tensor_tensor(out=out, in0=in0, in1=in1, op=op), reads=reads, writes=writes)


def cp(P, eng, out, in_, reads, writes):
    if eng == "pool":
        eng = "act"
    if eng == "act":
        return act(P, out, in_, AF.Copy, reads, writes)
    return P.op(eng, lambda e: e.tensor_copy(out, in_), reads=reads, writes=writes)


class G:
    pass


def featvec(P, g, st, name, vec_ap, n):
    tmp = P.sb(name + "_t", [n, 128], F32, st)
    P.dma("sp", tmp[:], vec_ap.rearrange("(c p) -> c p", p=128), writes=[tmp])
    ps = P.ps(name + "_p", [128, n], F32, st)
    tr(P, ps[:], tmp[:], g.identf[0:n, 0:n], [tmp, g.identf], [ps])
    out = getattr(g, name)
    cp(P, "dve", out[:], ps[:], [ps], [out])
    return out


def stage0(P, g, D):
    g.identf = P.sb("identf", [128, 128], F32)
    g.identb = P.sb("identb", [128, 128], BF16)
    g.modT = P.sb("modT", [128, 48, 5], F32)
    g.n1w = P.sb("n1w", [128, 8], F32)
    g.n2w = P.sb("n2w", [128, 8], F32)
    g.bgT = P.sb("bgT", [128, 16], F32)
    g.scale1 = P.sb("scale1", [128, 8, 5], F32)
    g.scale2 = P.sb("scale2", [128, 8, 5], F32)
    st = ExitStack()
    P.dma("sp", g.identf[:], D["ident"][:, :], writes=[g.identf])
    cp(P, "dve", g.identb[:], g.identf[:], [g.identf], [g.identb])
    c5 = P.sb("c5", [5, 1024], F32, st)
    s5 = P.sb("s5", [5, 1024], F32, st)
    P.dma("sp", c5[:], D["cin"][:, :], writes=[c5])
    act(P, s5[:], c5[:], AF.Silu, [c5], [s5])
    sT = P.sb("sT", [128, 8, 5], F32, st)
    pst = P.ps("pst", [128, 8, 5], F32, st)
    for kc in range(8):
        tr(P, pst[:, kc, :], s5[0:5, kc * 128:(kc + 1) * 128], g.identf[0:5, 0:5], [s5, g.identf], [pst])
    cp(P, "dve", sT[:], pst[:], [pst], [sT])
    bada5 = P.sb("bada5", [5, 6144], F32, st)
    P.dma("sp", bada5[:], D["b_ada"][0:1, :].partition_broadcast(5), writes=[bada5])
    mod5 = P.sb("mod5", [5, 6144], F32, st)
    wa = [P.sb(f"wa{i}", [128, 8, 512], F32, st) for i in range(2)]
    pm = [P.ps(f"pm{i}", [5, 512], F32, st) for i in range(2)]
    for gi in range(12):
        w = wa[gi % 2]
        p = pm[gi % 2]
        P.dma("sp", w[:], D["w_ada"][:, gi * 512:(gi + 1) * 512].rearrange("(kc p) c -> p kc c", p=128), writes=[w])
        for kc in range(8):
            mm(P, p[:], sT[:, kc, :], w[:, kc, :], kc == 0, kc == 7, [sT, w], [p])
        tt(P, "dve", mod5[:, gi * 512:(gi + 1) * 512], p[:], bada5[:, gi * 512:(gi + 1) * 512], ALU.add, [p, bada5], [mod5])
    P.dma("sp", D["modd"][:, :], mod5[:], reads=[mod5], writes=[D["modd_t"]])
    pmt = P.ps("pmt", [128, 48, 5], F32, st)
    for c in range(48):
        tr(P, pmt[:, c, :], mod5[0:5, c * 128:(c + 1) * 128], g.identf[0:5, 0:5], [mod5, g.identf], [pmt])
    cp(P, "dve", g.modT[:], pmt[:], [pmt], [g.modT])
    n1 = featvec(P, g, st, "n1w", D["norm1_w"], 8)
    n2 = featvec(P, g, st, "n2w", D["norm2_w"], 8)
    g.bgT = featvec(P, g, st, "bgT", D["b_gate"], 16)
    for c in range(8):
        ts(P, "dve", g.scale1[:, c, :], g.modT[:, 8 + c, :], 1.0, n1[:, c:c + 1], ALU.add, ALU.mult, [g.modT, n1], [g.scale1])
        ts(P, "dve", g.scale2[:, c, :], g.modT[:, 32 + c, :], 1.0, n2[:, c:c + 1], ALU.add, ALU.mult, [g.modT, n2], [g.scale2])
    P.barrier()
    st.close()


def norm_to_featmajor(P, g, D, src_ap, hT, scale, shift_chunk0):
    st = ExitStack()
    xt = [P.sb(f"nx{i}", [128, 1024], F32, st) for i in range(2)]
    xn = [P.sb(f"nxn{i}", [128, 1024], BF16, st) for i in range(2)]
    junk = P.sb("njunk", [128, 1024], F32, st)
    ss = [P.sb(f"nss{i}", [128, 1], F32, st) for i in range(2)]
    rs = [P.sb(f"nrs{i}", [128, 1], F32, st) for i in range(2)]
    pT = [P.ps(f"npT{i}", [128, 8, 128], BF16, st) for i in range(2)]
    for ti, (r0, n) in enumerate(TILES):
        x, xb, s, r, p = xt[ti % 2], xn[ti % 2], ss[ti % 2], rs[ti % 2], pT[ti % 2]
        P.dma("sp", x[0:n, :], src_ap[r0:r0 + n, :], writes=[x])
        act(P, junk[0:n, :], x[0:n, :], AF.Square, [x], [junk, s], accum_out=s[0:n, :])
        act(P, s[0:n, :], s[0:n, :], AF.Sqrt, [s], [s], scale=1.0 / 1024, bias=g.epsc[0:n, :])
        P.op("dve", lambda e, r=r, s=s, n=n: e.reciprocal(r[0:n, :], s[0:n, :]), reads=[s], writes=[r])
        ts(P, "dve", xb[0:n, :], x[0:n, :], r[0:n, :], None, ALU.mult, None, [x, r], [xb])
        for kc in range(8):
            tr(P, p[:, kc, 0:n], xb[0:n, kc * 128:(kc + 1) * 128], g.identb[0:n, 0:n], [xb, g.identb], [p])
        for kc in range(8):
            if ti < 16:
                act(P, hT[:, kc, r0:r0 + n], p[:, kc, 0:n], AF.Identity, [p, scale, g.modT], [hT],
                    scale=scale[:, kc, 0:1], bias=g.modT[:, shift_chunk0 + kc, 0:1])
            else:
                for q in range(4):
                    act(P, hT[:, kc, r0 + 16 * q:r0 + 16 * q + 16], p[:, kc, 16 * q:16 * q + 16], AF.Identity,
                        [p, scale, g.modT], [hT], scale=scale[:, kc, 1 + q:2 + q], bias=g.modT[:, shift_chunk0 + kc, 1 + q:2 + q])
    P.barrier()
    st.close()


def stage2(P, g, D, hT):
    st = ExitStack()
    wf = [P.sb(f"wf{i}", [128, 8, 512], F32, st) for i in range(2)]
    wb = [P.sb(f"wb{i}", [128, 8, 512], BF16, st) for i in range(2)]
    stg = [P.sb(f"stg{i}", [128, 512], F32, st) for i in range(3)]
    pp = [P.ps(f"pp{i}", [128, 512], F32, st) for i in range(3)]
    k = 0
    groups = [(c0, min(512, TOKW - c0), False) for c0 in range(0, TOKW, 512)] + [(TOKW + i * 512, 512, True) for i in range(4)]
    for gi, (c0, gw, isgate) in enumerate(groups):
        w, b = wf[gi % 2], wb[gi % 2]
        P.dma("sp", w[:, :, 0:gw], D["w_in"][:, c0:c0 + gw].rearrange("(kc p) c -> p kc c", p=128), writes=[w])
        cp(P, "pool" if gi % 2 else "dve", b[:, :, 0:gw], w[:, :, 0:gw], [w], [b])
        if not isgate:
            for ti, (r0, n) in enumerate(TILES):
                p, s = pp[k % 3], stg[k % 3]
                for kc in range(8):
                    mm(P, p[0:n, 0:gw], hT[:, kc, r0:r0 + n], b[:, kc, 0:gw], kc == 0, kc == 7, [hT, b], [p])
                cp(P, "act" if k % 2 else "dve", s[0:n, 0:gw], p[0:n, 0:gw], [p], [s])
                P.dma("pool", D["Ptok"][r0:r0 + n, c0:c0 + gw], s[0:n, 0:gw], reads=[s], writes=[D["Ptok_t"].sub(ti)])
                k += 1
        else:
            for j in range(4):
                fch = (c0 - TOKW) // 128 + j
                for (n0, nb) in [(0, 512), (512, 512), (1024, 512), (1536, 512), (2048, 64)]:
                    p, s = pp[k % 3], stg[k % 3]
                    for kc in range(8):
                        mm(P, p[:, 0:nb], b[:, kc, j * 128:(j + 1) * 128], hT[:, kc, n0:n0 + nb], kc == 0, kc == 7, [hT, b], [p])
                    act(P, s[:, 0:nb], p[:, 0:nb], AF.Sigmoid, [p, g.bgT], [s], bias=g.bgT[:, fch:fch + 1])
                    P.dma("pool", D["GT"][fch * 128:(fch + 1) * 128, n0:n0 + nb], s[:, 0:nb], reads=[s], writes=[D["GT_t"]])
                    k += 1
    P.barrier()
    st.close()

C_Q, C_K, C_V, C_QI, C_KI, C_WI = 3360, 4384, 5408, 6432, 6944, 7008


def rope(P, eng, out4, in4, cosb, sinb, tmp, n, reads, writes):
    H = in4.shape[1]
    x1, x2 = in4[:, :, 0, :], in4[:, :, 1, :]
    t = [tmp[0:n, i, 0:H * 32].rearrange("p (h d) -> p h d", h=H) for i in range(4)]
    tt(P, eng, t[0], x1, cosb, ALU.mult, reads, [tmp])
    tt(P, eng, t[1], x2, sinb, ALU.mult, reads, [tmp])
    tt(P, eng, t[2], x2, cosb, ALU.mult, reads, [tmp])
    tt(P, eng, t[3], x1, sinb, ALU.mult, reads, [tmp])
    tt(P, eng, out4[:, :, 0, :], t[0], t[1], ALU.subtract, [tmp], writes)
    tt(P, eng, out4[:, :, 1, :], t[2], t[3], ALU.add, [tmp], writes)


def stage3_dsa(P, g, D):
    st = ExitStack()
    knw = P.sb("knw", [128, 64], F32, st)
    qnw = P.sb("qnw", [128, 64], F32, st)
    P.dma("sp", knw[:], D["k_norm_w"][0:1, :].partition_broadcast(128), writes=[knw])
    P.dma("sp", qnw[:], D["q_norm_w"][0:1, :].partition_broadcast(128), writes=[qnw])
    pd = [P.sb(f"pd{i}", [128, 3656], F32, st) for i in range(2)]
    cs = [P.sb(f"cs{i}", [128, 2, 32], F32, st) for i in range(2)]
    junk = P.sb("djunk", [128, 1024], F32, st)
    ssq = P.sb("dssq", [128, 16], F32, st)
    rst = P.sb("drst", [128, 16], F32, st)
    kn = P.sb("dkn", [128, 1024], F32, st)
    ko = [P.sb(f"dko{i}", [128, 1024], F32, st) for i in range(2)]
    kio = [P.sb(f"dkio{i}", [128, 64], F32, st) for i in range(2)]
    tmp = P.sb("dtmp", [128, 4, 512], F32, st)
    for ti, (r0, n) in enumerate(TILES):
        p, c = pd[ti % 2], cs[ti % 2]
        P.dma("sp", p[0:n, :], D["Ptok"][r0:r0 + n, C_Q:TOKW], reads=[D["Ptok_t"].sub(ti)], writes=[p])
        P.dma("sp", c[0:n, 0, :], D["cos"][r0:r0 + n, :], writes=[c])
        P.dma("sp", c[0:n, 1, :], D["sin"][r0:r0 + n, :], writes=[c])
        P.dma("pool", D["vout"][r0:r0 + n, :], D["Ptok"][r0:r0 + n, C_V:C_V + 1024], reads=[D["Ptok_t"].sub(ti)])
        k = p[0:n, C_K - C_Q:C_K - C_Q + 1024]
        act(P, junk[0:n, :], k, AF.Square, [p], [junk])
        P.op("dve", lambda e, n=n: e.tensor_reduce(out=ssq[0:n, :], in_=junk[0:n, :].rearrange("p (h d) -> p h d", h=16), axis=AX.X, op=ALU.add), reads=[junk], writes=[ssq])
        act(P, ssq[0:n, :], ssq[0:n, :], AF.Sqrt, [ssq], [ssq], scale=1.0 / 64, bias=g.epsc[0:n, :])
        P.op("dve", lambda e, n=n: e.reciprocal(rst[0:n, :], ssq[0:n, :]), reads=[ssq], writes=[rst])
        kn3 = kn[0:n, :].rearrange("p (h d) -> p h d", h=16)
        tt(P, "dve", kn3, k.rearrange("p (h d) -> p h d", h=16), rst[0:n, :].unsqueeze(2).to_broadcast([n, 16, 64]), ALU.mult, [p, rst], [kn])
        tt(P, "dve", kn3, kn3, knw[0:n, :].unsqueeze(1).to_broadcast([n, 16, 64]), ALU.mult, [kn, knw], [kn])
        o = ko[ti % 2]
        cosb = c[0:n, 0, :].unsqueeze(1).to_broadcast([n, 16, 32])
        sinb = c[0:n, 1, :].unsqueeze(1).to_broadcast([n, 16, 32])
        rope(P, "dve", o[0:n, :].rearrange("p (h t d) -> p h t d", h=16, t=2), kn[0:n, :].rearrange("p (h t d) -> p h t d", h=16, t=2),
             cosb, sinb, tmp, n, [kn, c], [o])
        P.dma("pool", D["kout"][r0:r0 + n, :], o[0:n, :], reads=[o], writes=[D["kout_t"].sub(ti)])
        ki = p[0:n, C_KI - C_Q:C_KI - C_Q + 64]
        oi = kio[ti % 2]
        rope(P, "pool", oi[0:n, :].rearrange("p (h t d) -> p h t d", h=1, t=2), ki.rearrange("p (h t d) -> p h t d", h=1, t=2),
             c[0:n, 0, :].unsqueeze(1), c[0:n, 1, :].unsqueeze(1), tmp, n, [p, c], [oi])
        P.dma("pool", D["kidx"][r0:r0 + n, :], oi[0:n, :], reads=[oi], writes=[D["kidx_t"].sub(ti)])
    P.dma("pool", D["shift"][0:1, :], D["Ptok"][2047:2048, 0:RW_IN], reads=[D["Ptok_t"].sub(15)])
    for q in range(4):
        P.dma("pool", D["shift"][1 + q:2 + q, :], D["Ptok"][2048 + 16 * q + 15:2048 + 16 * q + 16, 0:RW_IN], reads=[D["Ptok_t"].sub(16)])
    P.barrier()
    st.close()


def red(P, eng, out, in_, op, reads, writes):
    return P.op(eng, lambda e: e.tensor_reduce(out=out, in_=in_, axis=AX.X, op=op), reads=reads, writes=writes)


def stt(P, eng, out, in0, scalar, in1, op0, op1, reads, writes):
    return P.op(eng, lambda e: e.scalar_tensor_tensor(out=out, in0=in0, scalar=scalar, in1=in1, op0=op0, op1=op1), reads=reads, writes=writes)


def recip(P, out, in_, reads, writes):
    return P.op("dve", lambda e: e.reciprocal(out, in_), reads=reads, writes=writes)


def mset(P, eng, ap, val, writes):
    return P.op(eng, lambda e: e.memset(ap, val), writes=writes)

import os
STOP = 99
SKIP = ''
NUX = 18
NU = 18
UC = NU * 128
GN_EPS = 64e-5


def unit_rows(u):
    if u < 16:
        return [(0, 128 * u, 128)]
    b = 2048 + 32 * (u - 16)
    return [(0, b, 16), (64, b + 16, 16)]


def bload(P, tile, ap, n=128):
    P.dma("sp", tile[0:n, :], ap.partition_broadcast(n), writes=[tile])


def stage3_rw(P, g, D):
    st = ExitStack()
    sb = lambda name, shape, dt=F32: P.sb(name, shape, dt, st)
    mub = sb("mub", [128, RW_IN]); bload(P, mub, D["mu_rw"][0:1, :])
    w0b = sb("w0b", [128, 1024]); bload(P, w0b, D["w0"][0:1, :])
    a0b = sb("a0b", [128, 1024]); bload(P, a0b, D["a0"][0:1, :])
    kkb = sb("kkb", [128, 1024]); bload(P, kkb, D["k_k"][0:1, :])
    kab = sb("kab", [128, 1024]); bload(P, kab, D["k_a"][0:1, :])
    rkb = sb("rkb", [128, 1024]); bload(P, rkb, D["r_k"][0:1, :])
    loraW = sb("loraW", [128, 1024])
    P.dma("sp", loraW[0:64, :], D["w_up"][:, :], writes=[loraW])
    P.dma("sp", loraW[64:128, :], D["a_up"][:, :], writes=[loraW])
    gup1 = sb("gup1", [128, 1024]); P.dma("sp", gup1[:], D["g_up"][0:128, :], writes=[gup1])
    gup2 = sb("gup2", [32, 1024]); P.dma("sp", gup2[:], D["g_up"][128:160, :], writes=[gup2])
    tri = sb("tri", [128, 128]); P.dma("sp", tri[:], D["m_tri"][:, :], writes=[tri])
    ones = sb("onesb", [128, 128]); P.dma("sp", ones[:], D["m_ones"][:, :], writes=[ones])
    valid = sb("valid", [128, 2]); P.dma("sp", valid[:], D["m_valid"][:, :], writes=[valid])
    tiny = sb("tiny12", [128, 1]); mset(P, "dve", tiny[:], 1e-12, [tiny])
    Pc = sb("Pc", [128, RW_IN]); Pp = sb("Pp", [128, RW_IN])
    L = sb("L288", [128, 288]); LT = sb("LT", [128, 3, 128])
    W = {nm: sb("w_" + nm, [128, 1024]) for nm in ["zt", "za", "gt", "lw", "ah", "kk", "junk", "kmod", "b", "t2", "eL", "eN", "eLm", "eC", "gC", "Lsb", "rt", "at", "kt", "bt", "kp", "bp"]}
    ssq = sb("ssq", [128, 16]); rn = sb("rn", [128, 16]); bc = sb("bc", [128, 16])
    stgT = [sb(f"stgT{i}", [128, 8, 128]) for i in range(2)]
    pLT = P.ps("pLT", [128, 3, 128], F32, st)
    pw = [P.ps(f"pw{i}", [128, 512], F32, st) for i in range(4)]
    pT = [P.ps(f"pTr{i}", [128, 4, 128], F32, st) for i in range(2)]
    pk = 0
    tk = 0
    for u in range(NUX):
        samp = u >= 16
        if samp:
            mset(P, "pool", Pc[:], 0.0, [Pc])
            mset(P, "pool", Pp[:], 0.0, [Pp])
        for (d0, t0, nt) in unit_rows(u):
            ti = 16 if samp else u
            P.dma("sp", Pc[d0:d0 + nt, :], D["Ptok"][t0:t0 + nt, 0:RW_IN], reads=[D["Ptok_t"].sub(ti)], writes=[Pc])
            if samp:
                q = (t0 - 2048) // 16
                P.dma("sp", Pp[d0:d0 + 1, :], D["sshift"][q:q + 1, :], writes=[Pp])
                P.dma("sp", Pp[d0 + 1:d0 + 16, :], D["Ptok"][t0:t0 + 15, 0:RW_IN], reads=[D["Ptok_t"].sub(16)], writes=[Pp])
            elif u == 0:
                mset(P, "pool", Pp[0:1, :], 0.0, [Pp])
                P.dma("sp", Pp[1:128, :], D["Ptok"][0:127, 0:RW_IN], reads=[D["Ptok_t"].sub(0)], writes=[Pp])
            else:
                P.dma("sp", Pp[:, :], D["Ptok"][t0 - 1:t0 + 127, 0:RW_IN], reads=[D["Ptok_t"].sub(u), D["Ptok_t"].sub(u - 1)], writes=[Pp])
        tt(P, "dve", Pp[:], Pp[:], Pc[:], ALU.subtract, [Pp, Pc], [Pp])
        tt(P, "pool", Pp[:], Pp[:], mub[:], ALU.mult, [Pp, mub], [Pp])
        tt(P, "dve", Pc[:], Pc[:], Pp[:], ALU.add, [Pp, Pc], [Pc])
        if STOP == 1:
            continue
        r, k, v = Pc[:, 0:1024], Pc[:, 1024:2048], Pc[:, 2048:3072]
        act(P, L[:, 0:64], Pc[:, 3072:3136], AF.Tanh, [Pc], [L])
        cp(P, "pool", L[:, 64:128], Pc[:, 3136:3200], [Pc], [L])
        act(P, L[:, 128:288], Pc[:, 3200:3360], AF.Sigmoid, [Pc], [L])
        tr(P, pLT[:, 0, :], L[:, 0:128], g.identf[:], [L, g.identf], [pLT])
        tr(P, pLT[:, 1, :], L[:, 128:256], g.identf[:], [L, g.identf], [pLT])
        tr(P, pLT[0:32, 2, :], L[:, 256:288], g.identf[:], [L, g.identf], [pLT])
        cp(P, "dve", LT[:, 0:2, :], pLT[:, 0:2, :], [pLT], [LT])
        cp(P, "dve", LT[0:32, 2, :], pLT[0:32, 2, :], [pLT], [LT])
        if STOP == 2:
            continue
        for hf in range(2):
            cs = slice(hf * 512, (hf + 1) * 512)
            p = pw[pk % 4]; pk += 1
            mm(P, p[:], LT[0:64, 0, :], loraW[0:64, cs], True, True, [LT, loraW], [p])
            tt(P, "dve", W["zt"][:, cs], p[:], w0b[:, cs], ALU.add, [p, w0b], [W["zt"]])
            p = pw[pk % 4]; pk += 1
            mm(P, p[:], LT[64:128, 0, :], loraW[64:128, cs], True, True, [LT, loraW], [p])
            tt(P, "dve", W["za"][:, cs], p[:], a0b[:, cs], ALU.add, [p, a0b], [W["za"]])
            p = pw[pk % 4]; pk += 1
            mm(P, p[:], LT[:, 1, :], gup1[:, cs], True, False, [LT, gup1], [p])
            mm(P, p[:], LT[0:32, 2, :], gup2[0:32, cs], False, True, [LT, gup2], [p])
            cp(P, "act", W["gt"][:, cs], p[:], [p], [W["gt"]])
        if STOP == 3:
            continue
        act(P, W["lw"][:], W["zt"][:], AF.Sigmoid, [W["zt"]], [W["lw"]])
        ts(P, "dve", W["lw"][:], W["lw"][:], -0.6065306597126334, valid[:, (1 if samp else 0):(2 if samp else 1)], ALU.mult, ALU.mult, [W["lw"], valid], [W["lw"]])
        if STOP == 31:
            continue
        act(P, W["ah"][:], W["za"][:], AF.Sigmoid, [W["za"]], [W["ah"]])
        if STOP == 32:
            continue
        tt(P, "pool", W["kk"][:], k, kkb[:], ALU.mult, [Pc, kkb], [W["kk"]])
        act(P, W["junk"][:], W["kk"][:], AF.Square, [W["kk"]], [W["junk"]])
        if STOP == 33:
            continue
        red(P, "dve", ssq[:], W["junk"][:].rearrange("p (h d) -> p h d", h=16), ALU.add, [W["junk"]], [ssq])
        act(P, ssq[:], ssq[:], AF.Sqrt, [ssq, tiny], [ssq], bias=tiny[:])
        recip(P, rn[:], ssq[:], [ssq], [rn])
        if STOP == 34:
            continue
        kk3 = W["kk"][:].rearrange("p (h d) -> p h d", h=16)
        tt(P, "dve", kk3, kk3, rn[:].unsqueeze(2).to_broadcast([128, 16, 64]), ALU.mult, [W["kk"], rn], [W["kk"]])
        if STOP == 35:
            continue
        stt(P, "dve", W["t2"][:], W["ah"][:], -1.0, kab[:], ALU.add, ALU.mult, [W["ah"], kab], [W["t2"]])
        stt(P, "dve", W["kmod"][:], W["t2"][:], 1.0, k, ALU.add, ALU.mult, [W["t2"], Pc], [W["kmod"]])
        if STOP == 36:
            continue
        tt(P, "pool", W["b"][:], W["kk"][:], W["ah"][:], ALU.mult, [W["kk"], W["ah"]], [W["b"]])
        tt(P, "pool", W["t2"][:], r, W["kmod"][:], ALU.mult, [Pc, W["kmod"]], [W["t2"]])
        tt(P, "pool", W["t2"][:], W["t2"][:], rkb[:], ALU.mult, [W["t2"], rkb], [W["t2"]])
        if STOP == 37:
            continue
        red(P, "dve", bc[:], W["t2"][:].rearrange("p (h d) -> p h d", h=16), ALU.add, [W["t2"]], [bc])
        P.dma("pool", D["BC"][u * 128:(u + 1) * 128, :], bc[:], reads=[bc], writes=[D["BC_t"].sub(u)])
        if STOP == 4:
            continue
        for hf in range(2):
            cs = slice(hf * 512, (hf + 1) * 512)
            pL = pw[pk % 4]; pk += 1
            pLt = pw[pk % 4]; pk += 1
            mm(P, pL[:], tri[:], W["lw"][:, cs], True, True, [tri, W["lw"]], [pL])
            mm(P, pLt[:], ones[:], W["lw"][:, cs], True, True, [ones, W["lw"]], [pLt])
            if "a1" not in SKIP:
                act(P, W["eL"][:, cs], pL[:], AF.Exp, [pL], [W["eL"]])
            if "a2" not in SKIP:
                act(P, W["eN"][:, cs], pL[:], AF.Exp, [pL], [W["eN"]], scale=-1.0)
            act(P, W["Lsb"][:, cs], pL[:], AF.Identity, [pL], [W["Lsb"]])
            if "a3" not in SKIP:
                act(P, W["gC"][:, cs], pLt[:], AF.Exp, [pLt], [W["gC"]])
            if "d1" not in SKIP:
                tt(P, "dve", W["eC"][:, cs], pLt[:], W["Lsb"][:, cs], ALU.subtract, [pLt, W["Lsb"]], [W["eC"]])
            if "d2" not in SKIP:
                tt(P, "dve", W["eLm"][:, cs], W["Lsb"][:, cs], W["lw"][:, cs], ALU.subtract, [W["Lsb"], W["lw"]], [W["eLm"]])
        if "exp2" not in SKIP:
            act(P, W["eC"][:], W["eC"][:], AF.Exp, [W["eC"]], [W["eC"]])
            act(P, W["eLm"][:], W["eLm"][:], AF.Exp, [W["eLm"]], [W["eLm"]])
        if STOP == 5:
            continue
        tt(P, "dve", W["rt"][:], r, W["eL"][:], ALU.mult, [Pc, W["eL"]], [W["rt"]])
        stt(P, "dve", W["at"][:], W["kk"][:], -1.0, W["eLm"][:], ALU.mult, ALU.mult, [W["kk"], W["eLm"]], [W["at"]])
        tt(P, "pool", W["kt"][:], W["kmod"][:], W["eN"][:], ALU.mult, [W["kmod"], W["eN"]], [W["kt"]])
        tt(P, "pool", W["bt"][:], W["b"][:], W["eN"][:], ALU.mult, [W["b"], W["eN"]], [W["bt"]])
        tt(P, "pool", W["kp"][:], W["kmod"][:], W["eC"][:], ALU.mult, [W["kmod"], W["eC"]], [W["kp"]])
        tt(P, "dve", W["bp"][:], W["b"][:], W["eC"][:], ALU.mult, [W["b"], W["eC"]], [W["bp"]])
        rows = slice(u * 128, (u + 1) * 128)
        P.dma("pool", D["Vs"][rows, :], v, reads=[Pc], writes=[D["Vs_t"].sub(u)])
        P.dma("pool", D["KPs"][rows, :], W["kp"][:], reads=[W["kp"]], writes=[D["KPs_t"].sub(u)])
        P.dma("pool", D["BPs"][rows, :], W["bp"][:], reads=[W["bp"]], writes=[D["BPs_t"].sub(u)])
        P.dma("pool", D["Gs"][rows, :], W["gt"][:], reads=[W["gt"]], writes=[D["Gs_t"].sub(u)])
        if STOP == 6:
            continue
        for nm, dst in [("rt", "RT"), ("at", "AT"), ("kt", "KT"), ("bt", "BT"), ("gC", "GCT")]:
            s = stgT[tk % 2]
            for half in range(2):
                p = pT[tk % 2]
                for j in range(4):
                    fc = half * 4 + j
                    tr(P, p[:, j, :], W[nm][:, fc * 128:(fc + 1) * 128], g.identf[:], [W[nm], g.identf], [p])
                cp(P, "act" if half else "dve", s[:, half * 4:(half + 1) * 4, :], p[:], [p], [s])
                tk += 1
            P.dma("pool", D[dst].rearrange("(fc p) c -> p fc c", p=128)[:, :, u * 128:(u + 1) * 128], s[:], reads=[s], writes=[D[dst + "_t"].sub(u)])
    P.barrier()
    st.close()


def stage4_scan(P, g, D):
    st = ExitStack()
    sb = lambda name, shape, dt=F32: P.sb(name, shape, dt, st)
    msl = sb("msl", [128, 128]); P.dma("sp", msl[:], D["m_sl"][:, :], writes=[msl])
    msu = sb("msu", [128, 128]); P.dma("sp", msu[:], D["m_su"][:, :], writes=[msu])
    mu = sb("mu", [128, 128]); P.dma("sp", mu[:], D["m_u"][:, :], writes=[mu])
    lnw = sb("lnw", [128, 1024]); bload(P, lnw, D["lnx_w"][0:1, :])
    lnb = sb("lnb", [128, 1024]); bload(P, lnb, D["lnx_b"][0:1, :])
    gne = sb("gne", [128, 1]); mset(P, "dve", gne[:], GN_EPS, [gne])
    H = sb("H", [128, 8, 64])
    mset(P, "dve", H[:], 0.0, [H.sub((a, b)) for a in range(8) for b in range(2)])
    FM = {nm: [sb(f"fm_{nm}{i}", [128, 8, 128]) for i in range(2)] for nm in ["RT", "AT", "KT", "BT", "GCT"]}
    TM = {nm: [sb(f"tm_{nm}{i}", [128, 1024]) for i in range(2)] for nm in ["Vs", "KPs", "BPs"]}
    Y = [sb(f"Y{i}", [128, 1024]) for i in range(2)]
    NM = 16
    GS = 6
    MS = [[sb(f"ms{s}_{i}", [128, 128]) for i in range(NM)] for s in range(GS)]
    XU = [[sb(f"xu{s}_{i}", [128, 64]) for i in range(4)] for s in range(GS)]
    pLane = [P.ps(f"pLane{i}", [128, 512], F32, st) for i in range(GS)]
    laneM = [[SlotAP(pLane[l], pLane[l][:, j * 128:(j + 1) * 128]) for j in range(2)] for l in range(GS)]
    laneS = [[SlotAP(pLane[l], pLane[l][:, 256 + j * 64:256 + (j + 1) * 64]) for j in range(4)] for l in range(GS)]
    gt = sb("p_gt", [128, 1024]); bc = sb("p_bc", [128, 16]); vv = None
    pw = {nm: sb("p_" + nm, [128, 1024]) for nm in ["yc", "sq", "yb"]}
    st16 = {nm: sb("p16_" + nm, [128, 16]) for nm in ["mean", "var", "rstd"]}
    oab = sb("oab", [128, 1024], BF16)
    pO = [P.ps(f"pO{i}", [128, 8, 128], BF16, st) for i in range(1)]
    oT = sb("oT", [128, 8, 128], BF16)
    Ssb = [sb(f"Ssb{i}", [128, 64]) for i in range(GS)]
    Sout = sb("Sout", [128, 8, 64])

    def head_gen(u, fc, hp, lane, FMu, TMu, y):
        samp = u >= 16
        RT, AT, KT, BT, GC = FMu
        V, KP, BP = TMu
        mk = [0]; sk = [0]

        def nextM():
            mk[0] += 1
            return laneM[lane][mk[0] % 2]

        def nextS():
            sk[0] += 1
            return laneS[lane][sk[0] % 4]

        h = 2 * fc + hp
        pb = hp * 64
        hs = slice(h * 64, (h + 1) * 64)
        M = MS[lane]
        a_ = AT[pb:pb + 64, fc, :]; b_ = BT[pb:pb + 64, fc, :]; k_ = KT[pb:pb + 64, fc, :]; r_ = RT[pb:pb + 64, fc, :]
        A, N, AakT, RBt, RKt = M[0], M[1], M[2], M[3], M[4]
        for (dst, l, r, msk) in [(A, a_, b_, msl), (N, b_, a_, msu), (AakT, k_, a_, msu), (RBt, b_, r_, mu), (RKt, k_, r_, mu)]:
            p = nextM()
            mm(P, p[:], l, r, True, True, [AT, BT, KT, RT], [p])
            tt(P, "dve", dst[:], p[:], msk[:], ALU.mult, [p, msk], [dst])
            yield
        Ap = [A, M[5], M[6], M[7], M[8]]
        Np = [N, M[9], M[10], M[11], M[12], M[13]]
        for k in range(5):
            if k < 4:
                p = nextM()
                mm(P, p[:], Np[k][:], Ap[k][:], True, True, [Np[k], Ap[k]], [p])
                cp(P, "act", Ap[k + 1][:], p[:], [p], [Ap[k + 1]])
            p = nextM()
            mm(P, p[:], Ap[k][:], Np[k][:], True, True, [Np[k], Ap[k]], [p])
            cp(P, "dve" if k % 2 else "act", Np[k + 1][:], p[:], [p], [Np[k + 1]])
            yield
        Tt = [M[14], M[15]]
        tt(P, "dve", Tt[0][:], Np[5][:], g.identf[:], ALU.add, [Np[5], g.identf], [Tt[0]])
        cur = 0
        for k in [4, 3, 2, 1, 0]:
            p = nextM()
            mm(P, p[:], Ap[k][:], Tt[cur][:], True, True, [Ap[k], Tt[cur]], [p])
            tt(P, "dve", Tt[1 - cur][:], p[:], Tt[cur][:], ALU.add, [p, Tt[cur]], [Tt[1 - cur]])
            cur = 1 - cur
            yield
        TT_ = Tt[cur]
        for c in range(2):
            pc = c * 64
            cc = slice(c * 64, (c + 1) * 64)
            Hh = H[pb:pb + 64, fc, :]
            Hd = H.sub((fc, hp))
            if samp:
                q = 2 * (u - 16) + c
                P.dma("sp", Ssb[lane][pb:pb + 64, :], D["swkv"][q, h, :, :], writes=[Ssb[lane]])
                p = nextS()
                mm(P, p[pb:pb + 64, :], Ssb[lane][pb:pb + 64, :], g.identf[pb:pb + 64, pb:pb + 64], True, True, [Ssb[lane], g.identf], [p])
                cp(P, "act", Hh, p[pb:pb + 64, :], [p], [Hd])
                yield
            X_sb, U_sb = XU[lane][2 * c], XU[lane][2 * c + 1]
            p = nextS()
            mm(P, p[pc:pc + 64, :], a_[:, cc], Hh, True, False, [AT, Hd], [p])
            mm(P, p[pc:pc + 64, :], AakT[pc:pc + 64, cc], V[pc:pc + 64, hs], False, True, [AakT, V], [p])
            cp(P, "act", X_sb[pc:pc + 64, :], p[pc:pc + 64, :], [p], [X_sb])
            yield
            p = nextS()
            mm(P, p[pc:pc + 64, :], TT_[pc:pc + 64, cc], X_sb[pc:pc + 64, :], True, True, [TT_, X_sb], [p])
            cp(P, "act", U_sb[pc:pc + 64, :], p[pc:pc + 64, :], [p], [U_sb])
            yield
            p = nextS()
            mm(P, p[pc:pc + 64, :], r_[:, cc], Hh, True, False, [RT, Hd], [p])
            mm(P, p[pc:pc + 64, :], RBt[pc:pc + 64, cc], U_sb[pc:pc + 64, :], False, False, [RBt, U_sb], [p])
            mm(P, p[pc:pc + 64, :], RKt[pc:pc + 64, cc], V[pc:pc + 64, hs], False, True, [RKt, V], [p])
            cp(P, "act", y[pc:pc + 64, hs], p[pc:pc + 64, :], [p], [y.sub(h)])
            p = nextS()
            mm(P, p[pb:pb + 64, :], BP[pc:pc + 64, hs], U_sb[pc:pc + 64, :], True, False, [BP, U_sb], [p])
            mm(P, p[pb:pb + 64, :], KP[pc:pc + 64, hs], V[pc:pc + 64, hs], False, True, [KP, V], [p])
            stt(P, "dve", Hh, Hh, GC[pb:pb + 64, fc, c * 64:c * 64 + 1], p[pb:pb + 64, :], ALU.mult, ALU.add, [Hd, GC, p], [Hd])
            yield
            if samp or (u == 15 and c == 1):
                q = (1 + 2 * (u - 16) + c) if samp else 0
                p = nextS()
                mm(P, p[pb:pb + 64, :], Hh, g.identf[pb:pb + 64, pb:pb + 64], True, True, [Hd, g.identf], [p])
                cp(P, "act", Sout[pb:pb + 64, fc, :], p[pb:pb + 64, :], [p], [Sout.sub(h)])
                r0 = (q * 16 + h) * 64
                P.dma("pool", D["wkv"][r0:r0 + 64, :], Sout[pb:pb + 64, fc, :], reads=[Sout.sub(h)])
                yield

    for u in range(NU):
        samp = u >= 16
        b2 = u % 2
        cols = slice(u * 128, (u + 1) * 128)
        for nm in FM:
            P.dma("sp", FM[nm][b2][:], D[nm].rearrange("(fc p) c -> p fc c", p=128)[:, :, cols], reads=[D[nm + "_t"].sub(u)], writes=[FM[nm][b2]])
        for nm in TM:
            P.dma("sp", TM[nm][b2][:], D[nm][cols, :], reads=[D[nm + "_t"].sub(u)], writes=[TM[nm][b2]])
        FMu = [FM[nm][b2] for nm in ["RT", "AT", "KT", "BT", "GCT"]]
        TMu = [TM[nm][b2] for nm in ["Vs", "KPs", "BPs"]]
        V = TMu[0]
        y = Y[b2]
        heads = [(fc, hp) for fc in range(8) for hp in range(2)]
        active = []
        nxt = 0
        for lane in range(GS):
            fc, hp = heads[nxt]; nxt += 1
            active.append((lane, head_gen(u, fc, hp, lane, FMu, TMu, y)))
        while active:
            still = []
            for lane, gen in active:
                try:
                    next(gen)
                    still.append((lane, gen))
                except StopIteration:
                    if nxt < len(heads):
                        fc, hp = heads[nxt]; nxt += 1
                        still.append((lane, head_gen(u, fc, hp, lane, FMu, TMu, y)))
            active = still
        ysubs = [y.sub(h) for h in range(16)]
        P.dma("sp", gt[:], D["Gs"][cols, :], reads=[D["Gs_t"].sub(u)], writes=[gt])
        P.dma("sp", bc[:], D["BC"][cols, :], reads=[D["BC_t"].sub(u)], writes=[bc])
        y3 = y[:].rearrange("p (h d) -> p h d", h=16)
        bcast = lambda t16: t16[:].unsqueeze(2).to_broadcast([128, 16, 64])
        v3 = lambda t: t[:].rearrange("p (h d) -> p h d", h=16)
        red(P, "dve", st16["mean"][:], y3, ALU.add, ysubs, [st16["mean"]])
        ts(P, "dve", st16["mean"][:], st16["mean"][:], 1.0 / 64, None, ALU.mult, None, [st16["mean"]], [st16["mean"]])
        tt(P, "dve", v3(pw["yc"]), y3, bcast(st16["mean"]), ALU.subtract, ysubs + [st16["mean"]], [pw["yc"]])
        act(P, pw["sq"][:], pw["yc"][:], AF.Square, [pw["yc"]], [pw["sq"]])
        red(P, "dve", st16["var"][:], v3(pw["sq"]), ALU.add, [pw["sq"]], [st16["var"]])
        act(P, st16["var"][:], st16["var"][:], AF.Sqrt, [st16["var"], gne], [st16["var"]], scale=1.0 / 64, bias=gne[:])
        recip(P, st16["rstd"][:], st16["var"][:], [st16["var"]], [st16["rstd"]])
        tt(P, "dve", v3(pw["yc"]), v3(pw["yc"]), bcast(st16["rstd"]), ALU.mult, [pw["yc"], st16["rstd"]], [pw["yc"]])
        tt(P, "pool", pw["yc"][:], pw["yc"][:], lnw[:], ALU.mult, [pw["yc"], lnw], [pw["yc"]])
        tt(P, "pool", pw["yc"][:], pw["yc"][:], lnb[:], ALU.add, [pw["yc"], lnb], [pw["yc"]])
        tt(P, "dve", v3(pw["yb"]), v3(V), bcast(bc), ALU.mult, [V, bc], [pw["yb"]])
        tt(P, "pool", pw["yc"][:], pw["yc"][:], pw["yb"][:], ALU.add, [pw["yc"], pw["yb"]], [pw["yc"]])
        tt(P, "dve", oab[:], pw["yc"][:], gt[:], ALU.mult, [pw["yc"], gt], [oab])
        if D.get("OAdbg") is not None:
            P.dma("pool", D["OAdbg"][cols, :], pw["yc"][:], reads=[pw["yc"]])
        for fc in range(8):
            tr(P, pO[0][:, fc, :], oab[:, fc * 128:(fc + 1) * 128], g.identb[:], [oab, g.identb], [pO[0]])
        cp(P, "act", oT[:], pO[0][:], [pO[0]], [oT])
        dstv = D["OAT"].rearrange("(fc p) c -> p fc c", p=128)
        for (d0, t0, nt) in unit_rows(u):
            P.dma("pool", dstv[:, :, t0:t0 + nt], oT[:, :, d0:d0 + nt], reads=[oT], writes=[D["OAT_t"]])
    P.barrier()
    st.close()

TOPK = 256
NEG = -1.0e30
NBIS = 18


def headnorm(P, src, dst, nw, junk, ssq, rst, g, n, reads, pre=1.0):
    act(P, junk[0:n, :], src, AF.Square, reads, [junk])
    red(P, "dve", ssq[0:n, :], junk[0:n, :].rearrange("p (h d) -> p h d", h=16), ALU.add, [junk], [ssq])
    act(P, ssq[0:n, :], ssq[0:n, :], AF.Sqrt, [ssq], [ssq], scale=1.0 / 64, bias=g.epsc[0:n, :])
    recip(P, rst[0:n, :], ssq[0:n, :], [ssq], [rst])
    d3 = dst.rearrange("p (h d) -> p h d", h=16)
    tt(P, "dve", d3, src.rearrange("p (h d) -> p h d", h=16), rst[0:n, :].unsqueeze(2).to_broadcast([n, 16, 64]), ALU.mult, list(reads) + [rst], [junk])
    stt(P, "dve", d3, d3, pre, nw[0:n, :].unsqueeze(1).to_broadcast([n, 16, 64]), ALU.mult, ALU.mult, [junk, nw], [junk])


def stage3_dsa(P, g, D):
    st = ExitStack()
    sb = lambda name, shape, dt=F32: P.sb(name, shape, dt, st)
    knw = sb("knw", [128, 64]); qnw = sb("qnw", [128, 64])
    P.dma("sp", knw[:], D["k_norm_w"][0:1, :].partition_broadcast(128), writes=[knw])
    P.dma("sp", qnw[:], D["q_norm_w"][0:1, :].partition_broadcast(128), writes=[qnw])
    pd = [sb(f"pd{i}", [128, 3656]) for i in range(2)]
    cs = [sb(f"cs{i}", [128, 2, 32]) for i in range(2)]
    junk = sb("djunk", [128, 1024]); ssq = sb("dssq", [128, 16]); rst = sb("drst", [128, 16])
    nrm = sb("dnrm", [128, 1024])
    ko = [sb(f"dko{i}", [128, 1024]) for i in range(2)]
    qo = sb("dqo", [128, 1024]); qio = sb("dqio", [128, 512])
    kio = [sb(f"dkio{i}", [128, 64]) for i in range(2)]
    wio = [sb(f"dwio{i}", [128, 8]) for i in range(2)]
    tmp = sb("dtmp", [128, 4, 512])
    cat = sb("dcat", [128, 2624], BF16)
    pT = [P.ps(f"dpT{i}", [128, 8, 128], BF16, st) for i in range(2)]
    sT = [sb(f"dsT{i}", [128, 8, 128], BF16) for i in range(2)]
    tk = 0
    for ti, (r0, n) in enumerate(TILES):
        p, c = pd[ti % 2], cs[ti % 2]
        P.dma("sp", p[0:n, :], D["Ptok"][r0:r0 + n, C_Q:TOKW], reads=[D["Ptok_t"].sub(ti)], writes=[p])
        P.dma("sp", c[0:n, 0, :], D["cos"][r0:r0 + n, :], writes=[c])
        P.dma("sp", c[0:n, 1, :], D["sin"][r0:r0 + n, :], writes=[c])
        P.dma("pool", D["vout"][r0:r0 + n, :], D["Ptok"][r0:r0 + n, C_V:C_V + 1024], reads=[D["Ptok_t"].sub(ti)])
        cosb = c[0:n, 0, :].unsqueeze(1).to_broadcast([n, 16, 32])
        sinb = c[0:n, 1, :].unsqueeze(1).to_broadcast([n, 16, 32])
        r4 = lambda ap, H: ap.rearrange("p (h t d) -> p h t d", h=H, t=2)
        headnorm(P, p[0:n, C_K - C_Q:C_K - C_Q + 1024], junk[0:n, :], knw, junk, ssq, rst, g, n, [p])
        o = ko[ti % 2]
        rope(P, "dve", r4(o[0:n, :], 16), r4(junk[0:n, :], 16), cosb, sinb, tmp, n, [junk, c], [o])
        P.dma("pool", D["kout"][r0:r0 + n, :], o[0:n, :], reads=[o], writes=[D["kout_t"].sub(ti)])
        cp(P, "pool", cat[0:n, 1024:2048], o[0:n, :], [o], [cat])
        headnorm(P, p[0:n, 0:1024], junk[0:n, :], qnw, junk, ssq, rst, g, n, [p], pre=0.125)
        rope(P, "dve", r4(qo[0:n, :], 16), r4(junk[0:n, :], 16), cosb, sinb, tmp, n, [junk, c], [qo])
        cp(P, "pool", cat[0:n, 0:1024], qo[0:n, :], [qo], [cat])
        rope(P, "pool", r4(qio[0:n, :], 8), r4(p[0:n, C_QI - C_Q:C_QI - C_Q + 512], 8), c[0:n, 0, :].unsqueeze(1).to_broadcast([n, 8, 32]),
             c[0:n, 1, :].unsqueeze(1).to_broadcast([n, 8, 32]), tmp, n, [p, c], [qio])
        cp(P, "pool", cat[0:n, 2048:2560], qio[0:n, :], [qio], [cat])
        oi = kio[ti % 2]
        rope(P, "pool", r4(oi[0:n, :], 1), r4(p[0:n, C_KI - C_Q:C_KI - C_Q + 64], 1), c[0:n, 0, :].unsqueeze(1), c[0:n, 1, :].unsqueeze(1), tmp, n, [p, c], [oi])
        P.dma("pool", D["kidx"][r0:r0 + n, :], oi[0:n, :], reads=[oi], writes=[D["kidx_t"].sub(ti)])
        cp(P, "pool", cat[0:n, 2560:2624], oi[0:n, :], [oi], [cat])
        w = wio[ti % 2]
        ts(P, "dve", w[0:n, :], p[0:n, C_WI - C_Q:C_WI - C_Q + 8], 512.0 ** -0.5, None, ALU.mult, None, [p], [w])
        P.dma("pool", D["WI"][r0:r0 + n, :], w[0:n, :], reads=[w], writes=[D["WI_t"].sub(ti)])
        for (c0, nch, dst) in [(0, 8, "QT"), (1024, 8, "KTn"), (2048, 4, "QIT")]:
            pt, s_ = pT[tk % 2], sT[tk % 2]; tk += 1
            for j in range(nch):
                tr(P, pt[:, j, 0:n], cat[0:n, c0 + j * 128:c0 + (j + 1) * 128], g.identb[0:n, 0:n], [cat, g.identb], [pt])
            cp(P, "act", s_[:, 0:nch, 0:n], pt[:, 0:nch, 0:n], [pt], [s_])
            P.dma("pool", D[dst].rearrange("(fc p) c -> p fc c", p=128)[:, :, r0:r0 + n], s_[:, 0:nch, 0:n], reads=[s_], writes=[D[dst + "_t"].sub(ti)])
        pt, s_ = pT[tk % 2], sT[tk % 2]; tk += 1
        tr(P, pt[0:64, 0, 0:n], cat[0:n, 2560:2624], g.identb[0:n, 0:n], [cat, g.identb], [pt])
        cp(P, "act", s_[0:64, 0, 0:n], pt[0:64, 0, 0:n], [pt], [s_])
        P.dma("pool", D["KIT"][:, r0:r0 + n], s_[0:64, 0, 0:n], reads=[s_], writes=[D["KIT_t"].sub(ti)])
    P.dma("pool", D["shift"][0:1, :], D["Ptok"][2047:2048, 0:RW_IN], reads=[D["Ptok_t"].sub(15)])
    for q in range(4):
        P.dma("pool", D["shift"][1 + q:2 + q, :], D["Ptok"][2048 + 16 * q + 15:2048 + 16 * q + 16, 0:RW_IN], reads=[D["Ptok_t"].sub(16)])
    P.barrier()
    st.close()


def stage5_attn(P, g, D):
    st = ExitStack()
    sb = lambda name, shape, dt=F32: P.sb(name, shape, dt, st)
    kT = sb("kT", [128, 8, 2064], BF16)
    Vb = sb("Vb", [128, 17, 16, 65], BF16)
    kiT = sb("kiT", [128, 2064], BF16)
    Ibufs = [sb(f"Ibuf{i}", [128, 2064]) for i in range(2)]; junkI = sb("junkI", [128, 2064], BF16)
    maskbs = [sb(f"maskb{i}", [128, 2064], BF16) for i in range(2)]; maskTs = [sb(f"maskT{i}", [128, 17, 128], BF16) for i in range(2)]
    qT = [sb(f"qT{i}", [128, 8, 128], BF16) for i in range(2)]
    qiT = [sb(f"qiT{i}", [128, 4, 128], BF16) for i in range(2)]
    wi = [sb(f"wi{i}", [128, 8]) for i in range(2)]
    rl = [sb(f"rl{i}", [128, 512]) for i in range(2)]
    E = [sb(f"E{i}", [128, 4, 128], BF16) for i in range(3)]
    p2 = sb("pow2", [128, NBIS]); P.dma("sp", p2[:], D["m_pow2"][:, :], writes=[p2])
    s1s = [{nm: sb(f"s1_{nm}{i}", [128, 1]) for nm in ["B", "mid", "cnt", "t2", "t3", "thr"]} for i in range(2)]
    dtabs = [sb(f"dtab{i}", [128, NBIS]) for i in range(2)]
    osb = sb("osb", [128, 1024]); osbb = sb("osbb", [128, 1024], BF16); rec = sb("orec", [128, 4])
    oT = sb("oTb", [128, 8, 128], BF16)
    vst = [sb(f"vst{i}", [128, 1024]) for i in range(2)]
    kst = sb("kstb", [128, 1024], BF16); kis = sb("kis", [128, 64]); kisb = sb("kisb", [128, 128], BF16)
    pI = [P.ps(f"pI{i}", [128, 512], F32, st) for i in range(2)]
    pS = [P.ps(f"pSc{i}", [128, 4, 128], F32, st) for i in range(2)]
    pO = [P.ps(f"pOa{i}", [128, 4, 65], F32, st) for i in range(2)]
    pmT = P.ps("pmT", [128, 8, 128], BF16, st)
    mset(P, "pool", Vb[:, :, :, 64:65], 1.0, [Vb])
    cnt = {"rl": 0, "E": 0, "S": 0, "I": 0, "v": 0}

    def load_v_block(j, src_ap, nrow):
        v = vst[cnt["v"] % 2]; cnt["v"] += 1
        P.dma("sp", v[0:nrow, :], src_ap, writes=[v])
        cp(P, "pool", Vb[0:nrow, j, :, 0:64], v[0:nrow, :].rearrange("p (h d) -> p h d", h=16), [v], [Vb])

    def phase1(nq, tcol, nblk, lastw, prompt_tile, obt_cols, par):
        S = (nblk - 1) * 128 + lastw
        q_, qi_, w_ = qT[par], qiT[par], wi[par]
        Ibuf, maskb, maskT, s1, dtab = Ibufs[par], maskbs[par], maskTs[par], s1s[par], dtabs[par]
        P.dma("sp", q_[:, :, 0:nq], D["QT"].rearrange("(fc p) c -> p fc c", p=128)[:, :, tcol:tcol + nq], reads=[D["QT_t"]], writes=[q_])
        P.dma("sp", qi_[:, :, 0:nq], D["QIT"].rearrange("(fc p) c -> p fc c", p=128)[:, :, tcol:tcol + nq], reads=[D["QIT_t"]], writes=[qi_])
        P.dma("sp", w_[0:nq, :], D["WI"][tcol:tcol + nq, :], reads=[D["WI_t"]], writes=[w_])
        for s0 in range(0, S, 512):
            w = min(512, S - s0)
            for h in range(8):
                pb = (h % 2) * 64
                p = pI[cnt["I"] % 2]; cnt["I"] += 1
                mm(P, p[0:nq, 0:w], qi_[pb:pb + 64, h // 2, 0:nq], kiT[pb:pb + 64, s0:s0 + w], True, True, [qi_, kiT], [p])
                r = rl[cnt["rl"] % 2]; cnt["rl"] += 1
                act(P, r[0:nq, 0:w], p[0:nq, 0:w], AF.Relu, [p], [r])
                if h == 0:
                    ts(P, "dve", Ibuf[0:nq, s0:s0 + w], r[0:nq, 0:w], w_[0:nq, 0:1], None, ALU.mult, None, [r, w_], [Ibuf])
                else:
                    stt(P, "dve", Ibuf[0:nq, s0:s0 + w], r[0:nq, 0:w], w_[0:nq, h:h + 1], Ibuf[0:nq, s0:s0 + w], ALU.mult, ALU.add, [r, w_, Ibuf], [Ibuf])
        P.op("dve", lambda e: e.tensor_reduce(out=s1["B"][0:nq, :], in_=Ibuf[0:nq, 0:S], axis=AX.X, op=ALU.max, apply_absolute_value=True), reads=[Ibuf], writes=[s1["B"]])
        ts(P, "dve", s1["B"][0:nq, :], s1["B"][0:nq, :], 1.001, 1e-6, ALU.mult, ALU.add, [s1["B"]], [s1["B"]])
        ts(P, "dve", dtab[0:nq, :], p2[0:nq, :], s1["B"][0:nq, :], None, ALU.mult, None, [p2, s1["B"]], [dtab])
        if prompt_tile:
            mset(P, "dve", Ibuf[0:64, S - 64:S], NEG, [Ibuf])
        mset(P, "dve", s1["mid"][0:nq, :], 0.0, [s1["mid"]])
        for k in range(NBIS):
            ts(P, "dve", junkI[0:nq, 0:S], Ibuf[0:nq, 0:S], s1["mid"][0:nq, :], None, ALU.is_ge, ALU.add, [Ibuf, s1["mid"]], [junkI, s1["cnt"]], accum_out=s1["cnt"][0:nq, :])
            ts(P, "dve", s1["t2"][0:nq, :], s1["cnt"][0:nq, :], TOPK - 0.5, 2.0, ALU.is_ge, ALU.mult, [s1["cnt"]], [s1["t2"]])
            ts(P, "dve", s1["t3"][0:nq, :], s1["t2"][0:nq, :], -1.0, dtab[0:nq, k:k + 1], ALU.add, ALU.mult, [s1["t2"], dtab], [s1["t3"]])
            tt(P, "dve", s1["mid"][0:nq, :], s1["mid"][0:nq, :], s1["t3"][0:nq, :], ALU.add, [s1["mid"], s1["t3"]], [s1["mid"]])
        tt(P, "dve", s1["thr"][0:nq, :], s1["mid"][0:nq, :], dtab[0:nq, NBIS - 1:NBIS], ALU.subtract, [s1["mid"], dtab], [s1["thr"]])
        ts(P, "dve", maskb[0:nq, 0:S], Ibuf[0:nq, 0:S], s1["thr"][0:nq, :], None, ALU.is_ge, None, [Ibuf, s1["thr"]], [maskb])
        for j0 in range(0, nblk, 8):
            nb_ = min(8, nblk - j0)
            for jj in range(nb_):
                j = j0 + jj
                wj = 128 if j < nblk - 1 else lastw
                tr(P, pmT[0:wj, jj, 0:nq], maskb[0:nq, j * 128:j * 128 + wj], g.identb[0:nq, 0:nq], [maskb, g.identb], [pmT])
            full = nb_ if (j0 + nb_ < nblk or lastw == 128) else nb_ - 1
            if full > 0:
                cp(P, "act", maskT[:, j0:j0 + full, 0:nq], pmT[:, 0:full, 0:nq], [pmT], [maskT])
            if full < nb_:
                cp(P, "act", maskT[0:lastw, j0 + full, 0:nq], pmT[0:lastw, full, 0:nq], [pmT], [maskT])

    def phase2(nq, tcol, nblk, lastw, prompt_tile, obt_cols, par):
        q_ = qT[par]
        maskT = maskTs[par]
        items = [(h, j0) for h in range(16) for j0 in range(0, nblk, 4)]

        def qk(it):
            h, j0 = it
            pb = (h % 2) * 64
            p = pS[cnt["S"] % 2]; cnt["S"] += 1
            for jj in range(min(4, nblk - j0)):
                j = j0 + jj
                wj = 128 if j < nblk - 1 else lastw
                mm(P, p[0:wj, jj, 0:nq], kT[pb:pb + 64, h // 2, j * 128:j * 128 + wj], q_[pb:pb + 64, h // 2, 0:nq], True, True, [kT, q_], [p])
            return p

        pcur = qk(items[0])
        for k, it in enumerate(items):
            pnext = qk(items[k + 1]) if k + 1 < len(items) else None
            h, j0 = it
            nb_ = min(4, nblk - j0)
            e = E[cnt["E"] % 3]; cnt["E"] += 1
            full = nb_ if (j0 + nb_ < nblk or lastw == 128) else nb_ - 1
            if full > 0:
                act(P, e[:, 0:full, 0:nq], pcur[:, 0:full, 0:nq], AF.Exp, [pcur], [e])
                tt(P, "pool" if k % 3 == 2 else "dve", e[:, 0:full, 0:nq], e[:, 0:full, 0:nq], maskT[:, j0:j0 + full, 0:nq], ALU.mult, [e, maskT], [e])
            if full < nb_:
                act(P, e[0:lastw, full, 0:nq], pcur[0:lastw, full, 0:nq], AF.Exp, [pcur], [e])
                tt(P, "dve", e[0:lastw, full, 0:nq], e[0:lastw, full, 0:nq], maskT[0:lastw, j0 + full, 0:nq], ALU.mult, [e, maskT], [e])
            po = pO[(h // 4) % 2]
            for jj in range(nb_):
                j = j0 + jj
                wj = 128 if j < nblk - 1 else lastw
                mm(P, po[0:nq, h % 4, :], e[0:wj, jj, 0:nq], Vb[0:wj, j, h, :], j == 0, j == nblk - 1, [e, Vb], [po])
            if j0 + nb_ >= nblk and h % 4 == 3:
                recip(P, rec[0:nq, :], po[0:nq, :, 64], [po], [rec])
                tt(P, "dve", osb[0:nq, (h - 3) * 64:(h + 1) * 64].rearrange("p (h d) -> p h d", h=4), po[0:nq, :, 0:64],
                   rec[0:nq, :].unsqueeze(2).to_broadcast([nq, 4, 64]), ALU.mult, [po, rec], [osb])
            pcur = pnext
        if D.get("OBdbg") is not None:
            P.dma("pool", D["OBdbg"][obt_cols:obt_cols + nq, :], osb[0:nq, :], reads=[osb])
        cp(P, "pool", osbb[0:nq, :], osb[0:nq, :], [osb], [osbb])
        for fc in range(8):
            tr(P, pmT[:, fc, 0:nq], osbb[0:nq, fc * 128:(fc + 1) * 128], g.identb[0:nq, 0:nq], [osbb, g.identb], [pmT])
        cp(P, "act", oT[:, :, 0:nq], pmT[:, :, 0:nq], [pmT], [oT])
        P.dma("pool", D["OBT"].rearrange("(fc p) c -> p fc c", p=128)[:, :, obt_cols:obt_cols + nq], oT[:, :, 0:nq], reads=[oT], writes=[D["OBT_t"]])

    P.dma("sp", kT[:, :, 0:2048], D["KTn"].rearrange("(fc p) c -> p fc c", p=128)[:, :, 0:2048], reads=[D["KTn_t"]], writes=[kT])
    P.dma("sp", kiT[0:64, 0:2048], D["KIT"][:, 0:2048], reads=[D["KIT_t"]], writes=[kiT])
    P.dma("sp", kiT[64:128, 0:2048], D["KIT"][:, 0:2048], reads=[D["KIT_t"]], writes=[kiT])
    for j in range(16):
        load_v_block(j, D["Ptok"][j * 128:(j + 1) * 128, C_V:C_V + 1024], 128)
    NPT = 16
    args = [(128, 128 * i, i + 1, 128, True, 128 * i, i % 2) for i in range(NPT)]
    if args:
        phase1(*args[0])
    for i in range(len(args)):
        if i + 1 < len(args):
            phase1(*args[i + 1])
        phase2(*args[i])
    NSQ = 4
    for q in range(NSQ):
        tok = 2048 + 16 * q
        for j in range(16):
            v = vst[cnt["v"] % 2]; cnt["v"] += 1
            P.dma("sp", v[:, :], D["cache_k"][q, j * 128:(j + 1) * 128, :], writes=[v])
            cp(P, "pool", kst[:, :], v[:, :], [v], [kst])
            for fc in range(8):
                tr(P, pmT[:, fc, :], kst[:, fc * 128:(fc + 1) * 128], g.identb[:], [kst, g.identb], [pmT])
            cp(P, "act", kT[:, :, j * 128:(j + 1) * 128], pmT[:, :, :], [pmT], [kT])
            load_v_block(j, D["cache_v"][q, j * 128:(j + 1) * 128, :], 128)
            P.dma("sp", kis[:, :], D["cache_kidx"][q, j * 128:(j + 1) * 128, :], writes=[kis])
            cp(P, "dve", kisb[:, 0:64], kis[:, :], [kis], [kisb])
            cp(P, "dve", kisb[:, 64:128], kis[:, :], [kis], [kisb])
            tr(P, pmT[:, 0, :], kisb[:, :], g.identb[:], [kisb, g.identb], [pmT])
            cp(P, "act", kiT[:, j * 128:(j + 1) * 128], pmT[:, 0, :], [pmT], [kiT])
        P.dma("sp", kT[:, :, 2048:2064], D["KTn"].rearrange("(fc p) c -> p fc c", p=128)[:, :, tok:tok + 16], reads=[D["KTn_t"]], writes=[kT])
        P.dma("sp", kiT[0:64, 2048:2064], D["KIT"][:, tok:tok + 16], reads=[D["KIT_t"]], writes=[kiT])
        P.dma("sp", kiT[64:128, 2048:2064], D["KIT"][:, tok:tok + 16], reads=[D["KIT_t"]], writes=[kiT])
        load_v_block(16, D["Ptok"][tok:tok + 16, C_V:C_V + 1024], 16)
        phase1(16, tok, 17, 16, False, tok, q % 2)
        phase2(16, tok, 17, 16, False, tok, q % 2)
    P.barrier()
    st.close()


def load_w_bf16(P, dst, src_ap, stg, k0):
    for hf in range(2):
        s = stg[(k0 + hf) % 2]
        P.dma("sp", s[:], src_ap[:, hf * 512:(hf + 1) * 512].rearrange("(kc p) c -> p kc c", p=128), writes=[s])
        cp(P, "pool" if hf else "dve", dst[:, :, hf * 512:(hf + 1) * 512], s[:], [s], [dst])


def stage6(P, g, D):
    st = ExitStack()
    sb = lambda name, shape, dt=F32: P.sb(name, shape, dt, st)
    wpa = sb("wpa", [128, 8, 1024], BF16); wpb = sb("wpb", [128, 8, 1024], BF16); wo = sb("wo", [128, 8, 1024], BF16)
    stg = [sb(f"wstg{i}", [128, 8, 512]) for i in range(2)]
    load_w_bf16(P, wpa, D["w_proj_a"], stg, 0)
    load_w_bf16(P, wpb, D["w_proj_b"], stg, 0)
    load_w_bf16(P, wo, D["w_out"], stg, 0)
    oat = sb("oat", [128, 8, 512], BF16); obt = sb("obt", [128, 8, 512], BF16)
    mT = sb("mT", [128, 8, 512], BF16)
    ga = [sb(f"ga{i}", [128, 512]) for i in range(2)]; gb = [sb(f"gb{i}", [128, 512]) for i in range(2)]
    m1 = [sb(f"m1_{i}", [128, 512]) for i in range(2)]; m2 = [sb(f"m2_{i}", [128, 512]) for i in range(2)]
    g1r = sb("g1r", [128, 1024]); g1s = sb("g1s", [128, 1024])
    P.dma("sp", g1r[:], D["modd"][0:1, 2048:3072].partition_broadcast(128), reads=[D["modd_t"]], writes=[g1r])
    for q in range(4):
        P.dma("sp", g1s[16 * q:16 * q + 16, :], D["modd"][1 + q:2 + q, 2048:3072].partition_broadcast(16), reads=[D["modd_t"]], writes=[g1s])
    xt = [sb(f"x6_{i}", [128, 1024]) for i in range(2)]
    x1 = [sb(f"x1_{i}", [128, 1024]) for i in range(2)]
    pa = [P.ps(f"pa{i}", [128, 512], F32, st) for i in range(2)]
    pb = [P.ps(f"pb{i}", [128, 512], F32, st) for i in range(2)]
    px = [P.ps(f"px{i}", [128, 512], F32, st) for i in range(2)]
    k = 0
    xk = 0
    for (n0, nb) in [(0, 512), (512, 512), (1024, 512), (1536, 512), (2048, 64)]:
        P.dma("sp", oat[:, :, 0:nb], D["OAT"].rearrange("(fc p) c -> p fc c", p=128)[:, :, n0:n0 + nb], reads=[D["OAT_t"]], writes=[oat])
        P.dma("sp", obt[:, :, 0:nb], D["OBT"].rearrange("(fc p) c -> p fc c", p=128)[:, :, n0:n0 + nb], reads=[D["OBT_t"]], writes=[obt])
        for fo in range(8):
            a_, b_ = pa[k % 2], pb[k % 2]
            ga_, gb_, m1_, m2_ = ga[k % 2], gb[k % 2], m1[k % 2], m2[k % 2]
            k += 1
            P.dma("sp", ga_[:, 0:nb], D["GT"][fo * 128:(fo + 1) * 128, n0:n0 + nb], reads=[D["GT_t"]], writes=[ga_])
            P.dma("sp", gb_[:, 0:nb], D["GT"][1024 + fo * 128:1024 + (fo + 1) * 128, n0:n0 + nb], reads=[D["GT_t"]], writes=[gb_])
            for kc in range(8):
                mm(P, a_[:, 0:nb], wpa[:, kc, fo * 128:(fo + 1) * 128], oat[:, kc, 0:nb], kc == 0, kc == 7, [wpa, oat], [a_])
            for kc in range(8):
                mm(P, b_[:, 0:nb], wpb[:, kc, fo * 128:(fo + 1) * 128], obt[:, kc, 0:nb], kc == 0, kc == 7, [wpb, obt], [b_])
            tt(P, "dve", m1_[:, 0:nb], a_[:, 0:nb], ga_[:, 0:nb], ALU.mult, [a_, ga_], [m1_])
            tt(P, "dve", m2_[:, 0:nb], b_[:, 0:nb], gb_[:, 0:nb], ALU.mult, [b_, gb_], [m2_])
            tt(P, "pool", mT[:, fo, 0:nb], m1_[:, 0:nb], m2_[:, 0:nb], ALU.add, [m1_, m2_], [mT])
        for t0 in range(0, nb, 128):
            n = min(128, nb - t0)
            x_, x1_ = xt[xk % 2], x1[xk % 2]; xk += 1
            P.dma("sp", x_[0:n, :], D["xin"][n0 + t0:n0 + t0 + n, :], writes=[x_])
            g1 = g1r if n0 < 2048 else g1s
            for hf in range(2):
                p = px[hf]
                cs = slice(hf * 512, (hf + 1) * 512)
                for kc in range(8):
                    mm(P, p[0:n, :], mT[:, kc, t0:t0 + n], wo[:, kc, cs], kc == 0, kc == 7, [mT, wo], [p])
                tt(P, "dve", x1_[0:n, cs], p[0:n, :], g1[0:n, cs], ALU.mult, [p, g1], [x1_])
                tt(P, "pool", x1_[0:n, cs], x1_[0:n, cs], x_[0:n, cs], ALU.add, [x1_, x_], [x1_])
            P.dma("pool", D["X1"][n0 + t0:n0 + t0 + n, :], x1_[0:n, :], reads=[x1_], writes=[D["X1_t"]])
    P.barrier()
    st.close()


def stage6b(P, g, D):
    st = ExitStack()
    h2T = P.sb("h2T", [128, 8, NT], BF16, st)
    norm_to_featmajor(P, g, D, D["X1"], h2T, g.scale2, 24)
    st2 = ExitStack()
    sb = lambda name, shape, dt=F32: P.sb(name, shape, dt, st2)
    P.dma("pool", D["H2T"].rearrange("(fc p) c -> p fc c", p=128)[:, :, :], h2T[:], reads=[h2T], writes=[D["H2T_t"]])
    wq = sb("wq", [128, 8, 1024], BF16)
    stg = [sb(f"wstgq{i}", [128, 8, 512]) for i in range(2)]
    load_w_bf16(P, wq, D["w_pq"], stg, 0)
    qs = [sb(f"qs{i}", [128, 512], BF16) for i in range(2)]
    pq = [P.ps(f"pq{i}", [128, 512], F32, st2) for i in range(2)]
    k = 0
    for (n0, nb) in [(0, 512), (512, 512), (1024, 512), (1536, 512), (2048, 64)]:
        for fo in range(8):
            p, s = pq[k % 2], qs[k % 2]; k += 1
            for kc in range(8):
                mm(P, p[:, 0:nb], wq[:, kc, fo * 128:(fo + 1) * 128], h2T[:, kc, n0:n0 + nb], kc == 0, kc == 7, [wq, h2T], [p])
            cp(P, "act", s[:, 0:nb], p[:, 0:nb], [p], [s])
            P.dma("pool", D["QPT"][fo * 128:(fo + 1) * 128, n0:n0 + nb], s[:, 0:nb], reads=[s], writes=[D["QPT_t"]])
    P.barrier()
    st2.close()
    st.close()


def vmax(P, out, in_, reads, writes):
    return P.op("dve", lambda e: e.max(out=out, in_=in_), reads=reads, writes=writes)


def mrep(P, out, rep, vals, reads, writes):
    return P.op("dve", lambda e: e.match_replace(out=out, in_to_replace=rep, in_values=vals, imm_value=NEG), reads=reads, writes=writes)


def stage7_prep(P, g, D):
    st = ExitStack()
    sb = lambda name, shape, dt=F32: P.sb(name, shape, dt, st)
    uf = [sb(f"uf{i}", [128, 1024]) for i in range(2)]
    ub = [sb(f"ub{i}", [128, 1024], BF16) for i in range(2)]
    vf = [sb(f"vf{i}", [128, 1024]) for i in range(2)]
    vb = [sb(f"vb{i}", [128, 1024], BF16) for i in range(2)]
    sT = [sb(f"usT{i}", [128, 8, 512], BF16) for i in range(2)]
    pT = [P.ps(f"upT{i}", [128, 8, 128], BF16, st) for i in range(2)]
    NE = 128
    for et in range(NE):
        u, ubb, v, vbb = uf[et % 2], ub[et % 2], vf[et % 2], vb[et % 2]
        P.dma("sp", u[:], D["peer_u"][et * 128:(et + 1) * 128, :], writes=[u])
        P.dma("sp", v[:], D["peer_v"][et * 128:(et + 1) * 128, :], writes=[v])
        cp(P, "dve", ubb[:], u[:], [u], [ubb])
        cp(P, "pool", vbb[:], v[:], [v], [vbb])
        P.dma("pool", D["Vbf"].rearrange("(g j p) d -> g p j d", j=4, p=128)[et // 4, :, et % 4, :], vbb[:], reads=[vbb], writes=[D["Vbf_t"]])
        p = pT[et % 2]
        s = sT[(et // 4) % 2]
        for kc in range(8):
            tr(P, p[:, kc, :], ubb[:, kc * 128:(kc + 1) * 128], g.identb[:], [ubb, g.identb], [p])
        cp(P, "act", s[:, :, (et % 4) * 128:(et % 4 + 1) * 128], p[:], [p], [s])
        if et % 4 == 3:
            e0 = (et - 3) * 128
            P.dma("pool", D["UT"].rearrange("(g p) (kc e) -> g p kc e", p=128, kc=8)[e0 // 512, :, :, :], s[:], reads=[s], writes=[D["UT_t"]])
    P.barrier()
    st.close()


def stage7_peer(P, g, D):
    st = ExitStack()
    sb = lambda name, shape, dt=F32: P.sb(name, shape, dt, st)
    keysT = sb("keysT", [128, 8, 128], BF16)
    kt = sb("kt_f", [128, 128])
    pT = [P.ps(f"ppT{i}", [128, 8, 128], BF16, st) for i in range(1)]
    ps12 = P.ps("ps12", [128, 4, 128], F32, st)
    for h in range(8):
        P.dma("sp", kt[:, 0:64], D["peer_keys"][h, 0, :, :], writes=[kt])
        P.dma("sp", kt[:, 64:128], D["peer_keys"][h, 1, :, :], writes=[kt])
        tr(P, ps12[:, h % 4, :], kt[:], g.identf[:], [kt, g.identf], [ps12])
        cp(P, "act", keysT[:, h, :], ps12[:, h % 4, :], [ps12], [keysT])
    g2r = sb("g2r", [128, 1024]); g2s = sb("g2s", [128, 1024])
    P.dma("sp", g2r[:], D["modd"][0:1, 5120:6144].partition_broadcast(128), reads=[D["modd_t"]], writes=[g2r])
    for q in range(4):
        P.dma("sp", g2s[16 * q:16 * q + 16, :], D["modd"][1 + q:2 + q, 5120:6144].partition_broadcast(16), reads=[D["modd_t"]], writes=[g2s])
    h2t = [sb(f"h2t{i}", [128, 8, 128], BF16) for i in range(2)]; qpt = [sb(f"qpt{i}", [128, 8, 128], BF16) for i in range(2)]
    S12 = sb("S12", [128, 16, 128]); S1P = sb("S1P", [128, 8, 128]); srep = sb("srep", [128, 16, 128])
    T16 = sb("T16", [128, 16, 16]); cand = sb("cand", [128, 8, 256]); crep = sb("crep", [128, 8, 256])
    top16c = sb("top16c", [128, 8, 16]); ez = sb("ez", [128, 8, 16]); Z = sb("Zp", [128, 8]); cinv = sb("cinv", [128, 8]); lnc = sb("lnc", [128, 8]); thr = sb("pthr", [128, 8]); mtiny = sb("mtiny", [128, 1])
    mset(P, "dve", mtiny[:], -1e-5, [mtiny])
    SUBI = 8
    NSUB = 128 // SUBI
    W_ = SUBI * 128
    zb = [sb(f"zb{i}", [128, W_]) for i in range(3)]; eb = [sb(f"eb{i}", [128, W_]) for i in range(3)]
    gmb = [sb(f"gmb{i}", [128, W_], BF16) for i in range(3)]
    Gp = [P.ps(f"pGp{i}", [128, 512], F32, st) for i in range(2)]
    Aqs = [sb(f"Aq{i}", [128, W_], BF16) for i in range(2)]; GAs = [sb(f"GAq{i}", [128, W_], BF16) for i in range(2)]
    GATs = [sb(f"GAT{i}", [128, SUBI, 128], BF16) for i in range(2)]
    ub = [sb(f"pub{i}", [128, 8, 512], BF16) for i in range(3)]
    vb = [sb(f"pvb{i}", [128, 4, 1024], BF16) for i in range(3)]
    x1t = [sb(f"px1_{i}", [128, 1024]) for i in range(2)]; yt = sb("pyt", [128, 1024])
    pA = [P.ps(f"ppA{i}", [128, 512], F32, st) for i in range(2)]
    po = [P.ps(f"ppo{i}", [128, 512], F32, st) for i in range(2)]
    uk = 0; vk = 0; ak = 0; zk = 0; tk = 0; gk = 0
    NTL = 17
    for ti, (r0, n) in enumerate(TILES[:NTL]):
        h2t_, qpt_, x1t_ = h2t[ti % 2], qpt[ti % 2], x1t[ti % 2]
        P.dma("sp", h2t_[:, :, 0:n], D["H2T"].rearrange("(fc p) c -> p fc c", p=128)[:, :, r0:r0 + n], reads=[D["H2T_t"]], writes=[h2t_])
        P.dma("sp", qpt_[:, :, 0:n], D["QPT"].rearrange("(fc p) c -> p fc c", p=128)[:, :, r0:r0 + n], reads=[D["QPT_t"]], writes=[qpt_])
        P.dma("sp", x1t_[0:n, :], D["X1"][r0:r0 + n, :], reads=[D["X1_t"]], writes=[x1t_])
        for hg in range(4):
            p = ps12
            for j in range(4):
                hp = hg * 4 + j
                h, pp = hp // 2, hp % 2
                pb = pp * 64
                mm(P, p[0:n, j, :], qpt_[pb:pb + 64, h, 0:n], keysT[pb:pb + 64, h, :], True, True, [qpt_, keysT], [p])
            cp(P, "act", S12[0:n, hg * 4:(hg + 1) * 4, :], p[0:n, :, :], [p], [S12])
        for hp in range(16):
            vmax(P, T16[0:n, hp, 0:8], S12[0:n, hp, :], [S12], [T16.sub(("a", hp))])
        for hp in range(16):
            mrep(P, srep[0:n, hp, :], T16[0:n, hp, 0:8], S12[0:n, hp, :], [S12, T16.sub(("a", hp))], [srep.sub(hp)])
        for hp in range(16):
            vmax(P, T16[0:n, hp, 8:16], srep[0:n, hp, :], [srep.sub(hp)], [T16.sub(("b", hp))])
        T16all = [T16.sub((x, hp)) for x in "ab" for hp in range(16)]
        T16v = T16[0:n, :, :].rearrange("p (h t) k -> p h t k", t=2)
        tt(P, "dve", cand[0:n, :, :].rearrange("p h (i j) -> p h i j", i=16), T16v[:, :, 0, :].unsqueeze(3).to_broadcast([n, 8, 16, 16]),
           T16v[:, :, 1, :].unsqueeze(2).to_broadcast([n, 8, 16, 16]), ALU.add, T16all, [cand])
        for h in range(8):
            vmax(P, top16c[0:n, h, 0:8], cand[0:n, h, :], [cand], [top16c.sub(("a", h))])
        for h in range(8):
            mrep(P, crep[0:n, h, :], top16c[0:n, h, 0:8], cand[0:n, h, :], [cand, top16c.sub(("a", h))], [crep.sub(h)])
        for h in range(8):
            vmax(P, top16c[0:n, h, 8:16], crep[0:n, h, :], [crep.sub(h)], [top16c.sub(("b", h))])
        tcall = [top16c.sub((x, h)) for x in "ab" for h in range(8)]
        tau = top16c[0:n, :, 15:16]
        tt(P, "dve", ez[0:n, :, :], top16c[0:n, :, :], tau.to_broadcast([n, 8, 16]), ALU.subtract, tcall, [ez])
        act(P, ez[0:n, :, :], ez[0:n, :, :], AF.Exp, [ez], [ez])
        red(P, "dve", Z[0:n, :], ez[0:n, :, :], ALU.add, [ez], [Z])
        recip(P, cinv[0:n, :], Z[0:n, :], [Z], [cinv])
        act(P, lnc[0:n, :], cinv[0:n, :], AF.Ln, [cinv], [lnc])
        S12v = S12[0:n, :, :].rearrange("p (h t) k -> p h t k", t=2)
        tt(P, "dve", S1P[0:n, :, :], S12v[:, :, 0, :], tau.to_broadcast([n, 8, 128]), ALU.subtract, [S12] + tcall, [S1P])
        def emitA(ib):
            nonlocal uk, ak
            Aq = Aqs[ib % 2]
            for eg in range(W_ // 512):
                e0 = ib * W_ + eg * 512
                u = ub[uk % 3]; uk += 1
                P.dma("sp", u[:], D["UT"].rearrange("(g p) (kc e) -> g p kc e", p=128, kc=8)[e0 // 512, :, :, :], reads=[D["UT_t"]], writes=[u])
                p = pA[ak % 2]; ak += 1
                for kc in range(8):
                    mm(P, p[0:n, :], h2t_[:, kc, 0:n], u[:, kc, :], kc == 0, kc == 7, [h2t_, u], [p])
                act(P, Aq[0:n, eg * 512:(eg + 1) * 512], p[0:n, :], AF.Gelu_apprx_tanh, [p], [Aq])

        def emitZ(ib, h):
            nonlocal zk
            z_, e_ = zb[zk % 3], eb[zk % 3]; zk += 1
            z3 = z_[0:n, :].rearrange("p (i j) -> p i j", i=SUBI)
            tt(P, "dve", z3, S1P[0:n, h, ib * SUBI:(ib + 1) * SUBI].unsqueeze(2).to_broadcast([n, SUBI, 128]),
               S12v[:, h, 1, :].unsqueeze(1).to_broadcast([n, SUBI, 128]), ALU.add, [S1P, S12], [z_])
            act(P, e_[0:n, :], z_[0:n, :], AF.Exp, [z_, lnc], [e_], bias=lnc[0:n, h:h + 1])
            return z_, e_

        emitA(0)
        pend = emitZ(0, 0)
        for ib in range(NSUB):
            Aq, GA, GAT = Aqs[ib % 2], GAs[ib % 2], GATs[ib % 2]
            if ib + 1 < NSUB:
                emitA(ib + 1)
            for h in range(8):
                z_, e_ = pend
                if h + 1 < 8:
                    pend = emitZ(ib, h + 1)
                elif ib + 1 < NSUB:
                    pend = emitZ(ib + 1, 0)
                gm = gmb[gk % 3]; gk += 1
                stt(P, "dve", gm[0:n, :], z_[0:n, :], -1e-5, e_[0:n, :], ALU.is_ge, ALU.mult, [z_, e_], [gm])
                for cg in range(W_ // 512):
                    mm(P, Gp[cg][0:n, :], g.identb[0:n, 0:n], gm[0:n, cg * 512:(cg + 1) * 512], h == 0, h == 7, [gm, g.identb], [Gp[cg]])
            for cg in range(W_ // 512):
                tt(P, "dve", GA[0:n, cg * 512:(cg + 1) * 512], Gp[cg][0:n, :], Aq[0:n, cg * 512:(cg + 1) * 512], ALU.mult, [Gp[cg], Aq], [GA])
            for j0 in range(0, SUBI, 8):
                pt = pT[0]; tk += 1
                for jj in range(8):
                    et = j0 + jj
                    tr(P, pt[:, jj, 0:n], GA[0:n, et * 128:(et + 1) * 128], g.identb[0:n, 0:n], [GA, g.identb], [pt])
                cp(P, "act", GAT[:, j0:j0 + 8, 0:n], pt[:, :, 0:n], [pt], [GAT])
            for vg in range(SUBI // 4):
                v = vb[vk % 3]; vk += 1
                e0 = (ib * SUBI + vg * 4) * 128
                P.dma("pool", v[:], D["Vbf"].rearrange("(g j p) d -> g p j d", j=4, p=128)[e0 // 512, :, :, :], reads=[D["Vbf_t"]], writes=[v])
                for j in range(4):
                    et = vg * 4 + j
                    first = (ib == 0 and et == 0)
                    last = (ib == NSUB - 1 and et == SUBI - 1)
                    for hf in range(2):
                        mm(P, po[hf][0:n, :], GAT[:, et, 0:n], v[:, j, hf * 512:(hf + 1) * 512], first, last, [GAT, v], [po[hf]])
        g2 = g2r if r0 < 2048 else g2s
        for hf in range(2):
            cs = slice(hf * 512, (hf + 1) * 512)
            tt(P, "dve", yt[0:n, cs], po[hf][0:n, :], g2[0:n, cs], ALU.mult, [po[hf], g2], [yt])
        if D.get("PEERdbg") is not None:
            P.dma("pool", D["PEERdbg"][r0:r0 + n, :], yt[0:n, :], reads=[yt])
        tt(P, "pool", yt[0:n, :], yt[0:n, :], x1t_[0:n, :], ALU.add, [yt, x1t_], [yt])
        P.dma("pool", D["y"][r0:r0 + n, :], yt[0:n, :], reads=[yt], writes=[D["y_t"]])
    P.barrier()
    st.close()


def declare(nc, P, D, debug=False):
    def din(name, shape, dt=F32):
        D[name] = nc.dram_tensor(name, list(shape), dt, kind="ExternalInput").ap()

    def dscr(name, shape, dt=F32, kind="Internal"):
        D[name] = nc.dram_tensor(name, list(shape), dt, kind=("ExternalOutput" if debug else kind)).ap()
        D[name + "_t"] = T(None, name)

    din("ident", [128, 128]); din("cin", [5, 1024]); din("xin", [NT, 1024])
    din("w_ada", [1024, 6144]); din("b_ada", [1, 6144]); din("norm1_w", [1024]); din("norm2_w", [1024]); din("b_gate", [2048])
    din("w_in", [1024, 9064]); din("cos", [NT, 32]); din("sin", [NT, 32]); din("k_norm_w", [1, 64]); din("q_norm_w", [1, 64])
    din("mu_rw", [1, RW_IN])
    for nm in ["w0", "a0", "k_k", "k_a", "r_k", "lnx_w", "lnx_b"]:
        din(nm, [1, 1024])
    din("w_up", [64, 1024]); din("a_up", [64, 1024]); din("g_up", [160, 1024])
    din("sshift", [4, RW_IN]); din("swkv", [4, 16, 64, 64])
    for nm in ["m_tri", "m_ones", "m_sl", "m_su", "m_u"]:
        din(nm, [128, 128])
    din("m_valid", [128, 2])
    dscr("modd", [5, 6144]); dscr("Ptok", [NT, TOKW]); dscr("GT", [2048, NT])
    dscr("BC", [UC, 16])
    for nm in ["Vs", "KPs", "BPs", "Gs"]:
        dscr(nm, [UC, 1024])
    for nm in ["RT", "AT", "KT", "BT", "GCT"]:
        dscr(nm, [1024, UC])
    for nm in ["OAT", "OBT", "QT", "KTn", "H2T", "QPT"]:
        dscr(nm, [1024, NT], BF16)
    dscr("QIT", [512, NT], BF16); dscr("KIT", [64, NT], BF16); dscr("WI", [NT, 8])
    dscr("X1", [NT, 1024], F32, "Internal")
    dscr("UT", [4096, 4096], BF16); dscr("Vbf", [16384, 1024], BF16)
    din("cache_k", [4, 2048, 1024]); din("cache_v", [4, 2048, 1024]); din("cache_kidx", [4, 2048, 64]); din("m_pow2", [128, NBIS])
    for nm in ["w_proj_a", "w_proj_b", "w_out", "w_pq"]:
        din(nm, [1024, 1024])
    din("peer_keys", [8, 2, 128, 64]); din("peer_u", [16384, 1024]); din("peer_v", [16384, 1024])
    for nm, shp in [("y", [NT, 1024]), ("wkv", [5 * 16 * 64, 64]), ("kout", [NT, 1024]), ("vout", [NT, 1024]), ("kidx", [NT, 64]), ("shift", [5, RW_IN])]:
        D[nm] = nc.dram_tensor(nm, shp, F32, kind="ExternalOutput").ap()
        D[nm + "_t"] = T(None, nm)


def host_consts():
    inv = (10000.0 ** (-np.arange(32, dtype=np.float32) / 32)).astype(np.float32)
    pos = np.concatenate([np.arange(2048), np.tile(2048 + np.arange(16), 4)]).astype(np.float32)
    ang = pos[:, None] * inv[None, :]
    idx = np.arange(128)
    same = (idx[:, None] // 64) == (idx[None, :] // 64)
    f32 = lambda a: np.ascontiguousarray(a, dtype=np.float32)
    valid = np.ones((128, 2), np.float32)
    valid[:, 1] = ((idx % 64) < 16)
    return {"ident": np.eye(128, dtype=np.float32), "cos": np.cos(ang).astype(np.float32), "sin": np.sin(ang).astype(np.float32),
            "m_tri": f32(same & (idx[:, None] <= idx[None, :])), "m_ones": f32(same), "m_sl": f32(same & (idx[:, None] > idx[None, :])),
            "m_su": f32(same & (idx[:, None] < idx[None, :])), "m_u": f32(same & (idx[:, None] <= idx[None, :])), "m_valid": valid,
            "m_pow2": np.tile((2.0 ** -(np.arange(NBIS) + 1.0)).astype(np.float32)[None, :], (128, 1))}


def run_all(P, g, D):
    stage0(P, g, D)
    st = ExitStack()
    hT = P.sb("hT", [128, 8, NT], BF16, st)
    norm_to_featmajor(P, g, D, D["xin"], hT, g.scale1, 0)
    stage2(P, g, D, hT)
    st.close()
    stage3_dsa(P, g, D)
    stage3_rw(P, g, D)
    stage4_scan(P, g, D)
    stage5_attn(P, g, D)
    stage6(P, g, D)
    stage6b(P, g, D)
    stage7_prep(P, g, D)
    stage7_peer(P, g, D)


def core_inputs(inp, c, consts, local=False):
    f = lambda a: np.ascontiguousarray(np.asarray(a, dtype=np.float32))
    m = dict(consts)
    W = lambda k: f(np.asarray(inp[k])[0])
    m.update({"w_ada": W("w_ada"), "b_ada": f(inp["b_ada"]), "norm1_w": W("norm1_w"), "norm2_w": W("norm2_w"), "b_gate": W("b_gate"), "w_in": W("w_in"),
              "k_norm_w": f(inp["k_norm_w"]), "q_norm_w": f(inp["q_norm_w"]), "mu_rw": f(inp["mu_rw"]), "w0": f(inp["w0"]), "a0": f(inp["a0"]),
              "k_k": f(inp["k_k"]), "k_a": f(inp["k_a"]), "r_k": f(np.asarray(inp["r_k"]).reshape(1, 1024)), "lnx_w": f(inp["lnx_w"]), "lnx_b": f(inp["lnx_b"]),
              "w_up": W("w_up"), "a_up": W("a_up"), "g_up": W("g_up"), "w_proj_a": W("w_proj_a"), "w_proj_b": W("w_proj_b"), "w_out": W("w_out"),
              "w_pq": W("w_pq"), "peer_keys": W("peer_keys"), "peer_u": W("peer_u"), "peer_v": W("peer_v")})
    pc = 0 if local else c
    sl = slice(0, 4) if local else slice(4 * c, 4 * c + 4)
    m["xin"] = f(np.concatenate([np.asarray(inp["x_prompt"])[pc], np.asarray(inp["x_sample"])[sl].reshape(64, 1024)], 0))
    m["cin"] = f(np.concatenate([np.asarray(inp["c_prompt"])[pc:pc + 1], np.asarray(inp["c_sample"])[sl]], 0))
    m["sshift"] = f(np.asarray(inp["state_shift"])[0][sl])
    m["swkv"] = f(np.asarray(inp["state_wkv"])[0][sl])
    m["cache_k"] = f(np.asarray(inp["cache_k"])[0][sl].reshape(4, 2048, 1024))
    m["cache_v"] = f(np.asarray(inp["cache_v"])[0][sl].reshape(4, 2048, 1024))
    m["cache_kidx"] = f(np.asarray(inp["cache_kidx"])[0][sl])
    return m


def build_program():
    nc = bass.Bass("TRN2", target_bir_lowering=False)
    P = Prog(nc)
    D = {}
    declare(nc, P, D)
    g = G()
    g.epsc = P.sb("epsc", [128, 1], F32)
    mset(P, "dve", g.epsc[:], EPS, [g.epsc])
    run_all(P, g, D)
    P.emit()
    return nc


def kernel(**inp):
    nc = build_program()
    consts = host_consts()
    in_maps = [core_inputs(inp, c, consts) for c in range(8)]
    res = run_bass_kernel_spmd(nc, in_maps, core_ids=list(range(8)))
    R = res.results
    cat = lambda name, sl, shp: np.stack([R[c][name][sl].reshape(shp) for c in range(8)], 0)
    y_p = cat("y", slice(0, 2048), (2048, 1024))
    y_s = cat("y", slice(2048, NT), (4, 16, 1024)).reshape(32, 16, 1024)
    wkv_p = cat("wkv", slice(0, 1024), (16, 64, 64))[None]
    wkv_s = cat("wkv", slice(1024, 5120), (4, 16, 64, 64)).reshape(32, 16, 64, 64)[None]
    sh_p = cat("shift", slice(0, 1), (RW_IN,))[None]
    sh_s = cat("shift", slice(1, 5), (4, RW_IN)).reshape(32, RW_IN)[None]
    k_p = cat("kout", slice(0, 2048), (2048, 16, 64))[None]
    k_s = cat("kout", slice(2048, NT), (4, 16, 16, 64)).reshape(32, 16, 16, 64)[None]
    v_p = cat("vout", slice(0, 2048), (2048, 16, 64))[None]
    v_s = cat("vout", slice(2048, NT), (4, 16, 16, 64)).reshape(32, 16, 16, 64)[None]
    ki_p = cat("kidx", slice(0, 2048), (2048, 64))[None]
    ki_s = cat("kidx", slice(2048, NT), (4, 16, 64)).reshape(32, 16, 64)[None]
    return (y_p, y_s, wkv_p, sh_p, k_p, v_p, ki_p, wkv_s, sh_s, k_s, v_s, ki_s)
```

```python
import numpy as np
from contextlib import ExitStack
import concourse.bass as bass
import concourse.mybir as mybir
from concourse.bass_utils import run_bass_kernel_spmd

F32 = mybir.dt.float32
BF16 = mybir.dt.bfloat16
I32 = mybir.dt.int32
U32 = mybir.dt.uint32
AF = mybir.ActivationFunctionType
ALU = mybir.AluOpType
AX = mybir.AxisListType

ENGS = ["pe", "act", "dve", "pool", "sp"]
NDMA = {"sp": 40, "pool": 24, "act": 8}


class Dep:
    __slots__ = ("w", "r", "name")

    def __init__(self, name=""):
        self.w = None
        self.r = []
        self.name = name


class Bank:
    def __init__(self):
        self.last = {}
        self.pe_rows = None


class T:
    def __init__(self, h, name):
        self.h = h
        self.name = name
        self.dep = Dep(name)
        self.subs = {}
        self.bank = None

    def __getitem__(self, idx):
        return self.h[idx]

    def sub(self, key):
        if key not in self.subs:
            self.subs[key] = Dep(f"{self.name}.{key}")
        return self.subs[key]


class Slot:
    def __init__(self, t, i):
        self.base = t.h[:, i, :]
        self.dep = Dep(f"{t.name}[{i}]")
        self.bank = t.bank

    def __getitem__(self, idx):
        return self.base[idx]


class SlotAP:
    def __init__(self, t, ap):
        self.base = ap
        self.dep = Dep(t.name + "[ap]")
        self.bank = t.bank

    def __getitem__(self, idx):
        return self.base[idx]


def _dep(x):
    return x.dep if hasattr(x, "dep") else x


class Prog:
    def __init__(self, nc):
        self.nc = nc
        self.es = ExitStack()
        self.ops = {e: [] for e in ENGS}
        self.cnt = {e: 0 for e in ENGS}
        self.esem = {}
        for e in ENGS:
            self.esem[e] = self.es.enter_context(nc.semaphore("s_" + e))
        self.dsem = {}
        self.dval = {}
        self.dnext = {}
        for q, n in NDMA.items():
            self.dsem[q] = [self.es.enter_context(nc.semaphore(f"d_{q}{i}")) for i in range(n)]
            self.dval[q] = [0] * n
            self.dnext[q] = 0
        self.seen = {e: {} for e in ENGS}
        self.semobj = {}
        self.nwaits = 0

    def _uniq(self, name):
        if not hasattr(self, "_names"):
            self._names = {}
        k = self._names.get(name, 0)
        self._names[name] = k + 1
        return name if k == 0 else f"{name}__{k}"

    def sb(self, name, shape, dtype, stack=None):
        name = self._uniq(name)
        h = (stack or self.es).enter_context(self.nc.sbuf_tensor(name, list(shape), dtype))
        return T(h, name)

    def ps(self, name, shape, dtype=F32, stack=None):
        name = self._uniq(name)
        h = (stack or self.es).enter_context(self.nc.psum_tensor(name, list(shape), dtype))
        t = T(h, name)
        t.bank = Bank()
        return t

    def dram(self, name, shape, dtype, kind="Internal"):
        h = self.nc.dram_tensor(name, list(shape), dtype, kind=kind)
        return T(h, name)

    def _waits(self, eng, reads, writes, extra=()):
        evs = list(extra)
        for b in reads:
            b = _dep(b)
            if b.w is not None:
                evs.append(b.w)
        for b in writes:
            b = _dep(b)
            if b.w is not None:
                evs.append(b.w)
            evs.extend(b.r)
        need = {}
        for (key, sem, val, src) in evs:
            if src == "pe" and eng == "pe":
                continue
            if self.seen[eng].get(key, 0) >= val:
                continue
            if need.get(key, (None, 0))[1] < val:
                need[key] = (sem, val)
        for key, (sem, val) in need.items():
            self.seen[eng][key] = val
        return list(need.values())

    def _record(self, ev, reads, writes):
        for b in reads:
            _dep(b).r.append(ev)
        for b in writes:
            b = _dep(b)
            b.w = ev
            b.r = []

    def op(self, eng, fn, reads=(), writes=(), pe_rows=None):
        banks = {}
        for b in list(reads) + list(writes):
            bk = getattr(b, "bank", None)
            if bk is not None:
                banks[id(bk)] = bk
        extra = [ev for bk in banks.values() for e2, ev in bk.last.items() if e2 != eng]
        if eng == "pe" and pe_rows is not None:
            for bk in banks.values():
                if bk.pe_rows is not None and bk.pe_rows != pe_rows and "pe" in bk.last and (pe_rows[1] < 128 or bk.pe_rows[1] < 128):
                    k_, s_, v_, _ = bk.last["pe"]
                    extra.append((k_, s_, v_, "force"))
                bk.pe_rows = pe_rows
        waits = self._waits(eng, reads, writes, extra)
        self.cnt[eng] += 1
        ev = (eng, self.esem[eng], self.cnt[eng], eng)
        for bk in banks.values():
            bk.last[eng] = ev
        self._record(ev, reads, writes)
        self.ops[eng].append((waits, fn, (self.esem[eng], 1)))
        self.nwaits += len(waits)
        return ev

    def dma(self, q, out, in_, reads=(), writes=(), **kw):
        i = self.dnext[q]
        self.dnext[q] = (i + 1) % len(self.dsem[q])
        sem = self.dsem[q][i]
        key = (q, i)
        waits = self._waits(q, reads, writes)
        prev = self.dval[q][i]
        if prev > 0 and self.seen[q].get(key, 0) < prev:
            waits.append((sem, prev))
            self.seen[q][key] = prev
        self.dval[q][i] = prev + 16
        ev = (key, sem, prev + 16, "dma")
        self._record(ev, reads, writes)
        self.ops[q].append((waits, lambda e: e.dma_start(out=out, in_=in_, **kw), (sem, 16)))
        self.nwaits += len(waits)
        return ev

    def barrier(self):
        evs = []
        for e in ENGS:
            if self.cnt[e] > 0:
                evs.append((e, self.esem[e], self.cnt[e]))
        for q in self.dsem:
            for i, v in enumerate(self.dval[q]):
                if v > 0:
                    evs.append(((q, i), self.dsem[q][i], v))
        for e in ENGS:
            waits = []
            for key, sem, val in evs:
                if key == e:
                    continue
                if self.seen[e].get(key, 0) >= val:
                    continue
                self.seen[e][key] = val
                waits.append((sem, val))
            if waits:
                self.ops[e].append((waits, None, None))

    def emit(self):
        self.barrier()
        nc = self.nc
        with nc.Block() as block:
            def run(e, lst):
                for waits, fn, inc in lst:
                    for sem, val in waits:
                        e.wait_ge(sem, val)
                    if fn is not None:
                        ins = fn(e)
                        ins.then_inc(inc[0], inc[1])

            @block.tensor
            def _(e):
                run(e, self.ops["pe"])

            @block.scalar
            def _(e):
                run(e, self.ops["act"])

            @block.vector
            def _(e):
                run(e, self.ops["dve"])

            @block.gpsimd
            def _(e):
                run(e, self.ops["pool"])

            @block.sync
            def _(e):
                run(e, self.ops["sp"])
        self.es.close()
EPS = 1e-6
NT = 2112
TILES = [(i * 128, 128) for i in range(16)] + [(2048, 64)]
RW_IN = 3360
TOKW = 7016


def mm(P, out, lhsT, rhs, start, stop, reads, writes):
    rows = (lhsT.base_partition(), lhsT.partition_size())
    return P.op("pe", lambda e: e.matmul(out, lhsT, rhs, start=start, stop=stop), reads=reads, writes=writes, pe_rows=rows)


def tr(P, out, in_, ident, reads, writes):
    return P.op("pe", lambda e: e.transpose(out, in_, ident), reads=reads, writes=writes)


def act(P, out, in_, func, reads, writes, **kw):
    return P.op("act", lambda e: e.activation(out=out, in_=in_, func=func, **kw), reads=reads, writes=writes)


def ts(P, eng, out, in0, s1, s2, op0, op1=None, reads=(), writes=(), **kw):
    def f(e):
        if op1 is None:
            return e.tensor_scalar(out=out, in0=in0, scalar1=s1, scalar2=s2, op0=op0, **kw)
        return e.tensor_scalar(out=out, in0=in0, scalar1=s1, scalar2=s2, op0=op0, op1=op1, **kw)
    return P.op(eng, f, reads=reads, writes=writes)


def tt(P, eng, out, in0, in1, op, reads, writes):
    if eng == "pool":
        eng = "dve"
    return P.op(eng, lambda e: e.tensor_tensor(out=out, in0=in0, in1=in1, op=op), reads=reads, writes=writes)


def cp(P, eng, out, in_, reads, writes):
    if eng == "pool":
        eng = "act"
    if eng == "act":
        return act(P, out, in_, AF.Copy, reads, writes)
    return P.op(eng, lambda e: e.tensor_copy(out, in_), reads=reads, writes=writes)


class G:
    pass


def featvec(P, g, st, name, vec_ap, n):
    tmp = P.sb(name + "_t", [n, 128], F32, st)
    P.dma("sp", tmp[:], vec_ap.rearrange("(c p) -> c p", p=128), writes=[tmp])
    ps = P.ps(name + "_p", [128, n], F32, st)
    tr(P, ps[:], tmp[:], g.identf[0:n, 0:n], [tmp, g.identf], [ps])
    out = getattr(g, name)
    cp(P, "dve", out[:], ps[:], [ps], [out])
    return out


def stage0(P, g, D):
    g.identf = P.sb("identf", [128, 128], F32)
    g.identb = P.sb("identb", [128, 128], BF16)
    g.modT = P.sb("modT", [128, 48, 5], F32)
    g.n1w = P.sb("n1w", [128, 8], F32)
    g.n2w = P.sb("n2w", [128, 8], F32)
    g.bgT = P.sb("bgT", [128, 16], F32)
    g.scale1 = P.sb("scale1", [128, 8, 5], F32)
    g.scale2 = P.sb("scale2", [128, 8, 5], F32)
    st = ExitStack()
    P.dma("sp", g.identf[:], D["ident"][:, :], writes=[g.identf])
    cp(P, "dve", g.identb[:], g.identf[:], [g.identf], [g.identb])
    c5 = P.sb("c5", [5, 1024], F32, st)
    s5 = P.sb("s5", [5, 1024], F32, st)
    P.dma("sp", c5[:], D["cin"][:, :], writes=[c5])
    act(P, s5[:], c5[:], AF.Silu, [c5], [s5])
    sT = P.sb("sT", [128, 8, 5], F32, st)
    pst = P.ps("pst", [128, 8, 5], F32, st)
    for kc in range(8):
        tr(P, pst[:, kc, :], s5[0:5, kc * 128:(kc + 1) * 128], g.identf[0:5, 0:5], [s5, g.identf], [pst])
    cp(P, "dve", sT[:], pst[:], [pst], [sT])
    bada5 = P.sb("bada5", [5, 6144], F32, st)
    P.dma("sp", bada5[:], D["b_ada"][0:1, :].partition_broadcast(5), writes=[bada5])
    mod5 = P.sb("mod5", [5, 6144], F32, st)
    wa = [P.sb(f"wa{i}", [128, 8, 512], F32, st) for i in range(2)]
    pm = [P.ps(f"pm{i}", [5, 512], F32, st) for i in range(2)]
    for gi in range(12):
        w = wa[gi % 2]
        p = pm[gi % 2]
        P.dma("sp", w[:], D["w_ada"][:, gi * 512:(gi + 1) * 512].rearrange("(kc p) c -> p kc c", p=128), writes=[w])
        for kc in range(8):
            mm(P, p[:], sT[:, kc, :], w[:, kc, :], kc == 0, kc == 7, [sT, w], [p])
        tt(P, "dve", mod5[:, gi * 512:(gi + 1) * 512], p[:], bada5[:, gi * 512:(gi + 1) * 512], ALU.add, [p, bada5], [mod5])
    P.dma("sp", D["modd"][:, :], mod5[:], reads=[mod5], writes=[D["modd_t"]])
    pmt = P.ps("pmt", [128, 48, 5], F32, st)
    for c in range(48):
        tr(P, pmt[:, c, :], mod5[0:5, c * 128:(c + 1) * 128], g.identf[0:5, 0:5], [mod5, g.identf], [pmt])
    cp(P, "dve", g.modT[:], pmt[:], [pmt], [g.modT])
    n1 = featvec(P, g, st, "n1w", D["norm1_w"], 8)
    n2 = featvec(P, g, st, "n2w", D["norm2_w"], 8)
    g.bgT = featvec(P, g, st, "bgT", D["b_gate"], 16)
    for c in range(8):
        ts(P, "dve", g.scale1[:, c, :], g.modT[:, 8 + c, :], 1.0, n1[:, c:c + 1], ALU.add, ALU.mult, [g.modT, n1], [g.scale1])
        ts(P, "dve", g.scale2[:, c, :], g.modT[:, 32 + c, :], 1.0, n2[:, c:c + 1], ALU.add, ALU.mult, [g.modT, n2], [g.scale2])
    P.barrier()
    st.close()


def norm_to_featmajor(P, g, D, src_ap, hT, scale, shift_chunk0):
    st = ExitStack()
    xt = [P.sb(f"nx{i}", [128, 1024], F32, st) for i in range(2)]
    xn = [P.sb(f"nxn{i}", [128, 1024], BF16, st) for i in range(2)]
    junk = P.sb("njunk", [128, 1024], F32, st)
    ss = [P.sb(f"nss{i}", [128, 1], F32, st) for i in range(2)]
    rs = [P.sb(f"nrs{i}", [128, 1], F32, st) for i in range(2)]
    pT = [P.ps(f"npT{i}", [128, 8, 128], BF16, st) for i in range(2)]
    for ti, (r0, n) in enumerate(TILES):
        x, xb, s, r, p = xt[ti % 2], xn[ti % 2], ss[ti % 2], rs[ti % 2], pT[ti % 2]
        P.dma("sp", x[0:n, :], src_ap[r0:r0 + n, :], writes=[x])
        act(P, junk[0:n, :], x[0:n, :], AF.Square, [x], [junk, s], accum_out=s[0:n, :])
        act(P, s[0:n, :], s[0:n, :], AF.Sqrt, [s], [s], scale=1.0 / 1024, bias=g.epsc[0:n, :])
        P.op("dve", lambda e, r=r, s=s, n=n: e.reciprocal(r[0:n, :], s[0:n, :]), reads=[s], writes=[r])
        ts(P, "dve", xb[0:n, :], x[0:n, :], r[0:n, :], None, ALU.mult, None, [x, r], [xb])
        for kc in range(8):
            tr(P, p[:, kc, 0:n], xb[0:n, kc * 128:(kc + 1) * 128], g.identb[0:n, 0:n], [xb, g.identb], [p])
        for kc in range(8):
            if ti < 16:
                act(P, hT[:, kc, r0:r0 + n], p[:, kc, 0:n], AF.Identity, [p, scale, g.modT], [hT],
                    scale=scale[:, kc, 0:1], bias=g.modT[:, shift_chunk0 + kc, 0:1])
            else:
                for q in range(4):
                    act(P, hT[:, kc, r0 + 16 * q:r0 + 16 * q + 16], p[:, kc, 16 * q:16 * q + 16], AF.Identity,
                        [p, scale, g.modT], [hT], scale=scale[:, kc, 1 + q:2 + q], bias=g.modT[:, shift_chunk0 + kc, 1 + q:2 + q])
    P.barrier()
    st.close()


def stage2(P, g, D, hT):
    st = ExitStack()
    wf = [P.sb(f"wf{i}", [128, 8, 512], F32, st) for i in range(2)]
    wb = [P.sb(f"wb{i}", [128, 8, 512], BF16, st) for i in range(2)]
    stg = [P.sb(f"stg{i}", [128, 512], F32, st) for i in range(3)]
    pp = [P.ps(f"pp{i}", [128, 512], F32, st) for i in range(3)]
    k = 0
    groups = [(c0, min(512, TOKW - c0), False) for c0 in range(0, TOKW, 512)] + [(TOKW + i * 512, 512, True) for i in range(4)]
    for gi, (c0, gw, isgate) in enumerate(groups):
        w, b = wf[gi % 2], wb[gi % 2]
        P.dma("sp", w[:, :, 0:gw], D["w_in"][:, c0:c0 + gw].rearrange("(kc p) c -> p kc c", p=128), writes=[w])
        cp(P, "pool" if gi % 2 else "dve", b[:, :, 0:gw], w[:, :, 0:gw], [w], [b])
        if not isgate:
            for ti, (r0, n) in enumerate(TILES):
                p, s = pp[k % 3], stg[k % 3]
                for kc in range(8):
                    mm(P, p[0:n, 0:gw], hT[:, kc, r0:r0 + n], b[:, kc, 0:gw], kc == 0, kc == 7, [hT, b], [p])
                cp(P, "act" if k % 2 else "dve", s[0:n, 0:gw], p[0:n, 0:gw], [p], [s])
                P.dma("pool", D["Ptok"][r0:r0 + n, c0:c0 + gw], s[0:n, 0:gw], reads=[s], writes=[D["Ptok_t"].sub(ti)])
                k += 1
        else:
            for j in range(4):
                fch = (c0 - TOKW) // 128 + j
                for (n0, nb) in [(0, 512), (512, 512), (1024, 512), (1536, 512), (2048, 64)]:
                    p, s = pp[k % 3], stg[k % 3]
                    for kc in range(8):
                        mm(P, p[:, 0:nb], b[:, kc, j * 128:(j + 1) * 128], hT[:, kc, n0:n0 + nb], kc == 0, kc == 7, [hT, b], [p])
                    act(P, s[:, 0:nb], p[:, 0:nb], AF.Sigmoid, [p, g.bgT], [s], bias=g.bgT[:, fch:fch + 1])
                    P.dma("pool", D["GT"][fch * 128:(fch + 1) * 128, n0:n0 + nb], s[:, 0:nb], reads=[s], writes=[D["GT_t"]])
                    k += 1
    P.barrier()
    st.close()

C_Q, C_K, C_V, C_QI, C_KI, C_WI = 3360, 4384, 5408, 6432, 6944, 7008


def rope(P, eng, out4, in4, cosb, sinb, tmp, n, reads, writes):
    H = in4.shape[1]
    x1, x2 = in4[:, :, 0, :], in4[:, :, 1, :]
    t = [tmp[0:n, i, 0:H * 32].rearrange("p (h d) -> p h d", h=H) for i in range(4)]
    tt(P, eng, t[0], x1, cosb, ALU.mult, reads, [tmp])
    tt(P, eng, t[1], x2, sinb, ALU.mult, reads, [tmp])
    tt(P, eng, t[2], x2, cosb, ALU.mult, reads, [tmp])
    tt(P, eng, t[3], x1, sinb, ALU.mult, reads, [tmp])
    tt(P, eng, out4[:, :, 0, :], t[0], t[1], ALU.subtract, [tmp], writes)
    tt(P, eng, out4[:, :, 1, :], t[2], t[3], ALU.add, [tmp], writes)


def stage3_dsa(P, g, D):
    st = ExitStack()
    knw = P.sb("knw", [128, 64], F32, st)
    qnw = P.sb("qnw", [128, 64], F32, st)
    P.dma("sp", knw[:], D["k_norm_w"][0:1, :].partition_broadcast(128), writes=[knw])
    P.dma("sp", qnw[:], D["q_norm_w"][0:1, :].partition_broadcast(128), writes=[qnw])
    pd = [P.sb(f"pd{i}", [128, 3656], F32, st) for i in range(2)]
    cs = [P.sb(f"cs{i}", [128, 2, 32], F32, st) for i in range(2)]
    junk = P.sb("djunk", [128, 1024], F32, st)
    ssq = P.sb("dssq", [128, 16], F32, st)
    rst = P.sb("drst", [128, 16], F32, st)
    kn = P.sb("dkn", [128, 1024], F32, st)
    ko = [P.sb(f"dko{i}", [128, 1024], F32, st) for i in range(2)]
    kio = [P.sb(f"dkio{i}", [128, 64], F32, st) for i in range(2)]
    tmp = P.sb("dtmp", [128, 4, 512], F32, st)
    for ti, (r0, n) in enumerate(TILES):
        p, c = pd[ti % 2], cs[ti % 2]
        P.dma("sp", p[0:n, :], D["Ptok"][r0:r0 + n, C_Q:TOKW], reads=[D["Ptok_t"].sub(ti)], writes=[p])
        P.dma("sp", c[0:n, 0, :], D["cos"][r0:r0 + n, :], writes=[c])
        P.dma("sp", c[0:n, 1, :], D["sin"][r0:r0 + n, :], writes=[c])
        P.dma("pool", D["vout"][r0:r0 + n, :], D["Ptok"][r0:r0 + n, C_V:C_V + 1024], reads=[D["Ptok_t"].sub(ti)])
        k = p[0:n, C_K - C_Q:C_K - C_Q + 1024]
        act(P, junk[0:n, :], k, AF.Square, [p], [junk])
        P.op("dve", lambda e, n=n: e.tensor_reduce(out=ssq[0:n, :], in_=junk[0:n, :].rearrange("p (h d) -> p h d", h=16), axis=AX.X, op=ALU.add), reads=[junk], writes=[ssq])
        act(P, ssq[0:n, :], ssq[0:n, :], AF.Sqrt, [ssq], [ssq], scale=1.0 / 64, bias=g.epsc[0:n, :])
        P.op("dve", lambda e, n=n: e.reciprocal(rst[0:n, :], ssq[0:n, :]), reads=[ssq], writes=[rst])
        kn3 = kn[0:n, :].rearrange("p (h d) -> p h d", h=16)
        tt(P, "dve", kn3, k.rearrange("p (h d) -> p h d", h=16), rst[0:n, :].unsqueeze(2).to_broadcast([n, 16, 64]), ALU.mult, [p, rst], [kn])
        tt(P, "dve", kn3, kn3, knw[0:n, :].unsqueeze(1).to_broadcast([n, 16, 64]), ALU.mult, [kn, knw], [kn])
        o = ko[ti % 2]
        cosb = c[0:n, 0, :].unsqueeze(1).to_broadcast([n, 16, 32])
        sinb = c[0:n, 1, :].unsqueeze(1).to_broadcast([n, 16, 32])
        rope(P, "dve", o[0:n, :].rearrange("p (h t d) -> p h t d", h=16, t=2), kn[0:n, :].rearrange("p (h t d) -> p h t d", h=16, t=2),
             cosb, sinb, tmp, n, [kn, c], [o])
        P.dma("pool", D["kout"][r0:r0 + n, :], o[0:n, :], reads=[o], writes=[D["kout_t"].sub(ti)])
        ki = p[0:n, C_KI - C_Q:C_KI - C_Q + 64]
        oi = kio[ti % 2]
        rope(P, "pool", oi[0:n, :].rearrange("p (h t d) -> p h t d", h=1, t=2), ki.rearrange("p (h t d) -> p h t d", h=1, t=2),
             c[0:n, 0, :].unsqueeze(1), c[0:n, 1, :].unsqueeze(1), tmp, n, [p, c], [oi])
        P.dma("pool", D["kidx"][r0:r0 + n, :], oi[0:n, :], reads=[oi], writes=[D["kidx_t"].sub(ti)])
    P.dma("pool", D["shift"][0:1, :], D["Ptok"][2047:2048, 0:RW_IN], reads=[D["Ptok_t"].sub(15)])
    for q in range(4):
        P.dma("pool", D["shift"][1 + q:2 + q, :], D["Ptok"][2048 + 16 * q + 15:2048 + 16 * q + 16, 0:RW_IN], reads=[D["Ptok_t"].sub(16)])
    P.barrier()
    st.close()


def red(P, eng, out, in_, op, reads, writes):
    return P.op(eng, lambda e: e.tensor_reduce(out=out, in_=in_, axis=AX.X, op=op), reads=reads, writes=writes)


def stt(P, eng, out, in0, scalar, in1, op0, op1, reads, writes):
    return P.op(eng, lambda e: e.scalar_tensor_tensor(out=out, in0=in0, scalar=scalar, in1=in1, op0=op0, op1=op1), reads=reads, writes=writes)


def recip(P, out, in_, reads, writes):
    return P.op("dve", lambda e: e.reciprocal(out, in_), reads=reads, writes=writes)


def mset(P, eng, ap, val, writes):
    return P.op(eng, lambda e: e.memset(ap, val), writes=writes)

import os
STOP = 99
SKIP = ''
NUX = 18
NU = 18
UC = NU * 128
GN_EPS = 64e-5


def unit_rows(u):
    if u < 16:
        return [(0, 128 * u, 128)]
    b = 2048 + 32 * (u - 16)
    return [(0, b, 16), (64, b + 16, 16)]


def bload(P, tile, ap, n=128):
    P.dma("sp", tile[0:n, :], ap.partition_broadcast(n), writes=[tile])


def stage3_rw(P, g, D):
    st = ExitStack()
    sb = lambda name, shape, dt=F32: P.sb(name, shape, dt, st)
    mub = sb("mub", [128, RW_IN]); bload(P, mub, D["mu_rw"][0:1, :])
    w0b = sb("w0b", [128, 1024]); bload(P, w0b, D["w0"][0:1, :])
    a0b = sb("a0b", [128, 1024]); bload(P, a0b, D["a0"][0:1, :])
    kkb = sb("kkb", [128, 1024]); bload(P, kkb, D["k_k"][0:1, :])
    kab = sb("kab", [128, 1024]); bload(P, kab, D["k_a"][0:1, :])
    rkb = sb("rkb", [128, 1024]); bload(P, rkb, D["r_k"][0:1, :])
    loraW = sb("loraW", [128, 1024])
    P.dma("sp", loraW[0:64, :], D["w_up"][:, :], writes=[loraW])
    P.dma("sp", loraW[64:128, :], D["a_up"][:, :], writes=[loraW])
    gup1 = sb("gup1", [128, 1024]); P.dma("sp", gup1[:], D["g_up"][0:128, :], writes=[gup1])
    gup2 = sb("gup2", [32, 1024]); P.dma("sp", gup2[:], D["g_up"][128:160, :], writes=[gup2])
    tri = sb("tri", [128, 128]); P.dma("sp", tri[:], D["m_tri"][:, :], writes=[tri])
    ones = sb("onesb", [128, 128]); P.dma("sp", ones[:], D["m_ones"][:, :], writes=[ones])
    valid = sb("valid", [128, 2]); P.dma("sp", valid[:], D["m_valid"][:, :], writes=[valid])
    tiny = sb("tiny12", [128, 1]); mset(P, "dve", tiny[:], 1e-12, [tiny])
    Pc = sb("Pc", [128, RW_IN]); Pp = sb("Pp", [128, RW_IN])
    L = sb("L288", [128, 288]); LT = sb("LT", [128, 3, 128])
    W = {nm: sb("w_" + nm, [128, 1024]) for nm in ["zt", "za", "gt", "lw", "ah", "kk", "junk", "kmod", "b", "t2", "eL", "eN", "eLm", "eC", "gC", "Lsb", "rt", "at", "kt", "bt", "kp", "bp"]}
    ssq = sb("ssq", [128, 16]); rn = sb("rn", [128, 16]); bc = sb("bc", [128, 16])
    stgT = [sb(f"stgT{i}", [128, 8, 128]) for i in range(2)]
    pLT = P.ps("pLT", [128, 3, 128], F32, st)
    pw = [P.ps(f"pw{i}", [128, 512], F32, st) for i in range(4)]
    pT = [P.ps(f"pTr{i}", [128, 4, 128], F32, st) for i in range(2)]
    pk = 0
    tk = 0
    for u in range(NUX):
        samp = u >= 16
        if samp:
            mset(P, "pool", Pc[:], 0.0, [Pc])
            mset(P, "pool", Pp[:], 0.0, [Pp])
        for (d0, t0, nt) in unit_rows(u):
            ti = 16 if samp else u
            P.dma("sp", Pc[d0:d0 + nt, :], D["Ptok"][t0:t0 + nt, 0:RW_IN], reads=[D["Ptok_t"].sub(ti)], writes=[Pc])
            if samp:
                q = (t0 - 2048) // 16
                P.dma("sp", Pp[d0:d0 + 1, :], D["sshift"][q:q + 1, :], writes=[Pp])
                P.dma("sp", Pp[d0 + 1:d0 + 16, :], D["Ptok"][t0:t0 + 15, 0:RW_IN], reads=[D["Ptok_t"].sub(16)], writes=[Pp])
            elif u == 0:
                mset(P, "pool", Pp[0:1, :], 0.0, [Pp])
                P.dma("sp", Pp[1:128, :], D["Ptok"][0:127, 0:RW_IN], reads=[D["Ptok_t"].sub(0)], writes=[Pp])
            else:
                P.dma("sp", Pp[:, :], D["Ptok"][t0 - 1:t0 + 127, 0:RW_IN], reads=[D["Ptok_t"].sub(u), D["Ptok_t"].sub(u - 1)], writes=[Pp])
        tt(P, "dve", Pp[:], Pp[:], Pc[:], ALU.subtract, [Pp, Pc], [Pp])
        tt(P, "pool", Pp[:], Pp[:], mub[:], ALU.mult, [Pp, mub], [Pp])
        tt(P, "dve", Pc[:], Pc[:], Pp[:], ALU.add, [Pp, Pc], [Pc])
        if STOP == 1:
            continue
        r, k, v = Pc[:, 0:1024], Pc[:, 1024:2048], Pc[:, 2048:3072]
        act(P, L[:, 0:64], Pc[:, 3072:3136], AF.Tanh, [Pc], [L])
        cp(P, "pool", L[:, 64:128], Pc[:, 3136:3200], [Pc], [L])
        act(P, L[:, 128:288], Pc[:, 3200:3360], AF.Sigmoid, [Pc], [L])
        tr(P, pLT[:, 0, :], L[:, 0:128], g.identf[:], [L, g.identf], [pLT])
        tr(P, pLT[:, 1, :], L[:, 128:256], g.identf[:], [L, g.identf], [pLT])
        tr(P, pLT[0:32, 2, :], L[:, 256:288], g.identf[:], [L, g.identf], [pLT])
        cp(P, "dve", LT[:, 0:2, :], pLT[:, 0:2, :], [pLT], [LT])
        cp(P, "dve", LT[0:32, 2, :], pLT[0:32, 2, :], [pLT], [LT])
        if STOP == 2:
            continue
        for hf in range(2):
            cs = slice(hf * 512, (hf + 1) * 512)
            p = pw[pk % 4]; pk += 1
            mm(P, p[:], LT[0:64, 0, :], loraW[0:64, cs], True, True, [LT, loraW], [p])
            tt(P, "dve", W["zt"][:, cs], p[:], w0b[:, cs], ALU.add, [p, w0b], [W["zt"]])
            p = pw[pk % 4]; pk += 1
            mm(P, p[:], LT[64:128, 0, :], loraW[64:128, cs], True, True, [LT, loraW], [p])
            tt(P, "dve", W["za"][:, cs], p[:], a0b[:, cs], ALU.add, [p, a0b], [W["za"]])
            p = pw[pk % 4]; pk += 1
            mm(P, p[:], LT[:, 1, :], gup1[:, cs], True, False, [LT, gup1], [p])
            mm(P, p[:], LT[0:32, 2, :], gup2[0:32, cs], False, True, [LT, gup2], [p])
            cp(P, "act", W["gt"][:, cs], p[:], [p], [W["gt"]])
        if STOP == 3:
            continue
        act(P, W["lw"][:], W["zt"][:], AF.Sigmoid, [W["zt"]], [W["lw"]])
        ts(P, "dve", W["lw"][:], W["lw"][:], -0.6065306597126334, valid[:, (1 if samp else 0):(2 if samp else 1)], ALU.mult, ALU.mult, [W["lw"], valid], [W["lw"]])
        if STOP == 31:
            continue
        act(P, W["ah"][:], W["za"][:], AF.Sigmoid, [W["za"]], [W["ah"]])
        if STOP == 32:
            continue
        tt(P, "pool", W["kk"][:], k, kkb[:], ALU.mult, [Pc, kkb], [W["kk"]])
        act(P, W["junk"][:], W["kk"][:], AF.Square, [W["kk"]], [W["junk"]])
        if STOP == 33:
            continue
        red(P, "dve", ssq[:], W["junk"][:].rearrange("p (h d) -> p h d", h=16), ALU.add, [W["junk"]], [ssq])
        act(P, ssq[:], ssq[:], AF.Sqrt, [ssq, tiny], [ssq], bias=tiny[:])
        recip(P, rn[:], ssq[:], [ssq], [rn])
        if STOP == 34:
            continue
        kk3 = W["kk"][:].rearrange("p (h d) -> p h d", h=16)
        tt(P, "dve", kk3, kk3, rn[:].unsqueeze(2).to_broadcast([128, 16, 64]), ALU.mult, [W["kk"], rn], [W["kk"]])
        if STOP == 35:
            continue
        stt(P, "dve", W["t2"][:], W["ah"][:], -1.0, kab[:], ALU.add, ALU.mult, [W["ah"], kab], [W["t2"]])
        stt(P, "dve", W["kmod"][:], W["t2"][:], 1.0, k, ALU.add, ALU.mult, [W["t2"], Pc], [W["kmod"]])
        if STOP == 36:
            continue
        tt(P, "pool", W["b"][:], W["kk"][:], W["ah"][:], ALU.mult, [W["kk"], W["ah"]], [W["b"]])
        tt(P, "pool", W["t2"][:], r, W["kmod"][:], ALU.mult, [Pc, W["kmod"]], [W["t2"]])
        tt(P, "pool", W["t2"][:], W["t2"][:], rkb[:], ALU.mult, [W["t2"], rkb], [W["t2"]])
        if STOP == 37:
            continue
        red(P, "dve", bc[:], W["t2"][:].rearrange("p (h d) -> p h d", h=16), ALU.add, [W["t2"]], [bc])
        P.dma("pool", D["BC"][u * 128:(u + 1) * 128, :], bc[:], reads=[bc], writes=[D["BC_t"].sub(u)])
        if STOP == 4:
            continue
        for hf in range(2):
            cs = slice(hf * 512, (hf + 1) * 512)
            pL = pw[pk % 4]; pk += 1
            pLt = pw[pk % 4]; pk += 1
            mm(P, pL[:], tri[:], W["lw"][:, cs], True, True, [tri, W["lw"]], [pL])
            mm(P, pLt[:], ones[:], W["lw"][:, cs], True, True, [ones, W["lw"]], [pLt])
            if "a1" not in SKIP:
                act(P, W["eL"][:, cs], pL[:], AF.Exp, [pL], [W["eL"]])
            if "a2" not in SKIP:
                act(P, W["eN"][:, cs], pL[:], AF.Exp, [pL], [W["eN"]], scale=-1.0)
            act(P, W["Lsb"][:, cs], pL[:], AF.Identity, [pL], [W["Lsb"]])
            if "a3" not in SKIP:
                act(P, W["gC"][:, cs], pLt[:], AF.Exp, [pLt], [W["gC"]])
            if "d1" not in SKIP:
                tt(P, "dve", W["eC"][:, cs], pLt[:], W["Lsb"][:, cs], ALU.subtract, [pLt, W["Lsb"]], [W["eC"]])
            if "d2" not in SKIP:
                tt(P, "dve", W["eLm"][:, cs], W["Lsb"][:, cs], W["lw"][:, cs], ALU.subtract, [W["Lsb"], W["lw"]], [W["eLm"]])
        if "exp2" not in SKIP:
            act(P, W["eC"][:], W["eC"][:], AF.Exp, [W["eC"]], [W["eC"]])
            act(P, W["eLm"][:], W["eLm"][:], AF.Exp, [W["eLm"]], [W["eLm"]])
        if STOP == 5:
            continue
        tt(P, "dve", W["rt"][:], r, W["eL"][:], ALU.mult, [Pc, W["eL"]], [W["rt"]])
        stt(P, "dve", W["at"][:], W["kk"][:], -1.0, W["eLm"][:], ALU.mult, ALU.mult, [W["kk"], W["eLm"]], [W["at"]])
        tt(P, "pool", W["kt"][:], W["kmod"][:], W["eN"][:], ALU.mult, [W["kmod"], W["eN"]], [W["kt"]])
        tt(P, "pool", W["bt"][:], W["b"][:], W["eN"][:], ALU.mult, [W["b"], W["eN"]], [W["bt"]])
        tt(P, "pool", W["kp"][:], W["kmod"][:], W["eC"][:], ALU.mult, [W["kmod"], W["eC"]], [W["kp"]])
        tt(P, "dve", W["bp"][:], W["b"][:], W["eC"][:], ALU.mult, [W["b"], W["eC"]], [W["bp"]])
        rows = slice(u * 128, (u + 1) * 128)
        P.dma("pool", D["Vs"][rows, :], v, reads=[Pc], writes=[D["Vs_t"].sub(u)])
        P.dma("pool", D["KPs"][rows, :], W["kp"][:], reads=[W["kp"]], writes=[D["KPs_t"].sub(u)])
        P.dma("pool", D["BPs"][rows, :], W["bp"][:], reads=[W["bp"]], writes=[D["BPs_t"].sub(u)])
        P.dma("pool", D["Gs"][rows, :], W["gt"][:], reads=[W["gt"]], writes=[D["Gs_t"].sub(u)])
        if STOP == 6:
            continue
        for nm, dst in [("rt", "RT"), ("at", "AT"), ("kt", "KT"), ("bt", "BT"), ("gC", "GCT")]:
            s = stgT[tk % 2]
            for half in range(2):
                p = pT[tk % 2]
                for j in range(4):
                    fc = half * 4 + j
                    tr(P, p[:, j, :], W[nm][:, fc * 128:(fc + 1) * 128], g.identf[:], [W[nm], g.identf], [p])
                cp(P, "act" if half else "dve", s[:, half * 4:(half + 1) * 4, :], p[:], [p], [s])
                tk += 1
            P.dma("pool", D[dst].rearrange("(fc p) c -> p fc c", p=128)[:, :, u * 128:(u + 1) * 128], s[:], reads=[s], writes=[D[dst + "_t"].sub(u)])
    P.barrier()
    st.close()


def stage4_scan(P, g, D):
    st = ExitStack()
    sb = lambda name, shape, dt=F32: P.sb(name, shape, dt, st)
    msl = sb("msl", [128, 128]); P.dma("sp", msl[:], D["m_sl"][:, :], writes=[msl])
    msu = sb("msu", [128, 128]); P.dma("sp", msu[:], D["m_su"][:, :], writes=[msu])
    mu = sb("mu", [128, 128]); P.dma("sp", mu[:], D["m_u"][:, :], writes=[mu])
    lnw = sb("lnw", [128, 1024]); bload(P, lnw, D["lnx_w"][0:1, :])
    lnb = sb("lnb", [128, 1024]); bload(P, lnb, D["lnx_b"][0:1, :])
    gne = sb("gne", [128, 1]); mset(P, "dve", gne[:], GN_EPS, [gne])
    H = sb("H", [128, 8, 64])
    mset(P, "dve", H[:], 0.0, [H.sub((a, b)) for a in range(8) for b in range(2)])
    FM = {nm: [sb(f"fm_{nm}{i}", [128, 8, 128]) for i in range(2)] for nm in ["RT", "AT", "KT", "BT", "GCT"]}
    TM = {nm: [sb(f"tm_{nm}{i}", [128, 1024]) for i in range(2)] for nm in ["Vs", "KPs", "BPs"]}
    Y = [sb(f"Y{i}", [128, 1024]) for i in range(2)]
    NM = 16
    GS = 6
    MS = [[sb(f"ms{s}_{i}", [128, 128]) for i in range(NM)] for s in range(GS)]
    XU = [[sb(f"xu{s}_{i}", [128, 64]) for i in range(4)] for s in range(GS)]
    pLane = [P.ps(f"pLane{i}", [128, 512], F32, st) for i in range(GS)]
    laneM = [[SlotAP(pLane[l], pLane[l][:, j * 128:(j + 1) * 128]) for j in range(2)] for l in range(GS)]
    laneS = [[SlotAP(pLane[l], pLane[l][:, 256 + j * 64:256 + (j + 1) * 64]) for j in range(4)] for l in range(GS)]
    gt = sb("p_gt", [128, 1024]); bc = sb("p_bc", [128, 16]); vv = None
    pw = {nm: sb("p_" + nm, [128, 1024]) for nm in ["yc", "sq", "yb"]}
    st16 = {nm: sb("p16_" + nm, [128, 16]) for nm in ["mean", "var", "rstd"]}
    oab = sb("oab", [128, 1024], BF16)
    pO = [P.ps(f"pO{i}", [128, 8, 128], BF16, st) for i in range(1)]
    oT = sb("oT", [128, 8, 128], BF16)
    Ssb = [sb(f"Ssb{i}", [128, 64]) for i in range(GS)]
    Sout = sb("Sout", [128, 8, 64])

    def head_gen(u, fc, hp, lane, FMu, TMu, y):
        samp = u >= 16
        RT, AT, KT, BT, GC = FMu
        V, KP, BP = TMu
        mk = [0]; sk = [0]

        def nextM():
            mk[0] += 1
            return laneM[lane][mk[0] % 2]

        def nextS():
            sk[0] += 1
            return laneS[lane][sk[0] % 4]

        h = 2 * fc + hp
        pb = hp * 64
        hs = slice(h * 64, (h + 1) * 64)
        M = MS[lane]
        a_ = AT[pb:pb + 64, fc, :]; b_ = BT[pb:pb + 64, fc, :]; k_ = KT[pb:pb + 64, fc, :]; r_ = RT[pb:pb + 64, fc, :]
        A, N, AakT, RBt, RKt = M[0], M[1], M[2], M[3], M[4]
        for (dst, l, r, msk) in [(A, a_, b_, msl), (N, b_, a_, msu), (AakT, k_, a_, msu), (RBt, b_, r_, mu), (RKt, k_, r_, mu)]:
            p = nextM()
            mm(P, p[:], l, r, True, True, [AT, BT, KT, RT], [p])
            tt(P, "dve", dst[:], p[:], msk[:], ALU.mult, [p, msk], [dst])
            yield
        Ap = [A, M[5], M[6], M[7], M[8]]
        Np = [N, M[9], M[10], M[11], M[12], M[13]]
        for k in range(5):
            if k < 4:
                p = nextM()
                mm(P, p[:], Np[k][:], Ap[k][:], True, True, [Np[k], Ap[k]], [p])
                cp(P, "act", Ap[k + 1][:], p[:], [p], [Ap[k + 1]])
            p = nextM()
            mm(P, p[:], Ap[k][:], Np[k][:], True, True, [Np[k], Ap[k]], [p])
            cp(P, "dve" if k % 2 else "act", Np[k + 1][:], p[:], [p], [Np[k + 1]])
            yield
        Tt = [M[14], M[15]]
        tt(P, "dve", Tt[0][:], Np[5][:], g.identf[:], ALU.add, [Np[5], g.identf], [Tt[0]])
        cur = 0
        for k in [4, 3, 2, 1, 0]:
            p = nextM()
            mm(P, p[:], Ap[k][:], Tt[cur][:], True, True, [Ap[k], Tt[cur]], [p])
            tt(P, "dve", Tt[1 - cur][:], p[:], Tt[cur][:], ALU.add, [p, Tt[cur]], [Tt[1 - cur]])
            cur = 1 - cur
            yield
        TT_ = Tt[cur]
        for c in range(2):
            pc = c * 64
            cc = slice(c * 64, (c + 1) * 64)
            Hh = H[pb:pb + 64, fc, :]
            Hd = H.sub((fc, hp))
            if samp:
                q = 2 * (u - 16) + c
                P.dma("sp", Ssb[lane][pb:pb + 64, :], D["swkv"][q, h, :, :], writes=[Ssb[lane]])
                p = nextS()
                mm(P, p[pb:pb + 64, :], Ssb[lane][pb:pb + 64, :], g.identf[pb:pb + 64, pb:pb + 64], True, True, [Ssb[lane], g.identf], [p])
                cp(P, "act", Hh, p[pb:pb + 64, :], [p], [Hd])
                yield
            X_sb, U_sb = XU[lane][2 * c], XU[lane][2 * c + 1]
            p = nextS()
            mm(P, p[pc:pc + 64, :], a_[:, cc], Hh, True, False, [AT, Hd], [p])
            mm(P, p[pc:pc + 64, :], AakT[pc:pc + 64, cc], V[pc:pc + 64, hs], False, True, [AakT, V], [p])
            cp(P, "act", X_sb[pc:pc + 64, :], p[pc:pc + 64, :], [p], [X_sb])
            yield
            p = nextS()
            mm(P, p[pc:pc + 64, :], TT_[pc:pc + 64, cc], X_sb[pc:pc + 64, :], True, True, [TT_, X_sb], [p])
            cp(P, "act", U_sb[pc:pc + 64, :], p[pc:pc + 64, :], [p], [U_sb])
            yield
            p = nextS()
            mm(P, p[pc:pc + 64, :], r_[:, cc], Hh, True, False, [RT, Hd], [p])
            mm(P, p[pc:pc + 64, :], RBt[pc:pc + 64, cc], U_sb[pc:pc + 64, :], False, False, [RBt, U_sb], [p])
            mm(P, p[pc:pc + 64, :], RKt[pc:pc + 64, cc], V[pc:pc + 64, hs], False, True, [RKt, V], [p])
            cp(P, "act", y[pc:pc + 64, hs], p[pc:pc + 64, :], [p], [y.sub(h)])
            p = nextS()
            mm(P, p[pb:pb + 64, :], BP[pc:pc + 64, hs], U_sb[pc:pc + 64, :], True, False, [BP, U_sb], [p])
            mm(P, p[pb:pb + 64, :], KP[pc:pc + 64, hs], V[pc:pc + 64, hs], False, True, [KP, V], [p])
            stt(P, "dve", Hh, Hh, GC[pb:pb + 64, fc, c * 64:c * 64 + 1], p[pb:pb + 64, :], ALU.mult, ALU.add, [Hd, GC, p], [Hd])
            yield
            if samp or (u == 15 and c == 1):
                q = (1 + 2 * (u - 16) + c) if samp else 0
                p = nextS()
                mm(P, p[pb:pb + 64, :], Hh, g.identf[pb:pb + 64, pb:pb + 64], True, True, [Hd, g.identf], [p])
                cp(P, "act", Sout[pb:pb + 64, fc, :], p[pb:pb + 64, :], [p], [Sout.sub(h)])
                r0 = (q * 16 + h) * 64
                P.dma("pool", D["wkv"][r0:r0 + 64, :], Sout[pb:pb + 64, fc, :], reads=[Sout.sub(h)])
                yield

    for u in range(NU):
        samp = u >= 16
        b2 = u % 2
        cols = slice(u * 128, (u + 1) * 128)
        for nm in FM:
            P.dma("sp", FM[nm][b2][:], D[nm].rearrange("(fc p) c -> p fc c", p=128)[:, :, cols], reads=[D[nm + "_t"].sub(u)], writes=[FM[nm][b2]])
        for nm in TM:
            P.dma("sp", TM[nm][b2][:], D[nm][cols, :], reads=[D[nm + "_t"].sub(u)], writes=[TM[nm][b2]])
        FMu = [FM[nm][b2] for nm in ["RT", "AT", "KT", "BT", "GCT"]]
        TMu = [TM[nm][b2] for nm in ["Vs", "KPs", "BPs"]]
        V = TMu[0]
        y = Y[b2]
        heads = [(fc, hp) for fc in range(8) for hp in range(2)]
        active = []
        nxt = 0
        for lane in range(GS):
            fc, hp = heads[nxt]; nxt += 1
            active.append((lane, head_gen(u, fc, hp, lane, FMu, TMu, y)))
        while active:
            still = []
            for lane, gen in active:
                try:
                    next(gen)
                    still.append((lane, gen))
                except StopIteration:
                    if nxt < len(heads):
                        fc, hp = heads[nxt]; nxt += 1
                        still.append((lane, head_gen(u, fc, hp, lane, FMu, TMu, y)))
            active = still
        ysubs = [y.sub(h) for h in range(16)]
        P.dma("sp", gt[:], D["Gs"][cols, :], reads=[D["Gs_t"].sub(u)], writes=[gt])
        P.dma("sp", bc[:], D["BC"][cols, :], reads=[D["BC_t"].sub(u)], writes=[bc])
        y3 = y[:].rearrange("p (h d) -> p h d", h=16)
        bcast = lambda t16: t16[:].unsqueeze(2).to_broadcast([128, 16, 64])
        v3 = lambda t: t[:].rearrange("p (h d) -> p h d", h=16)
        red(P, "dve", st16["mean"][:], y3, ALU.add, ysubs, [st16["mean"]])
        ts(P, "dve", st16["mean"][:], st16["mean"][:], 1.0 / 64, None, ALU.mult, None, [st16["mean"]], [st16["mean"]])
        tt(P, "dve", v3(pw["yc"]), y3, bcast(st16["mean"]), ALU.subtract, ysubs + [st16["mean"]], [pw["yc"]])
        act(P, pw["sq"][:], pw["yc"][:], AF.Square, [pw["yc"]], [pw["sq"]])
        red(P, "dve", st16["var"][:], v3(pw["sq"]), ALU.add, [pw["sq"]], [st16["var"]])
        act(P, st16["var"][:], st16["var"][:], AF.Sqrt, [st16["var"], gne], [st16["var"]], scale=1.0 / 64, bias=gne[:])
        recip(P, st16["rstd"][:], st16["var"][:], [st16["var"]], [st16["rstd"]])
        tt(P, "dve", v3(pw["yc"]), v3(pw["yc"]), bcast(st16["rstd"]), ALU.mult, [pw["yc"], st16["rstd"]], [pw["yc"]])
        tt(P, "pool", pw["yc"][:], pw["yc"][:], lnw[:], ALU.mult, [pw["yc"], lnw], [pw["yc"]])
        tt(P, "pool", pw["yc"][:], pw["yc"][:], lnb[:], ALU.add, [pw["yc"], lnb], [pw["yc"]])
        tt(P, "dve", v3(pw["yb"]), v3(V), bcast(bc), ALU.mult, [V, bc], [pw["yb"]])
        tt(P, "pool", pw["yc"][:], pw["yc"][:], pw["yb"][:], ALU.add, [pw["yc"], pw["yb"]], [pw["yc"]])
        tt(P, "dve", oab[:], pw["yc"][:], gt[:], ALU.mult, [pw["yc"], gt], [oab])
        if D.get("OAdbg") is not None:
            P.dma("pool", D["OAdbg"][cols, :], pw["yc"][:], reads=[pw["yc"]])
        for fc in range(8):
            tr(P, pO[0][:, fc, :], oab[:, fc * 128:(fc + 1) * 128], g.identb[:], [oab, g.identb], [pO[0]])
        cp(P, "act", oT[:], pO[0][:], [pO[0]], [oT])
        dstv = D["OAT"].rearrange("(fc p) c -> p fc c", p=128)
        for (d0, t0, nt) in unit_rows(u):
            P.dma("pool", dstv[:, :, t0:t0 + nt], oT[:, :, d0:d0 + nt], reads=[oT], writes=[D["OAT_t"]])
    P.barrier()
    st.close()

TOPK = 256
NEG = -1.0e30
NBIS = 18


def headnorm(P, src, dst, nw, junk, ssq, rst, g, n, reads, pre=1.0):
    act(P, junk[0:n, :], src, AF.Square, reads, [junk])
    red(P, "dve", ssq[0:n, :], junk[0:n, :].rearrange("p (h d) -> p h d", h=16), ALU.add, [junk], [ssq])
    act(P, ssq[0:n, :], ssq[0:n, :], AF.Sqrt, [ssq], [ssq], scale=1.0 / 64, bias=g.epsc[0:n, :])
    recip(P, rst[0:n, :], ssq[0:n, :], [ssq], [rst])
    d3 = dst.rearrange("p (h d) -> p h d", h=16)
    tt(P, "dve", d3, src.rearrange("p (h d) -> p h d", h=16), rst[0:n, :].unsqueeze(2).to_broadcast([n, 16, 64]), ALU.mult, list(reads) + [rst], [junk])
    stt(P, "dve", d3, d3, pre, nw[0:n, :].unsqueeze(1).to_broadcast([n, 16, 64]), ALU.mult, ALU.mult, [junk, nw], [junk])


def stage3_dsa(P, g, D):
    st = ExitStack()
    sb = lambda name, shape, dt=F32: P.sb(name, shape, dt, st)
    knw = sb("knw", [128, 64]); qnw = sb("qnw", [128, 64])
    P.dma("sp", knw[:], D["k_norm_w"][0:1, :].partition_broadcast(128), writes=[knw])
    P.dma("sp", qnw[:], D["q_norm_w"][0:1, :].partition_broadcast(128), writes=[qnw])
    pd = [sb(f"pd{i}", [128, 3656]) for i in range(2)]
    cs = [sb(f"cs{i}", [128, 2, 32]) for i in range(2)]
    junk = sb("djunk", [128, 1024]); ssq = sb("dssq", [128, 16]); rst = sb("drst", [128, 16])
    nrm = sb("dnrm", [128, 1024])
    ko = [sb(f"dko{i}", [128, 1024]) for i in range(2)]
    qo = sb("dqo", [128, 1024]); qio = sb("dqio", [128, 512])
    kio = [sb(f"dkio{i}", [128, 64]) for i in range(2)]
    wio = [sb(f"dwio{i}", [128, 8]) for i in range(2)]
    tmp = sb("dtmp", [128, 4, 512])
    cat = sb("dcat", [128, 2624], BF16)
    pT = [P.ps(f"dpT{i}", [128, 8, 128], BF16, st) for i in range(2)]
    sT = [sb(f"dsT{i}", [128, 8, 128], BF16) for i in range(2)]
    tk = 0
    for ti, (r0, n) in enumerate(TILES):
        p, c = pd[ti % 2], cs[ti % 2]
        P.dma("sp", p[0:n, :], D["Ptok"][r0:r0 + n, C_Q:TOKW], reads=[D["Ptok_t"].sub(ti)], writes=[p])
        P.dma("sp", c[0:n, 0, :], D["cos"][r0:r0 + n, :], writes=[c])
        P.dma("sp", c[0:n, 1, :], D["sin"][r0:r0 + n, :], writes=[c])
        P.dma("pool", D["vout"][r0:r0 + n, :], D["Ptok"][r0:r0 + n, C_V:C_V + 1024], reads=[D["Ptok_t"].sub(ti)])
        cosb = c[0:n, 0, :].unsqueeze(1).to_broadcast([n, 16, 32])
        sinb = c[0:n, 1, :].unsqueeze(1).to_broadcast([n, 16, 32])
        r4 = lambda ap, H: ap.rearrange("p (h t d) -> p h t d", h=H, t=2)
        headnorm(P, p[0:n, C_K - C_Q:C_K - C_Q + 1024], junk[0:n, :], knw, junk, ssq, rst, g, n, [p])
        o = ko[ti % 2]
        rope(P, "dve", r4(o[0:n, :], 16), r4(junk[0:n, :], 16), cosb, sinb, tmp, n, [junk, c], [o])
        P.dma("pool", D["kout"][r0:r0 + n, :], o[0:n, :], reads=[o], writes=[D["kout_t"].sub(ti)])
        cp(P, "pool", cat[0:n, 1024:2048], o[0:n, :], [o], [cat])
        headnorm(P, p[0:n, 0:1024], junk[0:n, :], qnw, junk, ssq, rst, g, n, [p], pre=0.125)
        rope(P, "dve", r4(qo[0:n, :], 16), r4(junk[0:n, :], 16), cosb, sinb, tmp, n, [junk, c], [qo])
        cp(P, "pool", cat[0:n, 0:1024], qo[0:n, :], [qo], [cat])
        rope(P, "pool", r4(qio[0:n, :], 8), r4(p[0:n, C_QI - C_Q:C_QI - C_Q + 512], 8), c[0:n, 0, :].unsqueeze(1).to_broadcast([n, 8, 32]),
             c[0:n, 1, :].unsqueeze(1).to_broadcast([n, 8, 32]), tmp, n, [p, c], [qio])
        cp(P, "pool", cat[0:n, 2048:2560], qio[0:n, :], [qio], [cat])
        oi = kio[ti % 2]
        rope(P, "pool", r4(oi[0:n, :], 1), r4(p[0:n, C_KI - C_Q:C_KI - C_Q + 64], 1), c[0:n, 0, :].unsqueeze(1), c[0:n, 1, :].unsqueeze(1), tmp, n, [p, c], [oi])
        P.dma("pool", D["kidx"][r0:r0 + n, :], oi[0:n, :], reads=[oi], writes=[D["kidx_t"].sub(ti)])
        cp(P, "pool", cat[0:n, 2560:2624], oi[0:n, :], [oi], [cat])
        w = wio[ti % 2]
        ts(P, "dve", w[0:n, :], p[0:n, C_WI - C_Q:C_WI - C_Q + 8], 512.0 ** -0.5, None, ALU.mult, None, [p], [w])
        P.dma("pool", D["WI"][r0:r0 + n, :], w[0:n, :], reads=[w], writes=[D["WI_t"].sub(ti)])
        for (c0, nch, dst) in [(0, 8, "QT"), (1024, 8, "KTn"), (2048, 4, "QIT")]:
            pt, s_ = pT[tk % 2], sT[tk % 2]; tk += 1
            for j in range(nch):
                tr(P, pt[:, j, 0:n], cat[0:n, c0 + j * 128:c0 + (j + 1) * 128], g.identb[0:n, 0:n], [cat, g.identb], [pt])
            cp(P, "act", s_[:, 0:nch, 0:n], pt[:, 0:nch, 0:n], [pt], [s_])
            P.dma("pool", D[dst].rearrange("(fc p) c -> p fc c", p=128)[:, :, r0:r0 + n], s_[:, 0:nch, 0:n], reads=[s_], writes=[D[dst + "_t"].sub(ti)])
        pt, s_ = pT[tk % 2], sT[tk % 2]; tk += 1
        tr(P, pt[0:64, 0, 0:n], cat[0:n, 2560:2624], g.identb[0:n, 0:n], [cat, g.identb], [pt])
        cp(P, "act", s_[0:64, 0, 0:n], pt[0:64, 0, 0:n], [pt], [s_])
        P.dma("pool", D["KIT"][:, r0:r0 + n], s_[0:64, 0, 0:n], reads=[s_], writes=[D["KIT_t"].sub(ti)])
    P.dma("pool", D["shift"][0:1, :], D["Ptok"][2047:2048, 0:RW_IN], reads=[D["Ptok_t"].sub(15)])
    for q in range(4):
        P.dma("pool", D["shift"][1 + q:2 + q, :], D["Ptok"][2048 + 16 * q + 15:2048 + 16 * q + 16, 0:RW_IN], reads=[D["Ptok_t"].sub(16)])
    P.barrier()
    st.close()


def stage5_attn(P, g, D):
    st = ExitStack()
    sb = lambda name, shape, dt=F32: P.sb(name, shape, dt, st)
    kT = sb("kT", [128, 8, 2064], BF16)
    Vb = sb("Vb", [128, 17, 16, 65], BF16)
    kiT = sb("kiT", [128, 2064], BF16)
    Ibufs = [sb(f"Ibuf{i}", [128, 2064]) for i in range(2)]; junkI = sb("junkI", [128, 2064], BF16)
    maskbs = [sb(f"maskb{i}", [128, 2064], BF16) for i in range(2)]; maskTs = [sb(f"maskT{i}", [128, 17, 128], BF16) for i in range(2)]
    qT = [sb(f"qT{i}", [128, 8, 128], BF16) for i in range(2)]
    qiT = [sb(f"qiT{i}", [128, 4, 128], BF16) for i in range(2)]
    wi = [sb(f"wi{i}", [128, 8]) for i in range(2)]
    rl = [sb(f"rl{i}", [128, 512]) for i in range(2)]
    E = [sb(f"E{i}", [128, 4, 128], BF16) for i in range(3)]
    p2 = sb("pow2", [128, NBIS]); P.dma("sp", p2[:], D["m_pow2"][:, :], writes=[p2])
    s1s = [{nm: sb(f"s1_{nm}{i}", [128, 1]) for nm in ["B", "mid", "cnt", "t2", "t3", "thr"]} for i in range(2)]
    dtabs = [sb(f"dtab{i}", [128, NBIS]) for i in range(2)]
    osb = sb("osb", [128, 1024]); osbb = sb("osbb", [128, 1024], BF16); rec = sb("orec", [128, 4])
    oT = sb("oTb", [128, 8, 128], BF16)
    vst = [sb(f"vst{i}", [128, 1024]) for i in range(2)]
    kst = sb("kstb", [128, 1024], BF16); kis = sb("kis", [128, 64]); kisb = sb("kisb", [128, 128], BF16)
    pI = [P.ps(f"pI{i}", [128, 512], F32, st) for i in range(2)]
    pS = [P.ps(f"pSc{i}", [128, 4, 128], F32, st) for i in range(2)]
    pO = [P.ps(f"pOa{i}", [128, 4, 65], F32, st) for i in range(2)]
    pmT = P.ps("pmT", [128, 8, 128], BF16, st)
    mset(P, "pool", Vb[:, :, :, 64:65], 1.0, [Vb])
    cnt = {"rl": 0, "E": 0, "S": 0, "I": 0, "v": 0}

    def load_v_block(j, src_ap, nrow):
        v = vst[cnt["v"] % 2]; cnt["v"] += 1
        P.dma("sp", v[0:nrow, :], src_ap, writes=[v])
        cp(P, "pool", Vb[0:nrow, j, :, 0:64], v[0:nrow, :].rearrange("p (h d) -> p h d", h=16), [v], [Vb])

    def phase1(nq, tcol, nblk, lastw, prompt_tile, obt_cols, par):
        S = (nblk - 1) * 128 + lastw
        q_, qi_, w_ = qT[par], qiT[par], wi[par]
        Ibuf, maskb, maskT, s1, dtab = Ibufs[par], maskbs[par], maskTs[par], s1s[par], dtabs[par]
        P.dma("sp", q_[:, :, 0:nq], D["QT"].rearrange("(fc p) c -> p fc c", p=128)[:, :, tcol:tcol + nq], reads=[D["QT_t"]], writes=[q_])
        P.dma("sp", qi_[:, :, 0:nq], D["QIT"].rearrange("(fc p) c -> p fc c", p=128)[:, :, tcol:tcol + nq], reads=[D["QIT_t"]], writes=[qi_])
        P.dma("sp", w_[0:nq, :], D["WI"][tcol:tcol + nq, :], reads=[D["WI_t"]], writes=[w_])
        for s0 in range(0, S, 512):
            w = min(512, S - s0)
            for h in range(8):
                pb = (h % 2) * 64
                p = pI[cnt["I"] % 2]; cnt["I"] += 1
                mm(P, p[0:nq, 0:w], qi_[pb:pb + 64, h // 2, 0:nq], kiT[pb:pb + 64, s0:s0 + w], True, True, [qi_, kiT], [p])
                r = rl[cnt["rl"] % 2]; cnt["rl"] += 1
                act(P, r[0:nq, 0:w], p[0:nq, 0:w], AF.Relu, [p], [r])
                if h == 0:
                    ts(P, "dve", Ibuf[0:nq, s0:s0 + w], r[0:nq, 0:w], w_[0:nq, 0:1], None, ALU.mult, None, [r, w_], [Ibuf])
                else:
                    stt(P, "dve", Ibuf[0:nq, s0:s0 + w], r[0:nq, 0:w], w_[0:nq, h:h + 1], Ibuf[0:nq, s0:s0 + w], ALU.mult, ALU.add, [r, w_, Ibuf], [Ibuf])
        P.op("dve", lambda e: e.tensor_reduce(out=s1["B"][0:nq, :], in_=Ibuf[0:nq, 0:S], axis=AX.X, op=ALU.max, apply_absolute_value=True), reads=[Ibuf], writes=[s1["B"]])
        ts(P, "dve", s1["B"][0:nq, :], s1["B"][0:nq, :], 1.001, 1e-6, ALU.mult, ALU.add, [s1["B"]], [s1["B"]])
        ts(P, "dve", dtab[0:nq, :], p2[0:nq, :], s1["B"][0:nq, :], None, ALU.mult, None, [p2, s1["B"]], [dtab])
        if prompt_tile:
            mset(P, "dve", Ibuf[0:64, S - 64:S], NEG, [Ibuf])
        mset(P, "dve", s1["mid"][0:nq, :], 0.0, [s1["mid"]])
        for k in range(NBIS):
            ts(P, "dve", junkI[0:nq, 0:S], Ibuf[0:nq, 0:S], s1["mid"][0:nq, :], None, ALU.is_ge, ALU.add, [Ibuf, s1["mid"]], [junkI, s1["cnt"]], accum_out=s1["cnt"][0:nq, :])
            ts(P, "dve", s1["t2"][0:nq, :], s1["cnt"][0:nq, :], TOPK - 0.5, 2.0, ALU.is_ge, ALU.mult, [s1["cnt"]], [s1["t2"]])
            ts(P, "dve", s1["t3"][0:nq, :], s1["t2"][0:nq, :], -1.0, dtab[0:nq, k:k + 1], ALU.add, ALU.mult, [s1["t2"], dtab], [s1["t3"]])
            tt(P, "dve", s1["mid"][0:nq, :], s1["mid"][0:nq, :], s1["t3"][0:nq, :], ALU.add, [s1["mid"], s1["t3"]], [s1["mid"]])
        tt(P, "dve", s1["thr"][0:nq, :], s1["mid"][0:nq, :], dtab[0:nq, NBIS - 1:NBIS], ALU.subtract, [s1["mid"], dtab], [s1["thr"]])
        ts(P, "dve", maskb[0:nq, 0:S], Ibuf[0:nq, 0:S], s1["thr"][0:nq, :], None, ALU.is_ge, None, [Ibuf, s1["thr"]], [maskb])
        for j0 in range(0, nblk, 8):
            nb_ = min(8, nblk - j0)
            for jj in range(nb_):
                j = j0 + jj
                wj = 128 if j < nblk - 1 else lastw
                tr(P, pmT[0:wj, jj, 0:nq], maskb[0:nq, j * 128:j * 128 + wj], g.identb[0:nq, 0:nq], [maskb, g.identb], [pmT])
            full = nb_ if (j0 + nb_ < nblk or lastw == 128) else nb_ - 1
            if full > 0:
                cp(P, "act", maskT[:, j0:j0 + full, 0:nq], pmT[:, 0:full, 0:nq], [pmT], [maskT])
            if full < nb_:
                cp(P, "act", maskT[0:lastw, j0 + full, 0:nq], pmT[0:lastw, full, 0:nq], [pmT], [maskT])

    def phase2(nq, tcol, nblk, lastw, prompt_tile, obt_cols, par):
        q_ = qT[par]
        maskT = maskTs[par]
        items = [(h, j0) for h in range(16) for j0 in range(0, nblk, 4)]

        def qk(it):
            h, j0 = it
            pb = (h % 2) * 64
            p = pS[cnt["S"] % 2]; cnt["S"] += 1
            for jj in range(min(4, nblk - j0)):
                j = j0 + jj
                wj = 128 if j < nblk - 1 else lastw
                mm(P, p[0:wj, jj, 0:nq], kT[pb:pb + 64, h // 2, j * 128:j * 128 + wj], q_[pb:pb + 64, h // 2, 0:nq], True, True, [kT, q_], [p])
            return p

        pcur = qk(items[0])
        for k, it in enumerate(items):
            pnext = qk(items[k + 1]) if k + 1 < len(items) else None
            h, j0 = it
            nb_ = min(4, nblk - j0)
            e = E[cnt["E"] % 3]; cnt["E"] += 1
            full = nb_ if (j0 + nb_ < nblk or lastw == 128) else nb_ - 1
            if full > 0:
                act(P, e[:, 0:full, 0:nq], pcur[:, 0:full, 0:nq], AF.Exp, [pcur], [e])
                tt(P, "pool" if k % 3 == 2 else "dve", e[:, 0:full, 0:nq], e[:, 0:full, 0:nq], maskT[:, j0:j0 + full, 0:nq], ALU.mult, [e, maskT], [e])
            if full < nb_:
                act(P, e[0:lastw, full, 0:nq], pcur[0:lastw, full, 0:nq], AF.Exp, [pcur], [e])
                tt(P, "dve", e[0:lastw, full, 0:nq], e[0:lastw, full, 0:nq], maskT[0:lastw, j0 + full, 0:nq], ALU.mult, [e, maskT], [e])
            po = pO[(h // 4) % 2]
            for jj in range(nb_):
                j = j0 + jj
                wj = 128 if j < nblk - 1 else lastw
                mm(P, po[0:nq, h % 4, :], e[0:wj, jj, 0:nq], Vb[0:wj, j, h, :], j == 0, j == nblk - 1, [e, Vb], [po])
            if j0 + nb_ >= nblk and h % 4 == 3:
                recip(P, rec[0:nq, :], po[0:nq, :, 64], [po], [rec])
                tt(P, "dve", osb[0:nq, (h - 3) * 64:(h + 1) * 64].rearrange("p (h d) -> p h d", h=4), po[0:nq, :, 0:64],
                   rec[0:nq, :].unsqueeze(2).to_broadcast([nq, 4, 64]), ALU.mult, [po, rec], [osb])
            pcur = pnext
        if D.get("OBdbg") is not None:
            P.dma("pool", D["OBdbg"][obt_cols:obt_cols + nq, :], osb[0:nq, :], reads=[osb])
        cp(P, "pool", osbb[0:nq, :], osb[0:nq, :], [osb], [osbb])
        for fc in range(8):
            tr(P, pmT[:, fc, 0:nq], osbb[0:nq, fc * 128:(fc + 1) * 128], g.identb[0:nq, 0:nq], [osbb, g.identb], [pmT])
        cp(P, "act", oT[:, :, 0:nq], pmT[:, :, 0:nq], [pmT], [oT])
        P.dma("pool", D["OBT"].rearrange("(fc p) c -> p fc c", p=128)[:, :, obt_cols:obt_cols + nq], oT[:, :, 0:nq], reads=[oT], writes=[D["OBT_t"]])

    P.dma("sp", kT[:, :, 0:2048], D["KTn"].rearrange("(fc p) c -> p fc c", p=128)[:, :, 0:2048], reads=[D["KTn_t"]], writes=[kT])
    P.dma("sp", kiT[0:64, 0:2048], D["KIT"][:, 0:2048], reads=[D["KIT_t"]], writes=[kiT])
    P.dma("sp", kiT[64:128, 0:2048], D["KIT"][:, 0:2048], reads=[D["KIT_t"]], writes=[kiT])
    for j in range(16):
        load_v_block(j, D["Ptok"][j * 128:(j + 1) * 128, C_V:C_V + 1024], 128)
    NPT = 16
    args = [(128, 128 * i, i + 1, 128, True, 128 * i, i % 2) for i in range(NPT)]
    if args:
        phase1(*args[0])
    for i in range(len(args)):
        if i + 1 < len(args):
            phase1(*args[i + 1])
        phase2(*args[i])
    NSQ = 4
    for q in range(NSQ):
        tok = 2048 + 16 * q
        for j in range(16):
            v = vst[cnt["v"] % 2]; cnt["v"] += 1
            P.dma("sp", v[:, :], D["cache_k"][q, j * 128:(j + 1) * 128, :], writes=[v])
            cp(P, "pool", kst[:, :], v[:, :], [v], [kst])
            for fc in range(8):
                tr(P, pmT[:, fc, :], kst[:, fc * 128:(fc + 1) * 128], g.identb[:], [kst, g.identb], [pmT])
            cp(P, "act", kT[:, :, j * 128:(j + 1) * 128], pmT[:, :, :], [pmT], [kT])
            load_v_block(j, D["cache_v"][q, j * 128:(j + 1) * 128, :], 128)
            P.dma("sp", kis[:, :], D["cache_kidx"][q, j * 128:(j + 1) * 128, :], writes=[kis])
            cp(P, "dve", kisb[:, 0:64], kis[:, :], [kis], [kisb])
            cp(P, "dve", kisb[:, 64:128], kis[:, :], [kis], [kisb])
            tr(P, pmT[:, 0, :], kisb[:, :], g.identb[:], [kisb, g.identb], [pmT])
            cp(P, "act", kiT[:, j * 128:(j + 1) * 128], pmT[:, 0, :], [pmT], [kiT])
        P.dma("sp", kT[:, :, 2048:2064], D["KTn"].rearrange("(fc p) c -> p fc c", p=128)[:, :, tok:tok + 16], reads=[D["KTn_t"]], writes=[kT])
        P.dma("sp", kiT[0:64, 2048:2064], D["KIT"][:, tok:tok + 16], reads=[D["KIT_t"]], writes=[kiT])
        P.dma("sp", kiT[64:128, 2048:2064], D["KIT"][:, tok:tok + 16], reads=[D["KIT_t"]], writes=[kiT])
        load_v_block(16, D["Ptok"][tok:tok + 16, C_V:C_V + 1024], 16)
        phase1(16, tok, 17, 16, False, tok, q % 2)
        phase2(16, tok, 17, 16, False, tok, q % 2)
    P.barrier()
    st.close()


def load_w_bf16(P, dst, src_ap, stg, k0):
    for hf in range(2):
        s = stg[(k0 + hf) % 2]
        P.dma("sp", s[:], src_ap[:, hf * 512:(hf + 1) * 512].rearrange("(kc p) c -> p kc c", p=128), writes=[s])
        cp(P, "pool" if hf else "dve", dst[:, :, hf * 512:(hf + 1) * 512], s[:], [s], [dst])


def stage6(P, g, D):
    st = ExitStack()
    sb = lambda name, shape, dt=F32: P.sb(name, shape, dt, st)
    wpa = sb("wpa", [128, 8, 1024], BF16); wpb = sb("wpb", [128, 8, 1024], BF16); wo = sb("wo", [128, 8, 1024], BF16)
    stg = [sb(f"wstg{i}", [128, 8, 512]) for i in range(2)]
    load_w_bf16(P, wpa, D["w_proj_a"], stg, 0)
    load_w_bf16(P, wpb, D["w_proj_b"], stg, 0)
    load_w_bf16(P, wo, D["w_out"], stg, 0)
    oat = sb("oat", [128, 8, 512], BF16); obt = sb("obt", [128, 8, 512], BF16)
    mT = sb("mT", [128, 8, 512], BF16)
    ga = [sb(f"ga{i}", [128, 512]) for i in range(2)]; gb = [sb(f"gb{i}", [128, 512]) for i in range(2)]
    m1 = [sb(f"m1_{i}", [128, 512]) for i in range(2)]; m2 = [sb(f"m2_{i}", [128, 512]) for i in range(2)]
    g1r = sb("g1r", [128, 1024]); g1s = sb("g1s", [128, 1024])
    P.dma("sp", g1r[:], D["modd"][0:1, 2048:3072].partition_broadcast(128), reads=[D["modd_t"]], writes=[g1r])
    for q in range(4):
        P.dma("sp", g1s[16 * q:16 * q + 16, :], D["modd"][1 + q:2 + q, 2048:3072].partition_broadcast(16), reads=[D["modd_t"]], writes=[g1s])
    xt = [sb(f"x6_{i}", [128, 1024]) for i in range(2)]
    x1 = [sb(f"x1_{i}", [128, 1024]) for i in range(2)]
    pa = [P.ps(f"pa{i}", [128, 512], F32, st) for i in range(2)]
    pb = [P.ps(f"pb{i}", [128, 512], F32, st) for i in range(2)]
    px = [P.ps(f"px{i}", [128, 512], F32, st) for i in range(2)]
    k = 0
    xk = 0
    for (n0, nb) in [(0, 512), (512, 512), (1024, 512), (1536, 512), (2048, 64)]:
        P.dma("sp", oat[:, :, 0:nb], D["OAT"].rearrange("(fc p) c -> p fc c", p=128)[:, :, n0:n0 + nb], reads=[D["OAT_t"]], writes=[oat])
        P.dma("sp", obt[:, :, 0:nb], D["OBT"].rearrange("(fc p) c -> p fc c", p=128)[:, :, n0:n0 + nb], reads=[D["OBT_t"]], writes=[obt])
        for fo in range(8):
            a_, b_ = pa[k % 2], pb[k % 2]
            ga_, gb_, m1_, m2_ = ga[k % 2], gb[k % 2], m1[k % 2], m2[k % 2]
            k += 1
            P.dma("sp", ga_[:, 0:nb], D["GT"][fo * 128:(fo + 1) * 128, n0:n0 + nb], reads=[D["GT_t"]], writes=[ga_])
            P.dma("sp", gb_[:, 0:nb], D["GT"][1024 + fo * 128:1024 + (fo + 1) * 128, n0:n0 + nb], reads=[D["GT_t"]], writes=[gb_])
            for kc in range(8):
                mm(P, a_[:, 0:nb], wpa[:, kc, fo * 128:(fo + 1) * 128], oat[:, kc, 0:nb], kc == 0, kc == 7, [wpa, oat], [a_])
            for kc in range(8):
                mm(P, b_[:, 0:nb], wpb[:, kc, fo * 128:(fo + 1) * 128], obt[:, kc, 0:nb], kc == 0, kc == 7, [wpb, obt], [b_])
            tt(P, "dve", m1_[:, 0:nb], a_[:, 0:nb], ga_[:, 0:nb], ALU.mult, [a_, ga_], [m1_])
            tt(P, "dve", m2_[:, 0:nb], b_[:, 0:nb], gb_[:, 0:nb], ALU.mult, [b_, gb_], [m2_])
            tt(P, "pool", mT[:, fo, 0:nb], m1_[:, 0:nb], m2_[:, 0:nb], ALU.add, [m1_, m2_], [mT])
        for t0 in range(0, nb, 128):
            n = min(128, nb - t0)
            x_, x1_ = xt[xk % 2], x1[xk % 2]; xk += 1
            P.dma("sp", x_[0:n, :], D["xin"][n0 + t0:n0 + t0 + n, :], writes=[x_])
            g1 = g1r if n0 < 2048 else g1s
            for hf in range(2):
                p = px[hf]
                cs = slice(hf * 512, (hf + 1) * 512)
                for kc in range(8):
                    mm(P, p[0:n, :], mT[:, kc, t0:t0 + n], wo[:, kc, cs], kc == 0, kc == 7, [mT, wo], [p])
                tt(P, "dve", x1_[0:n, cs], p[0:n, :], g1[0:n, cs], ALU.mult, [p, g1], [x1_])
                tt(P, "pool", x1_[0:n, cs], x1_[0:n, cs], x_[0:n, cs], ALU.add, [x1_, x_], [x1_])
            P.dma("pool", D["X1"][n0 + t0:n0 + t0 + n, :], x1_[0:n, :], reads=[x1_], writes=[D["X1_t"]])
    P.barrier()
    st.close()


def stage6b(P, g, D):
    st = ExitStack()
    h2T = P.sb("h2T", [128, 8, NT], BF16, st)
    norm_to_featmajor(P, g, D, D["X1"], h2T, g.scale2, 24)
    st2 = ExitStack()
    sb = lambda name, shape, dt=F32: P.sb(name, shape, dt, st2)
    P.dma("pool", D["H2T"].rearrange("(fc p) c -> p fc c", p=128)[:, :, :], h2T[:], reads=[h2T], writes=[D["H2T_t"]])
    wq = sb("wq", [128, 8, 1024], BF16)
    stg = [sb(f"wstgq{i}", [128, 8, 512]) for i in range(2)]
    load_w_bf16(P, wq, D["w_pq"], stg, 0)
    qs = [sb(f"qs{i}", [128, 512], BF16) for i in range(2)]
    pq = [P.ps(f"pq{i}", [128, 512], F32, st2) for i in range(2)]
    k = 0
    for (n0, nb) in [(0, 512), (512, 512), (1024, 512), (1536, 512), (2048, 64)]:
        for fo in range(8):
            p, s = pq[k % 2], qs[k % 2]; k += 1
            for kc in range(8):
                mm(P, p[:, 0:nb], wq[:, kc, fo * 128:(fo + 1) * 128], h2T[:, kc, n0:n0 + nb], kc == 0, kc == 7, [wq, h2T], [p])
            cp(P, "act", s[:, 0:nb], p[:, 0:nb], [p], [s])
            P.dma("pool", D["QPT"][fo * 128:(fo + 1) * 128, n0:n0 + nb], s[:, 0:nb], reads=[s], writes=[D["QPT_t"]])
    P.barrier()
    st2.close()
    st.close()


def vmax(P, out, in_, reads, writes):
    return P.op("dve", lambda e: e.max(out=out, in_=in_), reads=reads, writes=writes)


def mrep(P, out, rep, vals, reads, writes):
    return P.op("dve", lambda e: e.match_replace(out=out, in_to_replace=rep, in_values=vals, imm_value=NEG), reads=reads, writes=writes)


def stage7_prep(P, g, D):
    st = ExitStack()
    sb = lambda name, shape, dt=F32: P.sb(name, shape, dt, st)
    uf = [sb(f"uf{i}", [128, 1024]) for i in range(2)]
    ub = [sb(f"ub{i}", [128, 1024], BF16) for i in range(2)]
    vf = [sb(f"vf{i}", [128, 1024]) for i in range(2)]
    vb = [sb(f"vb{i}", [128, 1024], BF16) for i in range(2)]
    sT = [sb(f"usT{i}", [128, 8, 512], BF16) for i in range(2)]
    pT = [P.ps(f"upT{i}", [128, 8, 128], BF16, st) for i in range(2)]
    NE = 128
    for et in range(NE):
        u, ubb, v, vbb = uf[et % 2], ub[et % 2], vf[et % 2], vb[et % 2]
        P.dma("sp", u[:], D["peer_u"][et * 128:(et + 1) * 128, :], writes=[u])
        P.dma("sp", v[:], D["peer_v"][et * 128:(et + 1) * 128, :], writes=[v])
        cp(P, "dve", ubb[:], u[:], [u], [ubb])
        cp(P, "pool", vbb[:], v[:], [v], [vbb])
        P.dma("pool", D["Vbf"].rearrange("(g j p) d -> g p j d", j=4, p=128)[et // 4, :, et % 4, :], vbb[:], reads=[vbb], writes=[D["Vbf_t"]])
        p = pT[et % 2]
        s = sT[(et // 4) % 2]
        for kc in range(8):
            tr(P, p[:, kc, :], ubb[:, kc * 128:(kc + 1) * 128], g.identb[:], [ubb, g.identb], [p])
        cp(P, "act", s[:, :, (et % 4) * 128:(et % 4 + 1) * 128], p[:], [p], [s])
        if et % 4 == 3:
            e0 = (et - 3) * 128
            P.dma("pool", D["UT"].rearrange("(g p) (kc e) -> g p kc e", p=128, kc=8)[e0 // 512, :, :, :], s[:], reads=[s], writes=[D["UT_t"]])
    P.barrier()
    st.close()


def stage7_peer(P, g, D):
    st = ExitStack()
    sb = lambda name, shape, dt=F32: P.sb(name, shape, dt, st)
    keysT = sb("keysT", [128, 8, 128], BF16)
    kt = sb("kt_f", [128, 128])
    pT = [P.ps(f"ppT{i}", [128, 8, 128], BF16, st) for i in range(1)]
    ps12 = P.ps("ps12", [128, 4, 128], F32, st)
    for h in range(8):
        P.dma("sp", kt[:, 0:64], D["peer_keys"][h, 0, :, :], writes=[kt])
        P.dma("sp", kt[:, 64:128], D["peer_keys"][h, 1, :, :], writes=[kt])
        tr(P, ps12[:, h % 4, :], kt[:], g.identf[:], [kt, g.identf], [ps12])
        cp(P, "act", keysT[:, h, :], ps12[:, h % 4, :], [ps12], [keysT])
    g2r = sb("g2r", [128, 1024]); g2s = sb("g2s", [128, 1024])
    P.dma("sp", g2r[:], D["modd"][0:1, 5120:6144].partition_broadcast(128), reads=[D["modd_t"]], writes=[g2r])
    for q in range(4):
        P.dma("sp", g2s[16 * q:16 * q + 16, :], D["modd"][1 + q:2 + q, 5120:6144].partition_broadcast(16), reads=[D["modd_t"]], writes=[g2s])
    h2t = [sb(f"h2t{i}", [128, 8, 128], BF16) for i in range(2)]; qpt = [sb(f"qpt{i}", [128, 8, 128], BF16) for i in range(2)]
    S12 = sb("S12", [128, 16, 128]); S1P = sb("S1P", [128, 8, 128]); srep = sb("srep", [128, 16, 128])
    T16 = sb("T16", [128, 16, 16]); cand = sb("cand", [128, 8, 256]); crep = sb("crep", [128, 8, 256])
    top16c = sb("top16c", [128, 8, 16]); ez = sb("ez", [128, 8, 16]); Z = sb("Zp", [128, 8]); cinv = sb("cinv", [128, 8]); lnc = sb("lnc", [128, 8]); thr = sb("pthr", [128, 8]); mtiny = sb("mtiny", [128, 1])
    mset(P, "dve", mtiny[:], -1e-5, [mtiny])
    SUBI = 8
    NSUB = 128 // SUBI
    W_ = SUBI * 128
    zb = [sb(f"zb{i}", [128, W_]) for i in range(3)]; eb = [sb(f"eb{i}", [128, W_]) for i in range(3)]
    gmb = [sb(f"gmb{i}", [128, W_], BF16) for i in range(3)]
    Gp = [P.ps(f"pGp{i}", [128, 512], F32, st) for i in range(2)]
    Aqs = [sb(f"Aq{i}", [128, W_], BF16) for i in range(3)]; GAs = [sb(f"GAq{i}", [128, W_], BF16) for i in range(2)]
    GATs = [sb(f"GAT{i}", [128, SUBI, 128], BF16) for i in range(2)]
    ub = [sb(f"pub{i}", [128, 8, 512], BF16) for i in range(3)]
    vb = [sb(f"pvb{i}", [128, 4, 1024], BF16) for i in range(3)]
    x1t = [sb(f"px1_{i}", [128, 1024]) for i in range(2)]; yt = sb("pyt", [128, 1024])
    pA = [P.ps(f"ppA{i}", [128, 512], F32, st) for i in range(2)]
    po = [P.ps(f"ppo{i}", [128, 512], F32, st) for i in range(2)]
    uk = 0; vk = 0; ak = 0; zk = 0; tk = 0; gk = 0
    NTL = 17
    for ti, (r0, n) in enumerate(TILES[:NTL]):
        h2t_, qpt_, x1t_ = h2t[ti % 2], qpt[ti % 2], x1t[ti % 2]
        P.dma("sp", h2t_[:, :, 0:n], D["H2T"].rearrange("(fc p) c -> p fc c", p=128)[:, :, r0:r0 + n], reads=[D["H2T_t"]], writes=[h2t_])
        P.dma("sp", qpt_[:, :, 0:n], D["QPT"].rearrange("(fc p) c -> p fc c", p=128)[:, :, r0:r0 + n], reads=[D["QPT_t"]], writes=[qpt_])
        P.dma("sp", x1t_[0:n, :], D["X1"][r0:r0 + n, :], reads=[D["X1_t"]], writes=[x1t_])
        for hg in range(4):
            p = ps12
            for j in range(4):
                hp = hg * 4 + j
                h, pp = hp // 2, hp % 2
                pb = pp * 64
                mm(P, p[0:n, j, :], qpt_[pb:pb + 64, h, 0:n], keysT[pb:pb + 64, h, :], True, True, [qpt_, keysT], [p])
            cp(P, "act", S12[0:n, hg * 4:(hg + 1) * 4, :], p[0:n, :, :], [p], [S12])
        for hp in range(16):
            vmax(P, T16[0:n, hp, 0:8], S12[0:n, hp, :], [S12], [T16.sub(("a", hp))])
        for hp in range(16):
            mrep(P, srep[0:n, hp, :], T16[0:n, hp, 0:8], S12[0:n, hp, :], [S12, T16.sub(("a", hp))], [srep.sub(hp)])
        for hp in range(16):
            vmax(P, T16[0:n, hp, 8:16], srep[0:n, hp, :], [srep.sub(hp)], [T16.sub(("b", hp))])
        T16all = [T16.sub((x, hp)) for x in "ab" for hp in range(16)]
        T16v = T16[0:n, :, :].rearrange("p (h t) k -> p h t k", t=2)
        tt(P, "dve", cand[0:n, :, :].rearrange("p h (i j) -> p h i j", i=16), T16v[:, :, 0, :].unsqueeze(3).to_broadcast([n, 8, 16, 16]),
           T16v[:, :, 1, :].unsqueeze(2).to_broadcast([n, 8, 16, 16]), ALU.add, T16all, [cand])
        for h in range(8):
            vmax(P, top16c[0:n, h, 0:8], cand[0:n, h, :], [cand], [top16c.sub(("a", h))])
        for h in range(8):
            mrep(P, crep[0:n, h, :], top16c[0:n, h, 0:8], cand[0:n, h, :], [cand, top16c.sub(("a", h))], [crep.sub(h)])
        for h in range(8):
            vmax(P, top16c[0:n, h, 8:16], crep[0:n, h, :], [crep.sub(h)], [top16c.sub(("b", h))])
        tcall = [top16c.sub((x, h)) for x in "ab" for h in range(8)]
        tau = top16c[0:n, :, 15:16]
        tt(P, "dve", ez[0:n, :, :], top16c[0:n, :, :], tau.to_broadcast([n, 8, 16]), ALU.subtract, tcall, [ez])
        act(P, ez[0:n, :, :], ez[0:n, :, :], AF.Exp, [ez], [ez])
        red(P, "dve", Z[0:n, :], ez[0:n, :, :], ALU.add, [ez], [Z])
        recip(P, cinv[0:n, :], Z[0:n, :], [Z], [cinv])
        act(P, lnc[0:n, :], cinv[0:n, :], AF.Ln, [cinv], [lnc])
        S12v = S12[0:n, :, :].rearrange("p (h t) k -> p h t k", t=2)
        tt(P, "dve", S1P[0:n, :, :], S12v[:, :, 0, :], tau.to_broadcast([n, 8, 128]), ALU.subtract, [S12] + tcall, [S1P])
        def emitA(ib):
            nonlocal uk, ak
            Aq = Aqs[ib % 3]
            for eg in range(W_ // 512):
                e0 = ib * W_ + eg * 512
                u = ub[uk % 3]; uk += 1
                P.dma("sp", u[:], D["UT"].rearrange("(g p) (kc e) -> g p kc e", p=128, kc=8)[e0 // 512, :, :, :], reads=[D["UT_t"]], writes=[u])
                p = pA[ak % 2]; ak += 1
                for kc in range(8):
                    mm(P, p[0:n, :], h2t_[:, kc, 0:n], u[:, kc, :], kc == 0, kc == 7, [h2t_, u], [p])
                act(P, Aq[0:n, eg * 512:(eg + 1) * 512], p[0:n, :], AF.Gelu_apprx_tanh, [p], [Aq])

        def emitZ(ib, h):
            nonlocal zk
            z_, e_ = zb[zk % 3], eb[zk % 3]; zk += 1
            z3 = z_[0:n, :].rearrange("p (i j) -> p i j", i=SUBI)
            tt(P, "dve", z3, S1P[0:n, h, ib * SUBI:(ib + 1) * SUBI].unsqueeze(2).to_broadcast([n, SUBI, 128]),
               S12v[:, h, 1, :].unsqueeze(1).to_broadcast([n, SUBI, 128]), ALU.add, [S1P, S12], [z_])
            act(P, e_[0:n, :], z_[0:n, :], AF.Exp, [z_, lnc], [e_], bias=lnc[0:n, h:h + 1])
            return z_, e_

        emitA(0)
        emitA(1)
        pend = emitZ(0, 0)
        for ib in range(NSUB):
            Aq, GA, GAT = Aqs[ib % 3], GAs[ib % 2], GATs[ib % 2]
            if ib + 2 < NSUB:
                emitA(ib + 2)
            for h in range(8):
                z_, e_ = pend
                if h + 1 < 8:
                    pend = emitZ(ib, h + 1)
                elif ib + 1 < NSUB:
                    pend = emitZ(ib + 1, 0)
                gm = gmb[gk % 3]; gk += 1
                stt(P, "dve", gm[0:n, :], z_[0:n, :], -1e-5, e_[0:n, :], ALU.is_ge, ALU.mult, [z_, e_], [gm])
                for cg in range(W_ // 512):
                    mm(P, Gp[cg][0:n, :], g.identb[0:n, 0:n], gm[0:n, cg * 512:(cg + 1) * 512], h == 0, h == 7, [gm, g.identb], [Gp[cg]])
            for cg in range(W_ // 512):
                tt(P, "dve", GA[0:n, cg * 512:(cg + 1) * 512], Gp[cg][0:n, :], Aq[0:n, cg * 512:(cg + 1) * 512], ALU.mult, [Gp[cg], Aq], [GA])
            for j0 in range(0, SUBI, 8):
                pt = pT[0]; tk += 1
                for jj in range(8):
                    et = j0 + jj
                    tr(P, pt[:, jj, 0:n], GA[0:n, et * 128:(et + 1) * 128], g.identb[0:n, 0:n], [GA, g.identb], [pt])
                cp(P, "act", GAT[:, j0:j0 + 8, 0:n], pt[:, :, 0:n], [pt], [GAT])
            for vg in range(SUBI // 4):
                v = vb[vk % 3]; vk += 1
                e0 = (ib * SUBI + vg * 4) * 128
                P.dma("pool", v[:], D["Vbf"].rearrange("(g j p) d -> g p j d", j=4, p=128)[e0 // 512, :, :, :], reads=[D["Vbf_t"]], writes=[v])
                for j in range(4):
                    et = vg * 4 + j
                    first = (ib == 0 and et == 0)
                    last = (ib == NSUB - 1 and et == SUBI - 1)
                    for hf in range(2):
                        mm(P, po[hf][0:n, :], GAT[:, et, 0:n], v[:, j, hf * 512:(hf + 1) * 512], first, last, [GAT, v], [po[hf]])
        g2 = g2r if r0 < 2048 else g2s
        for hf in range(2):
            cs = slice(hf * 512, (hf + 1) * 512)
            tt(P, "dve", yt[0:n, cs], po[hf][0:n, :], g2[0:n, cs], ALU.mult, [po[hf], g2], [yt])
        if D.get("PEERdbg") is not None:
            P.dma("pool", D["PEERdbg"][r0:r0 + n, :], yt[0:n, :], reads=[yt])
        tt(P, "pool", yt[0:n, :], yt[0:n, :], x1t_[0:n, :], ALU.add, [yt, x1t_], [yt])
        P.dma("pool", D["y"][r0:r0 + n, :], yt[0:n, :], reads=[yt], writes=[D["y_t"]])
    P.barrier()
    st.close()


def declare(nc, P, D, debug=False):
    def din(name, shape, dt=F32):
        D[name] = nc.dram_tensor(name, list(shape), dt, kind="ExternalInput").ap()

    def dscr(name, shape, dt=F32, kind="Internal"):
        D[name] = nc.dram_tensor(name, list(shape), dt, kind=("ExternalOutput" if debug else kind)).ap()
        D[name + "_t"] = T(None, name)

    din("ident", [128, 128]); din("cin", [5, 1024]); din("xin", [NT, 1024])
    din("w_ada", [1024, 6144]); din("b_ada", [1, 6144]); din("norm1_w", [1024]); din("norm2_w", [1024]); din("b_gate", [2048])
    din("w_in", [1024, 9064]); din("cos", [NT, 32]); din("sin", [NT, 32]); din("k_norm_w", [1, 64]); din("q_norm_w", [1, 64])
    din("mu_rw", [1, RW_IN])
    for nm in ["w0", "a0", "k_k", "k_a", "r_k", "lnx_w", "lnx_b"]:
        din(nm, [1, 1024])
    din("w_up", [64, 1024]); din("a_up", [64, 1024]); din("g_up", [160, 1024])
    din("sshift", [4, RW_IN]); din("swkv", [4, 16, 64, 64])
    for nm in ["m_tri", "m_ones", "m_sl", "m_su", "m_u"]:
        din(nm, [128, 128])
    din("m_valid", [128, 2])
    dscr("modd", [5, 6144]); dscr("Ptok", [NT, TOKW]); dscr("GT", [2048, NT])
    dscr("BC", [UC, 16])
    for nm in ["Vs", "KPs", "BPs", "Gs"]:
        dscr(nm, [UC, 1024])
    for nm in ["RT", "AT", "KT", "BT", "GCT"]:
        dscr(nm, [1024, UC])
    for nm in ["OAT", "OBT", "QT", "KTn", "H2T", "QPT"]:
        dscr(nm, [1024, NT], BF16)
    dscr("QIT", [512, NT], BF16); dscr("KIT", [64, NT], BF16); dscr("WI", [NT, 8])
    dscr("X1", [NT, 1024], F32, "Internal")
    dscr("UT", [4096, 4096], BF16); dscr("Vbf", [16384, 1024], BF16)
    din("cache_k", [4, 2048, 1024]); din("cache_v", [4, 2048, 1024]); din("cache_kidx", [4, 2048, 64]); din("m_pow2", [128, NBIS])
    for nm in ["w_proj_a", "w_proj_b", "w_out", "w_pq"]:
        din(nm, [1024, 1024])
    din("peer_keys", [8, 2, 128, 64]); din("peer_u", [16384, 1024]); din("peer_v", [16384, 1024])
    for nm, shp in [("y", [NT, 1024]), ("wkv", [5 * 16 * 64, 64]), ("kout", [NT, 1024]), ("vout", [NT, 1024]), ("kidx", [NT, 64]), ("shift", [5, RW_IN])]:
        D[nm] = nc.dram_tensor(nm, shp, F32, kind="ExternalOutput").ap()
        D[nm + "_t"] = T(None, nm)


def host_consts():
    inv = (10000.0 ** (-np.arange(32, dtype=np.float32) / 32)).astype(np.float32)
    pos = np.concatenate([np.arange(2048), np.tile(2048 + np.arange(16), 4)]).astype(np.float32)
    ang = pos[:, None] * inv[None, :]
    idx = np.arange(128)
    same = (idx[:, None] // 64) == (idx[None, :] // 64)
    f32 = lambda a: np.ascontiguousarray(a, dtype=np.float32)
    valid = np.ones((128, 2), np.float32)
    valid[:, 1] = ((idx % 64) < 16)
    return {"ident": np.eye(128, dtype=np.float32), "cos": np.cos(ang).astype(np.float32), "sin": np.sin(ang).astype(np.float32),
            "m_tri": f32(same & (idx[:, None] <= idx[None, :])), "m_ones": f32(same), "m_sl": f32(same & (idx[:, None] > idx[None, :])),
            "m_su": f32(same & (idx[:, None] < idx[None, :])), "m_u": f32(same & (idx[:, None] <= idx[None, :])), "m_valid": valid,
            "m_pow2": np.tile((2.0 ** -(np.arange(NBIS) + 1.0)).astype(np.float32)[None, :], (128, 1))}


def run_all(P, g, D):
    stage0(P, g, D)
    st = ExitStack()
    hT = P.sb("hT", [128, 8, NT], BF16, st)
    norm_to_featmajor(P, g, D, D["xin"], hT, g.scale1, 0)
    stage2(P, g, D, hT)
    st.close()
    stage3_dsa(P, g, D)
    stage3_rw(P, g, D)
    stage4_scan(P, g, D)
    stage5_attn(P, g, D)
    stage6(P, g, D)
    stage6b(P, g, D)
    stage7_prep(P, g, D)
    stage7_peer(P, g, D)


def core_inputs(inp, c, consts, local=False):
    f = lambda a: np.ascontiguousarray(np.asarray(a, dtype=np.float32))
    m = dict(consts)
    W = lambda k: f(np.asarray(inp[k])[0])
    m.update({"w_ada": W("w_ada"), "b_ada": f(inp["b_ada"]), "norm1_w": W("norm1_w"), "norm2_w": W("norm2_w"), "b_gate": W("b_gate"), "w_in": W("w_in"),
              "k_norm_w": f(inp["k_norm_w"]), "q_norm_w": f(inp["q_norm_w"]), "mu_rw": f(inp["mu_rw"]), "w0": f(inp["w0"]), "a0": f(inp["a0"]),
              "k_k": f(inp["k_k"]), "k_a": f(inp["k_a"]), "r_k": f(np.asarray(inp["r_k"]).reshape(1, 1024)), "lnx_w": f(inp["lnx_w"]), "lnx_b": f(inp["lnx_b"]),
              "w_up": W("w_up"), "a_up": W("a_up"), "g_up": W("g_up"), "w_proj_a": W("w_proj_a"), "w_proj_b": W("w_proj_b"), "w_out": W("w_out"),
              "w_pq": W("w_pq"), "peer_keys": W("peer_keys"), "peer_u": W("peer_u"), "peer_v": W("peer_v")})
    pc = 0 if local else c
    sl = slice(0, 4) if local else slice(4 * c, 4 * c + 4)
    m["xin"] = f(np.concatenate([np.asarray(inp["x_prompt"])[pc], np.asarray(inp["x_sample"])[sl].reshape(64, 1024)], 0))
    m["cin"] = f(np.concatenate([np.asarray(inp["c_prompt"])[pc:pc + 1], np.asarray(inp["c_sample"])[sl]], 0))
    m["sshift"] = f(np.asarray(inp["state_shift"])[0][sl])
    m["swkv"] = f(np.asarray(inp["state_wkv"])[0][sl])
    m["cache_k"] = f(np.asarray(inp["cache_k"])[0][sl].reshape(4, 2048, 1024))
    m["cache_v"] = f(np.asarray(inp["cache_v"])[0][sl].reshape(4, 2048, 1024))
    m["cache_kidx"] = f(np.asarray(inp["cache_kidx"])[0][sl])
    return m


def build_program():
    nc = bass.Bass("TRN2", target_bir_lowering=False)
    P = Prog(nc)
    D = {}
    declare(nc, P, D)
    g = G()
    g.epsc = P.sb("epsc", [128, 1], F32)
    mset(P, "dve", g.epsc[:], EPS, [g.epsc])
    run_all(P, g, D)
    P.emit()
    return nc


def kernel(**inp):
    nc = build_program()
    consts = host_consts()
    in_maps = [core_inputs(inp, c, consts) for c in range(8)]
    res = run_bass_kernel_spmd(nc, in_maps, core_ids=list(range(8)))
    R = res.results
    cat = lambda name, sl, shp: np.stack([R[c][name][sl].reshape(shp) for c in range(8)], 0)
    y_p = cat("y", slice(0, 2048), (2048, 1024))
    y_s = cat("y", slice(2048, NT), (4, 16, 1024)).reshape(32, 16, 1024)
    wkv_p = cat("wkv", slice(0, 1024), (16, 64, 64))[None]
    wkv_s = cat("wkv", slice(1024, 5120), (4, 16, 64, 64)).reshape(32, 16, 64, 64)[None]
    sh_p = cat("shift", slice(0, 1), (RW_IN,))[None]
    sh_s = cat("shift", slice(1, 5), (4, RW_IN)).reshape(32, RW_IN)[None]
    k_p = cat("kout", slice(0, 2048), (2048, 16, 64))[None]
    k_s = cat("kout", slice(2048, NT), (4, 16, 16, 64)).reshape(32, 16, 16, 64)[None]
    v_p = cat("vout", slice(0, 2048), (2048, 16, 64))[None]
    v_s = cat("vout", slice(2048, NT), (4, 16, 16, 64)).reshape(32, 16, 16, 64)[None]
    ki_p = cat("kidx", slice(0, 2048), (2048, 64))[None]
    ki_s = cat("kidx", slice(2048, NT), (4, 16, 64)).reshape(32, 16, 64)[None]
    return (y_p, y_s, wkv_p, sh_p, k_p, v_p, ki_p, wkv_s, sh_s, k_s, v_s, ki_s)
```
